# Optimizing a Trainium2 kernel written in Bass

```python
import jax
import jax.numpy as jnp
from jax import lax
import numpy as np

D_MODEL = 2048
BATCH = 1
SEQ = 16384
DEPTH = 1
DEC_BATCH = 32
DEC_SEQ = 4
PAST_LEN = 16384
PAGE_SIZE = 128

HEAD_DIM = 128
ATT_GROUPS = ((128, 1), (512, 4), (2048, 16))
H_G = 4
N_ATT_HEADS = H_G * len(ATT_GROUPS)
ATT_W = N_ATT_HEADS * HEAD_DIM
ATT_OUT = H_G * HEAD_DIM
ROT_DIM = HEAD_DIM // 4
ROPE_THETA = 500000.0
Q_BLOCK = 128

HG_HEADS = 8
HG_DK = 128
HG_DV = 128
HG_WK = HG_HEADS * HG_DK
HG_WV = HG_HEADS * HG_DV
HG_CHUNK = 64

D_FF = 5632
CONV_W = 3

N_MOD = 6
EPS = 1e-6

IN_SIZES = (ATT_W, ATT_W, ATT_W, HG_WK, HG_WK, HG_WV, HG_WV, D_MODEL, D_MODEL)
IN_TOTAL = sum(IN_SIZES)
IN_SPLITS = tuple(int(s) for s in np.cumsum(IN_SIZES)[:-1])

kernel_name = 'hybrid_dilated_attn_hgrn2_convffn_step'


def rms_norm(x, w):
    xf = x.astype(jnp.float32)
    y = xf * lax.rsqrt(jnp.mean(xf * xf, axis=-1, keepdims=True) + EPS)
    return (y * w.astype(jnp.float32)).astype(x.dtype)


def partial_rope(x, pos):
    half = ROT_DIM // 2
    inv_freq = ROPE_THETA ** (-jnp.arange(half, dtype=jnp.float32) * 2.0 / ROT_DIM)
    ang = pos.astype(jnp.float32)[:, None] * inv_freq[None, :]
    cos = jnp.cos(ang)[None, :, None, :]
    sin = jnp.sin(ang)[None, :, None, :]
    xr = x[..., :ROT_DIM].astype(jnp.float32)
    x1, x2 = xr[..., :half], xr[..., half:]
    rot = jnp.concatenate([x1 * cos - x2 * sin, x2 * cos + x1 * sin], axis=-1).astype(x.dtype)
    return jnp.concatenate([rot, x[..., ROT_DIM:]], axis=-1)


def dilated_attention(q_groups, ctx_groups, prefix_lens, start):
    B, T = q_groups[0].shape[:2]
    qb = min(Q_BLOCK, T)
    n_blk = -(-T // qb)
    pad = n_blk * qb - T
    qs = [jnp.pad(q, ((0, 0), (0, pad), (0, 0), (0, 0))) for q in q_groups]
    scale = HEAD_DIM ** -0.5

    def block(n):
        t = n * qb + jnp.arange(qb, dtype=jnp.int32)
        lses, outs = [], []
        for (win, dil), q, ctx, P in zip(ATT_GROUPS, qs, ctx_groups, prefix_lens):
            j = jnp.arange(win // dil + 1, dtype=jnp.int32)
            row = P + t[:, None] - dil * j[None, :]
            valid = (row >= 0) & (row + (start - P) >= 0)
            kv = jnp.take(ctx, jnp.clip(row, 0, ctx.shape[1] - 1), axis=1).astype(jnp.float32)
            qblk = lax.dynamic_slice_in_dim(q, n * qb, qb, axis=1).astype(jnp.float32)
            s = jnp.einsum('bqhd,bqkhd->bhqk', qblk, kv[:, :, :, 0]) * scale
            s = jnp.where(valid[None, None], s, -jnp.inf)
            m = jnp.max(s, axis=-1)
            p = jnp.exp(s - m[..., None])
            l = jnp.sum(p, axis=-1)
            o = jnp.einsum('bhqk,bqkhd->bqhd', p, kv[:, :, :, 1]) / jnp.transpose(l, (0, 2, 1))[..., None]
            lses.append(m + jnp.log(l))
            outs.append(o)
        wts = jax.nn.softmax(jnp.stack(lses), axis=0)
        wts = jnp.transpose(wts, (0, 1, 3, 2))[..., None]
        return jnp.sum(wts * jnp.stack(outs), axis=0).astype(q_groups[0].dtype)

    out = lax.map(block, jnp.arange(n_blk, dtype=jnp.int32))
    return jnp.moveaxis(out, 0, 1).reshape(B, n_blk * qb, H_G, HEAD_DIM)[:, :T]


def hgrn2_recurrence(q, k, v, logf, s0):
    B, T, H, DK = q.shape
    DV = v.shape[-1]
    C = min(HG_CHUNK, T)
    n = -(-T // C)
    pad = n * C - T

    def chunks(a):
        a = jnp.pad(a, ((0, 0), (0, pad), (0, 0), (0, 0)))
        return jnp.moveaxis(a.reshape(B, n, C, H, a.shape[-1]), 1, 0)

    causal = jnp.tril(jnp.ones((C, C), dtype=bool))

    def step(S, inp):
        qc, kc, vc, gc = inp
        b = jnp.cumsum(gc, axis=1)
        diff = b[:, :, None] - b[:, None]
        decay = jnp.exp(jnp.where(causal[None, :, :, None, None], diff, -jnp.inf))
        att = jnp.einsum('bthd,btshd,bshd->bhts', qc, decay, kc)
        o = (jnp.einsum('bhts,bshv->bthv', att, vc)
             + jnp.einsum('bthd,bhdv->bthv', qc * jnp.exp(b), S))
        b_last = b[:, -1]
        S = (jnp.exp(b_last)[..., None] * S
             + jnp.einsum('bshd,bshv->bhdv', kc * jnp.exp(b_last[:, None] - b), vc))
        return S, o

    S, o = lax.scan(step, s0, (chunks(q), chunks(k), chunks(v), chunks(logf)))
    o = jnp.moveaxis(o, 0, 1).reshape(B, n * C, H, DV)[:, :T]
    return o, S


def conv_ffn(h, buf, w_a, w_b, conv_w, conv_b, w_down):
    T = h.shape[1]
    a = h @ w_a
    ctx = jnp.concatenate([buf.astype(a.dtype), a], axis=1)
    u = conv_b + sum(ctx[:, i:i + T] * conv_w[i] for i in range(CONV_W))
    y = jax.nn.silu(u) * (h @ w_b)
    return y @ w_down, ctx[:, T:]


def decoder_layer(x, c, start, kv_prefixes, hg_s0, conv_buf, lower_bound,
                  w_ada, b_ada, norm1_w, w_in, hg_norm_w, w_pa, w_pb, w_o,
                  norm2_w, w_ffn_a, w_ffn_b, conv_w, conv_b, w_ffn_down):
    B, T, _ = x.shape
    mod = jax.nn.silu(c) @ w_ada + b_ada
    sh1, sc1, g1, sh2, sc2, g2 = jnp.split(mod[:, None, :], N_MOD, axis=-1)

    h = rms_norm(x, norm1_w) * (1 + sc1) + sh1
    z = h @ w_in
    qa, ka, va, qh, fh, ih, gh, ga, gb = jnp.split(z, IN_SPLITS, axis=-1)

    pos = start + jnp.arange(T, dtype=jnp.int32)
    qa = partial_rope(qa.reshape(B, T, N_ATT_HEADS, HEAD_DIM), pos)
    ka = partial_rope(ka.reshape(B, T, N_ATT_HEADS, HEAD_DIM), pos)
    kv = jnp.stack([ka, va.reshape(B, T, N_ATT_HEADS, HEAD_DIM)], axis=2)
    q_groups, ctxs, prefix_lens, new_kv = [], [], [], []
    for g, (win, dil) in enumerate(ATT_GROUPS):
        hs = slice(g * H_G, (g + 1) * H_G)
        kv_g = kv[:, :, :, hs]
        prefix = kv_prefixes[g]
        if prefix is None:
            ctx, P, keep = kv_g, 0, min(win, T)
        else:
            ctx = jnp.concatenate([prefix.astype(kv_g.dtype), kv_g], axis=1)
            P = prefix.shape[1]
            keep = P
        q_groups.append(qa[:, :, hs])
        ctxs.append(ctx)
        prefix_lens.append(P)
        new_kv.append(ctx[:, ctx.shape[1] - keep:])
    o_att = dilated_attention(q_groups, ctxs, prefix_lens, start).reshape(B, T, ATT_OUT)

    qf = jax.nn.silu(qh.astype(jnp.float32)).reshape(B, T, HG_HEADS, HG_DK)
    fg = lower_bound + (1.0 - lower_bound) * jax.nn.sigmoid(fh.astype(jnp.float32))
    kf = (1.0 - fg).reshape(B, T, HG_HEADS, HG_DK)
    logf = jnp.log(fg).reshape(B, T, HG_HEADS, HG_DK)
    vf = ih.astype(jnp.float32).reshape(B, T, HG_HEADS, HG_DV)
    o_hg, s_new = hgrn2_recurrence(qf, kf, vf, logf, hg_s0.astype(jnp.float32))
    o_hg = rms_norm(o_hg, hg_norm_w) * jax.nn.silu(gh.astype(jnp.float32).reshape(B, T, HG_HEADS, HG_DV))
    o_hg = o_hg.reshape(B, T, HG_WV).astype(x.dtype)

    y_mix = jax.nn.sigmoid(ga) * (o_att @ w_pa) + jax.nn.sigmoid(gb) * (o_hg @ w_pb)
    x = x + g1 * (y_mix @ w_o)

    h2 = rms_norm(x, norm2_w) * (1 + sc2) + sh2
    f, new_buf = conv_ffn(h2, conv_buf, w_ffn_a, w_ffn_b, conv_w, conv_b, w_ffn_down)
    x = x + g2 * f
    return x, (new_kv[0], new_kv[1], new_kv[2], s_new, new_buf)


def setup_inputs(seed: int = 0) -> dict:
    key = jax.random.key(seed)
    keys = jax.random.split(key, 32)

    def nrm(i, shape, scale=1.0):
        return jax.random.normal(keys[i], shape, jnp.float32) * scale

    d = D_MODEL
    l0 = min(ATT_GROUPS[0][0], PAST_LEN)
    l1 = min(ATT_GROUPS[1][0], PAST_LEN)
    l2 = min(ATT_GROUPS[2][0], PAST_LEN)
    return {
        'x_prompt': nrm(0, (BATCH, SEQ, d)),
        'x_sample': nrm(1, (DEC_BATCH, DEC_SEQ, d)),
        'cache_kv_w128': nrm(2, (DEPTH, DEC_BATCH, l0, 2, H_G, HEAD_DIM)),
        'cache_kv_w512': nrm(3, (DEPTH, DEC_BATCH, l1, 2, H_G, HEAD_DIM)),
        'cache_kv_w2048': nrm(4, (DEPTH, DEC_BATCH, l2, 2, H_G, HEAD_DIM)),
        'state_hgrn': nrm(5, (DEPTH, DEC_BATCH, HG_HEADS, HG_DK, HG_DV), 0.5),
        'state_conv': nrm(6, (DEPTH, DEC_BATCH, CONV_W - 1, D_FF)),
        'c_prompt': nrm(7, (BATCH, d)),
        'c_sample': nrm(8, (DEC_BATCH, d)),
        'w_ada': nrm(9, (DEPTH, d, N_MOD * d), 0.5 * d ** -0.5),
        'b_ada': nrm(10, (DEPTH, N_MOD * d), 0.02),
        'norm1_w': 1.0 + nrm(11, (DEPTH, d), 0.02),
        'w_in': nrm(12, (DEPTH, d, IN_TOTAL), d ** -0.5),
        'hg_lb': nrm(13, (DEPTH + 1, HG_WK), 0.1),
        'hg_norm_w': 1.0 + nrm(14, (DEPTH, HG_DV), 0.02),
        'w_pa': nrm(15, (DEPTH, ATT_OUT, d), ATT_OUT ** -0.5),
        'w_pb': nrm(16, (DEPTH, HG_WV, d), HG_WV ** -0.5),
        'w_o': nrm(17, (DEPTH, d, d), d ** -0.5),
        'norm2_w': 1.0 + nrm(18, (DEPTH, d), 0.02),
        'w_ffn_a': nrm(19, (DEPTH, d, D_FF), d ** -0.5),
        'w_ffn_b': nrm(20, (DEPTH, d, D_FF), d ** -0.5),
        'conv_w': nrm(21, (DEPTH, CONV_W, D_FF), CONV_W ** -0.5),
        'conv_b': nrm(22, (DEPTH, D_FF), 0.02),
        'w_ffn_down': nrm(23, (DEPTH, D_FF, d), D_FF ** -0.5),
        'norm_f_w': 1.0 + nrm(24, (d,), 0.02),
    }


def reference(x_prompt, x_sample, cache_kv_w128, cache_kv_w512, cache_kv_w2048, state_hgrn, state_conv,
              c_prompt, c_sample, w_ada, b_ada, norm1_w, w_in, hg_lb, hg_norm_w, w_pa, w_pb, w_o,
              norm2_w, w_ffn_a, w_ffn_b, conv_w, conv_b, w_ffn_down, norm_f_w):
    lower = jnp.cumsum(jax.nn.softmax(hg_lb.astype(jnp.float32), axis=0), axis=0)
    xp, xs = x_prompt, x_sample
    states_p, states_s = [], []
    for l in range(DEPTH):
        lw = (lower[l], w_ada[l], b_ada[l], norm1_w[l], w_in[l], hg_norm_w[l], w_pa[l], w_pb[l], w_o[l],
              norm2_w[l], w_ffn_a[l], w_ffn_b[l], conv_w[l], conv_b[l], w_ffn_down[l])
        hg0 = jnp.zeros((xp.shape[0], HG_HEADS, HG_DK, HG_DV), jnp.float32)
        conv0 = jnp.zeros((xp.shape[0], CONV_W - 1, D_FF), xp.dtype)
        xp, st_p = decoder_layer(xp, c_prompt, 0, (None, None, None), hg0, conv0, *lw)
        xs, st_s = decoder_layer(xs, c_sample, PAST_LEN,
                                 (cache_kv_w128[l], cache_kv_w512[l], cache_kv_w2048[l]),
                                 state_hgrn[l], state_conv[l], *lw)
        states_p.append(st_p)
        states_s.append(st_s)
    y_prompt = rms_norm(xp, norm_f_w)
    y_sample = rms_norm(xs, norm_f_w)

    def stack(states, i):
        return jnp.stack([s[i] for s in states])

    return (y_prompt, y_sample,
            stack(states_p, 0), stack(states_p, 1), stack(states_p, 2), stack(states_p, 3), stack(states_p, 4),
            stack(states_s, 0), stack(states_s, 1), stack(states_s, 2), stack(states_s, 3), stack(states_s, 4))
```

```python
import numpy as np
import concourse.bass as bass
import concourse.mybir as mybir
from concourse.bass_utils import run_bass_kernel_spmd

F32 = mybir.dt.float32
BF16 = mybir.dt.bfloat16
AF = mybir.ActivationFunctionType
ALU = mybir.AluOpType
AX = mybir.AxisListType

D = 2048
NCORE = 8
TOK = 2048
MS = 32
NTOK = TOK + MS
DFF = 5632
NJ = 44
EPS = 1e-6
GROUPS = ((128, 1), (512, 4), (2048, 16))
NEG = -30000.0
import os as _os
UDEPTH = int(_os.environ.get("UDEPTH", "3"))


class Sched:
    def __init__(self):
        self.ops = []
        self.lw = {}
        self.rd = {}
        self.fence_deps = set()
        self.last_eng = {}
        self.last_ch = {}

    @staticmethod
    def _nk(k):
        n = k[0] if isinstance(k, tuple) else k
        if isinstance(n, str) and len(n) >= 3 and n.startswith('ps') and n[2] in 'ABCDEFTU':
            return n[:3]
        return k

    def add(self, eng, fn, r=(), w=(), ch=None):
        r = [self._nk(k) for k in r]
        w = [self._nk(k) for k in w]
        psr = [k for k in r if isinstance(k, str) and len(k) == 3 and k.startswith('ps')]
        if psr:
            r = [k for k in r if k not in psr]
            w = list(w) + psr
        deps = set(self.fence_deps)
        for k in r:
            if k in self.lw:
                deps.add(self.lw[k])
        for k in w:
            if k in self.lw:
                deps.add(self.lw[k])
            deps.update(self.rd.get(k, ()))
        i = len(self.ops)
        import sys as _s
        fr = _s._getframe(1)
        lines = []
        while fr is not None and len(lines) < 4:
            lines.append(fr.f_lineno)
            fr = fr.f_back
        self.ops.append(dict(eng=eng, fn=fn, deps=deps, ch=ch, ln=lines))
        for k in r:
            self.rd.setdefault(k, []).append(i)
        for k in w:
            self.lw[k] = i
            self.rd[k] = []
        if ch is None:
            self.last_eng[eng] = i
        else:
            self.last_ch[ch] = i
        return i

    def fence(self):
        self.fence_deps = set(self.last_eng.values()) | set(self.last_ch.values())

    def emit(self, nc, block, sems_eng, sems_ch, final_chs):
        ops = self.ops
        needed = [False] * len(ops)
        for o in ops:
            for d in o['deps']:
                dd = ops[d]
                if dd['ch'] is None and dd['eng'] == 'pe' and o['eng'] == 'pe' and o['ch'] is None:
                    continue
                needed[d] = True
        cnt = {}
        for i, o in enumerate(ops):
            if o['ch'] is not None:
                continue
            if needed[i]:
                cnt[o['eng']] = cnt.get(o['eng'], 0) + 1
                o['ms'] = cnt[o['eng']]
        chcnt = {}
        for i, o in enumerate(ops):
            if o['ch'] is not None:
                n = o['fn'].ndma
                chcnt[o['ch']] = chcnt.get(o['ch'], 0) + 16 * n
                o['val'] = chcnt[o['ch']]
        finals = {ch: chcnt[ch] for ch in final_chs if ch in chcnt}

        def run(engname, e):
            seen = {}
            for i, o in enumerate(ops):
                if o['eng'] != engname:
                    continue
                for d in sorted(o['deps']):
                    dd = ops[d]
                    if dd['ch'] is not None:
                        sem, val = sems_ch[dd['ch']], dd['val']
                    else:
                        if dd['eng'] == 'pe' and engname == 'pe' and o['ch'] is None:
                            continue
                        sem, val = sems_eng[dd['eng']], dd['ms']
                    key = id(sem)
                    if seen.get(key, 0) >= val:
                        continue
                    e.wait_ge(sem, val)
                    seen[key] = val
                if o['ch'] is not None:
                    for ins in o['fn'](e):
                        ins.then_inc(sems_ch[o['ch']], 16)
                else:
                    ins = o['fn'](e)
                    if needed[i]:
                        ins.then_inc(sems_eng[engname], 1)
            if engname == 'sp':
                for ch, v in finals.items():
                    e.wait_ge(sems_ch[ch], v)

        block.tensor(lambda e: run('pe', e))
        block.scalar(lambda e: run('act', e))
        block.vector(lambda e: run('dve', e))
        block.gpsimd(lambda e: run('pool', e))
        block.sync(lambda e: run('sp', e))


def dmafn(pairs):
    def f(e):
        return [e.dma_start(out=o, in_=i, allow_slow_non_contiguous=True) for (o, i) in pairs]
    f.ndma = len(pairs)
    return f


def build(stop=None, marks=None):
    nc = bass.Bass("TRN2", target_bir_lowering=False)
    S = Sched()

    def din(name, shape, dt=F32):
        return nc.dram_tensor(name, list(shape), dt, kind="ExternalInput").ap()

    def dout(name, shape):
        return nc.dram_tensor(name, list(shape), F32, kind="ExternalOutput").ap()

    def dscr(name, shape, dt):
        return nc.dram_tensor(name, list(shape), dt).ap()

    xall = din("xall", [33 * 128, D])
    c5 = din("c5", [5, D])
    wada_t = din("wada_t", [24, 128, 16 * 512])
    bada = din("bada", [1, 6 * D])
    vecA = din("vecA", [32, 128])
    normf = din("normf", [1, D])
    watt_t = din("watt_t", [12, 128, 16 * 384])
    whg_t = din("whg_t", [8, 128, 16 * 512])
    wB1_t = din("wB1_t", [16, 128, 44 * 128])
    wo_t = din("wo_t", [4, 128, 16 * 512])
    wab_t = din("wab_t", [NJ, 128, 32 * 128])
    wd_t = din("wd_t", [16, 128, 11 * 512])
    convw = din("convw", [4 * NJ, 128])
    hglb = din("hglb", [2, 1024])
    hgnw = din("hgnw", [1, 128])
    consts = din("consts", [128, 2048])
    rope_t = din("rope_t", [3, 32, 128, 32])
    rope_m = din("rope_m", [MS, 32])
    masks = din("masks", [128, 512])
    ck = [din("ck128", [4, 128, 2, 4, 128]), din("ck512", [4, 512, 2, 4, 128]), din("ck2048", [4, 2048, 2, 4, 128])]
    sthg = din("sthg", [4, 8, 128, 128])
    stconv = din("stconv", [8, DFF])
    cvalid = din("cvalid", [128, 1])

    y_own = dout("y_own", [TOK, D])
    y_misc = dout("y_misc", [MS, D])
    kvp = [dout("kvp128", [128, 2, 4, 128]), dout("kvp512", [512, 2, 4, 128]), dout("kvp2048", [2048, 2, 4, 128])]
    hgp = dout("hgp", [8, 128, 128])
    convp = dout("convp", [2, DFF])
    kvs = [dout("kvs128", [4, 128, 2, 4, 128]), dout("kvs512", [4, 512, 2, 4, 128]), dout("kvs2048", [4, 2048, 2, 4, 128])]
    hgs = dout("hgs", [4, 8, 128, 128])
    convs = dout("convs", [8, DFF])

    oatt_d = dscr("oatt_d", [4, 128, NTOK], BF16)
    ohg_d = dscr("ohg_d", [8, 128, NTOK], BF16)
    ymix_d = dscr("ymix_d", [16, 128, NTOK], BF16)
    x1_d = dscr("x1_d", [17 * 128, D], F32)
    x2_d = dscr("x2_d", [17 * 128, D], F32)
    modr_d = dscr("modr_d", [5, 2 * D], F32)
    hkv_d = dscr("hkv_d", [168, 128, 128], BF16)
    lm_d = dscr("lm_d", [3, 2, NTOK], F32)
    dbg_d = dscr("dbg_d", [10, 128, 128], BF16)
    dbg2_d = dscr("dbg2_d", [128, 2], F32)

    import contextlib
    es = contextlib.ExitStack()

    def sb(name, shape, dt=F32):
        return es.enter_context(nc.sbuf_tensor(name, list(shape), dt))

    def ps(name, shape, dt=F32):
        return es.enter_context(nc.psum_tensor(name, list(shape), dt))

    ph = [contextlib.ExitStack()]

    uniq = [0]

    def lsb(name, shape, dt=F32):
        uniq[0] += 1
        return ph[0].enter_context(nc.sbuf_tensor("%s_%d" % (name, uniq[0]), list(shape), dt))

    def phase_end():
        if marks is not None:
            marks.append(len(S.ops))
        S.fence()
        ph[0].close()
        ph[0] = contextlib.ExitStack()

    with es:
        BIGA = sb("BIGA", [128, 16 * NTOK], BF16)
        WB = [sb("WB0", [128, 8192], BF16), sb("WB1", [128, 8192], BF16)]
        XT = [sb("XT0", [128, D]), sb("XT1", [128, D])]
        XN = sb("XN", [128, D], BF16)
        CON = sb("CON", [128, 1280])
        identf = CON[:, 0:128]
        tri2 = CON[:, 128:256]
        blk2 = CON[:, 256:384]
        ind2 = CON[:, 384:386]
        cmask = CON[:, 512:576]
        tri_m = CON[0:32, 640:672]
        blk_m = CON[0:32, 672:704]
        ind_m = CON[0:32, 704:708]
        cmask_m = CON[0:32, 736:768]
        rowm = CON[0:32, 768:772]
        colm = CON[:, 896:1024]
        rowsel = CON[0:5, 1024:1056]
        ones_f = CON[:, 1152:1280]
        IDB = sb("IDB", [128, 128], BF16)
        ONEB = sb("ONEB", [128, 128], BF16)
        MSK = sb("MSK", [128, 512])
        ROPEM = sb("ROPEM", [MS, 32])
        MODT = sb("MODT", [128, 96 * 5])
        MR = sb("MR", [5, 512])
        SC = sb("SC", [128, 4 * 16])
        SCM = sb("SCM", [128, 4 * 16 * MS])
        NW = sb("NW", [128, 32])
        CW = sb("CW", [128, 4 * NJ])
        ST = sb("ST", [128, 64])
        STT = sb("STT", [128, 16 * 5], BF16)
        HGW = sb("HGW", [128, 128])
        CVAL = sb("CVAL", [128, 1])
        RTMP = sb("RTMP", [128, 512])
        CARRY = sb("CARRY", [128, NJ * 2])
        SCV = sb("SCV", [128, NJ * 8])
        ASV = sb("ASV", [128, NJ * 8])
        SST = sb("SST", [128, 8 * 128])
        SBF = sb("SBF", [128, 128], BF16)
        OHGPRE = sb("OHGPRE", [128, 16], BF16)
        Lc = {}

        psA = ps("psA", [128, 512])
        psB = ps("psB", [128, 512])
        psC = ps("psC", [128, 512])
        psD = ps("psD", [128, 512])
        psE = ps("psE", [128, 512])
        psF = ps("psF", [128, 512])
        psT = ps("psT", [128, 1024], BF16)
        psU = ps("psU", [128, 1024], BF16)

        hT = BIGA[:, :].rearrange("p (k t) -> p k t", k=16)
        psEb = psE[:, :].bitcast(BF16)

        def A(eng, fn, r=(), w=()):
            return S.add(eng, fn, r, w)

        def DMA(q, ch, pairs, r=(), w=()):
            return S.add(q, dmafn(pairs), r, w, ch=ch)

        def act(out, in_, func, r, w, **kw):
            A('act', lambda e: e.activation(out=out, in_=in_, func=func, **kw), r, w)

        def tt(out, a, b, op, r, w, eng='dve'):
            A(eng, lambda e: e.tensor_tensor(out, a, b, op=op), r, w)

        def tcopy(out, in_, r, w, eng='dve'):
            if eng == 'act':
                A('act', lambda e: e.copy(out=out, in_=in_), r, w)
            else:
                A(eng, lambda e: e.tensor_copy(out, in_), r, w)

        def tsc(out, a, s1, s2, op0, op1, r, w):
            A('dve', lambda e: e.tensor_scalar(out, a, s1, s2, op0=op0, op1=op1), r, w)

        def stt(out, a, s, b, op0, op1, r, w):
            A('dve', lambda e: e.scalar_tensor_tensor(out, a, s, b, op0=op0, op1=op1), r, w)

        def mms(lst, r, w):
            def f(e):
                ins = None
                for (o, l, rr, st, sp) in lst:
                    ins = e.matmul(o, l, rr, start=st, stop=sp)
                return ins
            A('pe', f, r, w)

        def trs(lst, r, w):
            def f(e):
                ins = None
                for (o, i, idn) in lst:
                    ins = e.transpose(o, i, idn)
                return ins
            A('pe', f, r, w)

        wslot = [0]

        def wload(src, ncols):
            s_ = wslot[0]
            wslot[0] ^= 1
            DMA('pool', 'w%d' % s_, [(WB[s_][:, 0:ncols], src)], w=[('WB', s_)])
            return s_

        DMA('sp', 'c0', [(CON[:, :], consts[:, 0:1280]), (MSK[:, :], masks), (ROPEM[:, :], rope_m),
                         (CVAL[:, :], cvalid), (HGW[:, :], hgnw.partition_broadcast(128))],
            w=['CON', 'MSK', 'ROPEM', 'CVAL', 'HGW'])
        S5 = lsb("S5", [5, D], BF16)
        BADA = lsb("BADA", [1, 6 * D], BF16)
        DMA('pool', 'c3', [(IDB[:, :], consts[:, 0:128]), (ONEB[:, :], consts[:, 1152:1280]), (BADA[:, :], bada)],
            w=['IDB', 'ONEB', 'BADA'])
        DMA('sp', 'c4', [(XT[0][0:5, :], c5)], w=['XT0'])
        act(S5[:, :], XT[0][0:5, :], AF.Silu, ['XT0'], ['S5'])
        trs([(psT[:, kc * 8:kc * 8 + 5], S5[0:5, kc * 128:(kc + 1) * 128], IDB[0:5, 0:5]) for kc in range(16)],
            ['S5', 'IDB'], ['psT'])
        tcopy(STT[:, :].rearrange("p (k e) -> p k e", e=5), psT[:, 0:128].rearrange("p (k e) -> p k e", e=8)[:, :, 0:5], ['psT'], ['STT'])
        DMA('sp', 'c5', [(XT[1][0:32, 0:128], vecA)], w=['XT1'])
        mms([(psB[:, 0:32], XT[1][0:32, 0:128], identf[0:32, 0:32], True, True)], ['XT1', 'CON'], ['psB'])
        tcopy(NW[:, :], psB[:, 0:32], ['psB'], ['NW'])
        DMA('sp', 'c6', [(XT[1][0:88, 128:256], convw[0:88, :]), (XT[1][0:88, 256:384], convw[88:176, :])], w=['XT1b'])
        mms([(psB[:, 64:152], XT[1][0:88, 128:256], identf[0:88, 0:88], True, True),
             (psB[:, 152:240], XT[1][0:88, 256:384], identf[0:88, 0:88], True, True)], ['XT1b', 'CON'], ['psBb'])
        tcopy(CW[:, :], psB[:, 64:240], ['psBb'], ['CW'])

        for j in range(24):
            s_ = wload(wada_t[j], 8192)
            wv = WB[s_][:, :].rearrange("p (k n) -> p k n", k=16)
            lst = []
            for cb in range(4):
                col = j * 4 + cb
                o = psA[:, col * 5:(col + 1) * 5]
                for kc in range(16):
                    lst.append((o, wv[:, kc, cb * 128:(cb + 1) * 128], STT[:, kc * 5:(kc + 1) * 5], kc == 0, False))
                lst.append((o, BADA[0:1, col * 128:(col + 1) * 128], ONEB[0:1, 0:5], False, True))
            mms(lst, [('WB', s_), 'STT', 'BADA', 'ONEB'], ['psA'])
            kind = j // 4
            if kind in (2, 5):
                lst = [(psC[0:5, :], STT[:, kc * 5:(kc + 1) * 5], wv[:, kc, :], kc == 0, False) for kc in range(16)]
                lst.append((psC[0:5, :], ONEB[0:1, 0:5], BADA[0:1, j * 512:(j + 1) * 512], False, True))
                mms(lst, [('WB', s_), 'STT', 'BADA', 'ONEB'], ['psC'])
                off = (0 if kind == 2 else D) + (j % 4) * 512
                tcopy(MR[:, :], psC[0:5, :], ['psC'], ['MR'])
                DMA('sp', 'mro', [(modr_d[:, off:off + 512], MR[:, :])], r=['MR'], w=['modr_d'])
        tcopy(MODT[:, :], psA[:, 0:480], ['psA'], ['MODT'])
        modT = MODT[:, :].rearrange("p (c r) -> p c r", r=5)
        SCv = SC[:, :].rearrange("p (a k) -> p a k", a=4)
        SCMv = SCM[:, :].rearrange("p (a k m) -> p a k m", a=4, k=16)
        for n_, (ksh, ksc, nwo) in enumerate(((0, 1, 0), (3, 4, 16))):
            tsc(ST[:, 0:16], modT[:, ksc * 16:(ksc + 1) * 16, 0], 1.0, None, ALU.add, ALU.bypass, ['MODT'], ['STa'])
            tt(SCv[:, 2 * n_, :], ST[:, 0:16], NW[:, nwo:nwo + 16], ALU.mult, ['STa', 'NW'], ['SC'])
            tcopy(SCv[:, 2 * n_ + 1, :], modT[:, ksh * 16:(ksh + 1) * 16, 0], ['MODT'], ['SC'])
            for b in range(4):
                tsc(ST[:, 16:32], modT[:, ksc * 16:(ksc + 1) * 16, 1 + b], 1.0, None, ALU.add, ALU.bypass, ['MODT'], ['STb'])
                tt(ST[:, 32:48], ST[:, 16:32], NW[:, nwo:nwo + 16], ALU.mult, ['STb', 'NW'], ['STc'])
                tcopy(SCMv[:, 2 * n_, :, 4 * b:4 * b + 4], ST[:, 32:48, None].broadcast_to([128, 16, 4]), ['STc'], ['SCM'])
                tcopy(SCMv[:, 2 * n_ + 1, :, 4 * b:4 * b + 4],
                      modT[:, ksh * 16:(ksh + 1) * 16, 1 + b:2 + b].broadcast_to([128, 16, 4]), ['MODT'], ['SCM'])
            tcopy(SCMv[:, 2 * n_, :, 16:32], SCv[:, 2 * n_, :, None].broadcast_to([128, 16, 16]), ['SC'], ['SCM'])
            tcopy(SCMv[:, 2 * n_ + 1, :, 16:32], SCv[:, 2 * n_ + 1, :, None].broadcast_to([128, 16, 16]), ['SC'], ['SCM'])

        phase_end()
        xslot = [0]

        def norm_tile(src_rows, nrows, n_, dst_cols, ncols, misc):
            s_ = xslot[0]
            xslot[0] ^= 1
            xk = 'XT%d' % s_
            X = XT[s_]
            DMA('sp', 'x%d' % s_, [(X[0:nrows, :], src_rows)], w=[xk])
            act(XN[0:nrows, :], X[0:nrows, :], AF.Square, [xk], ['XN', 'STn'], accum_out=ST[0:nrows, 48:49])
            act(ST[0:nrows, 49:50], ST[0:nrows, 48:49], AF.Sqrt, ['STn'], ['STn2'], scale=1.0 / D, bias=EPS)
            A('dve', lambda e: e.reciprocal(ST[0:nrows, 50:51], ST[0:nrows, 49:50]), ['STn2'], ['STn3'])
            act(XN[0:nrows, :], X[0:nrows, :], AF.Copy, [xk, 'STn3', 'XN'], ['XN'], scale=ST[0:nrows, 50:51])
            for half in range(2):
                P_ = psT if half == 0 else psU
                pk = 'psT' if half == 0 else 'psU'
                trs([(P_[:, q * ncols:(q + 1) * ncols], XN[0:nrows, (half * 8 + q) * 128:(half * 8 + q + 1) * 128],
                      IDB[0:nrows, 0:nrows]) for q in range(8)], ['XN', 'IDB'], [pk])
                pv = P_[:, 0:8 * ncols].rearrange("p (k t) -> p k t", k=8)
                dst = hT[:, half * 8:half * 8 + 8, dst_cols:dst_cols + ncols]
                if not misc:
                    sc_ = SCv[:, 2 * n_, half * 8:half * 8 + 8, None].broadcast_to([128, 8, ncols])
                    sh_ = SCv[:, 2 * n_ + 1, half * 8:half * 8 + 8, None].broadcast_to([128, 8, ncols])
                else:
                    sc_ = SCMv[:, 2 * n_, half * 8:half * 8 + 8, :]
                    sh_ = SCMv[:, 2 * n_ + 1, half * 8:half * 8 + 8, :]
                tmp = RTMP[:, :].bitcast(BF16)[:, 0:8 * ncols].rearrange("p (k t) -> p k t", k=8)
                tt(tmp, pv, sc_, ALU.mult, [pk, 'SC', 'SCM'], ['RTMPh'])
                tt(dst, tmp, sh_, ALU.add, ['RTMPh', 'SC', 'SCM'], [('hT', dst_cols // 128)])

        def hkeys(c0, c1):
            return [('hT', t) for t in range(c0 // 128, (c1 - 1) // 128 + 1)]

        ALLH = [('hT', t) for t in range(17)]

        ST2 = None
        QT = KT = VA = OTG = ATMP = APB = ASTAGE = MQK = MV = KVF = QKB = ROPE = HKS = LMS = CMB = None
        LB = None
        HT = HB = OHGH = S0 = S0B = YCH = GT = GBC = GBM = XQ = ACAT = UU = CROW = BIGB = NFW = None
        ropev = QTv = KTv = VAv = STG = None

        def alloc_attn():
            nonlocal ST2, QT, KT, VA, OTG, ATMP, APB, ASTAGE, MQK, MV, KVF, QKB, ROPE, HKS, LMS, CMB, YCH
            nonlocal ropev, QTv, KTv, VAv, STG
            QT = lsb("QT", [128, 16 * 128], BF16)
            KT = lsb("KT", [128, 17 * 128], BF16)
            VA = lsb("VA", [128, 17 * 128], BF16)
            OTG = [lsb("OTG%d" % g, [128, NTOK], BF16) for g in range(3)]
            ST2 = lsb("ST2", [128, 64])
            APB = lsb("APB", [128, 3 * 512], BF16)
            ASTAGE = lsb("ASTAGE", [128, 3 * 6 * 128], BF16)
            MQK = lsb("MQK", [128, 2 * MS], BF16)
            MV = lsb("MV", [MS, 128], BF16)
            KVF = lsb("KVF", [128, 2 * 256])
            QKB = lsb("QKB", [128, 2 * 256], BF16)
            ROPE = lsb("ROPE", [128, 32 * 32])
            HKS = lsb("HKS", [128, 3 * 256], BF16)
            LMS = lsb("LMS", [1, 768])
            CMB = lsb("CMB", [128, 11 * 128])
            YCH = lsb("YCHa", [128, NTOK], BF16)
            ropev = ROPE[:, :].rearrange("p (t c) -> p t c", t=32)
            QTv = QT[:, :].rearrange("p (t c) -> p t c", c=128)
            KTv = KT[:, :].rearrange("p (t c) -> p t c", c=128)
            VAv = VA[:, :].rearrange("p (t c) -> p t c", c=128)
            STG = ASTAGE[:, :].rearrange("p (t c) -> p t c", c=128)
            A('pool', lambda e, t=ASTAGE: e.memset(t[:, :], 0.0), [], [('STG', 0), ('STG', 1), ('STG', 2), ('STG5', 0), ('STG5', 1), ('STG5', 2)])
            A('pool', lambda e, t=HKS: e.memset(t[:, :], 0.0), [], [('HKS', 0), ('HKS', 1), ('HKS', 2)])
            for g in range(3):
                A('pool', lambda e, t=OTG[g]: e.memset(t[:, :], 0.0), [], [('OTG', g)])

        def alloc_hgrn():
            nonlocal HT, HB, OHGH, S0, S0B, LB
            LB = lsb("LB", [128, 2 * 1024])
            DMA('sp', 'c2', [(LB[:, 0:1024], hglb[0:1, :].partition_broadcast(128)),
                             (LB[:, 1024:2048], hglb[1:2, :].partition_broadcast(128))], w=['LB'])
            tt(LB[:, 0:1024], LB[:, 0:1024], LB[:, 1024:2048], ALU.subtract, ['LB'], ['LB'])
            act(LB[:, 0:1024], LB[:, 0:1024], AF.Sigmoid, ['LB'], ['LB'])
            tsc(LB[:, 1024:2048], LB[:, 0:1024], -1.0, 1.0, ALU.mult, ALU.add, ['LB'], ['LB'])
            HT = lsb("HT", [128, 2 * 1536])
            HB = lsb("HB", [128, 2 * 1024 + 1024], BF16)
            OHGH = lsb("OHGH", [128, NTOK], BF16)
            S0 = lsb("S0", [128, 4 * 128])
            S0B = lsb("S0B", [128, 4 * 128], BF16)
            A('pool', lambda e, t=OHGH: e.memset(t[:, :], 0.0), [], ['OHGH'])

        def hkv_idx(hh):
            g = hh // 4
            base = 0
            for h2 in range(hh):
                base += 2 * GROUPS[h2 // 4][1]
            return base

        def pipeline(gens, depth):
            active = []
            it = iter(gens)
            done = False
            while True:
                if not done and len(active) < depth:
                    try:
                        active.append(next(it))
                    except StopIteration:
                        done = True
                if not active and done:
                    break
                for g_ in list(active):
                    try:
                        next(g_)
                    except StopIteration:
                        active.remove(g_)

        uslot = [0]

        def proj_gen(ws, cols_ap_fn, M, g, tidx, misc, want_q, kdst, vdst, qdst, kvout, dkeys):
            ks = uslot[0] & 1
            uslot[0] += 1
            Z = psA if ks == 0 else psF
            zk = 'psA' if ks == 0 else 'psF'
            TP = psT if ks == 0 else psU
            tpk = 'psT' if ks == 0 else 'psU'
            wv = WB[ws][:, 0:16 * 384].rearrange("p (k n) -> p k n", k=16)
            mms([(Z[0:M, 0:384], cols_ap_fn(kc), wv[:, kc, :], kc == 0, kc == 15) for kc in range(16)],
                [('WB', ws)] + ALLH, [zk])
            yield
            cosv = (ROPEM[0:M, 0:16] if misc else ropev[0:M, tidx, 0:16])
            sinv = (ROPEM[0:M, 16:32] if misc else ropev[0:M, tidx, 16:32])
            zqk = Z[0:M, 0:256].rearrange("p (a c) -> p a c", a=2)
            x1 = zqk[:, :, 0:16]
            x2 = zqk[:, :, 16:32]
            cb_ = cosv[:, None, :].broadcast_to([M, 2, 16])
            sb_ = sinv[:, None, :].broadcast_to([M, 2, 16])
            T = RTMP[0:M, ks * 256:ks * 256 + 128].rearrange("p (q a c) -> p q a c", q=4, a=2)
            QF = RTMP[0:M, ks * 256 + 128:ks * 256 + 256]
            rt = lambda i: ('RT', ks, i)
            tt(T[:, 0], x1, cb_, ALU.mult, [zk, 'ROPE', 'ROPEM'], [rt(0)])
            tt(T[:, 1], x2, sb_, ALU.mult, [zk, 'ROPE', 'ROPEM'], [rt(1)])
            tt(T[:, 2], x2, cb_, ALU.mult, [zk, 'ROPE', 'ROPEM'], [rt(2)])
            tt(T[:, 3], x1, sb_, ALU.mult, [zk, 'ROPE', 'ROPEM'], [rt(3)])
            KF = KVF[0:M, ks * 256:ks * 256 + 256]
            kfk = ('KVF', ks)
            tcopy(KF[:, 128:256], Z[0:M, 256:384], [zk], [kfk], eng='act')
            tt(KF[:, 0:16], T[:, 0, 1], T[:, 1, 1], ALU.subtract, [rt(0), rt(1)], [kfk])
            tt(KF[:, 16:32], T[:, 2, 1], T[:, 3, 1], ALU.add, [rt(2), rt(3)], [kfk])
            tcopy(KF[:, 32:128], Z[0:M, 160:256], [zk], [kfk])
            qb = QKB[0:M, ks * 256:ks * 256 + 128]
            kb = QKB[0:M, ks * 256 + 128:ks * 256 + 256]
            qk_ = ('QKB', ks)
            tcopy(vdst, KF[:, 128:256], [kfk], [dkeys[1]], eng='act')
            tcopy(kb, KF[:, 0:128], [kfk], [qk_])
            if want_q:
                tt(QF[:, 0:16], T[:, 0, 0], T[:, 1, 0], ALU.subtract, [rt(0), rt(1)], [('QF', ks)])
                tt(QF[:, 16:32], T[:, 2, 0], T[:, 3, 0], ALU.add, [rt(2), rt(3)], [('QF', ks)])
                tcopy(QF[:, 32:128], Z[0:M, 32:128], [zk], [('QF', ks)])
                tcopy(qb, QF[:, 0:128], [('QF', ks)], [qk_])
            yield
            lst = [(TP[:, 128:128 + M], kb, IDB[0:M, 0:M])]
            if want_q:
                lst.append((TP[:, 0:M], qb, IDB[0:M, 0:M]))
            trs(lst, [qk_, 'IDB'], [tpk])
            if kvout is not None:
                DMA('sp', 'kvo%d' % ks, kvout(KF), r=[kfk])
            yield
            tcopy(kdst, TP[:, 128:128 + M], [tpk], [dkeys[0]])
            if want_q:
                tcopy(qdst, TP[:, 0:M], [tpk], [dkeys[2]], eng='act')

        aslot = [0]
        dbgsel = [0]

        def attn_unit_gen(g, prep, qT, kp, kc_, vp, vc, mask, outs, rk):
            s_ = aslot[0] % 3
            aslot[0] += 1
            if prep is not None:
                qT, kp, kc_, vp, vc, rk = prep(s_)
                yield
            stt_ = ST2[:, 4 * s_:4 * s_ + 2]
            pb = APB[:, s_ * 512:s_ * 512 + 256]
            pT = APB[:, s_ * 512 + 256:s_ * 512 + 512]
            P_ = (psB, psC, psD)[s_]
            pk = ('psB', 'psC', 'psD')[s_]
            TP = (psT, psU, psEb)[s_]
            tpk = ('psT', 'psU', 'psE')[s_]
            mask = mask_std if mask == 's' else mask_halo
            mms([(P_[:, 0:128], qT, kp, True, True), (P_[:, 128:256], qT, kc_, True, True)], rk, [pk])
            yield
            stt(P_[:, 0:256], P_[:, 0:256], 128.0 ** -0.5, mask, ALU.mult, ALU.add, [pk, 'MSK'], [pk])
            A('dve', lambda e: e.reduce_max(stt_[:, 0:1], P_[:, 0:256], axis=AX.X), [pk], [('mx', s_)])
            tsc(stt_[:, 1:2], stt_[:, 0:1], -1.0, None, ALU.mult, ALU.bypass, [('mx', s_)], [('nmx', s_)])
            act(pb, P_[:, 0:256], AF.Exp, [pk, ('nmx', s_)], [('pb', s_)], bias=stt_[:, 1:2])
            if dbgsel[0] == 1:
                dbgsel[0] = 2
                DMA('sp', 'dbgb', [(dbg_d[7], pb[:, 0:128]), (dbg_d[8], pb[:, 128:256]), (dbg2_d, stt_)], r=[('pb', s_), ('mx', s_), ('nmx', s_)])
            yield
            trs([(TP[:, 256:384], pb[:, 0:128], IDB[:, :]), (TP[:, 384:512], pb[:, 128:256], IDB[:, :])],
                [('pb', s_), 'IDB'], [tpk])
            yield
            tcopy(pT, TP[:, 256:512], [tpk], [('pT', s_)], eng='act')
            yield
            mms([(P_[:, 256:384], vp, pT[:, 0:128], True, False), (P_[:, 256:384], vc, pT[:, 128:256], False, True),
                 (P_[0:1, 384:512], ONEB[:, 0:1], pT[:, 0:128], True, False),
                 (P_[0:1, 384:512], ONEB[:, 0:1], pT[:, 128:256], False, True),
                 (P_[0:1, 0:128], stt_[:, 0:1], identf, True, True)],
                rk + [('pT', s_), ('mx', s_), 'ONEB', 'CON'], [pk])
            if dbgsel[0] == 2:
                dbgsel[0] = 3
                DMA('sp', 'dbg', [(dbg_d[0], vp), (dbg_d[1], vc), (dbg_d[2], pT[:, 0:128]), (dbg_d[3], pT[:, 128:256]),
                                  (dbg_d[4], kp), (dbg_d[5], kc_), (dbg_d[6], qT)], r=rk + [('pT', s_)])
            yield
            tcopy(LMS[0:1, s_ * 256:s_ * 256 + 128], P_[0:1, 384:512], [pk], [('LMS', s_)], eng='act')
            tcopy(LMS[0:1, s_ * 256 + 128:s_ * 256 + 256], P_[0:1, 0:128], [pk], [('LMS', s_)], eng='act')
            prs = []
            for (src, dst) in outs:
                tcopy(OTG[g][:, dst], P_[:, 256:384][:, src], [pk], [('OTG', g)])
                prs.append((lm_d[g, 0:1, dst], LMS[0:1, s_ * 256:s_ * 256 + 128][:, src]))
                prs.append((lm_d[g, 1:2, dst], LMS[0:1, s_ * 256 + 128:s_ * 256 + 256][:, src]))
            DMA('sp', 'lmo%d' % s_, prs, r=[('LMS', s_)], w=[('lm_d', g)])

        mask_std = MSK[:, 0:256]
        mask_halo = MSK[:, 256:512]

        def attn_head(hh, sweep):
            g, j = hh // 4, hh % 4
            W, d = GROUPS[g]
            nT = 16 // d
            ws = wload(watt_t[hh], 16 * 384)
            hb = hkv_idx(hh)
            DMA('sp', 'c1', [(ropev, rope_t[g].rearrange("t p c -> p t c"))], w=['ROPE'])
            HKSv = HKS[:, :].rearrange("p (s c) -> p s c", s=3)
            if sweep == 'H':
                def hgen(r):
                    st_ = TOK - 128 * d + r
                    sl = r % 3
                    yield from proj_gen(ws, lambda kc, st_=st_: hT[:, kc, st_:st_ + 128 * d:d], 128, g, r * (nT + 1), False, False,
                                        HKSv[:, sl, 0:128], HKSv[:, sl, 128:256], None, None, [('HKS', sl), ('HKS', sl), None])
                    DMA('sp', 'hkvo%d' % sl, [(hkv_d[hb + r], HKSv[:, sl, 0:128]), (hkv_d[hb + d + r], HKSv[:, sl, 128:256])],
                        r=[('HKS', sl)], w=[('hkv_d', hh)])
                pipeline((hgen(r) for r in range(d)), 2)
                return
            gens = []
            for r in range(d):
                for T_ in range(nT):
                    ti = r * nT + T_
                    st_ = r + d * 128 * T_
                    o0 = st_ - (TOK - W)
                    kvout = None
                    if o0 >= 0:
                        def kvout(KF, o0=o0, d=d, g=g, j=j):
                            return [(kvp[g][o0:o0 + 127 * d + 1:d, 0, j, :], KF[:, 0:128]),
                                    (kvp[g][o0:o0 + 127 * d + 1:d, 1, j, :], KF[:, 128:256])]
                    gens.append(proj_gen(ws, lambda kc, st_=st_: hT[:, kc, st_:st_ + 128 * d:d], 128, g, r * (nT + 1) + T_ + 1, False, True,
                                         KTv[:, ti, :], VAv[:, ti, :], QTv[:, ti, :], kvout, [('KT', ti), ('VA', ti), ('QT', ti)]))

            def kvout_m(KF, g=g, j=j, W=W):
                prs = []
                for b in range(4):
                    prs.append((kvs[g][b, W - 4:W, 0, j, :], KF[4 * b:4 * b + 4, 0:128]))
                    prs.append((kvs[g][b, W - 4:W, 1, j, :], KF[4 * b:4 * b + 4, 128:256]))
                return prs
            gens.append(proj_gen(ws, lambda kc: hT[:, kc, TOK:TOK + MS], MS, g, 0, True, True,
                                 MQK[:, MS:2 * MS], MV[:, :], MQK[:, 0:MS], kvout_m, ['MK', 'MV', 'MQ']))
            pipeline(gens, 2)
            qTm = MQK[:, 0:MS]
            kTm = MQK[:, MS:2 * MS]
            STGv = ASTAGE[:, :].rearrange("p (s t c) -> p s t c", s=3, t=6)
            units = []
            for r in range(d):
                for T_ in range(nT):
                    ti = r * nT + T_
                    st_ = r + d * 128 * T_
                    outs = [(slice(0, 128), slice(st_, st_ + 128 * d, d))]
                    if T_ == 0:
                        def prep(sl, ti=ti, r=r):
                            if _os.environ.get("DBGF") == "1":
                                S.fence()
                            DMA('sp', 'hkvl%d' % sl, [(HKSv[:, sl, 0:128], hkv_d[hb + r]), (HKSv[:, sl, 128:256], hkv_d[hb + d + r])],
                                r=[('hkv_d', hh)], w=[('HKS', sl)])
                            return (QTv[:, ti, :], HKSv[:, sl, 0:128], KTv[:, ti, :], HKSv[:, sl, 128:256], VAv[:, ti, :],
                                    [('QT', ti), ('KT', ti), ('VA', ti), ('HKS', sl)])
                        units.append(attn_unit_gen(g, prep, None, None, None, None, None, 'h', outs, None))
                    else:
                        units.append(attn_unit_gen(g, None, QTv[:, ti, :], KTv[:, ti - 1, :], KTv[:, ti, :], VAv[:, ti - 1, :], VAv[:, ti, :],
                                                   's', outs, [('QT', ti), ('KT', ti), ('VA', ti), ('KT', ti - 1), ('VA', ti - 1)]))
            pre = [(0, [126, 127], [16, 17], [(126, 18), (127, 19)])] if g == 0 else \
                  [((d - 2 + t), [127], [16 + t], [(127, 18 + 2 * g + t)]) for t in range(2)]
            for (r, qrows, qslots, extras) in pre:
                def prep(sl, r=r, qrows=qrows, qslots=qslots, extras=extras):
                    sk = ('STG', sl)
                    for qr, qs in zip(qrows, qslots):
                        tcopy(STGv[:, sl, 0, qr:qr + 1], qTm[:, qs:qs + 1], ['MQ'], [sk])
                    for (sl_, ms) in extras:
                        tcopy(STGv[:, sl, 1, sl_:sl_ + 1], kTm[:, ms:ms + 1], ['MK'], [sk])
                        DMA('sp', 'stg%d' % sl, [(STGv[sl_:sl_ + 1, sl, 3, :], MV[ms:ms + 1, :])], r=['MV'], w=[sk])
                    DMA('sp', 'hkvl%d' % sl, [(HKSv[:, sl, 0:128], hkv_d[hb + r]), (HKSv[:, sl, 128:256], hkv_d[hb + d + r])],
                        r=[('hkv_d', hh)], w=[('HKS', sl)])
                    return (STGv[:, sl, 0, :], STGv[:, sl, 1, :], HKSv[:, sl, 0:128], STGv[:, sl, 3, :], HKSv[:, sl, 128:256],
                            [sk, ('HKS', sl)])
                outs = [(slice(qr, qr + 1), slice(TOK + qs, TOK + qs + 1)) for qr, qs in zip(qrows, qslots)]
                units.append(attn_unit_gen(g, prep, None, None, None, None, None, 'h', outs, None))
            for b in range(4):
                ulist = [(0, [0, 1, 2, 3])] if g == 0 else [(t, [t]) for t in range(4)]
                for (t0, ts) in ulist:
                    def prep(sl, b=b, t0=t0, ts=ts):
                        sk = ('STG', sl)
                        TPs = (psT, psU, psEb)[sl]
                        tpk = ('psT', 'psU', 'psE')[sl]
                        rows = ck[g][b, t0:t0 + 127 * d + 1:d, :, j, :] if g > 0 else ck[g][b, :, :, j, :]
                        DMA('pool', 'kc%d' % sl, [(STGv[:, sl, 5, :], rows[:, 0, :]), (STGv[:, sl, 3, :], rows[:, 1, :])], w=[('STG5', sl), sk])
                        trs([(TPs[:, 512:640], STGv[:, sl, 5, :], IDB[:, :])], [('STG5', sl), 'IDB'], [tpk])
                        tcopy(STGv[:, sl, 1, :], TPs[:, 512:640], [tpk], [sk])
                        for n_, t in enumerate(ts):
                            ms = 4 * b + t
                            tcopy(STGv[:, sl, 0, n_:n_ + 1], qTm[:, ms:ms + 1], ['MQ'], [sk])
                            tcopy(STGv[:, sl, 2, n_:n_ + 1], kTm[:, ms:ms + 1], ['MK'], [sk])
                            DMA('sp', 'stg%d' % sl, [(STGv[n_:n_ + 1, sl, 4, :], MV[ms:ms + 1, :])], r=['MV'], w=[sk])
                        return (STGv[:, sl, 0, :], STGv[:, sl, 1, :], STGv[:, sl, 2, :], STGv[:, sl, 3, :], STGv[:, sl, 4, :], [sk])
                    outs = [(slice(n_, n_ + 1), slice(TOK + 4 * b + t, TOK + 4 * b + t + 1)) for n_, t in enumerate(ts)]
                    units.append(attn_unit_gen(g, prep, None, None, None, None, None, 's', outs, None))
            pipeline(units, UDEPTH)

        def attn_combine(slot):
            if marks is not None:
                marks.append(('combine_start', slot, len(S.ops)))
            lk = [('lm_d', 0), ('lm_d', 1), ('lm_d', 2)]
            K = lambda a_: ('cmb', a_)
            for c0 in range(0, NTOK, 128):
                n = min(128, NTOK - c0)
                Cv = lambda a_, n=n, CMB=CMB: CMB[:, a_ * 128:a_ * 128 + n]
                DMA('sp', 'cml', [(Cv(2 * g + k_), lm_d[g, k_:k_ + 1, c0:c0 + n].partition_broadcast(128)) for g in range(3) for k_ in range(2)],
                    r=lk, w=[K(i) for i in range(6)])
                for g in range(3):
                    act(Cv(6 + g), Cv(2 * g), AF.Ln, [K(2 * g)], [K(6 + g)])
                    tt(Cv(6 + g), Cv(6 + g), Cv(2 * g + 1), ALU.add, [K(6 + g), K(2 * g + 1)], [K(6 + g)])
                tt(Cv(9), Cv(6), Cv(7), ALU.max, [K(6), K(7)], [K(9)])
                tt(Cv(9), Cv(9), Cv(8), ALU.max, [K(9), K(8)], [K(9)])
                for g in range(3):
                    tt(Cv(6 + g), Cv(6 + g), Cv(9), ALU.subtract, [K(6 + g), K(9)], [K(6 + g)])
                    act(Cv(6 + g), Cv(6 + g), AF.Exp, [K(6 + g)], [K(6 + g)])
                tt(Cv(10), Cv(6), Cv(7), ALU.add, [K(6), K(7)], [K(10)])
                tt(Cv(10), Cv(10), Cv(8), ALU.add, [K(10), K(8)], [K(10)])
                A('dve', lambda e, Cv=Cv: e.reciprocal(Cv(10), Cv(10)), [K(10)], [K(10)])
                for g in range(3):
                    A('dve', lambda e, Cv=Cv, g=g: e.reciprocal(Cv(2 * g), Cv(2 * g)), [K(2 * g)], [K(2 * g)])
                    tt(Cv(6 + g), Cv(6 + g), Cv(10), ALU.mult, [K(6 + g), K(10)], [K(6 + g)])
                    tt(Cv(6 + g), Cv(6 + g), Cv(2 * g), ALU.mult, [K(6 + g), K(2 * g)], [K(6 + g)])
                    tt(Cv(6 + g), Cv(6 + g), OTG[g][:, c0:c0 + n], ALU.mult, [K(6 + g), ('OTG', g)], [K(6 + g)])
                tt(Cv(6), Cv(6), Cv(7), ALU.add, [K(6), K(7)], [K(6)])
                tt(YCH[:, c0:c0 + n], Cv(6), Cv(8), ALU.add, [K(6), K(8)], ['YCH'])
            DMA('sp', 'ych', [(oatt_d[slot], YCH[:, :])], r=['YCH'], w=[('oatt_d', slot)])

        SSTv = SST[:, :].rearrange("p (h c) -> p h c", h=8)
        hslot = [0]

        def hgrn_tile(ws, h, cols_fn, need_out, outs):
            s_ = hslot[0] & 1
            hslot[0] += 1
            wv = WB[ws][:, :].rearrange("p (k n) -> p k n", k=16)
            Z = psA if s_ == 0 else psF
            zk = 'psA' if s_ == 0 else 'psF'
            mms([(Z[:, :], cols_fn(kc), wv[:, kc, :], kc == 0, kc == 15) for kc in range(16)], [('WB', ws)] + ALLH, [zk])
            T = HT[:, s_ * 1536:(s_ + 1) * 1536]
            Bf = HB[:, s_ * 1024:(s_ + 1) * 1024]
            k_ = lambda n: ('h%s' % n, s_)
            f_, lf, kk, bs, eb, enb = (T[:, i * 128:(i + 1) * 128] for i in range(6))
            ekl, sq, sg, dif = (T[:, i * 128:(i + 1) * 128] for i in range(6, 10))
            qe, ke, kl, vb, og = (Bf[:, i * 128:(i + 1) * 128] for i in range(5))
            qkT = Bf[:, 640:896]
            attm = Bf[:, 896:960]
            act(f_, Z[:, 128:256], AF.Sigmoid, [zk], [k_('f')])
            tt(f_, f_, LB[:, 1024 + h * 128:1024 + (h + 1) * 128], ALU.mult, [k_('f'), 'LB'], [k_('f')])
            tt(f_, f_, LB[:, h * 128:(h + 1) * 128], ALU.add, [k_('f'), 'LB'], [k_('f')])
            act(lf, f_, AF.Ln, [k_('f')], [k_('lf')])
            tsc(kk, f_, -1.0, 1.0, ALU.mult, ALU.add, [k_('f')], [k_('k')])
            Pb = psB if s_ == 0 else psC
            pbk = 'psB' if s_ == 0 else 'psC'
            mms([(Pb[:, 0:128], tri2, lf, True, True), (Pb[:, 128:256], blk2, lf, True, True),
                 (Pb[:, 256:258], lf, ind2, True, True)], [k_('lf'), 'CON'], [pbk])
            tcopy(bs, Pb[:, 0:128], [pbk], [k_('bs')])
            act(eb, bs, AF.Exp, [k_('bs')], [k_('eb')])
            act(enb, bs, AF.Exp, [k_('bs')], [k_('enb')], scale=-1.0)
            tt(dif, Pb[:, 128:256], bs, ALU.subtract, [pbk, k_('bs')], [k_('dif')])
            act(ekl, dif, AF.Exp, [k_('dif')], [k_('ekl')])
            edec = ST[:, 56 + 2 * s_:58 + 2 * s_]
            act(edec, Pb[:, 256:258], AF.Exp, [pbk], [k_('edec')])
            act(sq, Z[:, 0:128], AF.Silu, [zk], [k_('sq')])
            tt(qe, sq, eb, ALU.mult, [k_('sq'), k_('eb')], [k_('qe')])
            tt(ke, kk, enb, ALU.mult, [k_('k'), k_('enb')], [k_('ke')])
            tt(kl, kk, ekl, ALU.mult, [k_('k'), k_('ekl')], [k_('kl')])
            tcopy(vb, Z[:, 256:384], [zk], [k_('v')], eng='act')
            if need_out:
                act(sg, Z[:, 384:512], AF.Silu, [zk], [k_('sg')])
            TP = psT if s_ == 0 else psU
            tpk = 'psT' if s_ == 0 else 'psU'
            trs([(TP[:, 512:640], qe, IDB[:, :]), (TP[:, 640:768], ke, IDB[:, :])], [k_('qe'), k_('ke'), 'IDB'], [tpk + 'h'])
            tcopy(qkT, TP[:, 512:768], [tpk + 'h'], [k_('qkT')])
            Po = psD if s_ == 0 else psE
            pok = 'psDh' if s_ == 0 else 'psEh'
            for c2 in range(2):
                rw = slice(c2 * 64, c2 * 64 + 64)
                mms([(Po[rw, 256:320], qkT[:, 128 + c2 * 64:192 + c2 * 64], qkT[:, c2 * 64:c2 * 64 + 64], True, True)],
                    [k_('qkT')], [pok + 'a%d' % c2])
                tt(attm[rw, :], Po[rw, 256:320], cmask[rw, :], ALU.mult, [pok + 'a%d' % c2, 'CON'], [k_('attm%d' % c2)])
                if need_out:
                    mms([(Po[rw, 0:128], attm[rw, :], vb[rw, :], True, False),
                         (Po[rw, 0:128], qkT[:, c2 * 64:c2 * 64 + 64], SBF[:, :], False, True)],
                        [k_('attm%d' % c2), k_('v'), k_('qkT'), 'SBF'], [pok + 'o%d' % c2])
                mms([(Po[:, 128:256], kl[rw, :], vb[rw, :], True, True)], [k_('kl'), k_('v')], [pok + 'u'])
                stt(SSTv[:, h, :], SSTv[:, h, :], edec[:, c2:c2 + 1], Po[:, 128:256], ALU.mult, ALU.add,
                    [pok + 'u', k_('edec'), 'SST'], ['SST'])
                tcopy(SBF[:, :], SSTv[:, h, :], ['SST'], ['SBF'], eng='act')
            if need_out:
                ok_ = [pok + 'o0', pok + 'o1']
                act(dif, Po[:, 0:128], AF.Square, ok_, [k_('dif'), k_('ssq')], accum_out=ST[:, 44 + s_:45 + s_])
                act(ST[:, 46 + s_:47 + s_], ST[:, 44 + s_:45 + s_], AF.Sqrt, [k_('ssq')], [k_('ssq2')], scale=1.0 / 128, bias=EPS)
                A('dve', lambda e: e.reciprocal(ST[:, 44 + s_:45 + s_], ST[:, 46 + s_:47 + s_]), [k_('ssq2')], [k_('rs')])
                stt(dif, Po[:, 0:128], ST[:, 44 + s_:45 + s_], HGW[:, :], ALU.mult, ALU.mult, ok_ + [k_('rs'), 'HGW'], [k_('dif')])
                tt(og, dif, sg, ALU.mult, [k_('dif'), k_('sg')], [k_('og')])
                trs([(TP[:, 768:896], og, IDB[:, :])], [k_('og'), 'IDB'], [tpk + 'g'])
                for (buf, bk, src, dst) in outs:
                    tcopy(buf[:, dst], TP[:, 768:896][:, src], [tpk + 'g'], [bk], eng='act')

        def hgrn_misc(ws, h):
            wv = WB[ws][:, :].rearrange("p (k n) -> p k n", k=16)
            M = MS
            mms([(psA[0:M, :], hT[:, kc, TOK:TOK + MS], wv[:, kc, :], kc == 0, kc == 15) for kc in range(16)], [('WB', ws)] + ALLH, ['psA'])
            T = HT[0:M, 0:1536]
            Bf = HB[0:M, 0:1024]
            f_, lf, kk, bs, eb, enb = (T[:, i * 128:(i + 1) * 128] for i in range(6))
            ekl, sq, sg, dif = (T[:, i * 128:(i + 1) * 128] for i in range(6, 10))
            qe, ke, kl, vb, og = (Bf[:, i * 128:(i + 1) * 128] for i in range(5))
            DMA('sp', 's0', [(S0[:, :].rearrange("p (b c) -> p b c", b=4), sthg[:, h].rearrange("b d v -> d b v"))], w=['S0'])
            tcopy(S0B[:, :], S0[:, :], ['S0'], ['S0B'])
            act(f_, psA[0:M, 128:256], AF.Sigmoid, ['psA'], ['mf'])
            tt(f_, f_, LB[0:M, 1024 + h * 128:1024 + (h + 1) * 128], ALU.mult, ['mf', 'LB'], ['mf'])
            tt(f_, f_, LB[0:M, h * 128:(h + 1) * 128], ALU.add, ['mf', 'LB'], ['mf'])
            act(lf, f_, AF.Ln, ['mf'], ['mlf'])
            tsc(kk, f_, -1.0, 1.0, ALU.mult, ALU.add, ['mf'], ['mk'])
            mms([(psB[0:M, 0:128], tri_m, lf, True, True), (psB[0:M, 128:256], blk_m, lf, True, True),
                 (psB[:, 256:260], lf, ind_m, True, True)], ['mlf', 'CON'], ['psB'])
            tcopy(bs, psB[0:M, 0:128], ['psB'], ['mbs'])
            act(eb, bs, AF.Exp, ['mbs'], ['meb'])
            act(enb, bs, AF.Exp, ['mbs'], ['menb'], scale=-1.0)
            tt(dif, psB[0:M, 128:256], bs, ALU.subtract, ['psB', 'mbs'], ['mdif'])
            act(ekl, dif, AF.Exp, ['mdif'], ['mekl'])
            edec = ST[:, 40:44]
            act(edec, psB[:, 256:260], AF.Exp, ['psB'], ['medec'])
            act(sq, psA[0:M, 0:128], AF.Silu, ['psA'], ['msq'])
            act(sg, psA[0:M, 384:512], AF.Silu, ['psA'], ['msg'])
            tt(qe, sq, eb, ALU.mult, ['msq', 'meb'], ['mqe'])
            tt(ke, kk, enb, ALU.mult, ['mk', 'menb'], ['mke'])
            tt(kl, kk, ekl, ALU.mult, ['mk', 'mekl'], ['mkl'])
            tcopy(vb, psA[0:M, 256:384], ['psA'], ['mv'], eng='act')
            trs([(psT[:, 512:512 + M], qe, IDB[0:M, 0:M]), (psT[:, 640:640 + M], ke, IDB[0:M, 0:M])], ['mqe', 'mke', 'IDB'], ['psTh'])
            qeT = HB[:, 1024:1024 + M]
            keT = HB[:, 1056:1056 + M]
            qeTm = HB[:, 1152:1152 + 4 * M].rearrange("p (b t) -> p b t", b=4)
            klm = HB[0:M, 1280:1280 + 512].rearrange("p (b c) -> p b c", b=4)
            tcopy(qeT, psT[:, 512:512 + M], ['psTh'], ['mqeT'])
            tcopy(keT, psT[:, 640:640 + M], ['psTh'], ['mkeT'])
            tt(qeTm, qeT[:, None, :].broadcast_to([128, 4, M]), colm.rearrange("p (b t) -> p b t", b=4), ALU.mult, ['mqeT', 'CON'], ['mqeTm'])
            tt(klm, kl[:, None, :].broadcast_to([M, 4, 128]), rowm[:, :, None].broadcast_to([M, 4, 128]), ALU.mult, ['mkl', 'CON'], ['mklm'])
            mms([(psD[0:M, 256:256 + M], keT, qeT, True, True)], ['mqeT', 'mkeT'], ['psDh'])
            attm = HB[0:M, 896:896 + M]
            tt(attm, psD[0:M, 256:256 + M], cmask_m, ALU.mult, ['psDh', 'CON'], ['mattm'])
            lst = [(psD[0:M, 0:128], attm, vb, True, False)]
            for b in range(4):
                lst.append((psD[0:M, 0:128], qeTm[:, b, :], S0B[:, b * 128:(b + 1) * 128], False, b == 3))
            mms(lst, ['mattm', 'mv', 'mqeTm', 'S0B'], ['psDo'])
            mms([(psE[:, b * 128:(b + 1) * 128], klm[:, b, :], vb, True, True) for b in range(4)], ['mklm', 'mv'], ['psE'])
            S0v = S0[:, :].rearrange("p (b c) -> p b c", b=4)
            tt(S0v, S0v, edec[:, :, None].broadcast_to([128, 4, 128]), ALU.mult, ['S0', 'medec', 'S0B'], ['S0'])
            tt(S0[:, :], S0[:, :], psE[:, :], ALU.add, ['S0', 'psE'], ['S0'])
            DMA('sp', 's0o', [(hgs[:, h].rearrange("b d v -> d b v"), S0v)], r=['S0'])
            act(dif, psD[0:M, 0:128], AF.Square, ['psDo'], ['mdif', 'mssq'], accum_out=ST[0:M, 36:37])
            act(ST[0:M, 37:38], ST[0:M, 36:37], AF.Sqrt, ['mssq'], ['mssq2'], scale=1.0 / 128, bias=EPS)
            A('dve', lambda e: e.reciprocal(ST[0:M, 38:39], ST[0:M, 37:38]), ['mssq2'], ['mrs'])
            stt(dif, psD[0:M, 0:128], ST[0:M, 38:39], HGW[0:M, :], ALU.mult, ALU.mult, ['psDo', 'mrs', 'HGW'], ['mdif'])
            tt(og, dif, sg, ALU.mult, ['mdif', 'msg'], ['mog'])
            trs([(psT[:, 768:768 + M], og, IDB[0:M, 0:M])], ['mog', 'IDB'], ['psTg'])
            tcopy(OHGH[:, TOK:TOK + 16], psT[:, 768:784], ['psTg'], ['OHGH'], eng='act')

        A('pool', lambda e: e.memset(SST[:, :], 0.0), [], ['SST'])
        A('pool', lambda e: e.memset(SBF[:, :], 0.0), [], ['SBF'])
        ONEROW = lsb("ONEROW", [1, NTOK])
        A('pool', lambda e: e.memset(ONEROW[:, :], 1.0), [], ['ONEROW'])
        DMA('sp', 'lmi', [(lm_d[g, k_:k_ + 1, :], ONEROW[:, :]) for g in range(3) for k_ in range(2)], r=['ONEROW'],
            w=[('lm_d', 0), ('lm_d', 1), ('lm_d', 2)])
        phase_end()
        for nt in range(16):
            norm_tile(xall[nt * 128:(nt + 1) * 128, :], 128, 0, nt * 128, 128, False)
        alloc_attn()
        for hh in range(12):
            attn_head(hh, 'H')
        phase_end()
        alloc_hgrn()
        OHGPv = OHGPRE[:, :].rearrange("p (h c) -> p h c", h=8)
        for h in range(8):
            ws = wload(whg_t[h], 8192)
            for nt in range(16):
                hgrn_tile(ws, h, lambda kc, nt=nt: hT[:, kc, nt * 128:(nt + 1) * 128], nt == 15,
                          [(OHGPRE, 'OHGPRE', slice(126, 128), slice(2 * h, 2 * h + 2))])
            tsc(SSTv[:, h, :], SSTv[:, h, :], CVAL[:, 0:1], None, ALU.mult, ALU.bypass, ['SST', 'CVAL'], ['SST'])
        phase_end()
        for nt in range(16):
            norm_tile(xall[(16 + nt) * 128:(17 + nt) * 128, :], 128, 0, nt * 128, 128, False)
        norm_tile(xall[32 * 128:32 * 128 + MS, :], MS, 0, TOK, MS, True)
        alloc_attn()
        for slot in range(4):
            for g in range(3):
                attn_head(4 * g + slot, 'O')
            attn_combine(slot)
        phase_end()
        alloc_hgrn()
        for h in range(8):
            ws = wload(whg_t[h], 8192)
            tcopy(SBF[:, :], SSTv[:, h, :], ['SST'], ['SBF'], eng='act')
            for nt in range(16):
                hgrn_tile(ws, h, lambda kc, nt=nt: hT[:, kc, nt * 128:(nt + 1) * 128], True,
                          [(OHGH, 'OHGH', slice(0, 128), slice(nt * 128, (nt + 1) * 128))])
            DMA('sp', 'hgp', [(hgp[h], SSTv[:, h, :])], r=['SST'])
            hgrn_misc(ws, h)
            tcopy(OHGH[:, TOK + 16:TOK + 18], OHGPv[:, h, :], ['OHGPRE'], ['OHGH'])
            DMA('sp', 'ohgo', [(ohg_d[h], OHGH[:, :])], r=['OHGH'], w=[('ohg_d', h)])
        phase_end()
        BIGB = lsb("BIGB", [128, 12 * NTOK], BF16)
        YCH = lsb("YCHb", [128, NTOK], BF16)
        GT = lsb("GT", [128, 2 * 512])
        OAT = BIGB[:, :].rearrange("p (k t) -> p k t", k=12)
        DMA('sp', 'oal', [(OAT[:, s_, :], oatt_d[s_]) for s_ in range(4)] + [(OAT[:, 4 + h, :], ohg_d[h]) for h in range(8)], w=['OAT'])
        for fc in range(16):
            ws = wload(wB1_t[fc], 44 * 128)
            wv = WB[ws][:, 0:44 * 128].rearrange("p (k n) -> p k n", k=44)
            for bi, c0 in enumerate(range(0, NTOK, 512)):
                n = min(512, NTOK - c0)
                lst = []
                for kc in range(16):
                    lst.append((psA[:, 0:n], wv[:, kc, :], hT[:, kc, c0:c0 + n], kc == 0, kc == 15))
                for kc in range(16):
                    lst.append((psB[:, 0:n], wv[:, 16 + kc, :], hT[:, kc, c0:c0 + n], kc == 0, kc == 15))
                for kc in range(4):
                    lst.append((psC[:, 0:n], wv[:, 32 + kc, :], OAT[:, kc, c0:c0 + n], kc == 0, kc == 3))
                for kc in range(8):
                    lst.append((psD[:, 0:n], wv[:, 36 + kc, :], OAT[:, 4 + kc, c0:c0 + n], kc == 0, kc == 7))
                mms(lst, [('WB', ws), 'OAT'] + ALLH, ['psA', 'psB', 'psC', 'psD'])
                act(GT[:, 0:n], psA[:, 0:n], AF.Sigmoid, ['psA'], ['GT0'])
                act(GT[:, 512:512 + n], psB[:, 0:n], AF.Sigmoid, ['psB'], ['GT1'])
                tt(GT[:, 0:n], GT[:, 0:n], psC[:, 0:n], ALU.mult, ['GT0', 'psC'], ['GT0'])
                tt(GT[:, 512:512 + n], GT[:, 512:512 + n], psD[:, 0:n], ALU.mult, ['GT1', 'psD'], ['GT1'])
                tt(YCH[:, c0:c0 + n], GT[:, 0:n], GT[:, 512:512 + n], ALU.add, ['GT0', 'GT1'], ['YCH'])
            DMA('sp', 'ych', [(ymix_d[fc], YCH[:, :])], r=['YCH'], w=[('ymix_d', fc)])
        phase_end()
        GT = lsb("GT2", [128, 512])
        GBC = lsb("GBC", [128, 512])
        GBM = lsb("GBM", [MS, 512])
        XQ = [lsb("XQ0", [128, 512]), lsb("XQ1", [128, 512])]
        ymT = BIGA[:, :].rearrange("p (k t) -> p k t", k=16)
        DMA('sp', 'yml', [(ymT[:, fc, :], ymix_d[fc]) for fc in range(16)], w=['ymT'])

        def gate_bc(off, cb):
            DMA('sp', 'mrl', [(MR[:, :], modr_d[:, off + cb * 512:off + (cb + 1) * 512])], r=['modr_d'], w=['MR'])
            mms([(psE[:, :], ones_f[0:1, :], MR[0:1, :], True, True),
                 (psF[0:MS, :], rowsel, MR[0:5, :], True, True)], ['MR', 'CON'], ['psE', 'psF'])
            tcopy(GBC[:, :], psE[:, :], ['psE'], ['GBC'])
            tcopy(GBM[:, :], psF[0:MS, :], ['psF'], ['GBM'], eng='act')

        for cb in range(4):
            ws = wload(wo_t[cb], 8192)
            wv = WB[ws][:, :].rearrange("p (k n) -> p k n", k=16)
            gate_bc(0, cb)
            for tl in range(17):
                M = 128 if tl < 16 else MS
                P_ = psA if tl % 2 == 0 else psB
                pk = 'psA' if tl % 2 == 0 else 'psB'
                mms([(P_[0:M, :], ymT[:, kc, tl * 128:tl * 128 + M], wv[:, kc, :], kc == 0, kc == 15) for kc in range(16)],
                    [('WB', ws), 'ymT'], [pk])
                xq = XQ[tl % 2]
                xk = 'XQ%d' % (tl % 2)
                srow = (16 + tl) * 128
                DMA('sp', 'xq%d' % (tl % 2), [(xq[0:M, :], xall[srow:srow + M, cb * 512:(cb + 1) * 512])], w=[xk])
                G_ = GBC if tl < 16 else GBM
                tt(GT[0:M, 0:512], P_[0:M, :], G_[0:M, :], ALU.mult, [pk, 'GBC', 'GBM'], ['GT0'])
                tt(xq[0:M, :], xq[0:M, :], GT[0:M, 0:512], ALU.add, [xk, 'GT0'], [xk])
                DMA('sp', 'x1o%d' % (tl % 2), [(x1_d[tl * 128:tl * 128 + M, cb * 512:(cb + 1) * 512], xq[0:M, :])], r=[xk], w=[('x1_d', tl)])
        phase_end()
        for tl in range(16):
            norm_tile(x1_d[tl * 128:(tl + 1) * 128, :], 128, 1, tl * 128, 128, False)
        norm_tile(x1_d[16 * 128:16 * 128 + MS, :], MS, 1, TOK, MS, True)
        phase_end()
        BIGB = lsb("YTB", [128, NJ * 512], BF16)
        GT = lsb("GT3", [128, 512])
        GBC = lsb("GBC3", [128, 512])
        GBM = lsb("GBM3", [MS, 512])
        XQ = [lsb("XQ03", [128, 512]), lsb("XQ13", [128, 512])]
        ACAT = lsb("ACAT", [128, 520])
        UU = lsb("UU", [128, 512])
        CROW = RTMP[0:8, :]
        YT = BIGB[:, 0:NJ * 512].rearrange("p (j t) -> p j t", j=NJ)
        CWv = CW[:, :].rearrange("p (a j) -> p a j", a=4)
        CARv = CARRY[:, :].rearrange("p (j c) -> p j c", c=2)
        SCVv = SCV[:, :].rearrange("p (j c) -> p j c", c=8)
        ASVv = ASV[:, :].rearrange("p (j c) -> p j c", c=8)
        for j0 in range(0, NJ, 4):
            DMA('sp', 'crw', [(CROW[:, :], stconv[:, j0 * 128:(j0 + 4) * 128])], w=['CROW'])
            mms([(psE[:, (j - j0) * 8:(j - j0) * 8 + 8], CROW[0:8, (j - j0) * 128:(j - j0 + 1) * 128], identf[0:8, 0:8], True, True) for j in range(j0, j0 + 4)],
                ['CROW', 'CON'], ['psE'])
            tcopy(SCV[:, j0 * 8:(j0 + 4) * 8], psE[:, 0:32], ['psE'], ['SCV'])
        blocks = [('m', TOK, MS)] + [('o', b * 512, 512) for b in range(4)]
        for (kind, c0, n) in blocks:
            for j in range(NJ):
                ws = wload(wab_t[j], 4096)
                wv = WB[ws][:, 0:4096].rearrange("p (k n) -> p k n", k=32)
                s_ = j & 1
                Pa = psA if s_ == 0 else psC
                Pb = psB if s_ == 0 else psD
                pak = 'psA' if s_ == 0 else 'psC'
                pbk = 'psB' if s_ == 0 else 'psD'
                lst = [(Pa[:, 0:n], wv[:, kc, :], hT[:, kc, c0:c0 + n], kc == 0, kc == 15) for kc in range(16)]
                lst += [(Pb[:, 0:n], wv[:, 16 + kc, :], hT[:, kc, c0:c0 + n], kc == 0, kc == 15) for kc in range(16)]
                mms(lst, [('WB', ws)] + ALLH, [pak, pbk])
                ac = ACAT[:, 0:520]
                u = UU[:, 0:512]
                ack = ('ac', 0)
                uk = ('u', 0)
                w0, w1, w2, cb_ = (CWv[:, a, j:j + 1] for a in range(4))
                if kind == 'o':
                    tcopy(ac[:, 0:2], CARv[:, j, :], ['CARRY'], [ack])
                    tcopy(ac[:, 2:2 + n], Pa[:, 0:n], [pak], [ack], eng='act')
                    tsc(u[:, 0:n], ac[:, 2:2 + n], w2, cb_, ALU.mult, ALU.add, [ack, 'CW'], [uk])
                    stt(u[:, 0:n], ac[:, 1:1 + n], w1, u[:, 0:n], ALU.mult, ALU.add, [ack, 'CW', uk], [uk])
                    stt(u[:, 0:n], ac[:, 0:n], w0, u[:, 0:n], ALU.mult, ALU.add, [ack, 'CW', uk], [uk])
                    tcopy(CARv[:, j, :], ac[:, n:n + 2], [ack], ['CARRY'])
                else:
                    acv = ac[:, 0:24].rearrange("p (b t) -> p b t", b=4)
                    tcopy(acv[:, :, 0:2], SCVv[:, j, :].rearrange("p (b t) -> p b t", b=4), ['SCV'], [ack])
                    tcopy(acv[:, :, 2:6], Pa[:, 0:16].rearrange("p (b t) -> p b t", b=4), [pak], [ack])
                    uv = u[:, 0:16].rearrange("p (b t) -> p b t", b=4)
                    tsc(uv, acv[:, :, 2:6], w2, cb_, ALU.mult, ALU.add, [ack, 'CW'], [uk])
                    stt(uv, acv[:, :, 1:5], w1, uv, ALU.mult, ALU.add, [ack, 'CW', uk], [uk])
                    stt(uv, acv[:, :, 0:4], w0, uv, ALU.mult, ALU.add, [ack, 'CW', uk], [uk])
                    A('pool', lambda e, u=u: e.memset(u[:, 16:32], 0.0), [uk], [uk])
                    tcopy(ASVv[:, j, :].rearrange("p (b t) -> p b t", b=4), acv[:, :, 4:6], [ack], ['ASV'])
                    tsc(CARv[:, j, :], Pa[:, 16:18], CVAL[:, 0:1], None, ALU.mult, ALU.bypass, [pak, 'CVAL'], ['CARRY'])
                act(u[:, 0:n], u[:, 0:n], AF.Silu, [uk], [uk])
                tt(YT[:, j, 0:n], u[:, 0:n], Pb[:, 0:n], ALU.mult, [uk, pbk], ['YT'])
            ntl = (n + 127) // 128
            for cb in range(4):
                gate_bc(D, cb)
                for jg in range(4):
                    ws = wload(wd_t[cb * 4 + jg], 11 * 512)
                    wv = WB[ws][:, 0:11 * 512].rearrange("p (k n) -> p k n", k=11)
                    Ps = [psA, psB, psC, psD]
                    for tl in range(ntl):
                        M = min(128, n - tl * 128)
                        mms([(Ps[tl][0:M, :], YT[:, jg * 11 + jj, tl * 128:tl * 128 + M], wv[:, jj, :], jg == 0 and jj == 0, jg == 3 and jj == 10)
                             for jj in range(11)], [('WB', ws), 'YT'], ['psA', 'psB', 'psC', 'psD'][tl:tl + 1])
                for tl in range(ntl):
                    M = min(128, n - tl * 128)
                    gt = (c0 // 128 + tl)
                    xq = XQ[tl % 2]
                    xk = 'XQ%d' % (tl % 2)
                    pk = ['psA', 'psB', 'psC', 'psD'][tl]
                    DMA('sp', 'xq%d' % (tl % 2), [(xq[0:M, :], x1_d[gt * 128:gt * 128 + M, cb * 512:(cb + 1) * 512])], w=[xk])
                    G_ = GBC if kind == 'o' else GBM
                    tt(GT[0:M, 0:512], [psA, psB, psC, psD][tl][0:M, :], G_[0:M, :], ALU.mult, [pk, 'GBC', 'GBM'], ['GT0'])
                    tt(xq[0:M, :], xq[0:M, :], GT[0:M, 0:512], ALU.add, [xk, 'GT0'], [xk])
                    DMA('sp', 'x1o%d' % (tl % 2), [(x2_d[gt * 128:gt * 128 + M, cb * 512:(cb + 1) * 512], xq[0:M, :])], r=[xk], w=[('x2_d', gt)])
        for (src, dstd, nr, ch_) in ((ASVv, convs, 8, 'cvo0'), (CARv, convp, 2, 'cvo1')):
            for j0 in range(0, NJ, 4):
                mms([(psE[0:nr, (j - j0) * 128:(j - j0 + 1) * 128], src[:, j, :], identf, True, True) for j in range(j0, j0 + 4)],
                    ['ASV', 'CARRY', 'CON'], ['psE'])
                tcopy(CROW[0:nr, 0:512], psE[0:nr, :], ['psE'], ['CROW'])
                DMA('sp', ch_, [(dstd[:, j0 * 128:(j0 + 4) * 128], CROW[0:nr, 0:512])], r=['CROW'], w=['CROWo'])
        phase_end()
        NFW = lsb("NFW", [128, D])
        DMA('sp', 'nfw', [(NFW[:, :], normf.partition_broadcast(128))], w=['NFW'])
        for tl in range(17):
            M = 128 if tl < 16 else MS
            s_ = tl % 2
            X = XT[s_]
            xk = 'XT%d' % s_
            DMA('sp', 'x%d' % s_, [(X[0:M, :], x2_d[tl * 128:tl * 128 + M, :])], w=[xk])
            act(XN[0:M, :], X[0:M, :], AF.Square, [xk], ['XN', 'STn'], accum_out=ST[0:M, 48:49])
            act(ST[0:M, 49:50], ST[0:M, 48:49], AF.Sqrt, ['STn'], ['STn2'], scale=1.0 / D, bias=EPS)
            A('dve', lambda e, M=M: e.reciprocal(ST[0:M, 50:51], ST[0:M, 49:50]), ['STn2'], ['STn3'])
            stt(X[0:M, :], X[0:M, :], ST[0:M, 50:51], NFW[0:M, :], ALU.mult, ALU.mult, [xk, 'STn3', 'NFW'], [xk])
            dst = y_own[tl * 128:(tl + 1) * 128, :] if tl < 16 else y_misc
            DMA('sp', 'yo%d' % s_, [(dst, X[0:M, :])], r=[xk])
        for g, (W, d) in enumerate(GROUPS):
            DMA('pool', 'kvc', [(kvs[g][b, 0:W - 4], ck[g][b, 4:W]) for b in range(4)])

        if stop is not None:
            S.ops = S.ops[:stop]
        chs = sorted({o['ch'] for o in S.ops if o['ch'] is not None})
        sems_ch = {ch: es.enter_context(nc.semaphore("c_" + ch)) for ch in chs}
        sems_eng = {e_: es.enter_context(nc.semaphore("e_" + e_)) for e_ in ('pe', 'act', 'dve', 'pool')}
        block = es.enter_context(nc.Block())
        S.emit(nc, block, sems_eng, sems_ch, chs)
        ph[0].close()
    return nc


def _tile_w(Wsub, kc):
    n = Wsub.shape[1]
    return np.ascontiguousarray(Wsub.reshape(kc, 128, n).transpose(1, 0, 2).reshape(128, kc * n))


def _consts():
    C = np.zeros((128, 2048), np.float32)
    C[:, 0:128] = np.eye(128)
    s = np.arange(128)[:, None]
    t = np.arange(128)[None, :]
    same = (s // 64) == (t // 64)
    C[:, 128:256] = (same & (s <= t))
    C[:, 256:384] = same
    C[:, 384] = (np.arange(128) < 64)
    C[:, 385] = (np.arange(128) >= 64)
    C[:, 512:576] = ((np.arange(128)[:, None] % 64) <= np.arange(64)[None, :])
    s = np.arange(32)[:, None]
    t = np.arange(32)[None, :]
    samem = ((s // 4) == (t // 4)) & (s < 16) & (t < 16)
    C[0:32, 640:672] = samem & (s <= t)
    C[0:32, 672:704] = samem
    for b in range(4):
        C[4 * b:4 * b + 4, 704 + b] = 1
        C[4 * b:4 * b + 4, 768 + b] = 1
        C[:, 896 + b * 32 + 4 * b:896 + b * 32 + 4 * b + 4] = 1
    C[0:32, 736:768] = samem & (s <= t)
    for m in range(32):
        C[(1 + m // 4) if m < 16 else 0, 1024 + m] = 1
    C[:, 1152:1280] = 1
    return C


def _rope(pos):
    half = 16
    inv = (np.float32(500000.0) ** (-np.arange(half, dtype=np.float32) * np.float32(2.0) / np.float32(32))).astype(np.float32)
    ang = pos.astype(np.float32)[:, None] * inv[None, :]
    return np.concatenate([np.cos(ang), np.sin(ang)], axis=1).astype(np.float32)


_NC = None


def kernel(x_prompt, x_sample, cache_kv_w128, cache_kv_w512, cache_kv_w2048, state_hgrn, state_conv,
           c_prompt, c_sample, w_ada, b_ada, norm1_w, w_in, hg_lb, hg_norm_w, w_pa, w_pb, w_o,
           norm2_w, w_ffn_a, w_ffn_b, conv_w, conv_b, w_ffn_down, norm_f_w):
    global _NC
    in_maps = _prep(x_prompt, x_sample, cache_kv_w128, cache_kv_w512, cache_kv_w2048, state_hgrn, state_conv,
                    c_prompt, c_sample, w_ada, b_ada, norm1_w, w_in, hg_lb, hg_norm_w, w_pa, w_pb, w_o,
                    norm2_w, w_ffn_a, w_ffn_b, conv_w, conv_b, w_ffn_down, norm_f_w)
    return _run(in_maps)


def _prep(x_prompt, x_sample, cache_kv_w128, cache_kv_w512, cache_kv_w2048, state_hgrn, state_conv,
          c_prompt, c_sample, w_ada, b_ada, norm1_w, w_in, hg_lb, hg_norm_w, w_pa, w_pb, w_o,
          norm2_w, w_ffn_a, w_ffn_b, conv_w, conv_b, w_ffn_down, norm_f_w):
    f = lambda a: np.asarray(a, dtype=np.float32)
    xp = f(x_prompt)[0]
    xs = f(x_sample)
    Win = f(w_in)[0]
    shared = {}
    shared["wada_t"] = np.stack([_tile_w(f(w_ada)[0][:, j * 512:(j + 1) * 512], 16) for j in range(24)])
    shared["bada"] = f(b_ada).reshape(1, -1)
    shared["vecA"] = np.concatenate([f(norm1_w)[0].reshape(16, 128), f(norm2_w)[0].reshape(16, 128)], 0)
    shared["normf"] = f(norm_f_w).reshape(1, -1)
    shared["watt_t"] = np.stack([_tile_w(np.concatenate([Win[:, hh * 128:(hh + 1) * 128], Win[:, 1536 + hh * 128:1536 + (hh + 1) * 128],
                                                          Win[:, 3072 + hh * 128:3072 + (hh + 1) * 128]], 1), 16) for hh in range(12)])
    shared["whg_t"] = np.stack([_tile_w(np.concatenate([Win[:, 4608 + k * 1024 + h * 128:4608 + k * 1024 + (h + 1) * 128] for k in range(4)], 1), 16)
                                for h in range(8)])
    Wpa, Wpb = f(w_pa)[0], f(w_pb)[0]
    shared["wB1_t"] = np.stack([np.concatenate([_tile_w(Win[:, 8704 + fc * 128:8704 + (fc + 1) * 128], 16),
                                                _tile_w(Win[:, 10752 + fc * 128:10752 + (fc + 1) * 128], 16),
                                                _tile_w(Wpa[:, fc * 128:(fc + 1) * 128], 4),
                                                _tile_w(Wpb[:, fc * 128:(fc + 1) * 128], 8)], 1) for fc in range(16)])
    Wo = f(w_o)[0]
    shared["wo_t"] = np.stack([_tile_w(Wo[:, cb * 512:(cb + 1) * 512], 16) for cb in range(4)])
    Wa, Wb, Wd = f(w_ffn_a)[0], f(w_ffn_b)[0], f(w_ffn_down)[0]
    shared["wab_t"] = np.stack([np.concatenate([_tile_w(Wa[:, j * 128:(j + 1) * 128], 16), _tile_w(Wb[:, j * 128:(j + 1) * 128], 16)], 1)
                                for j in range(NJ)])
    shared["wd_t"] = np.stack([_tile_w(Wd[jg * 11 * 128:(jg + 1) * 11 * 128, cb * 512:(cb + 1) * 512], 11)
                               for cb in range(4) for jg in range(4)])
    cw = f(conv_w)[0]
    shared["convw"] = np.concatenate([cw[0].reshape(NJ, 128), cw[1].reshape(NJ, 128), cw[2].reshape(NJ, 128),
                                      f(conv_b)[0].reshape(NJ, 128)], 0)
    shared["hglb"] = f(hg_lb)
    shared["hgnw"] = f(hg_norm_w).reshape(1, 128)
    shared["consts"] = _consts()
    i = np.arange(128)[:, None]
    ip = np.arange(256)[None, :]
    band = np.where((ip >= i) & (ip <= i + 128), 0.0, NEG).astype(np.float32)

    in_maps = []
    for c in range(NCORE):
        m = dict(shared)
        xall = np.zeros((33 * 128, D), np.float32)
        own0 = TOK * c

        def tokrow(p):
            return xp[p] if p >= 0 else np.zeros(D, np.float32)
        if c > 0:
            xall[0:2048] = xp[own0 - 2048:own0]
        xall[2048:4096] = xp[own0:own0 + TOK]
        mb = 32 * 128
        xall[mb:mb + 16] = xs[4 * c:4 * c + 4].reshape(16, D)
        pos_m = np.zeros(MS, np.int64)
        for b in range(4):
            for t in range(4):
                pos_m[4 * b + t] = 16384 + t
        for t in range(2):
            xall[mb + 16 + t] = tokrow(own0 - 2 + t)
            pos_m[16 + t] = own0 - 2 + t
        for g, (W, d) in enumerate(GROUPS):
            for t in range(2):
                p = own0 - 2 + t - 128 * d
                xall[mb + 18 + 2 * g + t] = tokrow(p)
                pos_m[18 + 2 * g + t] = p
        m["xall"] = xall
        m["c5"] = np.concatenate([f(c_prompt), f(c_sample)[4 * c:4 * c + 4]], 0)
        rt = np.zeros((3, 32, 128, 32), np.float32)
        for g, (W, d) in enumerate(GROUPS):
            nT = 16 // d
            for r in range(d):
                for T_ in range(-1, nT):
                    pos = own0 + r + d * (128 * T_ + np.arange(128))
                    rt[g, r * (nT + 1) + T_ + 1] = _rope(pos)
        m["rope_t"] = rt
        m["rope_m"] = _rope(pos_m)
        mh = band.copy()
        if c == 0:
            mh[:, 0:128] = NEG
        m["masks"] = np.concatenate([band, mh], 1)
        m["ck128"] = f(cache_kv_w128)[0, 4 * c:4 * c + 4]
        m["ck512"] = f(cache_kv_w512)[0, 4 * c:4 * c + 4]
        m["ck2048"] = f(cache_kv_w2048)[0, 4 * c:4 * c + 4]
        m["sthg"] = f(state_hgrn)[0, 4 * c:4 * c + 4]
        m["stconv"] = f(state_conv)[0, 4 * c:4 * c + 4].reshape(8, DFF)
        m["cvalid"] = np.full((128, 1), 0.0 if c == 0 else 1.0, np.float32)
        in_maps.append(m)
    return in_maps


def _run(in_maps):
    global _NC
    if _NC is None:
        _NC = build()
    res = run_bass_kernel_spmd(_NC, in_maps, core_ids=list(range(NCORE)))
    R = res.results
    y_prompt = np.concatenate([R[c]["y_own"] for c in range(NCORE)], 0)[None]
    y_sample = np.concatenate([R[c]["y_misc"][0:16].reshape(4, 4, D) for c in range(NCORE)], 0)
    L = R[NCORE - 1]
    outs = [y_prompt, y_sample,
            L["kvp128"][None, None], L["kvp512"][None, None], L["kvp2048"][None, None],
            L["hgp"][None, None], L["convp"].reshape(1, 1, 2, DFF)]
    for nm in ("kvs128", "kvs512", "kvs2048"):
        outs.append(np.concatenate([R[c][nm] for c in range(NCORE)], 0)[None])
    outs.append(np.concatenate([R[c]["hgs"] for c in range(NCORE)], 0)[None])
    outs.append(np.concatenate([R[c]["convs"].reshape(4, 2, DFF) for c in range(NCORE)], 0)[None])
    return tuple(np.ascontiguousarray(o, dtype=np.float32) for o in outs)
```

```python
import numpy as np
import concourse.bass as bass
import concourse.mybir as mybir
from concourse.bass_utils import run_bass_kernel_spmd

F32 = mybir.dt.float32
BF16 = mybir.dt.bfloat16
AF = mybir.ActivationFunctionType
ALU = mybir.AluOpType
AX = mybir.AxisListType

D = 2048
NCORE = 8
TOK = 2048
MS = 32
NTOK = TOK + MS
DFF = 5632
NJ = 44
EPS = 1e-6
GROUPS = ((128, 1), (512, 4), (2048, 16))
NEG = -30000.0
import os as _os
UDEPTH = int(_os.environ.get("UDEPTH", "3"))


class Sched:
    def __init__(self):
        self.ops = []
        self.lw = {}
        self.rd = {}
        self.fence_deps = set()
        self.last_eng = {}
        self.last_ch = {}

    @staticmethod
    def _nk(k):
        n = k[0] if isinstance(k, tuple) else k
        if isinstance(n, str) and len(n) >= 3 and n.startswith('ps') and n[2] in 'ABCDEFTU':
            return n[:3]
        return k

    def add(self, eng, fn, r=(), w=(), ch=None):
        r = [self._nk(k) for k in r]
        w = [self._nk(k) for k in w]
        psr = [k for k in r if isinstance(k, str) and len(k) == 3 and k.startswith('ps')]
        if psr:
            r = [k for k in r if k not in psr]
            w = list(w) + psr
        deps = set(self.fence_deps)
        for k in r:
            if k in self.lw:
                deps.add(self.lw[k])
        for k in w:
            if k in self.lw:
                deps.add(self.lw[k])
            deps.update(self.rd.get(k, ()))
        i = len(self.ops)
        import sys as _s
        fr = _s._getframe(1)
        lines = []
        while fr is not None and len(lines) < 4:
            lines.append(fr.f_lineno)
            fr = fr.f_back
        self.ops.append(dict(eng=eng, fn=fn, deps=deps, ch=ch, ln=lines))
        for k in r:
            self.rd.setdefault(k, []).append(i)
        for k in w:
            self.lw[k] = i
            self.rd[k] = []
        if ch is None:
            self.last_eng[eng] = i
        else:
            self.last_ch[ch] = i
        return i

    def fence(self):
        self.fence_deps = set(self.last_eng.values()) | set(self.last_ch.values())

    def emit(self, nc, block, sems_eng, sems_ch, final_chs):
        ops = self.ops
        needed = [False] * len(ops)
        for o in ops:
            for d in o['deps']:
                dd = ops[d]
                if dd['ch'] is None and dd['eng'] == 'pe' and o['eng'] == 'pe' and o['ch'] is None:
                    continue
                needed[d] = True
        cnt = {}
        for i, o in enumerate(ops):
            if o['ch'] is not None:
                continue
            if needed[i]:
                cnt[o['eng']] = cnt.get(o['eng'], 0) + 1
                o['ms'] = cnt[o['eng']]
        chcnt = {}
        for i, o in enumerate(ops):
            if o['ch'] is not None:
                n = o['fn'].ndma
                chcnt[o['ch']] = chcnt.get(o['ch'], 0) + 16 * n
                o['val'] = chcnt[o['ch']]
        finals = {ch: chcnt[ch] for ch in final_chs if ch in chcnt}

        def run(engname, e):
            seen = {}
            for i, o in enumerate(ops):
                if o['eng'] != engname:
                    continue
                for d in sorted(o['deps']):
                    dd = ops[d]
                    if dd['ch'] is not None:
                        sem, val = sems_ch[dd['ch']], dd['val']
                    else:
                        if dd['eng'] == 'pe' and engname == 'pe' and o['ch'] is None:
                            continue
                        sem, val = sems_eng[dd['eng']], dd['ms']
                    key = id(sem)
                    if seen.get(key, 0) >= val:
                        continue
                    e.wait_ge(sem, val)
                    seen[key] = val
                if o['ch'] is not None:
                    for ins in o['fn'](e):
                        ins.then_inc(sems_ch[o['ch']], 16)
                else:
                    ins = o['fn'](e)
                    if needed[i]:
                        ins.then_inc(sems_eng[engname], 1)
            if engname == 'sp':
                for ch, v in finals.items():
                    e.wait_ge(sems_ch[ch], v)

        block.tensor(lambda e: run('pe', e))
        block.scalar(lambda e: run('act', e))
        block.vector(lambda e: run('dve', e))
        block.gpsimd(lambda e: run('pool', e))
        block.sync(lambda e: run('sp', e))


def dmafn(pairs):
    def f(e):
        return [e.dma_start(out=o, in_=i, allow_slow_non_contiguous=True) for (o, i) in pairs]
    f.ndma = len(pairs)
    return f


def build(stop=None, marks=None):
    nc = bass.Bass("TRN2", target_bir_lowering=False)
    S = Sched()

    def din(name, shape, dt=F32):
        return nc.dram_tensor(name, list(shape), dt, kind="ExternalInput").ap()

    def dout(name, shape):
        return nc.dram_tensor(name, list(shape), F32, kind="ExternalOutput").ap()

    def dscr(name, shape, dt):
        return nc.dram_tensor(name, list(shape), dt).ap()

    xall = din("xall", [33 * 128, D])
    c5 = din("c5", [5, D])
    wada_t = din("wada_t", [24, 128, 16 * 512])
    bada = din("bada", [1, 6 * D])
    vecA = din("vecA", [32, 128])
    normf = din("normf", [1, D])
    watt_t = din("watt_t", [12, 128, 16 * 384])
    whg_t = din("whg_t", [8, 128, 16 * 512])
    wB1_t = din("wB1_t", [16, 128, 44 * 128])
    wo_t = din("wo_t", [4, 128, 16 * 512])
    wab_t = din("wab_t", [NJ, 128, 32 * 128])
    wd_t = din("wd_t", [16, 128, 11 * 512])
    convw = din("convw", [4 * NJ, 128])
    hglb = din("hglb", [2, 1024])
    hgnw = din("hgnw", [1, 128])
    consts = din("consts", [128, 2048])
    rope_t = din("rope_t", [3, 32, 128, 32])
    rope_m = din("rope_m", [MS, 32])
    masks = din("masks", [128, 512])
    ck = [din("ck128", [4, 128, 2, 4, 128]), din("ck512", [4, 512, 2, 4, 128]), din("ck2048", [4, 2048, 2, 4, 128])]
    sthg = din("sthg", [4, 8, 128, 128])
    stconv = din("stconv", [8, DFF])
    cvalid = din("cvalid", [128, 1])

    y_own = dout("y_own", [TOK, D])
    y_misc = dout("y_misc", [MS, D])
    kvp = [dout("kvp128", [128, 2, 4, 128]), dout("kvp512", [512, 2, 4, 128]), dout("kvp2048", [2048, 2, 4, 128])]
    hgp = dout("hgp", [8, 128, 128])
    convp = dout("convp", [2, DFF])
    kvs = [dout("kvs128", [4, 128, 2, 4, 128]), dout("kvs512", [4, 512, 2, 4, 128]), dout("kvs2048", [4, 2048, 2, 4, 128])]
    hgs = dout("hgs", [4, 8, 128, 128])
    convs = dout("convs", [8, DFF])

    oatt_d = dscr("oatt_d", [4, 128, NTOK], BF16)
    ohg_d = dscr("ohg_d", [8, 128, NTOK], BF16)
    ymix_d = dscr("ymix_d", [16, 128, NTOK], BF16)
    x1_d = dscr("x1_d", [17 * 128, D], F32)
    x2_d = dscr("x2_d", [17 * 128, D], F32)
    modr_d = dscr("modr_d", [5, 2 * D], F32)
    hkv_d = dscr("hkv_d", [168, 128, 128], BF16)
    lm_d = dscr("lm_d", [3, 2, 2176], F32)
    coef_d = dscr("coef_d", [3, 2176], F32)
    dbg_d = dscr("dbg_d", [10, 128, 128], BF16)
    dbg2_d = dscr("dbg2_d", [128, 2], F32)

    import contextlib
    es = contextlib.ExitStack()

    def sb(name, shape, dt=F32):
        return es.enter_context(nc.sbuf_tensor(name, list(shape), dt))

    def ps(name, shape, dt=F32):
        return es.enter_context(nc.psum_tensor(name, list(shape), dt))

    ph = [contextlib.ExitStack()]

    uniq = [0]

    def lsb(name, shape, dt=F32):
        uniq[0] += 1
        return ph[0].enter_context(nc.sbuf_tensor("%s_%d" % (name, uniq[0]), list(shape), dt))

    def phase_end():
        if marks is not None:
            marks.append(len(S.ops))
        S.fence()
        ph[0].close()
        ph[0] = contextlib.ExitStack()

    with es:
        BIGA = sb("BIGA", [128, 16 * NTOK], BF16)
        WB = [sb("WB0", [128, 8192], BF16), sb("WB1", [128, 8192], BF16)]
        XT = [sb("XT0", [128, D]), sb("XT1", [128, D])]
        XN = sb("XN", [128, D], BF16)
        CON = sb("CON", [128, 1280])
        identf = CON[:, 0:128]
        tri2 = CON[:, 128:256]
        blk2 = CON[:, 256:384]
        ind2 = CON[:, 384:386]
        cmask = CON[:, 512:576]
        tri_m = CON[0:32, 640:672]
        blk_m = CON[0:32, 672:704]
        ind_m = CON[0:32, 704:708]
        cmask_m = CON[0:32, 736:768]
        rowm = CON[0:32, 768:772]
        colm = CON[:, 896:1024]
        rowsel = CON[0:5, 1024:1056]
        ones_f = CON[:, 1152:1280]
        IDB = sb("IDB", [128, 128], BF16)
        ONEB = sb("ONEB", [128, 128], BF16)
        MSK = sb("MSK", [128, 512])
        ROPEM = sb("ROPEM", [MS, 32])
        MODT = sb("MODT", [128, 96 * 5])
        MR = sb("MR", [5, 512])
        SC = sb("SC", [128, 4 * 16])
        SCM = sb("SCM", [128, 4 * 16 * MS])
        NW = sb("NW", [128, 32])
        CW = sb("CW", [128, 4 * NJ])
        ST = sb("ST", [128, 64])
        STT = sb("STT", [128, 16 * 5], BF16)
        HGW = sb("HGW", [128, 128])
        CVAL = sb("CVAL", [128, 1])
        RTMP = sb("RTMP", [128, 512])
        CARRY = sb("CARRY", [128, NJ * 2])
        SCV = sb("SCV", [128, NJ * 8])
        ASV = sb("ASV", [128, NJ * 8])
        SST = sb("SST", [128, 8 * 128])
        SBF = sb("SBF", [128, 128], BF16)
        OHGPRE = sb("OHGPRE", [128, 16], BF16)
        Lc = {}

        psA = ps("psA", [128, 512])
        psB = ps("psB", [128, 512])
        psC = ps("psC", [128, 512])
        psD = ps("psD", [128, 512])
        psE = ps("psE", [128, 512])
        psF = ps("psF", [128, 512])
        psT = ps("psT", [128, 1024], BF16)
        psU = ps("psU", [128, 1024], BF16)

        hT = BIGA[:, :].rearrange("p (k t) -> p k t", k=16)
        psEb = psE[:, :].bitcast(BF16)

        def A(eng, fn, r=(), w=()):
            return S.add(eng, fn, r, w)

        def DMA(q, ch, pairs, r=(), w=()):
            return S.add(q, dmafn(pairs), r, w, ch=ch)

        def act(out, in_, func, r, w, **kw):
            A('act', lambda e: e.activation(out=out, in_=in_, func=func, **kw), r, w)

        def tt(out, a, b, op, r, w, eng='dve'):
            A(eng, lambda e: e.tensor_tensor(out, a, b, op=op), r, w)

        def tcopy(out, in_, r, w, eng='dve'):
            if eng == 'act':
                A('act', lambda e: e.copy(out=out, in_=in_), r, w)
            else:
                A(eng, lambda e: e.tensor_copy(out, in_), r, w)

        def tsc(out, a, s1, s2, op0, op1, r, w):
            A('dve', lambda e: e.tensor_scalar(out, a, s1, s2, op0=op0, op1=op1), r, w)

        def stt(out, a, s, b, op0, op1, r, w):
            A('dve', lambda e: e.scalar_tensor_tensor(out, a, s, b, op0=op0, op1=op1), r, w)

        def mms(lst, r, w):
            def f(e):
                ins = None
                for (o, l, rr, st, sp) in lst:
                    ins = e.matmul(o, l, rr, start=st, stop=sp)
                return ins
            A('pe', f, r, w)

        def trs(lst, r, w):
            def f(e):
                ins = None
                for (o, i, idn) in lst:
                    ins = e.transpose(o, i, idn)
                return ins
            A('pe', f, r, w)

        wslot = [0]

        def wload(src, ncols):
            s_ = wslot[0]
            wslot[0] ^= 1
            DMA('pool', 'w%d' % s_, [(WB[s_][:, 0:ncols], src)], w=[('WB', s_)])
            return s_

        DMA('sp', 'c0', [(CON[:, :], consts[:, 0:1280]), (MSK[:, :], masks), (ROPEM[:, :], rope_m),
                         (CVAL[:, :], cvalid), (HGW[:, :], hgnw.partition_broadcast(128))],
            w=['CON', 'MSK', 'ROPEM', 'CVAL', 'HGW'])
        S5 = lsb("S5", [5, D], BF16)
        BADA = lsb("BADA", [1, 6 * D], BF16)
        DMA('pool', 'c3', [(IDB[:, :], consts[:, 0:128]), (ONEB[:, :], consts[:, 1152:1280]), (BADA[:, :], bada)],
            w=['IDB', 'ONEB', 'BADA'])
        DMA('sp', 'c4', [(XT[0][0:5, :], c5)], w=['XT0'])
        act(S5[:, :], XT[0][0:5, :], AF.Silu, ['XT0'], ['S5'])
        trs([(psT[:, kc * 8:kc * 8 + 5], S5[0:5, kc * 128:(kc + 1) * 128], IDB[0:5, 0:5]) for kc in range(16)],
            ['S5', 'IDB'], ['psT'])
        tcopy(STT[:, :].rearrange("p (k e) -> p k e", e=5), psT[:, 0:128].rearrange("p (k e) -> p k e", e=8)[:, :, 0:5], ['psT'], ['STT'])
        DMA('sp', 'c5', [(XT[1][0:32, 0:128], vecA)], w=['XT1'])
        mms([(psB[:, 0:32], XT[1][0:32, 0:128], identf[0:32, 0:32], True, True)], ['XT1', 'CON'], ['psB'])
        tcopy(NW[:, :], psB[:, 0:32], ['psB'], ['NW'])
        DMA('sp', 'c6', [(XT[1][0:88, 128:256], convw[0:88, :]), (XT[1][0:88, 256:384], convw[88:176, :])], w=['XT1b'])
        mms([(psB[:, 64:152], XT[1][0:88, 128:256], identf[0:88, 0:88], True, True),
             (psB[:, 152:240], XT[1][0:88, 256:384], identf[0:88, 0:88], True, True)], ['XT1b', 'CON'], ['psBb'])
        tcopy(CW[:, :], psB[:, 64:240], ['psBb'], ['CW'])

        for j in range(24):
            s_ = wload(wada_t[j], 8192)
            wv = WB[s_][:, :].rearrange("p (k n) -> p k n", k=16)
            lst = []
            for cb in range(4):
                col = j * 4 + cb
                o = psA[:, col * 5:(col + 1) * 5]
                for kc in range(16):
                    lst.append((o, wv[:, kc, cb * 128:(cb + 1) * 128], STT[:, kc * 5:(kc + 1) * 5], kc == 0, False))
                lst.append((o, BADA[0:1, col * 128:(col + 1) * 128], ONEB[0:1, 0:5], False, True))
            mms(lst, [('WB', s_), 'STT', 'BADA', 'ONEB'], ['psA'])
            kind = j // 4
            if kind in (2, 5):
                lst = [(psC[0:5, :], STT[:, kc * 5:(kc + 1) * 5], wv[:, kc, :], kc == 0, False) for kc in range(16)]
                lst.append((psC[0:5, :], ONEB[0:1, 0:5], BADA[0:1, j * 512:(j + 1) * 512], False, True))
                mms(lst, [('WB', s_), 'STT', 'BADA', 'ONEB'], ['psC'])
                off = (0 if kind == 2 else D) + (j % 4) * 512
                tcopy(MR[:, :], psC[0:5, :], ['psC'], ['MR'])
                DMA('sp', 'mro', [(modr_d[:, off:off + 512], MR[:, :])], r=['MR'], w=['modr_d'])
        tcopy(MODT[:, :], psA[:, 0:480], ['psA'], ['MODT'])
        modT = MODT[:, :].rearrange("p (c r) -> p c r", r=5)
        SCv = SC[:, :].rearrange("p (a k) -> p a k", a=4)
        SCMv = SCM[:, :].rearrange("p (a k m) -> p a k m", a=4, k=16)
        for n_, (ksh, ksc, nwo) in enumerate(((0, 1, 0), (3, 4, 16))):
            tsc(ST[:, 0:16], modT[:, ksc * 16:(ksc + 1) * 16, 0], 1.0, None, ALU.add, ALU.bypass, ['MODT'], ['STa'])
            tt(SCv[:, 2 * n_, :], ST[:, 0:16], NW[:, nwo:nwo + 16], ALU.mult, ['STa', 'NW'], ['SC'])
            tcopy(SCv[:, 2 * n_ + 1, :], modT[:, ksh * 16:(ksh + 1) * 16, 0], ['MODT'], ['SC'])
            for b in range(4):
                tsc(ST[:, 16:32], modT[:, ksc * 16:(ksc + 1) * 16, 1 + b], 1.0, None, ALU.add, ALU.bypass, ['MODT'], ['STb'])
                tt(ST[:, 32:48], ST[:, 16:32], NW[:, nwo:nwo + 16], ALU.mult, ['STb', 'NW'], ['STc'])
                tcopy(SCMv[:, 2 * n_, :, 4 * b:4 * b + 4], ST[:, 32:48, None].broadcast_to([128, 16, 4]), ['STc'], ['SCM'])
                tcopy(SCMv[:, 2 * n_ + 1, :, 4 * b:4 * b + 4],
                      modT[:, ksh * 16:(ksh + 1) * 16, 1 + b:2 + b].broadcast_to([128, 16, 4]), ['MODT'], ['SCM'])
            tcopy(SCMv[:, 2 * n_, :, 16:32], SCv[:, 2 * n_, :, None].broadcast_to([128, 16, 16]), ['SC'], ['SCM'])
            tcopy(SCMv[:, 2 * n_ + 1, :, 16:32], SCv[:, 2 * n_ + 1, :, None].broadcast_to([128, 16, 16]), ['SC'], ['SCM'])

        phase_end()
        xslot = [0]

        def norm_tile(src_rows, nrows, n_, dst_cols, ncols, misc):
            s_ = xslot[0]
            xslot[0] ^= 1
            xk = 'XT%d' % s_
            X = XT[s_]
            DMA('sp', 'x%d' % s_, [(X[0:nrows, :], src_rows)], w=[xk])
            act(XN[0:nrows, :], X[0:nrows, :], AF.Square, [xk], ['XN', 'STn'], accum_out=ST[0:nrows, 48:49])
            act(ST[0:nrows, 49:50], ST[0:nrows, 48:49], AF.Sqrt, ['STn'], ['STn2'], scale=1.0 / D, bias=EPS)
            A('dve', lambda e: e.reciprocal(ST[0:nrows, 50:51], ST[0:nrows, 49:50]), ['STn2'], ['STn3'])
            act(XN[0:nrows, :], X[0:nrows, :], AF.Copy, [xk, 'STn3', 'XN'], ['XN'], scale=ST[0:nrows, 50:51])
            for half in range(2):
                P_ = psT if half == 0 else psU
                pk = 'psT' if half == 0 else 'psU'
                trs([(P_[:, q * ncols:(q + 1) * ncols], XN[0:nrows, (half * 8 + q) * 128:(half * 8 + q + 1) * 128],
                      IDB[0:nrows, 0:nrows]) for q in range(8)], ['XN', 'IDB'], [pk])
                pv = P_[:, 0:8 * ncols].rearrange("p (k t) -> p k t", k=8)
                dst = hT[:, half * 8:half * 8 + 8, dst_cols:dst_cols + ncols]
                if not misc:
                    sc_ = SCv[:, 2 * n_, half * 8:half * 8 + 8, None].broadcast_to([128, 8, ncols])
                    sh_ = SCv[:, 2 * n_ + 1, half * 8:half * 8 + 8, None].broadcast_to([128, 8, ncols])
                else:
                    sc_ = SCMv[:, 2 * n_, half * 8:half * 8 + 8, :]
                    sh_ = SCMv[:, 2 * n_ + 1, half * 8:half * 8 + 8, :]
                tmp = RTMP[:, :].bitcast(BF16)[:, 0:8 * ncols].rearrange("p (k t) -> p k t", k=8)
                tt(tmp, pv, sc_, ALU.mult, [pk, 'SC', 'SCM'], ['RTMPh'])
                tt(dst, tmp, sh_, ALU.add, ['RTMPh', 'SC', 'SCM'], [('hT', dst_cols // 128)])

        def hkeys(c0, c1):
            return [('hT', t) for t in range(c0 // 128, (c1 - 1) // 128 + 1)]

        ALLH = [('hT', t) for t in range(17)]

        ST2 = None
        QT = KT = VA = OTG = ATMP = APB = ASTAGE = MQK = MV = KVF = QKB = ROPE = HKS = LMS = CMB = None
        LB = None
        HT = HB = OHGH = S0 = S0B = YCH = GT = GBC = GBM = XQ = ACAT = UU = CROW = BIGB = NFW = None
        ropev = QTv = KTv = VAv = STG = None

        def alloc_attn():
            nonlocal ST2, QT, KT, VA, OTG, ATMP, APB, ASTAGE, MQK, MV, KVF, QKB, ROPE, HKS, LMS, CMB, YCH
            nonlocal ropev, QTv, KTv, VAv, STG
            QT = lsb("QT", [128, 16 * 128], BF16)
            KT = lsb("KT", [128, 17 * 128], BF16)
            VA = lsb("VA", [128, 17 * 128], BF16)
            OTG = [lsb("OTG%d" % g, [128, NTOK], BF16) for g in range(3)]
            ST2 = lsb("ST2", [128, 64])
            APB = lsb("APB", [128, 3 * 512], BF16)
            ASTAGE = lsb("ASTAGE", [128, 3 * 6 * 128], BF16)
            MQK = lsb("MQK", [128, 2 * MS], BF16)
            MV = lsb("MV", [MS, 128], BF16)
            KVF = lsb("KVF", [128, 2 * 256])
            QKB = lsb("QKB", [128, 2 * 256], BF16)
            ROPE = lsb("ROPE", [128, 32 * 32])
            HKS = lsb("HKS", [128, 3 * 256], BF16)
            LMS = lsb("LMS", [1, 768])
            CMB = lsb("CMB", [128, 192 + 6 * 192])
            YCH = lsb("YCHa", [128, NTOK], BF16)
            ropev = ROPE[:, :].rearrange("p (t c) -> p t c", t=32)
            QTv = QT[:, :].rearrange("p (t c) -> p t c", c=128)
            KTv = KT[:, :].rearrange("p (t c) -> p t c", c=128)
            VAv = VA[:, :].rearrange("p (t c) -> p t c", c=128)
            STG = ASTAGE[:, :].rearrange("p (t c) -> p t c", c=128)
            A('pool', lambda e, t=ASTAGE: e.memset(t[:, :], 0.0), [], [('STG', 0), ('STG', 1), ('STG', 2), ('STG5', 0), ('STG5', 1), ('STG5', 2)])
            A('pool', lambda e, t=HKS: e.memset(t[:, :], 0.0), [], [('HKS', 0), ('HKS', 1), ('HKS', 2)])
            for g in range(3):
                A('pool', lambda e, t=OTG[g]: e.memset(t[:, :], 0.0), [], [('OTG', g)])

        def alloc_hgrn():
            nonlocal HT, HB, OHGH, S0, S0B, LB
            LB = lsb("LB", [128, 2 * 1024])
            DMA('sp', 'c2', [(LB[:, 0:1024], hglb[0:1, :].partition_broadcast(128)),
                             (LB[:, 1024:2048], hglb[1:2, :].partition_broadcast(128))], w=['LB'])
            tt(LB[:, 0:1024], LB[:, 0:1024], LB[:, 1024:2048], ALU.subtract, ['LB'], ['LB'])
            act(LB[:, 0:1024], LB[:, 0:1024], AF.Sigmoid, ['LB'], ['LB'])
            tsc(LB[:, 1024:2048], LB[:, 0:1024], -1.0, 1.0, ALU.mult, ALU.add, ['LB'], ['LB'])
            HT = lsb("HT", [128, 2 * 1536])
            HB = lsb("HB", [128, 2 * 1024 + 1024], BF16)
            OHGH = lsb("OHGH", [128, NTOK], BF16)
            S0 = lsb("S0", [128, 4 * 128])
            S0B = lsb("S0B", [128, 4 * 128], BF16)
            A('pool', lambda e, t=OHGH: e.memset(t[:, :], 0.0), [], ['OHGH'])

        def hkv_idx(hh):
            g = hh // 4
            base = 0
            for h2 in range(hh):
                base += 2 * GROUPS[h2 // 4][1]
            return base

        def pipeline(gens, depth):
            active = []
            it = iter(gens)
            done = False
            while True:
                if not done and len(active) < depth:
                    try:
                        active.append(next(it))
                    except StopIteration:
                        done = True
                if not active and done:
                    break
                for g_ in list(active):
                    try:
                        next(g_)
                    except StopIteration:
                        active.remove(g_)

        uslot = [0]

        def proj_gen(ws, cols_ap_fn, M, g, tidx, misc, want_q, kdst, vdst, qdst, kvout, dkeys):
            ks = uslot[0] & 1
            uslot[0] += 1
            Z = psA if ks == 0 else psF
            zk = 'psA' if ks == 0 else 'psF'
            TP = psT if ks == 0 else psU
            tpk = 'psT' if ks == 0 else 'psU'
            wv = WB[ws][:, 0:16 * 384].rearrange("p (k n) -> p k n", k=16)
            mms([(Z[0:M, 0:384], cols_ap_fn(kc), wv[:, kc, :], kc == 0, kc == 15) for kc in range(16)],
                [('WB', ws)] + ALLH, [zk])
            yield
            cosv = (ROPEM[0:M, 0:16] if misc else ropev[0:M, tidx, 0:16])
            sinv = (ROPEM[0:M, 16:32] if misc else ropev[0:M, tidx, 16:32])
            zqk = Z[0:M, 0:256].rearrange("p (a c) -> p a c", a=2)
            x1 = zqk[:, :, 0:16]
            x2 = zqk[:, :, 16:32]
            cb_ = cosv[:, None, :].broadcast_to([M, 2, 16])
            sb_ = sinv[:, None, :].broadcast_to([M, 2, 16])
            T = RTMP[0:M, ks * 256:ks * 256 + 128].rearrange("p (q a c) -> p q a c", q=4, a=2)
            QF = RTMP[0:M, ks * 256 + 128:ks * 256 + 256]
            rt = lambda i: ('RT', ks, i)
            tt(T[:, 0], x1, cb_, ALU.mult, [zk, 'ROPE', 'ROPEM'], [rt(0)])
            tt(T[:, 1], x2, sb_, ALU.mult, [zk, 'ROPE', 'ROPEM'], [rt(1)])
            tt(T[:, 2], x2, cb_, ALU.mult, [zk, 'ROPE', 'ROPEM'], [rt(2)])
            tt(T[:, 3], x1, sb_, ALU.mult, [zk, 'ROPE', 'ROPEM'], [rt(3)])
            KF = KVF[0:M, ks * 256:ks * 256 + 256]
            kfk = ('KVF', ks)
            tcopy(KF[:, 128:256], Z[0:M, 256:384], [zk], [kfk], eng='act')
            tt(KF[:, 0:16], T[:, 0, 1], T[:, 1, 1], ALU.subtract, [rt(0), rt(1)], [kfk])
            tt(KF[:, 16:32], T[:, 2, 1], T[:, 3, 1], ALU.add, [rt(2), rt(3)], [kfk])
            tcopy(KF[:, 32:128], Z[0:M, 160:256], [zk], [kfk])
            qb = QKB[0:M, ks * 256:ks * 256 + 128]
            kb = QKB[0:M, ks * 256 + 128:ks * 256 + 256]
            qk_ = ('QKB', ks)
            tcopy(vdst, KF[:, 128:256], [kfk], [dkeys[1]], eng='act')
            tcopy(kb, KF[:, 0:128], [kfk], [qk_])
            if want_q:
                tt(QF[:, 0:16], T[:, 0, 0], T[:, 1, 0], ALU.subtract, [rt(0), rt(1)], [('QF', ks)])
                tt(QF[:, 16:32], T[:, 2, 0], T[:, 3, 0], ALU.add, [rt(2), rt(3)], [('QF', ks)])
                tcopy(QF[:, 32:128], Z[0:M, 32:128], [zk], [('QF', ks)])
                tcopy(qb, QF[:, 0:128], [('QF', ks)], [qk_])
            yield
            lst = [(TP[:, 128:128 + M], kb, IDB[0:M, 0:M])]
            if want_q:
                lst.append((TP[:, 0:M], qb, IDB[0:M, 0:M]))
            trs(lst, [qk_, 'IDB'], [tpk])
            if kvout is not None:
                DMA('sp', 'kvo%d' % ks, kvout(KF), r=[kfk])
            yield
            tcopy(kdst, TP[:, 128:128 + M], [tpk], [dkeys[0]])
            if want_q:
                tcopy(qdst, TP[:, 0:M], [tpk], [dkeys[2]], eng='act')

        aslot = [0]
        dbgsel = [0]

        def attn_unit_gen(g, prep, qT, kp, kc_, vp, vc, mask, outs, rk):
            s_ = aslot[0] % 3
            aslot[0] += 1
            if prep is not None:
                qT, kp, kc_, vp, vc, rk = prep(s_)
                yield
            stt_ = ST2[:, 4 * s_:4 * s_ + 2]
            pb = APB[:, s_ * 512:s_ * 512 + 256]
            pT = APB[:, s_ * 512 + 256:s_ * 512 + 512]
            P_ = (psB, psC, psD)[s_]
            pk = ('psB', 'psC', 'psD')[s_]
            TP = (psT, psU, psEb)[s_]
            tpk = ('psT', 'psU', 'psE')[s_]
            mask = mask_std if mask == 's' else mask_halo
            mms([(P_[:, 0:128], qT, kp, True, True), (P_[:, 128:256], qT, kc_, True, True)], rk, [pk])
            yield
            stt(P_[:, 0:256], P_[:, 0:256], 128.0 ** -0.5, mask, ALU.mult, ALU.add, [pk, 'MSK'], [pk])
            A('dve', lambda e: e.reduce_max(stt_[:, 0:1], P_[:, 0:256], axis=AX.X), [pk], [('mx', s_)])
            tsc(stt_[:, 1:2], stt_[:, 0:1], -1.0, None, ALU.mult, ALU.bypass, [('mx', s_)], [('nmx', s_)])
            act(pb, P_[:, 0:256], AF.Exp, [pk, ('nmx', s_)], [('pb', s_)], bias=stt_[:, 1:2])
            if dbgsel[0] == 1:
                dbgsel[0] = 2
                DMA('sp', 'dbgb', [(dbg_d[7], pb[:, 0:128]), (dbg_d[8], pb[:, 128:256]), (dbg2_d, stt_)], r=[('pb', s_), ('mx', s_), ('nmx', s_)])
            yield
            trs([(TP[:, 256:384], pb[:, 0:128], IDB[:, :]), (TP[:, 384:512], pb[:, 128:256], IDB[:, :])],
                [('pb', s_), 'IDB'], [tpk])
            yield
            tcopy(pT, TP[:, 256:512], [tpk], [('pT', s_)], eng='act')
            yield
            mms([(P_[:, 256:384], vp, pT[:, 0:128], True, False), (P_[:, 256:384], vc, pT[:, 128:256], False, True),
                 (P_[0:1, 384:512], ONEB[:, 0:1], pT[:, 0:128], True, False),
                 (P_[0:1, 384:512], ONEB[:, 0:1], pT[:, 128:256], False, True),
                 (P_[0:1, 0:128], stt_[:, 0:1], identf, True, True)],
                rk + [('pT', s_), ('mx', s_), 'ONEB', 'CON'], [pk])
            if dbgsel[0] == 2:
                dbgsel[0] = 3
                DMA('sp', 'dbg', [(dbg_d[0], vp), (dbg_d[1], vc), (dbg_d[2], pT[:, 0:128]), (dbg_d[3], pT[:, 128:256]),
                                  (dbg_d[4], kp), (dbg_d[5], kc_), (dbg_d[6], qT)], r=rk + [('pT', s_)])
            yield
            tcopy(LMS[0:1, s_ * 256:s_ * 256 + 128], P_[0:1, 384:512], [pk], [('LMS', s_)], eng='act')
            tcopy(LMS[0:1, s_ * 256 + 128:s_ * 256 + 256], P_[0:1, 0:128], [pk], [('LMS', s_)], eng='act')
            prs = []
            for (src, dst) in outs:
                tcopy(OTG[g][:, dst], P_[:, 256:384][:, src], [pk], [('OTG', g)])
                prs.append((lm_d[g, 0:1, dst], LMS[0:1, s_ * 256:s_ * 256 + 128][:, src]))
                prs.append((lm_d[g, 1:2, dst], LMS[0:1, s_ * 256 + 128:s_ * 256 + 256][:, src]))
            DMA('sp', 'lmo%d' % s_, prs, r=[('LMS', s_)], w=[('lm_d', g)])

        mask_std = MSK[:, 0:256]
        mask_halo = MSK[:, 256:512]

        def attn_head(hh, sweep):
            g, j = hh // 4, hh % 4
            W, d = GROUPS[g]
            nT = 16 // d
            ws = wload(watt_t[hh], 16 * 384)
            hb = hkv_idx(hh)
            DMA('sp', 'c1', [(ropev, rope_t[g].rearrange("t p c -> p t c"))], w=['ROPE'])
            HKSv = HKS[:, :].rearrange("p (s c) -> p s c", s=3)
            if sweep == 'H':
                def hgen(r):
                    st_ = TOK - 128 * d + r
                    sl = r % 3
                    yield from proj_gen(ws, lambda kc, st_=st_: hT[:, kc, st_:st_ + 128 * d:d], 128, g, r * (nT + 1), False, False,
                                        HKSv[:, sl, 0:128], HKSv[:, sl, 128:256], None, None, [('HKS', sl), ('HKS', sl), None])
                    DMA('sp', 'hkvo%d' % sl, [(hkv_d[hb + r], HKSv[:, sl, 0:128]), (hkv_d[hb + d + r], HKSv[:, sl, 128:256])],
                        r=[('HKS', sl)], w=[('hkv_d', hh)])
                pipeline((hgen(r) for r in range(d)), 2)
                return
            gens = []
            for r in range(d):
                for T_ in range(nT):
                    ti = r * nT + T_
                    st_ = r + d * 128 * T_
                    o0 = st_ - (TOK - W)
                    kvout = None
                    if o0 >= 0:
                        def kvout(KF, o0=o0, d=d, g=g, j=j):
                            return [(kvp[g][o0:o0 + 127 * d + 1:d, 0, j, :], KF[:, 0:128]),
                                    (kvp[g][o0:o0 + 127 * d + 1:d, 1, j, :], KF[:, 128:256])]
                    gens.append(proj_gen(ws, lambda kc, st_=st_: hT[:, kc, st_:st_ + 128 * d:d], 128, g, r * (nT + 1) + T_ + 1, False, True,
                                         KTv[:, ti, :], VAv[:, ti, :], QTv[:, ti, :], kvout, [('KT', ti), ('VA', ti), ('QT', ti)]))

            def kvout_m(KF, g=g, j=j, W=W):
                prs = []
                for b in range(4):
                    prs.append((kvs[g][b, W - 4:W, 0, j, :], KF[4 * b:4 * b + 4, 0:128]))
                    prs.append((kvs[g][b, W - 4:W, 1, j, :], KF[4 * b:4 * b + 4, 128:256]))
                return prs
            gens.append(proj_gen(ws, lambda kc: hT[:, kc, TOK:TOK + MS], MS, g, 0, True, True,
                                 MQK[:, MS:2 * MS], MV[:, :], MQK[:, 0:MS], kvout_m, ['MK', 'MV', 'MQ']))
            pipeline(gens, 2)
            qTm = MQK[:, 0:MS]
            kTm = MQK[:, MS:2 * MS]
            STGv = ASTAGE[:, :].rearrange("p (s t c) -> p s t c", s=3, t=6)
            units = []
            for r in range(d):
                for T_ in range(nT):
                    ti = r * nT + T_
                    st_ = r + d * 128 * T_
                    outs = [(slice(0, 128), slice(st_, st_ + 128 * d, d))]
                    if T_ == 0:
                        def prep(sl, ti=ti, r=r):
                            if _os.environ.get("DBGF") == "1":
                                S.fence()
                            DMA('sp', 'hkvl%d' % sl, [(HKSv[:, sl, 0:128], hkv_d[hb + r]), (HKSv[:, sl, 128:256], hkv_d[hb + d + r])],
                                r=[('hkv_d', hh)], w=[('HKS', sl)])
                            return (QTv[:, ti, :], HKSv[:, sl, 0:128], KTv[:, ti, :], HKSv[:, sl, 128:256], VAv[:, ti, :],
                                    [('QT', ti), ('KT', ti), ('VA', ti), ('HKS', sl)])
                        units.append(attn_unit_gen(g, prep, None, None, None, None, None, 'h', outs, None))
                    else:
                        units.append(attn_unit_gen(g, None, QTv[:, ti, :], KTv[:, ti - 1, :], KTv[:, ti, :], VAv[:, ti - 1, :], VAv[:, ti, :],
                                                   's', outs, [('QT', ti), ('KT', ti), ('VA', ti), ('KT', ti - 1), ('VA', ti - 1)]))
            pre = [(0, [126, 127], [16, 17], [(126, 18), (127, 19)])] if g == 0 else \
                  [((d - 2 + t), [127], [16 + t], [(127, 18 + 2 * g + t)]) for t in range(2)]
            for (r, qrows, qslots, extras) in pre:
                def prep(sl, r=r, qrows=qrows, qslots=qslots, extras=extras):
                    sk = ('STG', sl)
                    for qr, qs in zip(qrows, qslots):
                        tcopy(STGv[:, sl, 0, qr:qr + 1], qTm[:, qs:qs + 1], ['MQ'], [sk])
                    for (sl_, ms) in extras:
                        tcopy(STGv[:, sl, 1, sl_:sl_ + 1], kTm[:, ms:ms + 1], ['MK'], [sk])
                        DMA('sp', 'stg%d' % sl, [(STGv[sl_:sl_ + 1, sl, 3, :], MV[ms:ms + 1, :])], r=['MV'], w=[sk])
                    DMA('sp', 'hkvl%d' % sl, [(HKSv[:, sl, 0:128], hkv_d[hb + r]), (HKSv[:, sl, 128:256], hkv_d[hb + d + r])],
                        r=[('hkv_d', hh)], w=[('HKS', sl)])
                    return (STGv[:, sl, 0, :], STGv[:, sl, 1, :], HKSv[:, sl, 0:128], STGv[:, sl, 3, :], HKSv[:, sl, 128:256],
                            [sk, ('HKS', sl)])
                outs = [(slice(qr, qr + 1), slice(TOK + qs, TOK + qs + 1)) for qr, qs in zip(qrows, qslots)]
                units.append(attn_unit_gen(g, prep, None, None, None, None, None, 'h', outs, None))
            for b in range(4):
                ulist = [(0, [0, 1, 2, 3])] if g == 0 else [(t, [t]) for t in range(4)]
                for (t0, ts) in ulist:
                    def prep(sl, b=b, t0=t0, ts=ts):
                        sk = ('STG', sl)
                        TPs = (psT, psU, psEb)[sl]
                        tpk = ('psT', 'psU', 'psE')[sl]
                        rows = ck[g][b, t0:t0 + 127 * d + 1:d, :, j, :] if g > 0 else ck[g][b, :, :, j, :]
                        DMA('pool', 'kc%d' % sl, [(STGv[:, sl, 5, :], rows[:, 0, :]), (STGv[:, sl, 3, :], rows[:, 1, :])], w=[('STG5', sl), sk])
                        trs([(TPs[:, 512:640], STGv[:, sl, 5, :], IDB[:, :])], [('STG5', sl), 'IDB'], [tpk])
                        tcopy(STGv[:, sl, 1, :], TPs[:, 512:640], [tpk], [sk])
                        for n_, t in enumerate(ts):
                            ms = 4 * b + t
                            tcopy(STGv[:, sl, 0, n_:n_ + 1], qTm[:, ms:ms + 1], ['MQ'], [sk])
                            tcopy(STGv[:, sl, 2, n_:n_ + 1], kTm[:, ms:ms + 1], ['MK'], [sk])
                            DMA('sp', 'stg%d' % sl, [(STGv[n_:n_ + 1, sl, 4, :], MV[ms:ms + 1, :])], r=['MV'], w=[sk])
                        return (STGv[:, sl, 0, :], STGv[:, sl, 1, :], STGv[:, sl, 2, :], STGv[:, sl, 3, :], STGv[:, sl, 4, :], [sk])
                    outs = [(slice(n_, n_ + 1), slice(TOK + 4 * b + t, TOK + 4 * b + t + 1)) for n_, t in enumerate(ts)]
                    units.append(attn_unit_gen(g, prep, None, None, None, None, None, 's', outs, None))
            pipeline(units, UDEPTH)

        def attn_combine(slot):
            lk = [('lm_d', 0), ('lm_d', 1), ('lm_d', 2)]
            K = lambda a_: ('cmb', a_)
            Cv = lambda a_: CMB[:, a_ * 17:(a_ + 1) * 17]
            DMA('sp', 'cml', [(Cv(2 * g + k_), lm_d[g, k_].rearrange("(p c) -> p c", p=128)) for g in range(3) for k_ in range(2)],
                r=lk, w=[K(i) for i in range(6)])
            for g in range(3):
                act(Cv(6 + g), Cv(2 * g), AF.Ln, [K(2 * g)], [K(6 + g)])
                tt(Cv(6 + g), Cv(6 + g), Cv(2 * g + 1), ALU.add, [K(6 + g), K(2 * g + 1)], [K(6 + g)])
            tt(Cv(9), Cv(6), Cv(7), ALU.max, [K(6), K(7)], [K(9)])
            tt(Cv(9), Cv(9), Cv(8), ALU.max, [K(9), K(8)], [K(9)])
            for g in range(3):
                tt(Cv(6 + g), Cv(6 + g), Cv(9), ALU.subtract, [K(6 + g), K(9)], [K(6 + g)])
                act(Cv(6 + g), Cv(6 + g), AF.Exp, [K(6 + g)], [K(6 + g)])
            tt(Cv(10), Cv(6), Cv(7), ALU.add, [K(6), K(7)], [K(10)])
            tt(Cv(10), Cv(10), Cv(8), ALU.add, [K(10), K(8)], [K(10)])
            A('dve', lambda e, Cv=Cv: e.reciprocal(Cv(10), Cv(10)), [K(10)], [K(10)])
            for g in range(3):
                A('dve', lambda e, Cv=Cv, g=g: e.reciprocal(Cv(2 * g), Cv(2 * g)), [K(2 * g)], [K(2 * g)])
                tt(Cv(6 + g), Cv(6 + g), Cv(10), ALU.mult, [K(6 + g), K(10)], [K(6 + g)])
                tt(Cv(6 + g), Cv(6 + g), Cv(2 * g), ALU.mult, [K(6 + g), K(2 * g)], [K(6 + g)])
            DMA('sp', 'cfo', [(coef_d[g].rearrange("(p c) -> p c", p=128), Cv(6 + g)) for g in range(3)],
                r=[K(6), K(7), K(8)], w=['coef_d'])
            BW = 192
            for bi, c0 in enumerate(range(0, NTOK, BW)):
                n = min(BW, NTOK - c0)
                sl = bi % 2
                Bv = lambda g, n=n, sl=sl: CMB[:, 192 + (sl * 3 + g) * BW:192 + (sl * 3 + g) * BW + n]
                bk = lambda g, sl=sl: ('cbc', sl, g)
                DMA('sp', 'cbl%d' % sl, [(Bv(g), coef_d[g:g + 1, c0:c0 + n].partition_broadcast(128)) for g in range(3)],
                    r=['coef_d'], w=[bk(0), bk(1), bk(2)])
                for g in range(3):
                    tt(Bv(g), Bv(g), OTG[g][:, c0:c0 + n], ALU.mult, [bk(g), ('OTG', g)], [bk(g)])
                tt(Bv(0), Bv(0), Bv(1), ALU.add, [bk(0), bk(1)], [bk(0)])
                tt(YCH[:, c0:c0 + n], Bv(0), Bv(2), ALU.add, [bk(0), bk(2)], ['YCH'])
            DMA('sp', 'ych', [(oatt_d[slot], YCH[:, :])], r=['YCH'], w=[('oatt_d', slot)])

        SSTv = SST[:, :].rearrange("p (h c) -> p h c", h=8)
        hslot = [0]

        def hgrn_tile(ws, h, cols_fn, need_out, outs):
            s_ = hslot[0] & 1
            hslot[0] += 1
            wv = WB[ws][:, :].rearrange("p (k n) -> p k n", k=16)
            Z = psA if s_ == 0 else psF
            zk = 'psA' if s_ == 0 else 'psF'
            mms([(Z[:, :], cols_fn(kc), wv[:, kc, :], kc == 0, kc == 15) for kc in range(16)], [('WB', ws)] + ALLH, [zk])
            T = HT[:, s_ * 1536:(s_ + 1) * 1536]
            Bf = HB[:, s_ * 1024:(s_ + 1) * 1024]
            k_ = lambda n: ('h%s' % n, s_)
            f_, lf, kk, bs, eb, enb = (T[:, i * 128:(i + 1) * 128] for i in range(6))
            ekl, sq, sg, dif = (T[:, i * 128:(i + 1) * 128] for i in range(6, 10))
            qe, ke, kl, vb, og = (Bf[:, i * 128:(i + 1) * 128] for i in range(5))
            qkT = Bf[:, 640:896]
            attm = Bf[:, 896:960]
            act(f_, Z[:, 128:256], AF.Sigmoid, [zk], [k_('f')])
            tt(f_, f_, LB[:, 1024 + h * 128:1024 + (h + 1) * 128], ALU.mult, [k_('f'), 'LB'], [k_('f')])
            tt(f_, f_, LB[:, h * 128:(h + 1) * 128], ALU.add, [k_('f'), 'LB'], [k_('f')])
            act(lf, f_, AF.Ln, [k_('f')], [k_('lf')])
            tsc(kk, f_, -1.0, 1.0, ALU.mult, ALU.add, [k_('f')], [k_('k')])
            Pb = psB if s_ == 0 else psC
            pbk = 'psB' if s_ == 0 else 'psC'
            mms([(Pb[:, 0:128], tri2, lf, True, True), (Pb[:, 128:256], blk2, lf, True, True),
                 (Pb[:, 256:258], lf, ind2, True, True)], [k_('lf'), 'CON'], [pbk])
            tcopy(bs, Pb[:, 0:128], [pbk], [k_('bs')])
            act(eb, bs, AF.Exp, [k_('bs')], [k_('eb')])
            act(enb, bs, AF.Exp, [k_('bs')], [k_('enb')], scale=-1.0)
            tt(dif, Pb[:, 128:256], bs, ALU.subtract, [pbk, k_('bs')], [k_('dif')])
            act(ekl, dif, AF.Exp, [k_('dif')], [k_('ekl')])
            edec = ST[:, 56 + 2 * s_:58 + 2 * s_]
            act(edec, Pb[:, 256:258], AF.Exp, [pbk], [k_('edec')])
            act(sq, Z[:, 0:128], AF.Silu, [zk], [k_('sq')])
            tt(qe, sq, eb, ALU.mult, [k_('sq'), k_('eb')], [k_('qe')])
            tt(ke, kk, enb, ALU.mult, [k_('k'), k_('enb')], [k_('ke')])
            tt(kl, kk, ekl, ALU.mult, [k_('k'), k_('ekl')], [k_('kl')])
            tcopy(vb, Z[:, 256:384], [zk], [k_('v')], eng='act')
            if need_out:
                act(sg, Z[:, 384:512], AF.Silu, [zk], [k_('sg')])
            TP = psT if s_ == 0 else psU
            tpk = 'psT' if s_ == 0 else 'psU'
            trs([(TP[:, 512:640], qe, IDB[:, :]), (TP[:, 640:768], ke, IDB[:, :])], [k_('qe'), k_('ke'), 'IDB'], [tpk + 'h'])
            tcopy(qkT, TP[:, 512:768], [tpk + 'h'], [k_('qkT')])
            Po = psD if s_ == 0 else psE
            pok = 'psDh' if s_ == 0 else 'psEh'
            for c2 in range(2):
                rw = slice(c2 * 64, c2 * 64 + 64)
                mms([(Po[rw, 256:320], qkT[:, 128 + c2 * 64:192 + c2 * 64], qkT[:, c2 * 64:c2 * 64 + 64], True, True)],
                    [k_('qkT')], [pok + 'a%d' % c2])
                tt(attm[rw, :], Po[rw, 256:320], cmask[rw, :], ALU.mult, [pok + 'a%d' % c2, 'CON'], [k_('attm%d' % c2)])
                if need_out:
                    mms([(Po[rw, 0:128], attm[rw, :], vb[rw, :], True, False),
                         (Po[rw, 0:128], qkT[:, c2 * 64:c2 * 64 + 64], SBF[:, :], False, True)],
                        [k_('attm%d' % c2), k_('v'), k_('qkT'), 'SBF'], [pok + 'o%d' % c2])
                mms([(Po[:, 128:256], kl[rw, :], vb[rw, :], True, True)], [k_('kl'), k_('v')], [pok + 'u'])
                stt(SSTv[:, h, :], SSTv[:, h, :], edec[:, c2:c2 + 1], Po[:, 128:256], ALU.mult, ALU.add,
                    [pok + 'u', k_('edec'), 'SST'], ['SST'])
                tcopy(SBF[:, :], SSTv[:, h, :], ['SST'], ['SBF'], eng='act')
            if need_out:
                ok_ = [pok + 'o0', pok + 'o1']
                act(dif, Po[:, 0:128], AF.Square, ok_, [k_('dif'), k_('ssq')], accum_out=ST[:, 44 + s_:45 + s_])
                act(ST[:, 46 + s_:47 + s_], ST[:, 44 + s_:45 + s_], AF.Sqrt, [k_('ssq')], [k_('ssq2')], scale=1.0 / 128, bias=EPS)
                A('dve', lambda e: e.reciprocal(ST[:, 44 + s_:45 + s_], ST[:, 46 + s_:47 + s_]), [k_('ssq2')], [k_('rs')])
                stt(dif, Po[:, 0:128], ST[:, 44 + s_:45 + s_], HGW[:, :], ALU.mult, ALU.mult, ok_ + [k_('rs'), 'HGW'], [k_('dif')])
                tt(og, dif, sg, ALU.mult, [k_('dif'), k_('sg')], [k_('og')])
                trs([(TP[:, 768:896], og, IDB[:, :])], [k_('og'), 'IDB'], [tpk + 'g'])
                for (buf, bk, src, dst) in outs:
                    tcopy(buf[:, dst], TP[:, 768:896][:, src], [tpk + 'g'], [bk], eng='act')

        def hgrn_misc(ws, h):
            wv = WB[ws][:, :].rearrange("p (k n) -> p k n", k=16)
            M = MS
            mms([(psA[0:M, :], hT[:, kc, TOK:TOK + MS], wv[:, kc, :], kc == 0, kc == 15) for kc in range(16)], [('WB', ws)] + ALLH, ['psA'])
            T = HT[0:M, 0:1536]
            Bf = HB[0:M, 0:1024]
            f_, lf, kk, bs, eb, enb = (T[:, i * 128:(i + 1) * 128] for i in range(6))
            ekl, sq, sg, dif = (T[:, i * 128:(i + 1) * 128] for i in range(6, 10))
            qe, ke, kl, vb, og = (Bf[:, i * 128:(i + 1) * 128] for i in range(5))
            DMA('sp', 's0', [(S0[:, :].rearrange("p (b c) -> p b c", b=4), sthg[:, h].rearrange("b d v -> d b v"))], w=['S0'])
            tcopy(S0B[:, :], S0[:, :], ['S0'], ['S0B'])
            act(f_, psA[0:M, 128:256], AF.Sigmoid, ['psA'], ['mf'])
            tt(f_, f_, LB[0:M, 1024 + h * 128:1024 + (h + 1) * 128], ALU.mult, ['mf', 'LB'], ['mf'])
            tt(f_, f_, LB[0:M, h * 128:(h + 1) * 128], ALU.add, ['mf', 'LB'], ['mf'])
            act(lf, f_, AF.Ln, ['mf'], ['mlf'])
            tsc(kk, f_, -1.0, 1.0, ALU.mult, ALU.add, ['mf'], ['mk'])
            mms([(psB[0:M, 0:128], tri_m, lf, True, True), (psB[0:M, 128:256], blk_m, lf, True, True),
                 (psB[:, 256:260], lf, ind_m, True, True)], ['mlf', 'CON'], ['psB'])
            tcopy(bs, psB[0:M, 0:128], ['psB'], ['mbs'])
            act(eb, bs, AF.Exp, ['mbs'], ['meb'])
            act(enb, bs, AF.Exp, ['mbs'], ['menb'], scale=-1.0)
            tt(dif, psB[0:M, 128:256], bs, ALU.subtract, ['psB', 'mbs'], ['mdif'])
            act(ekl, dif, AF.Exp, ['mdif'], ['mekl'])
            edec = ST[:, 40:44]
            act(edec, psB[:, 256:260], AF.Exp, ['psB'], ['medec'])
            act(sq, psA[0:M, 0:128], AF.Silu, ['psA'], ['msq'])
            act(sg, psA[0:M, 384:512], AF.Silu, ['psA'], ['msg'])
            tt(qe, sq, eb, ALU.mult, ['msq', 'meb'], ['mqe'])
            tt(ke, kk, enb, ALU.mult, ['mk', 'menb'], ['mke'])
            tt(kl, kk, ekl, ALU.mult, ['mk', 'mekl'], ['mkl'])
            tcopy(vb, psA[0:M, 256:384], ['psA'], ['mv'], eng='act')
            trs([(psT[:, 512:512 + M], qe, IDB[0:M, 0:M]), (psT[:, 640:640 + M], ke, IDB[0:M, 0:M])], ['mqe', 'mke', 'IDB'], ['psTh'])
            qeT = HB[:, 1024:1024 + M]
            keT = HB[:, 1056:1056 + M]
            qeTm = HB[:, 1152:1152 + 4 * M].rearrange("p (b t) -> p b t", b=4)
            klm = HB[0:M, 1280:1280 + 512].rearrange("p (b c) -> p b c", b=4)
            tcopy(qeT, psT[:, 512:512 + M], ['psTh'], ['mqeT'])
            tcopy(keT, psT[:, 640:640 + M], ['psTh'], ['mkeT'])
            tt(qeTm, qeT[:, None, :].broadcast_to([128, 4, M]), colm.rearrange("p (b t) -> p b t", b=4), ALU.mult, ['mqeT', 'CON'], ['mqeTm'])
            tt(klm, kl[:, None, :].broadcast_to([M, 4, 128]), rowm[:, :, None].broadcast_to([M, 4, 128]), ALU.mult, ['mkl', 'CON'], ['mklm'])
            mms([(psD[0:M, 256:256 + M], keT, qeT, True, True)], ['mqeT', 'mkeT'], ['psDh'])
            attm = HB[0:M, 896:896 + M]
            tt(attm, psD[0:M, 256:256 + M], cmask_m, ALU.mult, ['psDh', 'CON'], ['mattm'])
            lst = [(psD[0:M, 0:128], attm, vb, True, False)]
            for b in range(4):
                lst.append((psD[0:M, 0:128], qeTm[:, b, :], S0B[:, b * 128:(b + 1) * 128], False, b == 3))
            mms(lst, ['mattm', 'mv', 'mqeTm', 'S0B'], ['psDo'])
            mms([(psE[:, b * 128:(b + 1) * 128], klm[:, b, :], vb, True, True) for b in range(4)], ['mklm', 'mv'], ['psE'])
            S0v = S0[:, :].rearrange("p (b c) -> p b c", b=4)
            tt(S0v, S0v, edec[:, :, None].broadcast_to([128, 4, 128]), ALU.mult, ['S0', 'medec', 'S0B'], ['S0'])
            tt(S0[:, :], S0[:, :], psE[:, :], ALU.add, ['S0', 'psE'], ['S0'])
            DMA('sp', 's0o', [(hgs[:, h].rearrange("b d v -> d b v"), S0v)], r=['S0'])
            act(dif, psD[0:M, 0:128], AF.Square, ['psDo'], ['mdif', 'mssq'], accum_out=ST[0:M, 36:37])
            act(ST[0:M, 37:38], ST[0:M, 36:37], AF.Sqrt, ['mssq'], ['mssq2'], scale=1.0 / 128, bias=EPS)
            A('dve', lambda e: e.reciprocal(ST[0:M, 38:39], ST[0:M, 37:38]), ['mssq2'], ['mrs'])
            stt(dif, psD[0:M, 0:128], ST[0:M, 38:39], HGW[0:M, :], ALU.mult, ALU.mult, ['psDo', 'mrs', 'HGW'], ['mdif'])
            tt(og, dif, sg, ALU.mult, ['mdif', 'msg'], ['mog'])
            trs([(psT[:, 768:768 + M], og, IDB[0:M, 0:M])], ['mog', 'IDB'], ['psTg'])
            tcopy(OHGH[:, TOK:TOK + 16], psT[:, 768:784], ['psTg'], ['OHGH'], eng='act')

        A('pool', lambda e: e.memset(SST[:, :], 0.0), [], ['SST'])
        A('pool', lambda e: e.memset(SBF[:, :], 0.0), [], ['SBF'])
        ONEROW = lsb("ONEROW", [1, 2176])
        A('pool', lambda e: e.memset(ONEROW[:, :], 1.0), [], ['ONEROW'])
        DMA('sp', 'lmi', [(lm_d[g, k_:k_ + 1, :], ONEROW[:, :]) for g in range(3) for k_ in range(2)], r=['ONEROW'],
            w=[('lm_d', 0), ('lm_d', 1), ('lm_d', 2)])
        phase_end()
        for nt in range(16):
            norm_tile(xall[nt * 128:(nt + 1) * 128, :], 128, 0, nt * 128, 128, False)
        alloc_attn()
        for hh in range(12):
            attn_head(hh, 'H')
        phase_end()
        alloc_hgrn()
        OHGPv = OHGPRE[:, :].rearrange("p (h c) -> p h c", h=8)
        for h in range(8):
            ws = wload(whg_t[h], 8192)
            for nt in range(16):
                hgrn_tile(ws, h, lambda kc, nt=nt: hT[:, kc, nt * 128:(nt + 1) * 128], nt == 15,
                          [(OHGPRE, 'OHGPRE', slice(126, 128), slice(2 * h, 2 * h + 2))])
            tsc(SSTv[:, h, :], SSTv[:, h, :], CVAL[:, 0:1], None, ALU.mult, ALU.bypass, ['SST', 'CVAL'], ['SST'])
        phase_end()
        for nt in range(16):
            norm_tile(xall[(16 + nt) * 128:(17 + nt) * 128, :], 128, 0, nt * 128, 128, False)
        norm_tile(xall[32 * 128:32 * 128 + MS, :], MS, 0, TOK, MS, True)
        alloc_attn()
        for slot in range(4):
            for g in range(3):
                attn_head(4 * g + slot, 'O')
            attn_combine(slot)
        phase_end()
        alloc_hgrn()
        for h in range(8):
            ws = wload(whg_t[h], 8192)
            tcopy(SBF[:, :], SSTv[:, h, :], ['SST'], ['SBF'], eng='act')
            for nt in range(16):
                hgrn_tile(ws, h, lambda kc, nt=nt: hT[:, kc, nt * 128:(nt + 1) * 128], True,
                          [(OHGH, 'OHGH', slice(0, 128), slice(nt * 128, (nt + 1) * 128))])
            DMA('sp', 'hgp', [(hgp[h], SSTv[:, h, :])], r=['SST'])
            hgrn_misc(ws, h)
            tcopy(OHGH[:, TOK + 16:TOK + 18], OHGPv[:, h, :], ['OHGPRE'], ['OHGH'])
            DMA('sp', 'ohgo', [(ohg_d[h], OHGH[:, :])], r=['OHGH'], w=[('ohg_d', h)])
        phase_end()
        BIGB = lsb("BIGB", [128, 12 * NTOK], BF16)
        YCH = lsb("YCHb", [128, NTOK], BF16)
        GT = lsb("GT", [128, 2 * 512])
        OAT = BIGB[:, :].rearrange("p (k t) -> p k t", k=12)
        DMA('sp', 'oal', [(OAT[:, s_, :], oatt_d[s_]) for s_ in range(4)] + [(OAT[:, 4 + h, :], ohg_d[h]) for h in range(8)], w=['OAT'])
        for fc in range(16):
            ws = wload(wB1_t[fc], 44 * 128)
            wv = WB[ws][:, 0:44 * 128].rearrange("p (k n) -> p k n", k=44)
            for bi, c0 in enumerate(range(0, NTOK, 512)):
                n = min(512, NTOK - c0)
                lst = []
                for kc in range(16):
                    lst.append((psA[:, 0:n], wv[:, kc, :], hT[:, kc, c0:c0 + n], kc == 0, kc == 15))
                for kc in range(16):
                    lst.append((psB[:, 0:n], wv[:, 16 + kc, :], hT[:, kc, c0:c0 + n], kc == 0, kc == 15))
                for kc in range(4):
                    lst.append((psC[:, 0:n], wv[:, 32 + kc, :], OAT[:, kc, c0:c0 + n], kc == 0, kc == 3))
                for kc in range(8):
                    lst.append((psD[:, 0:n], wv[:, 36 + kc, :], OAT[:, 4 + kc, c0:c0 + n], kc == 0, kc == 7))
                mms(lst, [('WB', ws), 'OAT'] + ALLH, ['psA', 'psB', 'psC', 'psD'])
                act(GT[:, 0:n], psA[:, 0:n], AF.Sigmoid, ['psA'], ['GT0'])
                act(GT[:, 512:512 + n], psB[:, 0:n], AF.Sigmoid, ['psB'], ['GT1'])
                tt(GT[:, 0:n], GT[:, 0:n], psC[:, 0:n], ALU.mult, ['GT0', 'psC'], ['GT0'])
                tt(GT[:, 512:512 + n], GT[:, 512:512 + n], psD[:, 0:n], ALU.mult, ['GT1', 'psD'], ['GT1'])
                tt(YCH[:, c0:c0 + n], GT[:, 0:n], GT[:, 512:512 + n], ALU.add, ['GT0', 'GT1'], ['YCH'])
            DMA('sp', 'ych', [(ymix_d[fc], YCH[:, :])], r=['YCH'], w=[('ymix_d', fc)])
        phase_end()
        GT = lsb("GT2", [128, 512])
        GBC = lsb("GBC", [128, 512])
        GBM = lsb("GBM", [MS, 512])
        XQ = [lsb("XQ0", [128, 512]), lsb("XQ1", [128, 512])]
        ymT = BIGA[:, :].rearrange("p (k t) -> p k t", k=16)
        DMA('sp', 'yml', [(ymT[:, fc, :], ymix_d[fc]) for fc in range(16)], w=['ymT'])

        def gate_bc(off, cb):
            DMA('sp', 'mrl', [(MR[:, :], modr_d[:, off + cb * 512:off + (cb + 1) * 512])], r=['modr_d'], w=['MR'])
            mms([(psE[:, :], ones_f[0:1, :], MR[0:1, :], True, True),
                 (psF[0:MS, :], rowsel, MR[0:5, :], True, True)], ['MR', 'CON'], ['psE', 'psF'])
            tcopy(GBC[:, :], psE[:, :], ['psE'], ['GBC'])
            tcopy(GBM[:, :], psF[0:MS, :], ['psF'], ['GBM'], eng='act')

        for cb in range(4):
            ws = wload(wo_t[cb], 8192)
            wv = WB[ws][:, :].rearrange("p (k n) -> p k n", k=16)
            gate_bc(0, cb)
            for tl in range(17):
                M = 128 if tl < 16 else MS
                P_ = psA if tl % 2 == 0 else psB
                pk = 'psA' if tl % 2 == 0 else 'psB'
                mms([(P_[0:M, :], ymT[:, kc, tl * 128:tl * 128 + M], wv[:, kc, :], kc == 0, kc == 15) for kc in range(16)],
                    [('WB', ws), 'ymT'], [pk])
                xq = XQ[tl % 2]
                xk = 'XQ%d' % (tl % 2)
                srow = (16 + tl) * 128
                DMA('sp', 'xq%d' % (tl % 2), [(xq[0:M, :], xall[srow:srow + M, cb * 512:(cb + 1) * 512])], w=[xk])
                G_ = GBC if tl < 16 else GBM
                tt(GT[0:M, 0:512], P_[0:M, :], G_[0:M, :], ALU.mult, [pk, 'GBC', 'GBM'], ['GT0'])
                tt(xq[0:M, :], xq[0:M, :], GT[0:M, 0:512], ALU.add, [xk, 'GT0'], [xk])
                DMA('sp', 'x1o%d' % (tl % 2), [(x1_d[tl * 128:tl * 128 + M, cb * 512:(cb + 1) * 512], xq[0:M, :])], r=[xk], w=[('x1_d', tl)])
        phase_end()
        for tl in range(16):
            norm_tile(x1_d[tl * 128:(tl + 1) * 128, :], 128, 1, tl * 128, 128, False)
        norm_tile(x1_d[16 * 128:16 * 128 + MS, :], MS, 1, TOK, MS, True)
        phase_end()
        BIGB = lsb("YTB", [128, NJ * 512], BF16)
        GT = lsb("GT3", [128, 512])
        GBC = lsb("GBC3", [128, 512])
        GBM = lsb("GBM3", [MS, 512])
        XQ = [lsb("XQ03", [128, 512]), lsb("XQ13", [128, 512])]
        ACAT = lsb("ACAT", [128, 520])
        UU = lsb("UU", [128, 512])
        CROW = RTMP[0:8, :]
        YT = BIGB[:, 0:NJ * 512].rearrange("p (j t) -> p j t", j=NJ)
        CWv = CW[:, :].rearrange("p (a j) -> p a j", a=4)
        CARv = CARRY[:, :].rearrange("p (j c) -> p j c", c=2)
        SCVv = SCV[:, :].rearrange("p (j c) -> p j c", c=8)
        ASVv = ASV[:, :].rearrange("p (j c) -> p j c", c=8)
        for j0 in range(0, NJ, 4):
            DMA('sp', 'crw', [(CROW[:, :], stconv[:, j0 * 128:(j0 + 4) * 128])], w=['CROW'])
            mms([(psE[:, (j - j0) * 8:(j - j0) * 8 + 8], CROW[0:8, (j - j0) * 128:(j - j0 + 1) * 128], identf[0:8, 0:8], True, True) for j in range(j0, j0 + 4)],
                ['CROW', 'CON'], ['psE'])
            tcopy(SCV[:, j0 * 8:(j0 + 4) * 8], psE[:, 0:32], ['psE'], ['SCV'])
        blocks = [('m', TOK, MS)] + [('o', b * 512, 512) for b in range(4)]
        for (kind, c0, n) in blocks:
            for j in range(NJ):
                ws = wload(wab_t[j], 4096)
                wv = WB[ws][:, 0:4096].rearrange("p (k n) -> p k n", k=32)
                s_ = j & 1
                Pa = psA if s_ == 0 else psC
                Pb = psB if s_ == 0 else psD
                pak = 'psA' if s_ == 0 else 'psC'
                pbk = 'psB' if s_ == 0 else 'psD'
                lst = [(Pa[:, 0:n], wv[:, kc, :], hT[:, kc, c0:c0 + n], kc == 0, kc == 15) for kc in range(16)]
                lst += [(Pb[:, 0:n], wv[:, 16 + kc, :], hT[:, kc, c0:c0 + n], kc == 0, kc == 15) for kc in range(16)]
                mms(lst, [('WB', ws)] + ALLH, [pak, pbk])
                ac = ACAT[:, 0:520]
                u = UU[:, 0:512]
                ack = ('ac', 0)
                uk = ('u', 0)
                w0, w1, w2, cb_ = (CWv[:, a, j:j + 1] for a in range(4))
                if kind == 'o':
                    tcopy(ac[:, 0:2], CARv[:, j, :], ['CARRY'], [ack])
                    tcopy(ac[:, 2:2 + n], Pa[:, 0:n], [pak], [ack], eng='act')
                    tsc(u[:, 0:n], ac[:, 2:2 + n], w2, cb_, ALU.mult, ALU.add, [ack, 'CW'], [uk])
                    stt(u[:, 0:n], ac[:, 1:1 + n], w1, u[:, 0:n], ALU.mult, ALU.add, [ack, 'CW', uk], [uk])
                    stt(u[:, 0:n], ac[:, 0:n], w0, u[:, 0:n], ALU.mult, ALU.add, [ack, 'CW', uk], [uk])
                    tcopy(CARv[:, j, :], ac[:, n:n + 2], [ack], ['CARRY'])
                else:
                    acv = ac[:, 0:24].rearrange("p (b t) -> p b t", b=4)
                    tcopy(acv[:, :, 0:2], SCVv[:, j, :].rearrange("p (b t) -> p b t", b=4), ['SCV'], [ack])
                    tcopy(acv[:, :, 2:6], Pa[:, 0:16].rearrange("p (b t) -> p b t", b=4), [pak], [ack])
                    uv = u[:, 0:16].rearrange("p (b t) -> p b t", b=4)
                    tsc(uv, acv[:, :, 2:6], w2, cb_, ALU.mult, ALU.add, [ack, 'CW'], [uk])
                    stt(uv, acv[:, :, 1:5], w1, uv, ALU.mult, ALU.add, [ack, 'CW', uk], [uk])
                    stt(uv, acv[:, :, 0:4], w0, uv, ALU.mult, ALU.add, [ack, 'CW', uk], [uk])
                    A('pool', lambda e, u=u: e.memset(u[:, 16:32], 0.0), [uk], [uk])
                    tcopy(ASVv[:, j, :].rearrange("p (b t) -> p b t", b=4), acv[:, :, 4:6], [ack], ['ASV'])
                    tsc(CARv[:, j, :], Pa[:, 16:18], CVAL[:, 0:1], None, ALU.mult, ALU.bypass, [pak, 'CVAL'], ['CARRY'])
                act(u[:, 0:n], u[:, 0:n], AF.Silu, [uk], [uk])
                tt(YT[:, j, 0:n], u[:, 0:n], Pb[:, 0:n], ALU.mult, [uk, pbk], ['YT'])
            ntl = (n + 127) // 128
            for cb in range(4):
                gate_bc(D, cb)
                for jg in range(4):
                    ws = wload(wd_t[cb * 4 + jg], 11 * 512)
                    wv = WB[ws][:, 0:11 * 512].rearrange("p (k n) -> p k n", k=11)
                    Ps = [psA, psB, psC, psD]
                    for tl in range(ntl):
                        M = min(128, n - tl * 128)
                        mms([(Ps[tl][0:M, :], YT[:, jg * 11 + jj, tl * 128:tl * 128 + M], wv[:, jj, :], jg == 0 and jj == 0, jg == 3 and jj == 10)
                             for jj in range(11)], [('WB', ws), 'YT'], ['psA', 'psB', 'psC', 'psD'][tl:tl + 1])
                for tl in range(ntl):
                    M = min(128, n - tl * 128)
                    gt = (c0 // 128 + tl)
                    xq = XQ[tl % 2]
                    xk = 'XQ%d' % (tl % 2)
                    pk = ['psA', 'psB', 'psC', 'psD'][tl]
                    DMA('sp', 'xq%d' % (tl % 2), [(xq[0:M, :], x1_d[gt * 128:gt * 128 + M, cb * 512:(cb + 1) * 512])], w=[xk])
                    G_ = GBC if kind == 'o' else GBM
                    tt(GT[0:M, 0:512], [psA, psB, psC, psD][tl][0:M, :], G_[0:M, :], ALU.mult, [pk, 'GBC', 'GBM'], ['GT0'])
                    tt(xq[0:M, :], xq[0:M, :], GT[0:M, 0:512], ALU.add, [xk, 'GT0'], [xk])
                    DMA('sp', 'x1o%d' % (tl % 2), [(x2_d[gt * 128:gt * 128 + M, cb * 512:(cb + 1) * 512], xq[0:M, :])], r=[xk], w=[('x2_d', gt)])
        for (src, dstd, nr, ch_) in ((ASVv, convs, 8, 'cvo0'), (CARv, convp, 2, 'cvo1')):
            for j0 in range(0, NJ, 4):
                mms([(psE[0:nr, (j - j0) * 128:(j - j0 + 1) * 128], src[:, j, :], identf, True, True) for j in range(j0, j0 + 4)],
                    ['ASV', 'CARRY', 'CON'], ['psE'])
                tcopy(CROW[0:nr, 0:512], psE[0:nr, :], ['psE'], ['CROW'])
                DMA('sp', ch_, [(dstd[:, j0 * 128:(j0 + 4) * 128], CROW[0:nr, 0:512])], r=['CROW'], w=['CROWo'])
        phase_end()
        NFW = lsb("NFW", [128, D])
        DMA('sp', 'nfw', [(NFW[:, :], normf.partition_broadcast(128))], w=['NFW'])
        for tl in range(17):
            M = 128 if tl < 16 else MS
            s_ = tl % 2
            X = XT[s_]
            xk = 'XT%d' % s_
            DMA('sp', 'x%d' % s_, [(X[0:M, :], x2_d[tl * 128:tl * 128 + M, :])], w=[xk])
            act(XN[0:M, :], X[0:M, :], AF.Square, [xk], ['XN', 'STn'], accum_out=ST[0:M, 48:49])
            act(ST[0:M, 49:50], ST[0:M, 48:49], AF.Sqrt, ['STn'], ['STn2'], scale=1.0 / D, bias=EPS)
            A('dve', lambda e, M=M: e.reciprocal(ST[0:M, 50:51], ST[0:M, 49:50]), ['STn2'], ['STn3'])
            stt(X[0:M, :], X[0:M, :], ST[0:M, 50:51], NFW[0:M, :], ALU.mult, ALU.mult, [xk, 'STn3', 'NFW'], [xk])
            dst = y_own[tl * 128:(tl + 1) * 128, :] if tl < 16 else y_misc
            DMA('sp', 'yo%d' % s_, [(dst, X[0:M, :])], r=[xk])
        for g, (W, d) in enumerate(GROUPS):
            DMA('pool', 'kvc', [(kvs[g][b, 0:W - 4], ck[g][b, 4:W]) for b in range(4)])

        if stop is not None:
            S.ops = S.ops[:stop]
        chs = sorted({o['ch'] for o in S.ops if o['ch'] is not None})
        sems_ch = {ch: es.enter_context(nc.semaphore("c_" + ch)) for ch in chs}
        sems_eng = {e_: es.enter_context(nc.semaphore("e_" + e_)) for e_ in ('pe', 'act', 'dve', 'pool')}
        block = es.enter_context(nc.Block())
        S.emit(nc, block, sems_eng, sems_ch, chs)
        ph[0].close()
    return nc


def _tile_w(Wsub, kc):
    n = Wsub.shape[1]
    return np.ascontiguousarray(Wsub.reshape(kc, 128, n).transpose(1, 0, 2).reshape(128, kc * n))


def _consts():
    C = np.zeros((128, 2048), np.float32)
    C[:, 0:128] = np.eye(128)
    s = np.arange(128)[:, None]
    t = np.arange(128)[None, :]
    same = (s // 64) == (t // 64)
    C[:, 128:256] = (same & (s <= t))
    C[:, 256:384] = same
    C[:, 384] = (np.arange(128) < 64)
    C[:, 385] = (np.arange(128) >= 64)
    C[:, 512:576] = ((np.arange(128)[:, None] % 64) <= np.arange(64)[None, :])
    s = np.arange(32)[:, None]
    t = np.arange(32)[None, :]
    samem = ((s // 4) == (t // 4)) & (s < 16) & (t < 16)
    C[0:32, 640:672] = samem & (s <= t)
    C[0:32, 672:704] = samem
    for b in range(4):
        C[4 * b:4 * b + 4, 704 + b] = 1
        C[4 * b:4 * b + 4, 768 + b] = 1
        C[:, 896 + b * 32 + 4 * b:896 + b * 32 + 4 * b + 4] = 1
    C[0:32, 736:768] = samem & (s <= t)
    for m in range(32):
        C[(1 + m // 4) if m < 16 else 0, 1024 + m] = 1
    C[:, 1152:1280] = 1
    return C


def _rope(pos):
    half = 16
    inv = (np.float32(500000.0) ** (-np.arange(half, dtype=np.float32) * np.float32(2.0) / np.float32(32))).astype(np.float32)
    ang = pos.astype(np.float32)[:, None] * inv[None, :]
    return np.concatenate([np.cos(ang), np.sin(ang)], axis=1).astype(np.float32)


_NC = None


def kernel(x_prompt, x_sample, cache_kv_w128, cache_kv_w512, cache_kv_w2048, state_hgrn, state_conv,
           c_prompt, c_sample, w_ada, b_ada, norm1_w, w_in, hg_lb, hg_norm_w, w_pa, w_pb, w_o,
           norm2_w, w_ffn_a, w_ffn_b, conv_w, conv_b, w_ffn_down, norm_f_w):
    global _NC
    in_maps = _prep(x_prompt, x_sample, cache_kv_w128, cache_kv_w512, cache_kv_w2048, state_hgrn, state_conv,
                    c_prompt, c_sample, w_ada, b_ada, norm1_w, w_in, hg_lb, hg_norm_w, w_pa, w_pb, w_o,
                    norm2_w, w_ffn_a, w_ffn_b, conv_w, conv_b, w_ffn_down, norm_f_w)
    return _run(in_maps)


def _prep(x_prompt, x_sample, cache_kv_w128, cache_kv_w512, cache_kv_w2048, state_hgrn, state_conv,
          c_prompt, c_sample, w_ada, b_ada, norm1_w, w_in, hg_lb, hg_norm_w, w_pa, w_pb, w_o,
          norm2_w, w_ffn_a, w_ffn_b, conv_w, conv_b, w_ffn_down, norm_f_w):
    f = lambda a: np.asarray(a, dtype=np.float32)
    xp = f(x_prompt)[0]
    xs = f(x_sample)
    Win = f(w_in)[0]
    shared = {}
    shared["wada_t"] = np.stack([_tile_w(f(w_ada)[0][:, j * 512:(j + 1) * 512], 16) for j in range(24)])
    shared["bada"] = f(b_ada).reshape(1, -1)
    shared["vecA"] = np.concatenate([f(norm1_w)[0].reshape(16, 128), f(norm2_w)[0].reshape(16, 128)], 0)
    shared["normf"] = f(norm_f_w).reshape(1, -1)
    shared["watt_t"] = np.stack([_tile_w(np.concatenate([Win[:, hh * 128:(hh + 1) * 128], Win[:, 1536 + hh * 128:1536 + (hh + 1) * 128],
                                                          Win[:, 3072 + hh * 128:3072 + (hh + 1) * 128]], 1), 16) for hh in range(12)])
    shared["whg_t"] = np.stack([_tile_w(np.concatenate([Win[:, 4608 + k * 1024 + h * 128:4608 + k * 1024 + (h + 1) * 128] for k in range(4)], 1), 16)
                                for h in range(8)])
    Wpa, Wpb = f(w_pa)[0], f(w_pb)[0]
    shared["wB1_t"] = np.stack([np.concatenate([_tile_w(Win[:, 8704 + fc * 128:8704 + (fc + 1) * 128], 16),
                                                _tile_w(Win[:, 10752 + fc * 128:10752 + (fc + 1) * 128], 16),
                                                _tile_w(Wpa[:, fc * 128:(fc + 1) * 128], 4),
                                                _tile_w(Wpb[:, fc * 128:(fc + 1) * 128], 8)], 1) for fc in range(16)])
    Wo = f(w_o)[0]
    shared["wo_t"] = np.stack([_tile_w(Wo[:, cb * 512:(cb + 1) * 512], 16) for cb in range(4)])
    Wa, Wb, Wd = f(w_ffn_a)[0], f(w_ffn_b)[0], f(w_ffn_down)[0]
    shared["wab_t"] = np.stack([np.concatenate([_tile_w(Wa[:, j * 128:(j + 1) * 128], 16), _tile_w(Wb[:, j * 128:(j + 1) * 128], 16)], 1)
                                for j in range(NJ)])
    shared["wd_t"] = np.stack([_tile_w(Wd[jg * 11 * 128:(jg + 1) * 11 * 128, cb * 512:(cb + 1) * 512], 11)
                               for cb in range(4) for jg in range(4)])
    cw = f(conv_w)[0]
    shared["convw"] = np.concatenate([cw[0].reshape(NJ, 128), cw[1].reshape(NJ, 128), cw[2].reshape(NJ, 128),
                                      f(conv_b)[0].reshape(NJ, 128)], 0)
    shared["hglb"] = f(hg_lb)
    shared["hgnw"] = f(hg_norm_w).reshape(1, 128)
    shared["consts"] = _consts()
    i = np.arange(128)[:, None]
    ip = np.arange(256)[None, :]
    band = np.where((ip >= i) & (ip <= i + 128), 0.0, NEG).astype(np.float32)

    in_maps = []
    for c in range(NCORE):
        m = dict(shared)
        xall = np.zeros((33 * 128, D), np.float32)
        own0 = TOK * c

        def tokrow(p):
            return xp[p] if p >= 0 else np.zeros(D, np.float32)
        if c > 0:
            xall[0:2048] = xp[own0 - 2048:own0]
        xall[2048:4096] = xp[own0:own0 + TOK]
        mb = 32 * 128
        xall[mb:mb + 16] = xs[4 * c:4 * c + 4].reshape(16, D)
        pos_m = np.zeros(MS, np.int64)
        for b in range(4):
            for t in range(4):
                pos_m[4 * b + t] = 16384 + t
        for t in range(2):
            xall[mb + 16 + t] = tokrow(own0 - 2 + t)
            pos_m[16 + t] = own0 - 2 + t
        for g, (W, d) in enumerate(GROUPS):
            for t in range(2):
                p = own0 - 2 + t - 128 * d
                xall[mb + 18 + 2 * g + t] = tokrow(p)
                pos_m[18 + 2 * g + t] = p
        m["xall"] = xall
        m["c5"] = np.concatenate([f(c_prompt), f(c_sample)[4 * c:4 * c + 4]], 0)
        rt = np.zeros((3, 32, 128, 32), np.float32)
        for g, (W, d) in enumerate(GROUPS):
            nT = 16 // d
            for r in range(d):
                for T_ in range(-1, nT):
                    pos = own0 + r + d * (128 * T_ + np.arange(128))
                    rt[g, r * (nT + 1) + T_ + 1] = _rope(pos)
        m["rope_t"] = rt
        m["rope_m"] = _rope(pos_m)
        mh = band.copy()
        if c == 0:
            mh[:, 0:128] = NEG
        m["masks"] = np.concatenate([band, mh], 1)
        m["ck128"] = f(cache_kv_w128)[0, 4 * c:4 * c + 4]
        m["ck512"] = f(cache_kv_w512)[0, 4 * c:4 * c + 4]
        m["ck2048"] = f(cache_kv_w2048)[0, 4 * c:4 * c + 4]
        m["sthg"] = f(state_hgrn)[0, 4 * c:4 * c + 4]
        m["stconv"] = f(state_conv)[0, 4 * c:4 * c + 4].reshape(8, DFF)
        m["cvalid"] = np.full((128, 1), 0.0 if c == 0 else 1.0, np.float32)
        in_maps.append(m)
    return in_maps


def _run(in_maps):
    global _NC
    if _NC is None:
        _NC = build()
    res = run_bass_kernel_spmd(_NC, in_maps, core_ids=list(range(NCORE)))
    R = res.results
    y_prompt = np.concatenate([R[c]["y_own"] for c in range(NCORE)], 0)[None]
    y_sample = np.concatenate([R[c]["y_misc"][0:16].reshape(4, 4, D) for c in range(NCORE)], 0)
    L = R[NCORE - 1]
    outs = [y_prompt, y_sample,
            L["kvp128"][None, None], L["kvp512"][None, None], L["kvp2048"][None, None],
            L["hgp"][None, None], L["convp"].reshape(1, 1, 2, DFF)]
    for nm in ("kvs128", "kvs512", "kvs2048"):
        outs.append(np.concatenate([R[c][nm] for c in range(NCORE)], 0)[None])
    outs.append(np.concatenate([R[c]["hgs"] for c in range(NCORE)], 0)[None])
    outs.append(np.concatenate([R[c]["convs"].reshape(4, 2, DFF) for c in range(NCORE)], 0)[None])
    return tuple(np.ascontiguousarray(o, dtype=np.float32) for o in outs)
```

```python
import numpy as np
import concourse.bass as bass
import concourse.mybir as mybir
from concourse.bass_utils import run_bass_kernel_spmd

F32 = mybir.dt.float32
BF16 = mybir.dt.bfloat16
AF = mybir.ActivationFunctionType
ALU = mybir.AluOpType
AX = mybir.AxisListType

D = 2048
NCORE = 8
TOK = 2048
MS = 32
NTOK = TOK + MS
DFF = 5632
NJ = 44
EPS = 1e-6
GROUPS = ((128, 1), (512, 4), (2048, 16))
NEG = -30000.0
import os as _os
UDEPTH = int(_os.environ.get("UDEPTH", "3"))


class Sched:
    def __init__(self):
        self.ops = []
        self.lw = {}
        self.rd = {}
        self.fence_deps = set()
        self.last_eng = {}
        self.last_ch = {}

    @staticmethod
    def _nk(k):
        n = k[0] if isinstance(k, tuple) else k
        if isinstance(n, str) and len(n) >= 3 and n.startswith('ps') and n[2] in 'ABCDEFTU':
            return n[:3]
        return k

    def add(self, eng, fn, r=(), w=(), ch=None):
        r = [self._nk(k) for k in r]
        w = [self._nk(k) for k in w]
        psr = [k for k in r if isinstance(k, str) and len(k) == 3 and k.startswith('ps')]
        if psr:
            r = [k for k in r if k not in psr]
            w = list(w) + psr
        deps = set(self.fence_deps)
        for k in r:
            if k in self.lw:
                deps.add(self.lw[k])
        for k in w:
            if k in self.lw:
                deps.add(self.lw[k])
            deps.update(self.rd.get(k, ()))
        i = len(self.ops)
        import sys as _s
        fr = _s._getframe(1)
        lines = []
        while fr is not None and len(lines) < 4:
            lines.append(fr.f_lineno)
            fr = fr.f_back
        self.ops.append(dict(eng=eng, fn=fn, deps=deps, ch=ch, ln=lines))
        for k in r:
            self.rd.setdefault(k, []).append(i)
        for k in w:
            self.lw[k] = i
            self.rd[k] = []
        if ch is None:
            self.last_eng[eng] = i
        else:
            self.last_ch[ch] = i
        return i

    def fence(self):
        self.fence_deps = set(self.last_eng.values()) | set(self.last_ch.values())

    def emit(self, nc, block, sems_eng, sems_ch, final_chs):
        ops = self.ops
        needed = [False] * len(ops)
        for o in ops:
            for d in o['deps']:
                dd = ops[d]
                if dd['ch'] is None and dd['eng'] == 'pe' and o['eng'] == 'pe' and o['ch'] is None:
                    continue
                needed[d] = True
        cnt = {}
        for i, o in enumerate(ops):
            if o['ch'] is not None:
                continue
            if needed[i]:
                cnt[o['eng']] = cnt.get(o['eng'], 0) + 1
                o['ms'] = cnt[o['eng']]
        chcnt = {}
        for i, o in enumerate(ops):
            if o['ch'] is not None:
                n = o['fn'].ndma
                chcnt[o['ch']] = chcnt.get(o['ch'], 0) + 16 * n
                o['val'] = chcnt[o['ch']]
        finals = {ch: chcnt[ch] for ch in final_chs if ch in chcnt}

        def run(engname, e):
            seen = {}
            for i, o in enumerate(ops):
                if o['eng'] != engname:
                    continue
                for d in sorted(o['deps']):
                    dd = ops[d]
                    if dd['ch'] is not None:
                        sem, val = sems_ch[dd['ch']], dd['val']
                    else:
                        if dd['eng'] == 'pe' and engname == 'pe' and o['ch'] is None:
                            continue
                        sem, val = sems_eng[dd['eng']], dd['ms']
                    key = id(sem)
                    if seen.get(key, 0) >= val:
                        continue
                    e.wait_ge(sem, val)
                    seen[key] = val
                if o['ch'] is not None:
                    for ins in o['fn'](e):
                        ins.then_inc(sems_ch[o['ch']], 16)
                else:
                    ins = o['fn'](e)
                    if needed[i]:
                        ins.then_inc(sems_eng[engname], 1)
            if engname == 'sp':
                for ch, v in finals.items():
                    e.wait_ge(sems_ch[ch], v)

        block.tensor(lambda e: run('pe', e))
        block.scalar(lambda e: run('act', e))
        block.vector(lambda e: run('dve', e))
        block.gpsimd(lambda e: run('pool', e))
        block.sync(lambda e: run('sp', e))


def dmafn(pairs):
    def f(e):
        return [e.dma_start(out=o, in_=i, allow_slow_non_contiguous=True) for (o, i) in pairs]
    f.ndma = len(pairs)
    return f


def build(stop=None, marks=None):
    nc = bass.Bass("TRN2", target_bir_lowering=False)
    S = Sched()

    def din(name, shape, dt=F32):
        return nc.dram_tensor(name, list(shape), dt, kind="ExternalInput").ap()

    def dout(name, shape):
        return nc.dram_tensor(name, list(shape), F32, kind="ExternalOutput").ap()

    def dscr(name, shape, dt):
        return nc.dram_tensor(name, list(shape), dt).ap()

    xall = din("xall", [33 * 128, D])
    c5 = din("c5", [5, D])
    wada_t = din("wada_t", [24, 128, 16 * 512])
    bada = din("bada", [1, 6 * D])
    vecA = din("vecA", [32, 128])
    normf = din("normf", [1, D])
    watt_t = din("watt_t", [12, 128, 16 * 384])
    whg_t = din("whg_t", [8, 128, 16 * 512])
    wB1_t = din("wB1_t", [16, 128, 44 * 128])
    wo_t = din("wo_t", [4, 128, 16 * 512])
    wab_t = din("wab_t", [NJ, 128, 32 * 128])
    wd_t = din("wd_t", [16, 128, 11 * 512])
    convw = din("convw", [4 * NJ, 128])
    hglb = din("hglb", [2, 1024])
    hgnw = din("hgnw", [1, 128])
    consts = din("consts", [128, 2048])
    rope_t = din("rope_t", [3, 32, 128, 32])
    rope_m = din("rope_m", [MS, 32])
    masks = din("masks", [128, 512])
    ck = [din("ck128", [4, 128, 2, 4, 128]), din("ck512", [4, 512, 2, 4, 128]), din("ck2048", [4, 2048, 2, 4, 128])]
    sthg = din("sthg", [4, 8, 128, 128])
    stconv = din("stconv", [8, DFF])
    cvalid = din("cvalid", [128, 1])

    y_own = dout("y_own", [TOK, D])
    y_misc = dout("y_misc", [MS, D])
    kvp = [dout("kvp128", [128, 2, 4, 128]), dout("kvp512", [512, 2, 4, 128]), dout("kvp2048", [2048, 2, 4, 128])]
    hgp = dout("hgp", [8, 128, 128])
    convp = dout("convp", [2, DFF])
    kvs = [dout("kvs128", [4, 128, 2, 4, 128]), dout("kvs512", [4, 512, 2, 4, 128]), dout("kvs2048", [4, 2048, 2, 4, 128])]
    hgs = dout("hgs", [4, 8, 128, 128])
    convs = dout("convs", [8, DFF])

    oatt_d = dscr("oatt_d", [4, 128, NTOK], BF16)
    ohg_d = dscr("ohg_d", [8, 128, NTOK], BF16)
    ymix_d = dscr("ymix_d", [16, 128, NTOK], BF16)
    x1_d = dscr("x1_d", [17 * 128, D], F32)
    x2_d = dscr("x2_d", [17 * 128, D], F32)
    modr_d = dscr("modr_d", [5, 2 * D], F32)
    hkv_d = dscr("hkv_d", [168, 128, 128], BF16)
    lm_d = dscr("lm_d", [3, 2, 2176], F32)
    coef_d = dscr("coef_d", [3, 2176], F32)
    dbg_d = dscr("dbg_d", [10, 128, 128], BF16)
    dbg2_d = dscr("dbg2_d", [128, 2], F32)

    import contextlib
    es = contextlib.ExitStack()

    def sb(name, shape, dt=F32):
        return es.enter_context(nc.sbuf_tensor(name, list(shape), dt))

    def ps(name, shape, dt=F32):
        return es.enter_context(nc.psum_tensor(name, list(shape), dt))

    ph = [contextlib.ExitStack()]

    uniq = [0]

    def lsb(name, shape, dt=F32):
        uniq[0] += 1
        return ph[0].enter_context(nc.sbuf_tensor("%s_%d" % (name, uniq[0]), list(shape), dt))

    def phase_end():
        if marks is not None:
            marks.append(len(S.ops))
        S.fence()
        ph[0].close()
        ph[0] = contextlib.ExitStack()

    with es:
        BIGA = sb("BIGA", [128, 16 * NTOK], BF16)
        WB = [sb("WB0", [128, 8192], BF16), sb("WB1", [128, 8192], BF16)]
        XT = [sb("XT0", [128, D]), sb("XT1", [128, D])]
        XN = sb("XN", [128, D], BF16)
        CON = sb("CON", [128, 1280])
        identf = CON[:, 0:128]
        tri2 = CON[:, 128:256]
        blk2 = CON[:, 256:384]
        ind2 = CON[:, 384:386]
        cmask = CON[:, 512:576]
        tri_m = CON[0:32, 640:672]
        blk_m = CON[0:32, 672:704]
        ind_m = CON[0:32, 704:708]
        cmask_m = CON[0:32, 736:768]
        rowm = CON[0:32, 768:772]
        colm = CON[:, 896:1024]
        rowsel = CON[0:5, 1024:1056]
        ones_f = CON[:, 1152:1280]
        IDB = sb("IDB", [128, 128], BF16)
        ONEB = sb("ONEB", [128, 128], BF16)
        MSK = sb("MSK", [128, 512])
        ROPEM = sb("ROPEM", [MS, 32])
        MODT = sb("MODT", [128, 96 * 5])
        MR = sb("MR", [5, 512])
        SC = sb("SC", [128, 4 * 16])
        SCM = sb("SCM", [128, 4 * 16 * MS])
        NW = sb("NW", [128, 32])
        CW = sb("CW", [128, 4 * NJ])
        ST = sb("ST", [128, 64])
        STT = sb("STT", [128, 16 * 5], BF16)
        HGW = sb("HGW", [128, 128])
        CVAL = sb("CVAL", [128, 1])
        RTMP = sb("RTMP", [128, 512])
        CARRY = sb("CARRY", [128, NJ * 2])
        SCV = sb("SCV", [128, NJ * 8])
        ASV = sb("ASV", [128, NJ * 8])
        SST = sb("SST", [128, 8 * 128])
        SBF = sb("SBF", [128, 128], BF16)
        OHGPRE = sb("OHGPRE", [128, 16], BF16)
        Lc = {}

        psA = ps("psA", [128, 512])
        psB = ps("psB", [128, 512])
        psC = ps("psC", [128, 512])
        psD = ps("psD", [128, 512])
        psE = ps("psE", [128, 512])
        psF = ps("psF", [128, 512])
        psT = ps("psT", [128, 1024], BF16)
        psU = ps("psU", [128, 1024], BF16)

        hT = BIGA[:, :].rearrange("p (k t) -> p k t", k=16)
        psEb = psE[:, :].bitcast(BF16)

        def A(eng, fn, r=(), w=()):
            return S.add(eng, fn, r, w)

        def DMA(q, ch, pairs, r=(), w=()):
            return S.add(q, dmafn(pairs), r, w, ch=ch)

        def act(out, in_, func, r, w, **kw):
            A('act', lambda e: e.activation(out=out, in_=in_, func=func, **kw), r, w)

        def tt(out, a, b, op, r, w, eng='dve'):
            A(eng, lambda e: e.tensor_tensor(out, a, b, op=op), r, w)

        def tcopy(out, in_, r, w, eng='dve'):
            if eng == 'act':
                A('act', lambda e: e.copy(out=out, in_=in_), r, w)
            else:
                A(eng, lambda e: e.tensor_copy(out, in_), r, w)

        def tsc(out, a, s1, s2, op0, op1, r, w):
            A('dve', lambda e: e.tensor_scalar(out, a, s1, s2, op0=op0, op1=op1), r, w)

        def stt(out, a, s, b, op0, op1, r, w):
            A('dve', lambda e: e.scalar_tensor_tensor(out, a, s, b, op0=op0, op1=op1), r, w)

        def mms(lst, r, w):
            def f(e):
                ins = None
                for (o, l, rr, st, sp) in lst:
                    ins = e.matmul(o, l, rr, start=st, stop=sp)
                return ins
            A('pe', f, r, w)

        def trs(lst, r, w):
            def f(e):
                ins = None
                for (o, i, idn) in lst:
                    ins = e.transpose(o, i, idn)
                return ins
            A('pe', f, r, w)

        wslot = [0]

        def wload(src, ncols):
            s_ = wslot[0]
            wslot[0] ^= 1
            DMA('pool', 'w%d' % s_, [(WB[s_][:, 0:ncols], src)], w=[('WB', s_)])
            return s_

        DMA('sp', 'c0', [(CON[:, :], consts[:, 0:1280]), (MSK[:, :], masks), (ROPEM[:, :], rope_m),
                         (CVAL[:, :], cvalid), (HGW[:, :], hgnw.partition_broadcast(128))],
            w=['CON', 'MSK', 'ROPEM', 'CVAL', 'HGW'])
        S5 = lsb("S5", [5, D], BF16)
        BADA = lsb("BADA", [1, 6 * D], BF16)
        DMA('pool', 'c3', [(IDB[:, :], consts[:, 0:128]), (ONEB[:, :], consts[:, 1152:1280]), (BADA[:, :], bada)],
            w=['IDB', 'ONEB', 'BADA'])
        DMA('sp', 'c4', [(XT[0][0:5, :], c5)], w=['XT0'])
        act(S5[:, :], XT[0][0:5, :], AF.Silu, ['XT0'], ['S5'])
        trs([(psT[:, kc * 8:kc * 8 + 5], S5[0:5, kc * 128:(kc + 1) * 128], IDB[0:5, 0:5]) for kc in range(16)],
            ['S5', 'IDB'], ['psT'])
        tcopy(STT[:, :].rearrange("p (k e) -> p k e", e=5), psT[:, 0:128].rearrange("p (k e) -> p k e", e=8)[:, :, 0:5], ['psT'], ['STT'])
        DMA('sp', 'c5', [(XT[1][0:32, 0:128], vecA)], w=['XT1'])
        mms([(psB[:, 0:32], XT[1][0:32, 0:128], identf[0:32, 0:32], True, True)], ['XT1', 'CON'], ['psB'])
        tcopy(NW[:, :], psB[:, 0:32], ['psB'], ['NW'])
        DMA('sp', 'c6', [(XT[1][0:88, 128:256], convw[0:88, :]), (XT[1][0:88, 256:384], convw[88:176, :])], w=['XT1b'])
        mms([(psB[:, 64:152], XT[1][0:88, 128:256], identf[0:88, 0:88], True, True),
             (psB[:, 152:240], XT[1][0:88, 256:384], identf[0:88, 0:88], True, True)], ['XT1b', 'CON'], ['psBb'])
        tcopy(CW[:, :], psB[:, 64:240], ['psBb'], ['CW'])

        for j in range(24):
            s_ = wload(wada_t[j], 8192)
            wv = WB[s_][:, :].rearrange("p (k n) -> p k n", k=16)
            lst = []
            for cb in range(4):
                col = j * 4 + cb
                o = psA[:, col * 5:(col + 1) * 5]
                for kc in range(16):
                    lst.append((o, wv[:, kc, cb * 128:(cb + 1) * 128], STT[:, kc * 5:(kc + 1) * 5], kc == 0, False))
                lst.append((o, BADA[0:1, col * 128:(col + 1) * 128], ONEB[0:1, 0:5], False, True))
            mms(lst, [('WB', s_), 'STT', 'BADA', 'ONEB'], ['psA'])
            kind = j // 4
            if kind in (2, 5):
                lst = [(psC[0:5, :], STT[:, kc * 5:(kc + 1) * 5], wv[:, kc, :], kc == 0, False) for kc in range(16)]
                lst.append((psC[0:5, :], ONEB[0:1, 0:5], BADA[0:1, j * 512:(j + 1) * 512], False, True))
                mms(lst, [('WB', s_), 'STT', 'BADA', 'ONEB'], ['psC'])
                off = (0 if kind == 2 else D) + (j % 4) * 512
                tcopy(MR[:, :], psC[0:5, :], ['psC'], ['MR'])
                DMA('sp', 'mro', [(modr_d[:, off:off + 512], MR[:, :])], r=['MR'], w=['modr_d'])
        tcopy(MODT[:, :], psA[:, 0:480], ['psA'], ['MODT'])
        modT = MODT[:, :].rearrange("p (c r) -> p c r", r=5)
        SCv = SC[:, :].rearrange("p (a k) -> p a k", a=4)
        SCMv = SCM[:, :].rearrange("p (a k m) -> p a k m", a=4, k=16)
        for n_, (ksh, ksc, nwo) in enumerate(((0, 1, 0), (3, 4, 16))):
            tsc(ST[:, 0:16], modT[:, ksc * 16:(ksc + 1) * 16, 0], 1.0, None, ALU.add, ALU.bypass, ['MODT'], ['STa'])
            tt(SCv[:, 2 * n_, :], ST[:, 0:16], NW[:, nwo:nwo + 16], ALU.mult, ['STa', 'NW'], ['SC'])
            tcopy(SCv[:, 2 * n_ + 1, :], modT[:, ksh * 16:(ksh + 1) * 16, 0], ['MODT'], ['SC'])
            for b in range(4):
                tsc(ST[:, 16:32], modT[:, ksc * 16:(ksc + 1) * 16, 1 + b], 1.0, None, ALU.add, ALU.bypass, ['MODT'], ['STb'])
                tt(ST[:, 32:48], ST[:, 16:32], NW[:, nwo:nwo + 16], ALU.mult, ['STb', 'NW'], ['STc'])
                tcopy(SCMv[:, 2 * n_, :, 4 * b:4 * b + 4], ST[:, 32:48, None].broadcast_to([128, 16, 4]), ['STc'], ['SCM'])
                tcopy(SCMv[:, 2 * n_ + 1, :, 4 * b:4 * b + 4],
                      modT[:, ksh * 16:(ksh + 1) * 16, 1 + b:2 + b].broadcast_to([128, 16, 4]), ['MODT'], ['SCM'])
            tcopy(SCMv[:, 2 * n_, :, 16:32], SCv[:, 2 * n_, :, None].broadcast_to([128, 16, 16]), ['SC'], ['SCM'])
            tcopy(SCMv[:, 2 * n_ + 1, :, 16:32], SCv[:, 2 * n_ + 1, :, None].broadcast_to([128, 16, 16]), ['SC'], ['SCM'])

        phase_end()
        xslot = [0]

        def norm_tile(src_rows, nrows, n_, dst_cols, ncols, misc):
            s_ = xslot[0]
            xslot[0] ^= 1
            xk = 'XT%d' % s_
            X = XT[s_]
            DMA('sp', 'x%d' % s_, [(X[0:nrows, :], src_rows)], w=[xk])
            act(XN[0:nrows, :], X[0:nrows, :], AF.Square, [xk], ['XN', 'STn'], accum_out=ST[0:nrows, 48:49])
            act(ST[0:nrows, 49:50], ST[0:nrows, 48:49], AF.Sqrt, ['STn'], ['STn2'], scale=1.0 / D, bias=EPS)
            A('dve', lambda e: e.reciprocal(ST[0:nrows, 50:51], ST[0:nrows, 49:50]), ['STn2'], ['STn3'])
            act(XN[0:nrows, :], X[0:nrows, :], AF.Copy, [xk, 'STn3', 'XN'], ['XN'], scale=ST[0:nrows, 50:51])
            for half in range(2):
                P_ = psT if half == 0 else psU
                pk = 'psT' if half == 0 else 'psU'
                trs([(P_[:, q * ncols:(q + 1) * ncols], XN[0:nrows, (half * 8 + q) * 128:(half * 8 + q + 1) * 128],
                      IDB[0:nrows, 0:nrows]) for q in range(8)], ['XN', 'IDB'], [pk])
                pv = P_[:, 0:8 * ncols].rearrange("p (k t) -> p k t", k=8)
                dst = hT[:, half * 8:half * 8 + 8, dst_cols:dst_cols + ncols]
                if not misc:
                    sc_ = SCv[:, 2 * n_, half * 8:half * 8 + 8, None].broadcast_to([128, 8, ncols])
                    sh_ = SCv[:, 2 * n_ + 1, half * 8:half * 8 + 8, None].broadcast_to([128, 8, ncols])
                else:
                    sc_ = SCMv[:, 2 * n_, half * 8:half * 8 + 8, :]
                    sh_ = SCMv[:, 2 * n_ + 1, half * 8:half * 8 + 8, :]
                tmp = RTMP[:, :].bitcast(BF16)[:, 0:8 * ncols].rearrange("p (k t) -> p k t", k=8)
                tt(tmp, pv, sc_, ALU.mult, [pk, 'SC', 'SCM'], ['RTMPh'])
                tt(dst, tmp, sh_, ALU.add, ['RTMPh', 'SC', 'SCM'], [('hT', dst_cols // 128)])

        def hkeys(c0, c1):
            return [('hT', t) for t in range(c0 // 128, (c1 - 1) // 128 + 1)]

        ALLH = [('hT', t) for t in range(17)]

        ST2 = None
        QT = KT = VA = OTG = ATMP = APB = ASTAGE = MQK = MV = KVF = QKB = ROPE = HKS = LMS = CMB = None
        LB = ST3 = SR = SRB = None
        HT = HB = OHGH = S0 = S0B = YCH = GT = GBC = GBM = XQ = ACAT = UU = CROW = BIGB = NFW = None
        ropev = QTv = KTv = VAv = STG = None

        def alloc_attn():
            nonlocal ST2, QT, KT, VA, OTG, ATMP, APB, ASTAGE, MQK, MV, KVF, QKB, ROPE, HKS, LMS, CMB, YCH
            nonlocal ropev, QTv, KTv, VAv, STG
            QT = lsb("QT", [128, 16 * 128], BF16)
            KT = lsb("KT", [128, 17 * 128], BF16)
            VA = lsb("VA", [128, 17 * 128], BF16)
            OTG = [lsb("OTG%d" % g, [128, NTOK], BF16) for g in range(3)]
            ST2 = lsb("ST2", [128, 64])
            APB = lsb("APB", [128, 3 * 512], BF16)
            ASTAGE = lsb("ASTAGE", [128, 3 * 6 * 128], BF16)
            MQK = lsb("MQK", [128, 2 * MS], BF16)
            MV = lsb("MV", [MS, 128], BF16)
            KVF = lsb("KVF", [128, 2 * 256])
            QKB = lsb("QKB", [128, 2 * 256], BF16)
            ROPE = lsb("ROPE", [128, 32 * 32])
            HKS = lsb("HKS", [128, 3 * 256], BF16)
            LMS = lsb("LMS", [1, 768])
            CMB = lsb("CMB", [128, 192 + 6 * 192])
            YCH = lsb("YCHa", [128, NTOK], BF16)
            ropev = ROPE[:, :].rearrange("p (t c) -> p t c", t=32)
            QTv = QT[:, :].rearrange("p (t c) -> p t c", c=128)
            KTv = KT[:, :].rearrange("p (t c) -> p t c", c=128)
            VAv = VA[:, :].rearrange("p (t c) -> p t c", c=128)
            STG = ASTAGE[:, :].rearrange("p (t c) -> p t c", c=128)
            A('pool', lambda e, t=ASTAGE: e.memset(t[:, :], 0.0), [], [('STG', 0), ('STG', 1), ('STG', 2), ('STG5', 0), ('STG5', 1), ('STG5', 2)])
            A('pool', lambda e, t=HKS: e.memset(t[:, :], 0.0), [], [('HKS', 0), ('HKS', 1), ('HKS', 2)])
            for g in range(3):
                A('pool', lambda e, t=OTG[g]: e.memset(t[:, :], 0.0), [], [('OTG', g)])

        def alloc_hgrn():
            nonlocal HT, HB, OHGH, S0, S0B, LB, ST3, SR, SRB
            LB = lsb("LB", [128, 2 * 1024])
            DMA('sp', 'c2', [(LB[:, 0:1024], hglb[0:1, :].partition_broadcast(128)),
                             (LB[:, 1024:2048], hglb[1:2, :].partition_broadcast(128))], w=['LB'])
            tt(LB[:, 0:1024], LB[:, 0:1024], LB[:, 1024:2048], ALU.subtract, ['LB'], ['LB'])
            act(LB[:, 0:1024], LB[:, 0:1024], AF.Sigmoid, ['LB'], ['LB'])
            tsc(LB[:, 1024:2048], LB[:, 0:1024], -1.0, 1.0, ALU.mult, ALU.add, ['LB'], ['LB'])
            HT = lsb("HT", [128, 3 * 1536])
            HB = lsb("HB", [128, 3 * 1024 + 1024], BF16)
            ST3 = lsb("ST3", [128, 16])
            SR = lsb("SR", [128, 4 * 128])
            SRB = lsb("SRB", [128, 4 * 128], BF16)
            OHGH = lsb("OHGH", [128, NTOK], BF16)
            S0 = lsb("S0", [128, 4 * 128])
            S0B = lsb("S0B", [128, 4 * 128], BF16)
            A('pool', lambda e, t=OHGH: e.memset(t[:, :], 0.0), [], ['OHGH'])

        def hkv_idx(hh):
            g = hh // 4
            base = 0
            for h2 in range(hh):
                base += 2 * GROUPS[h2 // 4][1]
            return base

        def pipeline(gens, depth):
            active = []
            it = iter(gens)
            done = False
            while True:
                if not done and len(active) < depth:
                    try:
                        active.append(next(it))
                    except StopIteration:
                        done = True
                if not active and done:
                    break
                for g_ in list(active):
                    try:
                        next(g_)
                    except StopIteration:
                        active.remove(g_)

        uslot = [0]

        def proj_gen(ws, cols_ap_fn, M, g, tidx, misc, want_q, kdst, vdst, qdst, kvout, dkeys):
            ks = uslot[0] & 1
            uslot[0] += 1
            Z = psA if ks == 0 else psF
            zk = 'psA' if ks == 0 else 'psF'
            TP = psT if ks == 0 else psU
            tpk = 'psT' if ks == 0 else 'psU'
            wv = WB[ws][:, 0:16 * 384].rearrange("p (k n) -> p k n", k=16)
            mms([(Z[0:M, 0:384], cols_ap_fn(kc), wv[:, kc, :], kc == 0, kc == 15) for kc in range(16)],
                [('WB', ws)] + ALLH, [zk])
            yield
            cosv = (ROPEM[0:M, 0:16] if misc else ropev[0:M, tidx, 0:16])
            sinv = (ROPEM[0:M, 16:32] if misc else ropev[0:M, tidx, 16:32])
            zqk = Z[0:M, 0:256].rearrange("p (a c) -> p a c", a=2)
            x1 = zqk[:, :, 0:16]
            x2 = zqk[:, :, 16:32]
            cb_ = cosv[:, None, :].broadcast_to([M, 2, 16])
            sb_ = sinv[:, None, :].broadcast_to([M, 2, 16])
            T = RTMP[0:M, ks * 256:ks * 256 + 128].rearrange("p (q a c) -> p q a c", q=4, a=2)
            QF = RTMP[0:M, ks * 256 + 128:ks * 256 + 256]
            rt = lambda i: ('RT', ks, i)
            tt(T[:, 0], x1, cb_, ALU.mult, [zk, 'ROPE', 'ROPEM'], [rt(0)])
            tt(T[:, 1], x2, sb_, ALU.mult, [zk, 'ROPE', 'ROPEM'], [rt(1)])
            tt(T[:, 2], x2, cb_, ALU.mult, [zk, 'ROPE', 'ROPEM'], [rt(2)])
            tt(T[:, 3], x1, sb_, ALU.mult, [zk, 'ROPE', 'ROPEM'], [rt(3)])
            KF = KVF[0:M, ks * 256:ks * 256 + 256]
            kfk = ('KVF', ks)
            tcopy(KF[:, 128:256], Z[0:M, 256:384], [zk], [kfk], eng='act')
            tt(KF[:, 0:16], T[:, 0, 1], T[:, 1, 1], ALU.subtract, [rt(0), rt(1)], [kfk])
            tt(KF[:, 16:32], T[:, 2, 1], T[:, 3, 1], ALU.add, [rt(2), rt(3)], [kfk])
            tcopy(KF[:, 32:128], Z[0:M, 160:256], [zk], [kfk])
            qb = QKB[0:M, ks * 256:ks * 256 + 128]
            kb = QKB[0:M, ks * 256 + 128:ks * 256 + 256]
            qk_ = ('QKB', ks)
            tcopy(vdst, KF[:, 128:256], [kfk], [dkeys[1]], eng='act')
            tcopy(kb, KF[:, 0:128], [kfk], [qk_])
            if want_q:
                tt(QF[:, 0:16], T[:, 0, 0], T[:, 1, 0], ALU.subtract, [rt(0), rt(1)], [('QF', ks)])
                tt(QF[:, 16:32], T[:, 2, 0], T[:, 3, 0], ALU.add, [rt(2), rt(3)], [('QF', ks)])
                tcopy(QF[:, 32:128], Z[0:M, 32:128], [zk], [('QF', ks)])
                tcopy(qb, QF[:, 0:128], [('QF', ks)], [qk_])
            yield
            lst = [(TP[:, 128:128 + M], kb, IDB[0:M, 0:M])]
            if want_q:
                lst.append((TP[:, 0:M], qb, IDB[0:M, 0:M]))
            trs(lst, [qk_, 'IDB'], [tpk])
            if kvout is not None:
                DMA('sp', 'kvo%d' % ks, kvout(KF), r=[kfk])
            yield
            tcopy(kdst, TP[:, 128:128 + M], [tpk], [dkeys[0]])
            if want_q:
                tcopy(qdst, TP[:, 0:M], [tpk], [dkeys[2]], eng='act')

        aslot = [0]
        dbgsel = [0]

        def attn_unit_gen(g, prep, qT, kp, kc_, vp, vc, mask, outs, rk):
            s_ = aslot[0] % 3
            aslot[0] += 1
            if prep is not None:
                qT, kp, kc_, vp, vc, rk = prep(s_)
                yield
            stt_ = ST2[:, 4 * s_:4 * s_ + 2]
            pb = APB[:, s_ * 512:s_ * 512 + 256]
            pT = APB[:, s_ * 512 + 256:s_ * 512 + 512]
            P_ = (psB, psC, psD)[s_]
            pk = ('psB', 'psC', 'psD')[s_]
            TP = (psT, psU, psEb)[s_]
            tpk = ('psT', 'psU', 'psE')[s_]
            mask = mask_std if mask == 's' else mask_halo
            mms([(P_[:, 0:128], qT, kp, True, True), (P_[:, 128:256], qT, kc_, True, True)], rk, [pk])
            yield
            stt(P_[:, 0:256], P_[:, 0:256], 128.0 ** -0.5, mask, ALU.mult, ALU.add, [pk, 'MSK'], [pk])
            A('dve', lambda e: e.reduce_max(stt_[:, 0:1], P_[:, 0:256], axis=AX.X), [pk], [('mx', s_)])
            tsc(stt_[:, 1:2], stt_[:, 0:1], -1.0, None, ALU.mult, ALU.bypass, [('mx', s_)], [('nmx', s_)])
            act(pb, P_[:, 0:256], AF.Exp, [pk, ('nmx', s_)], [('pb', s_)], bias=stt_[:, 1:2])
            if dbgsel[0] == 1:
                dbgsel[0] = 2
                DMA('sp', 'dbgb', [(dbg_d[7], pb[:, 0:128]), (dbg_d[8], pb[:, 128:256]), (dbg2_d, stt_)], r=[('pb', s_), ('mx', s_), ('nmx', s_)])
            yield
            trs([(TP[:, 256:384], pb[:, 0:128], IDB[:, :]), (TP[:, 384:512], pb[:, 128:256], IDB[:, :])],
                [('pb', s_), 'IDB'], [tpk])
            yield
            tcopy(pT, TP[:, 256:512], [tpk], [('pT', s_)], eng='act')
            yield
            mms([(P_[:, 256:384], vp, pT[:, 0:128], True, False), (P_[:, 256:384], vc, pT[:, 128:256], False, True),
                 (P_[0:1, 384:512], ONEB[:, 0:1], pT[:, 0:128], True, False),
                 (P_[0:1, 384:512], ONEB[:, 0:1], pT[:, 128:256], False, True),
                 (P_[0:1, 0:128], stt_[:, 0:1], identf, True, True)],
                rk + [('pT', s_), ('mx', s_), 'ONEB', 'CON'], [pk])
            if dbgsel[0] == 2:
                dbgsel[0] = 3
                DMA('sp', 'dbg', [(dbg_d[0], vp), (dbg_d[1], vc), (dbg_d[2], pT[:, 0:128]), (dbg_d[3], pT[:, 128:256]),
                                  (dbg_d[4], kp), (dbg_d[5], kc_), (dbg_d[6], qT)], r=rk + [('pT', s_)])
            yield
            tcopy(LMS[0:1, s_ * 256:s_ * 256 + 128], P_[0:1, 384:512], [pk], [('LMS', s_)], eng='act')
            tcopy(LMS[0:1, s_ * 256 + 128:s_ * 256 + 256], P_[0:1, 0:128], [pk], [('LMS', s_)], eng='act')
            prs = []
            for (src, dst) in outs:
                tcopy(OTG[g][:, dst], P_[:, 256:384][:, src], [pk], [('OTG', g)])
                prs.append((lm_d[g, 0:1, dst], LMS[0:1, s_ * 256:s_ * 256 + 128][:, src]))
                prs.append((lm_d[g, 1:2, dst], LMS[0:1, s_ * 256 + 128:s_ * 256 + 256][:, src]))
            DMA('sp', 'lmo%d' % s_, prs, r=[('LMS', s_)], w=[('lm_d', g)])

        mask_std = MSK[:, 0:256]
        mask_halo = MSK[:, 256:512]

        def attn_head(hh, sweep):
            g, j = hh // 4, hh % 4
            W, d = GROUPS[g]
            nT = 16 // d
            ws = wload(watt_t[hh], 16 * 384)
            hb = hkv_idx(hh)
            DMA('sp', 'c1', [(ropev, rope_t[g].rearrange("t p c -> p t c"))], w=['ROPE'])
            HKSv = HKS[:, :].rearrange("p (s c) -> p s c", s=3)
            if sweep == 'H':
                def hgen(r):
                    st_ = TOK - 128 * d + r
                    sl = r % 3
                    yield from proj_gen(ws, lambda kc, st_=st_: hT[:, kc, st_:st_ + 128 * d:d], 128, g, r * (nT + 1), False, False,
                                        HKSv[:, sl, 0:128], HKSv[:, sl, 128:256], None, None, [('HKS', sl), ('HKS', sl), None])
                    DMA('sp', 'hkvo%d' % sl, [(hkv_d[hb + r], HKSv[:, sl, 0:128]), (hkv_d[hb + d + r], HKSv[:, sl, 128:256])],
                        r=[('HKS', sl)], w=[('hkv_d', hh)])
                pipeline((hgen(r) for r in range(d)), 2)
                return
            gens = []
            for r in range(d):
                for T_ in range(nT):
                    ti = r * nT + T_
                    st_ = r + d * 128 * T_
                    o0 = st_ - (TOK - W)
                    kvout = None
                    if o0 >= 0:
                        def kvout(KF, o0=o0, d=d, g=g, j=j):
                            return [(kvp[g][o0:o0 + 127 * d + 1:d, 0, j, :], KF[:, 0:128]),
                                    (kvp[g][o0:o0 + 127 * d + 1:d, 1, j, :], KF[:, 128:256])]
                    gens.append(proj_gen(ws, lambda kc, st_=st_: hT[:, kc, st_:st_ + 128 * d:d], 128, g, r * (nT + 1) + T_ + 1, False, True,
                                         KTv[:, ti, :], VAv[:, ti, :], QTv[:, ti, :], kvout, [('KT', ti), ('VA', ti), ('QT', ti)]))

            def kvout_m(KF, g=g, j=j, W=W):
                prs = []
                for b in range(4):
                    prs.append((kvs[g][b, W - 4:W, 0, j, :], KF[4 * b:4 * b + 4, 0:128]))
                    prs.append((kvs[g][b, W - 4:W, 1, j, :], KF[4 * b:4 * b + 4, 128:256]))
                return prs
            gens.append(proj_gen(ws, lambda kc: hT[:, kc, TOK:TOK + MS], MS, g, 0, True, True,
                                 MQK[:, MS:2 * MS], MV[:, :], MQK[:, 0:MS], kvout_m, ['MK', 'MV', 'MQ']))
            pipeline(gens, 2)
            qTm = MQK[:, 0:MS]
            kTm = MQK[:, MS:2 * MS]
            STGv = ASTAGE[:, :].rearrange("p (s t c) -> p s t c", s=3, t=6)
            units = []
            for r in range(d):
                for T_ in range(nT):
                    ti = r * nT + T_
                    st_ = r + d * 128 * T_
                    outs = [(slice(0, 128), slice(st_, st_ + 128 * d, d))]
                    if T_ == 0:
                        def prep(sl, ti=ti, r=r):
                            if _os.environ.get("DBGF") == "1":
                                S.fence()
                            DMA('sp', 'hkvl%d' % sl, [(HKSv[:, sl, 0:128], hkv_d[hb + r]), (HKSv[:, sl, 128:256], hkv_d[hb + d + r])],
                                r=[('hkv_d', hh)], w=[('HKS', sl)])
                            return (QTv[:, ti, :], HKSv[:, sl, 0:128], KTv[:, ti, :], HKSv[:, sl, 128:256], VAv[:, ti, :],
                                    [('QT', ti), ('KT', ti), ('VA', ti), ('HKS', sl)])
                        units.append(attn_unit_gen(g, prep, None, None, None, None, None, 'h', outs, None))
                    else:
                        units.append(attn_unit_gen(g, None, QTv[:, ti, :], KTv[:, ti - 1, :], KTv[:, ti, :], VAv[:, ti - 1, :], VAv[:, ti, :],
                                                   's', outs, [('QT', ti), ('KT', ti), ('VA', ti), ('KT', ti - 1), ('VA', ti - 1)]))
            pre = [(0, [126, 127], [16, 17], [(126, 18), (127, 19)])] if g == 0 else \
                  [((d - 2 + t), [127], [16 + t], [(127, 18 + 2 * g + t)]) for t in range(2)]
            for (r, qrows, qslots, extras) in pre:
                def prep(sl, r=r, qrows=qrows, qslots=qslots, extras=extras):
                    sk = ('STG', sl)
                    for qr, qs in zip(qrows, qslots):
                        tcopy(STGv[:, sl, 0, qr:qr + 1], qTm[:, qs:qs + 1], ['MQ'], [sk])
                    for (sl_, ms) in extras:
                        tcopy(STGv[:, sl, 1, sl_:sl_ + 1], kTm[:, ms:ms + 1], ['MK'], [sk])
                        DMA('sp', 'stg%d' % sl, [(STGv[sl_:sl_ + 1, sl, 3, :], MV[ms:ms + 1, :])], r=['MV'], w=[sk])
                    DMA('sp', 'hkvl%d' % sl, [(HKSv[:, sl, 0:128], hkv_d[hb + r]), (HKSv[:, sl, 128:256], hkv_d[hb + d + r])],
                        r=[('hkv_d', hh)], w=[('HKS', sl)])
                    return (STGv[:, sl, 0, :], STGv[:, sl, 1, :], HKSv[:, sl, 0:128], STGv[:, sl, 3, :], HKSv[:, sl, 128:256],
                            [sk, ('HKS', sl)])
                outs = [(slice(qr, qr + 1), slice(TOK + qs, TOK + qs + 1)) for qr, qs in zip(qrows, qslots)]
                units.append(attn_unit_gen(g, prep, None, None, None, None, None, 'h', outs, None))
            for b in range(4):
                ulist = [(0, [0, 1, 2, 3])] if g == 0 else [(t, [t]) for t in range(4)]
                for (t0, ts) in ulist:
                    def prep(sl, b=b, t0=t0, ts=ts):
                        sk = ('STG', sl)
                        TPs = (psT, psU, psEb)[sl]
                        tpk = ('psT', 'psU', 'psE')[sl]
                        rows = ck[g][b, t0:t0 + 127 * d + 1:d, :, j, :] if g > 0 else ck[g][b, :, :, j, :]
                        DMA('pool', 'kc%d' % sl, [(STGv[:, sl, 5, :], rows[:, 0, :]), (STGv[:, sl, 3, :], rows[:, 1, :])], w=[('STG5', sl), sk])
                        trs([(TPs[:, 512:640], STGv[:, sl, 5, :], IDB[:, :])], [('STG5', sl), 'IDB'], [tpk])
                        tcopy(STGv[:, sl, 1, :], TPs[:, 512:640], [tpk], [sk])
                        for n_, t in enumerate(ts):
                            ms = 4 * b + t
                            tcopy(STGv[:, sl, 0, n_:n_ + 1], qTm[:, ms:ms + 1], ['MQ'], [sk])
                            tcopy(STGv[:, sl, 2, n_:n_ + 1], kTm[:, ms:ms + 1], ['MK'], [sk])
                            DMA('sp', 'stg%d' % sl, [(STGv[n_:n_ + 1, sl, 4, :], MV[ms:ms + 1, :])], r=['MV'], w=[sk])
                        return (STGv[:, sl, 0, :], STGv[:, sl, 1, :], STGv[:, sl, 2, :], STGv[:, sl, 3, :], STGv[:, sl, 4, :], [sk])
                    outs = [(slice(n_, n_ + 1), slice(TOK + 4 * b + t, TOK + 4 * b + t + 1)) for n_, t in enumerate(ts)]
                    units.append(attn_unit_gen(g, prep, None, None, None, None, None, 's', outs, None))
            pipeline(units, UDEPTH)

        def attn_combine(slot):
            lk = [('lm_d', 0), ('lm_d', 1), ('lm_d', 2)]
            K = lambda a_: ('cmb', a_)
            Cv = lambda a_: CMB[:, a_ * 17:(a_ + 1) * 17]
            DMA('sp', 'cml', [(Cv(2 * g + k_), lm_d[g, k_].rearrange("(p c) -> p c", p=128)) for g in range(3) for k_ in range(2)],
                r=lk, w=[K(i) for i in range(6)])
            for g in range(3):
                act(Cv(6 + g), Cv(2 * g), AF.Ln, [K(2 * g)], [K(6 + g)])
                tt(Cv(6 + g), Cv(6 + g), Cv(2 * g + 1), ALU.add, [K(6 + g), K(2 * g + 1)], [K(6 + g)])
            tt(Cv(9), Cv(6), Cv(7), ALU.max, [K(6), K(7)], [K(9)])
            tt(Cv(9), Cv(9), Cv(8), ALU.max, [K(9), K(8)], [K(9)])
            for g in range(3):
                tt(Cv(6 + g), Cv(6 + g), Cv(9), ALU.subtract, [K(6 + g), K(9)], [K(6 + g)])
                act(Cv(6 + g), Cv(6 + g), AF.Exp, [K(6 + g)], [K(6 + g)])
            tt(Cv(10), Cv(6), Cv(7), ALU.add, [K(6), K(7)], [K(10)])
            tt(Cv(10), Cv(10), Cv(8), ALU.add, [K(10), K(8)], [K(10)])
            A('dve', lambda e, Cv=Cv: e.reciprocal(Cv(10), Cv(10)), [K(10)], [K(10)])
            for g in range(3):
                A('dve', lambda e, Cv=Cv, g=g: e.reciprocal(Cv(2 * g), Cv(2 * g)), [K(2 * g)], [K(2 * g)])
                tt(Cv(6 + g), Cv(6 + g), Cv(10), ALU.mult, [K(6 + g), K(10)], [K(6 + g)])
                tt(Cv(6 + g), Cv(6 + g), Cv(2 * g), ALU.mult, [K(6 + g), K(2 * g)], [K(6 + g)])
            DMA('sp', 'cfo', [(coef_d[g].rearrange("(p c) -> p c", p=128), Cv(6 + g)) for g in range(3)],
                r=[K(6), K(7), K(8)], w=['coef_d'])
            BW = 192
            for bi, c0 in enumerate(range(0, NTOK, BW)):
                n = min(BW, NTOK - c0)
                sl = bi % 2
                Bv = lambda g, n=n, sl=sl: CMB[:, 192 + (sl * 3 + g) * BW:192 + (sl * 3 + g) * BW + n]
                bk = lambda g, sl=sl: ('cbc', sl, g)
                DMA('sp', 'cbl%d' % sl, [(Bv(g), coef_d[g:g + 1, c0:c0 + n].partition_broadcast(128)) for g in range(3)],
                    r=['coef_d'], w=[bk(0), bk(1), bk(2)])
                for g in range(3):
                    tt(Bv(g), Bv(g), OTG[g][:, c0:c0 + n], ALU.mult, [bk(g), ('OTG', g)], [bk(g)])
                tt(Bv(0), Bv(0), Bv(1), ALU.add, [bk(0), bk(1)], [bk(0)])
                tt(YCH[:, c0:c0 + n], Bv(0), Bv(2), ALU.add, [bk(0), bk(2)], ['YCH'])
            DMA('sp', 'ych', [(oatt_d[slot], YCH[:, :])], r=['YCH'], w=[('oatt_d', slot)])

        SSTv = SST[:, :].rearrange("p (h c) -> p h c", h=8)
        hslot = [0]
        ringn = [0]

        def hgrn_tile_gen(ws, h, cols_fn, need_out, outs):
            cnt = hslot[0]
            hslot[0] += 1
            s_ = cnt % 3
            p_ = cnt % 2
            n0 = ringn[0]
            ringn[0] += 2
            wv = WB[ws][:, :].rearrange("p (k n) -> p k n", k=16)
            Z, zk = (psA, 'psA') if p_ == 0 else (psF, 'psF')
            Pb, pbk = (psB, 'psB') if p_ == 0 else (psC, 'psC')
            Po, pok = (psD, 'psD') if p_ == 0 else (psE, 'psE')
            TP, tpk = (psT, 'psT') if p_ == 0 else (psU, 'psU')
            T = HT[:, s_ * 1536:(s_ + 1) * 1536]
            Bf = HB[:, s_ * 1024:(s_ + 1) * 1024]
            k_ = lambda n: ('h%s' % n, s_)
            f_, lf, kk, bs, eb, enb = (T[:, i * 128:(i + 1) * 128] for i in range(6))
            ekl, sq, sg, dif = (T[:, i * 128:(i + 1) * 128] for i in range(6, 10))
            e1, e2 = T[:, 1280:1408], T[:, 1408:1536]
            qe, ke, kl, vb, kl1 = (Bf[:, i * 128:(i + 1) * 128] for i in range(5))
            og = qe
            qkT = Bf[:, 640:896]
            attm = Bf[:, 896:1024]
            edec = ST3[:, 2 * s_:2 * s_ + 2]
            ssq = ST3[:, 8 + 2 * s_:10 + 2 * s_]
            SRv = lambda i: SR[:, (i % 4) * 128:(i % 4) * 128 + 128]
            SRBv = lambda i: SRB[:, (i % 4) * 128:(i % 4) * 128 + 128]
            rk = lambda i: ('SR', i % 4)
            rbk = lambda i: ('SRB', i % 4)
            mms([(Z[:, :], cols_fn(kc), wv[:, kc, :], kc == 0, kc == 15) for kc in range(16)], [('WB', ws)] + ALLH, [zk])
            yield
            act(f_, Z[:, 128:256], AF.Exp, [zk], [k_('f')], scale=-1.0)
            act(e1, Z[:, 0:128], AF.Exp, [zk], [k_('e1')], scale=-1.0)
            if need_out:
                act(e2, Z[:, 384:512], AF.Exp, [zk], [k_('e2')], scale=-1.0)
            tcopy(vb, Z[:, 256:384], [zk], [k_('v')], eng='act')
            tsc(f_, f_, 1.0, None, ALU.add, ALU.bypass, [k_('f')], [k_('f')])
            A('dve', lambda e: e.reciprocal(f_, f_), [k_('f')], [k_('f')])
            tt(f_, f_, LB[:, 1024 + h * 128:1024 + (h + 1) * 128], ALU.mult, [k_('f'), 'LB'], [k_('f')])
            tt(f_, f_, LB[:, h * 128:(h + 1) * 128], ALU.add, [k_('f'), 'LB'], [k_('f')])
            act(lf, f_, AF.Ln, [k_('f')], [k_('lf')])
            tsc(kk, f_, -1.0, 1.0, ALU.mult, ALU.add, [k_('f')], [k_('k')])
            tsc(e1, e1, 1.0, None, ALU.add, ALU.bypass, [k_('e1')], [k_('e1')])
            A('dve', lambda e: e.reciprocal(e1, e1), [k_('e1')], [k_('e1')])
            tt(sq, Z[:, 0:128], e1, ALU.mult, [zk, k_('e1')], [k_('sq')])
            if need_out:
                tsc(e2, e2, 1.0, None, ALU.add, ALU.bypass, [k_('e2')], [k_('e2')])
                A('dve', lambda e: e.reciprocal(e2, e2), [k_('e2')], [k_('e2')])
                tt(sg, Z[:, 384:512], e2, ALU.mult, [zk, k_('e2')], [k_('sg')])
            yield
            mms([(Pb[:, 0:128], tri2, lf, True, True), (Pb[:, 128:256], blk2, lf, True, True),
                 (Pb[:, 256:258], lf, ind2, True, True)], [k_('lf'), 'CON'], [pbk])
            yield
            tcopy(bs, Pb[:, 0:128], [pbk], [k_('bs')])
            act(edec, Pb[:, 256:258], AF.Exp, [pbk], [k_('edec')])
            tt(dif, Pb[:, 128:256], bs, ALU.subtract, [pbk, k_('bs')], [k_('dif')])
            act(eb, bs, AF.Exp, [k_('bs')], [k_('eb')])
            act(enb, bs, AF.Exp, [k_('bs')], [k_('enb')], scale=-1.0)
            act(ekl, dif, AF.Exp, [k_('dif')], [k_('ekl')])
            tt(qe, sq, eb, ALU.mult, [k_('sq'), k_('eb')], [k_('qe')])
            tt(ke, kk, enb, ALU.mult, [k_('k'), k_('enb')], [k_('ke')])
            stt(kl, kk, ind2[:, 0:1], ekl, ALU.mult, ALU.mult, [k_('k'), k_('ekl'), 'CON'], [k_('kl')])
            stt(kl1, kk, ind2[:, 1:2], ekl, ALU.mult, ALU.mult, [k_('k'), k_('ekl'), 'CON'], [k_('kl1')])
            yield
            trs([(TP[:, 512:640], qe, IDB[:, :]), (TP[:, 640:768], ke, IDB[:, :])], [k_('qe'), k_('ke'), 'IDB'], [tpk])
            yield
            tcopy(qkT, TP[:, 512:768], [tpk], [k_('qkT')])
            yield
            mms([(Po[:, 256:384], qkT[:, 128:256], qkT[:, 0:128], True, True),
                 (Po[:, 128:256], kl, vb, True, True),
                 (Po[:, 384:512], kl1, vb, True, True)], [k_('qkT'), k_('kl'), k_('kl1'), k_('v')], [pok])
            yield
            tt(attm, Po[:, 256:384], tri2, ALU.mult, [pok, 'CON'], [k_('attm')])
            stt(SRv(n0 + 1), SRv(n0), edec[:, 0:1], Po[:, 128:256], ALU.mult, ALU.add, [pok, k_('edec'), rk(n0)], [rk(n0 + 1)])
            stt(SRv(n0 + 2), SRv(n0 + 1), edec[:, 1:2], Po[:, 384:512], ALU.mult, ALU.add, [pok, k_('edec'), rk(n0 + 1)], [rk(n0 + 2)])
            if need_out:
                tcopy(SRBv(n0), SRv(n0), [rk(n0)], [rbk(n0)], eng='act')
                tcopy(SRBv(n0 + 1), SRv(n0 + 1), [rk(n0 + 1)], [rbk(n0 + 1)], eng='act')
            if not need_out:
                return
            yield
            mms([(Po[:, 0:128], attm, vb, True, False),
                 (Po[0:64, 0:128], qkT[:, 0:64], SRBv(n0), False, True),
                 (Po[64:128, 0:128], qkT[:, 64:128], SRBv(n0 + 1), False, True)],
                [k_('attm'), k_('v'), k_('qkT'), rbk(n0), rbk(n0 + 1)], [pok])
            yield
            act(dif, Po[:, 0:128], AF.Square, [pok], [k_('dif'), k_('ssq')], accum_out=ssq[:, 0:1])
            act(ssq[:, 1:2], ssq[:, 0:1], AF.Ln, [k_('ssq')], [k_('ssq2')], scale=1.0 / 128, bias=EPS)
            act(ssq[:, 1:2], ssq[:, 1:2], AF.Exp, [k_('ssq2')], [k_('rs')], scale=-0.5)
            stt(dif, Po[:, 0:128], ssq[:, 1:2], HGW[:, :], ALU.mult, ALU.mult, [pok, k_('rs'), 'HGW'], [k_('dif')])
            tt(og, dif, sg, ALU.mult, [k_('dif'), k_('sg')], [k_('qe')])
            yield
            trs([(TP[:, 768:896], og, IDB[:, :])], [k_('qe'), 'IDB'], [tpk])
            yield
            for (buf, bk, src, dst) in outs:
                tcopy(buf[:, dst], TP[:, 768:896][:, src], [tpk], [bk], eng='act')

        def hgrn_head_tiles(ws, h, nts, need_fn, outs_fn):
            ringn[0] = 0
            tcopy(SR[:, 0:128], SSTv[:, h, :], ['SST'], [('SR', 0)])
            pipeline((hgrn_tile_gen(ws, h, (lambda kc, nt=nt: hT[:, kc, nt * 128:(nt + 1) * 128]), need_fn(nt), outs_fn(nt))
                      for nt in nts), 3)
            nf = ringn[0] % 4
            tcopy(SSTv[:, h, :], SR[:, nf * 128:nf * 128 + 128], [('SR', nf)], ['SST'])

        def hgrn_misc(ws, h):
            wv = WB[ws][:, :].rearrange("p (k n) -> p k n", k=16)
            M = MS
            mms([(psA[0:M, :], hT[:, kc, TOK:TOK + MS], wv[:, kc, :], kc == 0, kc == 15) for kc in range(16)], [('WB', ws)] + ALLH, ['psA'])
            T = HT[0:M, 0:1536]
            Bf = HB[0:M, 0:1024]
            f_, lf, kk, bs, eb, enb = (T[:, i * 128:(i + 1) * 128] for i in range(6))
            ekl, sq, sg, dif = (T[:, i * 128:(i + 1) * 128] for i in range(6, 10))
            qe, ke, kl, vb, og = (Bf[:, i * 128:(i + 1) * 128] for i in range(5))
            DMA('sp', 's0', [(S0[:, :].rearrange("p (b c) -> p b c", b=4), sthg[:, h].rearrange("b d v -> d b v"))], w=['S0'])
            tcopy(S0B[:, :], S0[:, :], ['S0'], ['S0B'])
            act(f_, psA[0:M, 128:256], AF.Sigmoid, ['psA'], ['mf'])
            tt(f_, f_, LB[0:M, 1024 + h * 128:1024 + (h + 1) * 128], ALU.mult, ['mf', 'LB'], ['mf'])
            tt(f_, f_, LB[0:M, h * 128:(h + 1) * 128], ALU.add, ['mf', 'LB'], ['mf'])
            act(lf, f_, AF.Ln, ['mf'], ['mlf'])
            tsc(kk, f_, -1.0, 1.0, ALU.mult, ALU.add, ['mf'], ['mk'])
            mms([(psB[0:M, 0:128], tri_m, lf, True, True), (psB[0:M, 128:256], blk_m, lf, True, True),
                 (psB[:, 256:260], lf, ind_m, True, True)], ['mlf', 'CON'], ['psB'])
            tcopy(bs, psB[0:M, 0:128], ['psB'], ['mbs'])
            act(eb, bs, AF.Exp, ['mbs'], ['meb'])
            act(enb, bs, AF.Exp, ['mbs'], ['menb'], scale=-1.0)
            tt(dif, psB[0:M, 128:256], bs, ALU.subtract, ['psB', 'mbs'], ['mdif'])
            act(ekl, dif, AF.Exp, ['mdif'], ['mekl'])
            edec = ST[:, 40:44]
            act(edec, psB[:, 256:260], AF.Exp, ['psB'], ['medec'])
            act(sq, psA[0:M, 0:128], AF.Silu, ['psA'], ['msq'])
            act(sg, psA[0:M, 384:512], AF.Silu, ['psA'], ['msg'])
            tt(qe, sq, eb, ALU.mult, ['msq', 'meb'], ['mqe'])
            tt(ke, kk, enb, ALU.mult, ['mk', 'menb'], ['mke'])
            tt(kl, kk, ekl, ALU.mult, ['mk', 'mekl'], ['mkl'])
            tcopy(vb, psA[0:M, 256:384], ['psA'], ['mv'], eng='act')
            trs([(psT[:, 512:512 + M], qe, IDB[0:M, 0:M]), (psT[:, 640:640 + M], ke, IDB[0:M, 0:M])], ['mqe', 'mke', 'IDB'], ['psTh'])
            qeT = HB[:, 3072:3072 + M]
            keT = HB[:, 3104:3104 + M]
            qeTm = HB[:, 3200:3200 + 4 * M].rearrange("p (b t) -> p b t", b=4)
            klm = HB[0:M, 3328:3328 + 512].rearrange("p (b c) -> p b c", b=4)
            tcopy(qeT, psT[:, 512:512 + M], ['psTh'], ['mqeT'])
            tcopy(keT, psT[:, 640:640 + M], ['psTh'], ['mkeT'])
            tt(qeTm, qeT[:, None, :].broadcast_to([128, 4, M]), colm.rearrange("p (b t) -> p b t", b=4), ALU.mult, ['mqeT', 'CON'], ['mqeTm'])
            tt(klm, kl[:, None, :].broadcast_to([M, 4, 128]), rowm[:, :, None].broadcast_to([M, 4, 128]), ALU.mult, ['mkl', 'CON'], ['mklm'])
            mms([(psD[0:M, 256:256 + M], keT, qeT, True, True)], ['mqeT', 'mkeT'], ['psDh'])
            attm = HB[0:M, 896:896 + M]
            tt(attm, psD[0:M, 256:256 + M], cmask_m, ALU.mult, ['psDh', 'CON'], ['mattm'])
            lst = [(psD[0:M, 0:128], attm, vb, True, False)]
            for b in range(4):
                lst.append((psD[0:M, 0:128], qeTm[:, b, :], S0B[:, b * 128:(b + 1) * 128], False, b == 3))
            mms(lst, ['mattm', 'mv', 'mqeTm', 'S0B'], ['psDo'])
            mms([(psE[:, b * 128:(b + 1) * 128], klm[:, b, :], vb, True, True) for b in range(4)], ['mklm', 'mv'], ['psE'])
            S0v = S0[:, :].rearrange("p (b c) -> p b c", b=4)
            tt(S0v, S0v, edec[:, :, None].broadcast_to([128, 4, 128]), ALU.mult, ['S0', 'medec', 'S0B'], ['S0'])
            tt(S0[:, :], S0[:, :], psE[:, :], ALU.add, ['S0', 'psE'], ['S0'])
            DMA('sp', 's0o', [(hgs[:, h].rearrange("b d v -> d b v"), S0v)], r=['S0'])
            act(dif, psD[0:M, 0:128], AF.Square, ['psDo'], ['mdif', 'mssq'], accum_out=ST[0:M, 36:37])
            act(ST[0:M, 37:38], ST[0:M, 36:37], AF.Sqrt, ['mssq'], ['mssq2'], scale=1.0 / 128, bias=EPS)
            A('dve', lambda e: e.reciprocal(ST[0:M, 38:39], ST[0:M, 37:38]), ['mssq2'], ['mrs'])
            stt(dif, psD[0:M, 0:128], ST[0:M, 38:39], HGW[0:M, :], ALU.mult, ALU.mult, ['psDo', 'mrs', 'HGW'], ['mdif'])
            tt(og, dif, sg, ALU.mult, ['mdif', 'msg'], ['mog'])
            trs([(psT[:, 768:768 + M], og, IDB[0:M, 0:M])], ['mog', 'IDB'], ['psTg'])
            tcopy(OHGH[:, TOK:TOK + 16], psT[:, 768:784], ['psTg'], ['OHGH'], eng='act')

        A('pool', lambda e: e.memset(SST[:, :], 0.0), [], ['SST'])
        A('pool', lambda e: e.memset(SBF[:, :], 0.0), [], ['SBF'])
        ONEROW = lsb("ONEROW", [1, 2176])
        A('pool', lambda e: e.memset(ONEROW[:, :], 1.0), [], ['ONEROW'])
        DMA('sp', 'lmi', [(lm_d[g, k_:k_ + 1, :], ONEROW[:, :]) for g in range(3) for k_ in range(2)], r=['ONEROW'],
            w=[('lm_d', 0), ('lm_d', 1), ('lm_d', 2)])
        phase_end()
        for nt in range(16):
            norm_tile(xall[nt * 128:(nt + 1) * 128, :], 128, 0, nt * 128, 128, False)
        alloc_attn()
        for hh in range(12):
            attn_head(hh, 'H')
        phase_end()
        alloc_hgrn()
        OHGPv = OHGPRE[:, :].rearrange("p (h c) -> p h c", h=8)
        for h in range(8):
            ws = wload(whg_t[h], 8192)
            hgrn_head_tiles(ws, h, range(16), lambda nt: nt == 15,
                            lambda nt, h=h: [(OHGPRE, 'OHGPRE', slice(126, 128), slice(2 * h, 2 * h + 2))])
            tsc(SSTv[:, h, :], SSTv[:, h, :], CVAL[:, 0:1], None, ALU.mult, ALU.bypass, ['SST', 'CVAL'], ['SST'])
        phase_end()
        for nt in range(16):
            norm_tile(xall[(16 + nt) * 128:(17 + nt) * 128, :], 128, 0, nt * 128, 128, False)
        norm_tile(xall[32 * 128:32 * 128 + MS, :], MS, 0, TOK, MS, True)
        alloc_attn()
        for slot in range(4):
            for g in range(3):
                attn_head(4 * g + slot, 'O')
            attn_combine(slot)
        phase_end()
        alloc_hgrn()
        for h in range(8):
            ws = wload(whg_t[h], 8192)
            hgrn_head_tiles(ws, h, range(16), lambda nt: True,
                            lambda nt: [(OHGH, 'OHGH', slice(0, 128), slice(nt * 128, (nt + 1) * 128))])
            DMA('sp', 'hgp', [(hgp[h], SSTv[:, h, :])], r=['SST'])
            S.fence()
            hgrn_misc(ws, h)
            S.fence()
            tcopy(OHGH[:, TOK + 16:TOK + 18], OHGPv[:, h, :], ['OHGPRE'], ['OHGH'])
            DMA('sp', 'ohgo', [(ohg_d[h], OHGH[:, :])], r=['OHGH'], w=[('ohg_d', h)])
        phase_end()
        BIGB = lsb("BIGB", [128, 12 * NTOK], BF16)
        YCH = lsb("YCHb", [128, NTOK], BF16)
        GT = lsb("GT", [128, 2 * 512])
        OAT = BIGB[:, :].rearrange("p (k t) -> p k t", k=12)
        DMA('sp', 'oal', [(OAT[:, s_, :], oatt_d[s_]) for s_ in range(4)] + [(OAT[:, 4 + h, :], ohg_d[h]) for h in range(8)], w=['OAT'])
        for fc in range(16):
            ws = wload(wB1_t[fc], 44 * 128)
            wv = WB[ws][:, 0:44 * 128].rearrange("p (k n) -> p k n", k=44)
            for bi, c0 in enumerate(range(0, NTOK, 512)):
                n = min(512, NTOK - c0)
                lst = []
                for kc in range(16):
                    lst.append((psA[:, 0:n], wv[:, kc, :], hT[:, kc, c0:c0 + n], kc == 0, kc == 15))
                for kc in range(16):
                    lst.append((psB[:, 0:n], wv[:, 16 + kc, :], hT[:, kc, c0:c0 + n], kc == 0, kc == 15))
                for kc in range(4):
                    lst.append((psC[:, 0:n], wv[:, 32 + kc, :], OAT[:, kc, c0:c0 + n], kc == 0, kc == 3))
                for kc in range(8):
                    lst.append((psD[:, 0:n], wv[:, 36 + kc, :], OAT[:, 4 + kc, c0:c0 + n], kc == 0, kc == 7))
                mms(lst, [('WB', ws), 'OAT'] + ALLH, ['psA', 'psB', 'psC', 'psD'])
                act(GT[:, 0:n], psA[:, 0:n], AF.Sigmoid, ['psA'], ['GT0'])
                act(GT[:, 512:512 + n], psB[:, 0:n], AF.Sigmoid, ['psB'], ['GT1'])
                tt(GT[:, 0:n], GT[:, 0:n], psC[:, 0:n], ALU.mult, ['GT0', 'psC'], ['GT0'])
                tt(GT[:, 512:512 + n], GT[:, 512:512 + n], psD[:, 0:n], ALU.mult, ['GT1', 'psD'], ['GT1'])
                tt(YCH[:, c0:c0 + n], GT[:, 0:n], GT[:, 512:512 + n], ALU.add, ['GT0', 'GT1'], ['YCH'])
            DMA('sp', 'ych', [(ymix_d[fc], YCH[:, :])], r=['YCH'], w=[('ymix_d', fc)])
        phase_end()
        GT = lsb("GT2", [128, 512])
        GBC = lsb("GBC", [128, 512])
        GBM = lsb("GBM", [MS, 512])
        XQ = [lsb("XQ0", [128, 512]), lsb("XQ1", [128, 512])]
        ymT = BIGA[:, :].rearrange("p (k t) -> p k t", k=16)
        DMA('sp', 'yml', [(ymT[:, fc, :], ymix_d[fc]) for fc in range(16)], w=['ymT'])

        def gate_bc(off, cb):
            DMA('sp', 'mrl', [(MR[:, :], modr_d[:, off + cb * 512:off + (cb + 1) * 512])], r=['modr_d'], w=['MR'])
            mms([(psE[:, :], ones_f[0:1, :], MR[0:1, :], True, True),
                 (psF[0:MS, :], rowsel, MR[0:5, :], True, True)], ['MR', 'CON'], ['psE', 'psF'])
            tcopy(GBC[:, :], psE[:, :], ['psE'], ['GBC'])
            tcopy(GBM[:, :], psF[0:MS, :], ['psF'], ['GBM'], eng='act')

        for cb in range(4):
            ws = wload(wo_t[cb], 8192)
            wv = WB[ws][:, :].rearrange("p (k n) -> p k n", k=16)
            gate_bc(0, cb)
            for tl in range(17):
                M = 128 if tl < 16 else MS
                P_ = psA if tl % 2 == 0 else psB
                pk = 'psA' if tl % 2 == 0 else 'psB'
                mms([(P_[0:M, :], ymT[:, kc, tl * 128:tl * 128 + M], wv[:, kc, :], kc == 0, kc == 15) for kc in range(16)],
                    [('WB', ws), 'ymT'], [pk])
                xq = XQ[tl % 2]
                xk = 'XQ%d' % (tl % 2)
                srow = (16 + tl) * 128
                DMA('sp', 'xq%d' % (tl % 2), [(xq[0:M, :], xall[srow:srow + M, cb * 512:(cb + 1) * 512])], w=[xk])
                G_ = GBC if tl < 16 else GBM
                tt(GT[0:M, 0:512], P_[0:M, :], G_[0:M, :], ALU.mult, [pk, 'GBC', 'GBM'], ['GT0'])
                tt(xq[0:M, :], xq[0:M, :], GT[0:M, 0:512], ALU.add, [xk, 'GT0'], [xk])
                DMA('sp', 'x1o%d' % (tl % 2), [(x1_d[tl * 128:tl * 128 + M, cb * 512:(cb + 1) * 512], xq[0:M, :])], r=[xk], w=[('x1_d', tl)])
        phase_end()
        for tl in range(16):
            norm_tile(x1_d[tl * 128:(tl + 1) * 128, :], 128, 1, tl * 128, 128, False)
        norm_tile(x1_d[16 * 128:16 * 128 + MS, :], MS, 1, TOK, MS, True)
        phase_end()
        BIGB = lsb("YTB", [128, NJ * 512], BF16)
        GT = lsb("GT3", [128, 512])
        GBC = lsb("GBC3", [128, 512])
        GBM = lsb("GBM3", [MS, 512])
        XQ = [lsb("XQ03", [128, 512]), lsb("XQ13", [128, 512])]
        ACAT = lsb("ACAT", [128, 520])
        UU = lsb("UU", [128, 512])
        CROW = RTMP[0:8, :]
        YT = BIGB[:, 0:NJ * 512].rearrange("p (j t) -> p j t", j=NJ)
        CWv = CW[:, :].rearrange("p (a j) -> p a j", a=4)
        CARv = CARRY[:, :].rearrange("p (j c) -> p j c", c=2)
        SCVv = SCV[:, :].rearrange("p (j c) -> p j c", c=8)
        ASVv = ASV[:, :].rearrange("p (j c) -> p j c", c=8)
        for j0 in range(0, NJ, 4):
            DMA('sp', 'crw', [(CROW[:, :], stconv[:, j0 * 128:(j0 + 4) * 128])], w=['CROW'])
            mms([(psE[:, (j - j0) * 8:(j - j0) * 8 + 8], CROW[0:8, (j - j0) * 128:(j - j0 + 1) * 128], identf[0:8, 0:8], True, True) for j in range(j0, j0 + 4)],
                ['CROW', 'CON'], ['psE'])
            tcopy(SCV[:, j0 * 8:(j0 + 4) * 8], psE[:, 0:32], ['psE'], ['SCV'])
        blocks = [('m', TOK, MS)] + [('o', b * 512, 512) for b in range(4)]
        for (kind, c0, n) in blocks:
            for j in range(NJ):
                ws = wload(wab_t[j], 4096)
                wv = WB[ws][:, 0:4096].rearrange("p (k n) -> p k n", k=32)
                s_ = j & 1
                Pa = psA if s_ == 0 else psC
                Pb = psB if s_ == 0 else psD
                pak = 'psA' if s_ == 0 else 'psC'
                pbk = 'psB' if s_ == 0 else 'psD'
                lst = [(Pa[:, 0:n], wv[:, kc, :], hT[:, kc, c0:c0 + n], kc == 0, kc == 15) for kc in range(16)]
                lst += [(Pb[:, 0:n], wv[:, 16 + kc, :], hT[:, kc, c0:c0 + n], kc == 0, kc == 15) for kc in range(16)]
                mms(lst, [('WB', ws)] + ALLH, [pak, pbk])
                ac = ACAT[:, 0:520]
                u = UU[:, 0:512]
                ack = ('ac', 0)
                uk = ('u', 0)
                w0, w1, w2, cb_ = (CWv[:, a, j:j + 1] for a in range(4))
                if kind == 'o':
                    tcopy(ac[:, 0:2], CARv[:, j, :], ['CARRY'], [ack])
                    tcopy(ac[:, 2:2 + n], Pa[:, 0:n], [pak], [ack], eng='act')
                    tsc(u[:, 0:n], ac[:, 2:2 + n], w2, cb_, ALU.mult, ALU.add, [ack, 'CW'], [uk])
                    stt(u[:, 0:n], ac[:, 1:1 + n], w1, u[:, 0:n], ALU.mult, ALU.add, [ack, 'CW', uk], [uk])
                    stt(u[:, 0:n], ac[:, 0:n], w0, u[:, 0:n], ALU.mult, ALU.add, [ack, 'CW', uk], [uk])
                    tcopy(CARv[:, j, :], ac[:, n:n + 2], [ack], ['CARRY'])
                else:
                    acv = ac[:, 0:24].rearrange("p (b t) -> p b t", b=4)
                    tcopy(acv[:, :, 0:2], SCVv[:, j, :].rearrange("p (b t) -> p b t", b=4), ['SCV'], [ack])
                    tcopy(acv[:, :, 2:6], Pa[:, 0:16].rearrange("p (b t) -> p b t", b=4), [pak], [ack])
                    uv = u[:, 0:16].rearrange("p (b t) -> p b t", b=4)
                    tsc(uv, acv[:, :, 2:6], w2, cb_, ALU.mult, ALU.add, [ack, 'CW'], [uk])
                    stt(uv, acv[:, :, 1:5], w1, uv, ALU.mult, ALU.add, [ack, 'CW', uk], [uk])
                    stt(uv, acv[:, :, 0:4], w0, uv, ALU.mult, ALU.add, [ack, 'CW', uk], [uk])
                    A('pool', lambda e, u=u: e.memset(u[:, 16:32], 0.0), [uk], [uk])
                    tcopy(ASVv[:, j, :].rearrange("p (b t) -> p b t", b=4), acv[:, :, 4:6], [ack], ['ASV'])
                    tsc(CARv[:, j, :], Pa[:, 16:18], CVAL[:, 0:1], None, ALU.mult, ALU.bypass, [pak, 'CVAL'], ['CARRY'])
                act(u[:, 0:n], u[:, 0:n], AF.Silu, [uk], [uk])
                tt(YT[:, j, 0:n], u[:, 0:n], Pb[:, 0:n], ALU.mult, [uk, pbk], ['YT'])
            ntl = (n + 127) // 128
            for cb in range(4):
                gate_bc(D, cb)
                for jg in range(4):
                    ws = wload(wd_t[cb * 4 + jg], 11 * 512)
                    wv = WB[ws][:, 0:11 * 512].rearrange("p (k n) -> p k n", k=11)
                    Ps = [psA, psB, psC, psD]
                    for tl in range(ntl):
                        M = min(128, n - tl * 128)
                        mms([(Ps[tl][0:M, :], YT[:, jg * 11 + jj, tl * 128:tl * 128 + M], wv[:, jj, :], jg == 0 and jj == 0, jg == 3 and jj == 10)
                             for jj in range(11)], [('WB', ws), 'YT'], ['psA', 'psB', 'psC', 'psD'][tl:tl + 1])
                for tl in range(ntl):
                    M = min(128, n - tl * 128)
                    gt = (c0 // 128 + tl)
                    xq = XQ[tl % 2]
                    xk = 'XQ%d' % (tl % 2)
                    pk = ['psA', 'psB', 'psC', 'psD'][tl]
                    DMA('sp', 'xq%d' % (tl % 2), [(xq[0:M, :], x1_d[gt * 128:gt * 128 + M, cb * 512:(cb + 1) * 512])], w=[xk])
                    G_ = GBC if kind == 'o' else GBM
                    tt(GT[0:M, 0:512], [psA, psB, psC, psD][tl][0:M, :], G_[0:M, :], ALU.mult, [pk, 'GBC', 'GBM'], ['GT0'])
                    tt(xq[0:M, :], xq[0:M, :], GT[0:M, 0:512], ALU.add, [xk, 'GT0'], [xk])
                    DMA('sp', 'x1o%d' % (tl % 2), [(x2_d[gt * 128:gt * 128 + M, cb * 512:(cb + 1) * 512], xq[0:M, :])], r=[xk], w=[('x2_d', gt)])
        for (src, dstd, nr, ch_) in ((ASVv, convs, 8, 'cvo0'), (CARv, convp, 2, 'cvo1')):
            for j0 in range(0, NJ, 4):
                mms([(psE[0:nr, (j - j0) * 128:(j - j0 + 1) * 128], src[:, j, :], identf, True, True) for j in range(j0, j0 + 4)],
                    ['ASV', 'CARRY', 'CON'], ['psE'])
                tcopy(CROW[0:nr, 0:512], psE[0:nr, :], ['psE'], ['CROW'])
                DMA('sp', ch_, [(dstd[:, j0 * 128:(j0 + 4) * 128], CROW[0:nr, 0:512])], r=['CROW'], w=['CROWo'])
        phase_end()
        NFW = lsb("NFW", [128, D])
        DMA('sp', 'nfw', [(NFW[:, :], normf.partition_broadcast(128))], w=['NFW'])
        for tl in range(17):
            M = 128 if tl < 16 else MS
            s_ = tl % 2
            X = XT[s_]
            xk = 'XT%d' % s_
            DMA('sp', 'x%d' % s_, [(X[0:M, :], x2_d[tl * 128:tl * 128 + M, :])], w=[xk])
            act(XN[0:M, :], X[0:M, :], AF.Square, [xk], ['XN', 'STn'], accum_out=ST[0:M, 48:49])
            act(ST[0:M, 49:50], ST[0:M, 48:49], AF.Sqrt, ['STn'], ['STn2'], scale=1.0 / D, bias=EPS)
            A('dve', lambda e, M=M: e.reciprocal(ST[0:M, 50:51], ST[0:M, 49:50]), ['STn2'], ['STn3'])
            stt(X[0:M, :], X[0:M, :], ST[0:M, 50:51], NFW[0:M, :], ALU.mult, ALU.mult, [xk, 'STn3', 'NFW'], [xk])
            dst = y_own[tl * 128:(tl + 1) * 128, :] if tl < 16 else y_misc
            DMA('sp', 'yo%d' % s_, [(dst, X[0:M, :])], r=[xk])
        for g, (W, d) in enumerate(GROUPS):
            DMA('pool', 'kvc', [(kvs[g][b, 0:W - 4], ck[g][b, 4:W]) for b in range(4)])

        if stop is not None:
            S.ops = S.ops[:stop]
        chs = sorted({o['ch'] for o in S.ops if o['ch'] is not None})
        sems_ch = {ch: es.enter_context(nc.semaphore("c_" + ch)) for ch in chs}
        sems_eng = {e_: es.enter_context(nc.semaphore("e_" + e_)) for e_ in ('pe', 'act', 'dve', 'pool')}
        block = es.enter_context(nc.Block())
        S.emit(nc, block, sems_eng, sems_ch, chs)
        ph[0].close()
    return nc


def _tile_w(Wsub, kc):
    n = Wsub.shape[1]
    return np.ascontiguousarray(Wsub.reshape(kc, 128, n).transpose(1, 0, 2).reshape(128, kc * n))


def _consts():
    C = np.zeros((128, 2048), np.float32)
    C[:, 0:128] = np.eye(128)
    s = np.arange(128)[:, None]
    t = np.arange(128)[None, :]
    same = (s // 64) == (t // 64)
    C[:, 128:256] = (same & (s <= t))
    C[:, 256:384] = same
    C[:, 384] = (np.arange(128) < 64)
    C[:, 385] = (np.arange(128) >= 64)
    C[:, 512:576] = ((np.arange(128)[:, None] % 64) <= np.arange(64)[None, :])
    s = np.arange(32)[:, None]
    t = np.arange(32)[None, :]
    samem = ((s // 4) == (t // 4)) & (s < 16) & (t < 16)
    C[0:32, 640:672] = samem & (s <= t)
    C[0:32, 672:704] = samem
    for b in range(4):
        C[4 * b:4 * b + 4, 704 + b] = 1
        C[4 * b:4 * b + 4, 768 + b] = 1
        C[:, 896 + b * 32 + 4 * b:896 + b * 32 + 4 * b + 4] = 1
    C[0:32, 736:768] = samem & (s <= t)
    for m in range(32):
        C[(1 + m // 4) if m < 16 else 0, 1024 + m] = 1
    C[:, 1152:1280] = 1
    return C


def _rope(pos):
    half = 16
    inv = (np.float32(500000.0) ** (-np.arange(half, dtype=np.float32) * np.float32(2.0) / np.float32(32))).astype(np.float32)
    ang = pos.astype(np.float32)[:, None] * inv[None, :]
    return np.concatenate([np.cos(ang), np.sin(ang)], axis=1).astype(np.float32)


_NC = None


def kernel(x_prompt, x_sample, cache_kv_w128, cache_kv_w512, cache_kv_w2048, state_hgrn, state_conv,
           c_prompt, c_sample, w_ada, b_ada, norm1_w, w_in, hg_lb, hg_norm_w, w_pa, w_pb, w_o,
           norm2_w, w_ffn_a, w_ffn_b, conv_w, conv_b, w_ffn_down, norm_f_w):
    global _NC
    in_maps = _prep(x_prompt, x_sample, cache_kv_w128, cache_kv_w512, cache_kv_w2048, state_hgrn, state_conv,
                    c_prompt, c_sample, w_ada, b_ada, norm1_w, w_in, hg_lb, hg_norm_w, w_pa, w_pb, w_o,
                    norm2_w, w_ffn_a, w_ffn_b, conv_w, conv_b, w_ffn_down, norm_f_w)
    return _run(in_maps)


def _prep(x_prompt, x_sample, cache_kv_w128, cache_kv_w512, cache_kv_w2048, state_hgrn, state_conv,
          c_prompt, c_sample, w_ada, b_ada, norm1_w, w_in, hg_lb, hg_norm_w, w_pa, w_pb, w_o,
          norm2_w, w_ffn_a, w_ffn_b, conv_w, conv_b, w_ffn_down, norm_f_w):
    f = lambda a: np.asarray(a, dtype=np.float32)
    xp = f(x_prompt)[0]
    xs = f(x_sample)
    Win = f(w_in)[0]
    shared = {}
    shared["wada_t"] = np.stack([_tile_w(f(w_ada)[0][:, j * 512:(j + 1) * 512], 16) for j in range(24)])
    shared["bada"] = f(b_ada).reshape(1, -1)
    shared["vecA"] = np.concatenate([f(norm1_w)[0].reshape(16, 128), f(norm2_w)[0].reshape(16, 128)], 0)
    shared["normf"] = f(norm_f_w).reshape(1, -1)
    shared["watt_t"] = np.stack([_tile_w(np.concatenate([Win[:, hh * 128:(hh + 1) * 128], Win[:, 1536 + hh * 128:1536 + (hh + 1) * 128],
                                                          Win[:, 3072 + hh * 128:3072 + (hh + 1) * 128]], 1), 16) for hh in range(12)])
    shared["whg_t"] = np.stack([_tile_w(np.concatenate([Win[:, 4608 + k * 1024 + h * 128:4608 + k * 1024 + (h + 1) * 128] for k in range(4)], 1), 16)
                                for h in range(8)])
    Wpa, Wpb = f(w_pa)[0], f(w_pb)[0]
    shared["wB1_t"] = np.stack([np.concatenate([_tile_w(Win[:, 8704 + fc * 128:8704 + (fc + 1) * 128], 16),
                                                _tile_w(Win[:, 10752 + fc * 128:10752 + (fc + 1) * 128], 16),
                                                _tile_w(Wpa[:, fc * 128:(fc + 1) * 128], 4),
                                                _tile_w(Wpb[:, fc * 128:(fc + 1) * 128], 8)], 1) for fc in range(16)])
    Wo = f(w_o)[0]
    shared["wo_t"] = np.stack([_tile_w(Wo[:, cb * 512:(cb + 1) * 512], 16) for cb in range(4)])
    Wa, Wb, Wd = f(w_ffn_a)[0], f(w_ffn_b)[0], f(w_ffn_down)[0]
    shared["wab_t"] = np.stack([np.concatenate([_tile_w(Wa[:, j * 128:(j + 1) * 128], 16), _tile_w(Wb[:, j * 128:(j + 1) * 128], 16)], 1)
                                for j in range(NJ)])
    shared["wd_t"] = np.stack([_tile_w(Wd[jg * 11 * 128:(jg + 1) * 11 * 128, cb * 512:(cb + 1) * 512], 11)
                               for cb in range(4) for jg in range(4)])
    cw = f(conv_w)[0]
    shared["convw"] = np.concatenate([cw[0].reshape(NJ, 128), cw[1].reshape(NJ, 128), cw[2].reshape(NJ, 128),
                                      f(conv_b)[0].reshape(NJ, 128)], 0)
    shared["hglb"] = f(hg_lb)
    shared["hgnw"] = f(hg_norm_w).reshape(1, 128)
    shared["consts"] = _consts()
    i = np.arange(128)[:, None]
    ip = np.arange(256)[None, :]
    band = np.where((ip >= i) & (ip <= i + 128), 0.0, NEG).astype(np.float32)

    in_maps = []
    for c in range(NCORE):
        m = dict(shared)
        xall = np.zeros((33 * 128, D), np.float32)
        own0 = TOK * c

        def tokrow(p):
            return xp[p] if p >= 0 else np.zeros(D, np.float32)
        if c > 0:
            xall[0:2048] = xp[own0 - 2048:own0]
        xall[2048:4096] = xp[own0:own0 + TOK]
        mb = 32 * 128
        xall[mb:mb + 16] = xs[4 * c:4 * c + 4].reshape(16, D)
        pos_m = np.zeros(MS, np.int64)
        for b in range(4):
            for t in range(4):
                pos_m[4 * b + t] = 16384 + t
        for t in range(2):
            xall[mb + 16 + t] = tokrow(own0 - 2 + t)
            pos_m[16 + t] = own0 - 2 + t
        for g, (W, d) in enumerate(GROUPS):
            for t in range(2):
                p = own0 - 2 + t - 128 * d
                xall[mb + 18 + 2 * g + t] = tokrow(p)
                pos_m[18 + 2 * g + t] = p
        m["xall"] = xall
        m["c5"] = np.concatenate([f(c_prompt), f(c_sample)[4 * c:4 * c + 4]], 0)
        rt = np.zeros((3, 32, 128, 32), np.float32)
        for g, (W, d) in enumerate(GROUPS):
            nT = 16 // d
            for r in range(d):
                for T_ in range(-1, nT):
                    pos = own0 + r + d * (128 * T_ + np.arange(128))
                    rt[g, r * (nT + 1) + T_ + 1] = _rope(pos)
        m["rope_t"] = rt
        m["rope_m"] = _rope(pos_m)
        mh = band.copy()
        if c == 0:
            mh[:, 0:128] = NEG
        m["masks"] = np.concatenate([band, mh], 1)
        m["ck128"] = f(cache_kv_w128)[0, 4 * c:4 * c + 4]
        m["ck512"] = f(cache_kv_w512)[0, 4 * c:4 * c + 4]
        m["ck2048"] = f(cache_kv_w2048)[0, 4 * c:4 * c + 4]
        m["sthg"] = f(state_hgrn)[0, 4 * c:4 * c + 4]
        m["stconv"] = f(state_conv)[0, 4 * c:4 * c + 4].reshape(8, DFF)
        m["cvalid"] = np.full((128, 1), 0.0 if c == 0 else 1.0, np.float32)
        in_maps.append(m)
    return in_maps


def _run(in_maps):
    global _NC
    if _NC is None:
        _NC = build()
    res = run_bass_kernel_spmd(_NC, in_maps, core_ids=list(range(NCORE)))
    R = res.results
    y_prompt = np.concatenate([R[c]["y_own"] for c in range(NCORE)], 0)[None]
    y_sample = np.concatenate([R[c]["y_misc"][0:16].reshape(4, 4, D) for c in range(NCORE)], 0)
    L = R[NCORE - 1]
    outs = [y_prompt, y_sample,
            L["kvp128"][None, None], L["kvp512"][None, None], L["kvp2048"][None, None],
            L["hgp"][None, None], L["convp"].reshape(1, 1, 2, DFF)]
    for nm in ("kvs128", "kvs512", "kvs2048"):
        outs.append(np.concatenate([R[c][nm] for c in range(NCORE)], 0)[None])
    outs.append(np.concatenate([R[c]["hgs"] for c in range(NCORE)], 0)[None])
    outs.append(np.concatenate([R[c]["convs"].reshape(4, 2, DFF) for c in range(NCORE)], 0)[None])
    return tuple(np.ascontiguousarray(o, dtype=np.float32) for o in outs)
```

```python
import numpy as np
import concourse.bass as bass
import concourse.mybir as mybir
from concourse.bass_utils import run_bass_kernel_spmd

F32 = mybir.dt.float32
BF16 = mybir.dt.bfloat16
AF = mybir.ActivationFunctionType
ALU = mybir.AluOpType
AX = mybir.AxisListType

D = 2048
NCORE = 8
TOK = 2048
MS = 32
NTOK = TOK + MS
DFF = 5632
NJ = 44
EPS = 1e-6
GROUPS = ((128, 1), (512, 4), (2048, 16))
NEG = -30000.0
import os as _os
UDEPTH = int(_os.environ.get("UDEPTH", "3"))


class Sched:
    def __init__(self):
        self.ops = []
        self.lw = {}
        self.rd = {}
        self.fence_deps = set()
        self.last_eng = {}
        self.last_ch = {}

    @staticmethod
    def _nk(k):
        n = k[0] if isinstance(k, tuple) else k
        if isinstance(n, str) and len(n) >= 3 and n.startswith('ps') and n[2] in 'ABCDEFTU':
            return n[:3]
        return k

    def add(self, eng, fn, r=(), w=(), ch=None):
        r = [self._nk(k) for k in r]
        w = [self._nk(k) for k in w]
        psr = [k for k in r if isinstance(k, str) and len(k) == 3 and k.startswith('ps')]
        if psr:
            r = [k for k in r if k not in psr]
            w = list(w) + psr
        deps = set(self.fence_deps)
        for k in r:
            if k in self.lw:
                deps.add(self.lw[k])
        for k in w:
            if k in self.lw:
                deps.add(self.lw[k])
            deps.update(self.rd.get(k, ()))
        i = len(self.ops)
        import sys as _s
        fr = _s._getframe(1)
        lines = []
        while fr is not None and len(lines) < 4:
            lines.append(fr.f_lineno)
            fr = fr.f_back
        self.ops.append(dict(eng=eng, fn=fn, deps=deps, ch=ch, ln=lines))
        for k in r:
            self.rd.setdefault(k, []).append(i)
        for k in w:
            self.lw[k] = i
            self.rd[k] = []
        if ch is None:
            self.last_eng[eng] = i
        else:
            self.last_ch[ch] = i
        return i

    def fence(self):
        self.fence_deps = set(self.last_eng.values()) | set(self.last_ch.values())

    def emit(self, nc, block, sems_eng, sems_ch, final_chs):
        ops = self.ops
        needed = [False] * len(ops)
        for o in ops:
            for d in o['deps']:
                dd = ops[d]
                if dd['ch'] is None and dd['eng'] == 'pe' and o['eng'] == 'pe' and o['ch'] is None:
                    continue
                needed[d] = True
        cnt = {}
        for i, o in enumerate(ops):
            if o['ch'] is not None:
                continue
            if needed[i]:
                cnt[o['eng']] = cnt.get(o['eng'], 0) + 1
                o['ms'] = cnt[o['eng']]
        chcnt = {}
        for i, o in enumerate(ops):
            if o['ch'] is not None:
                n = o['fn'].ndma
                chcnt[o['ch']] = chcnt.get(o['ch'], 0) + 16 * n
                o['val'] = chcnt[o['ch']]
        finals = {ch: chcnt[ch] for ch in final_chs if ch in chcnt}

        def run(engname, e):
            seen = {}
            for i, o in enumerate(ops):
                if o['eng'] != engname:
                    continue
                for d in sorted(o['deps']):
                    dd = ops[d]
                    if dd['ch'] is not None:
                        sem, val = sems_ch[dd['ch']], dd['val']
                    else:
                        if dd['eng'] == 'pe' and engname == 'pe' and o['ch'] is None:
                            continue
                        sem, val = sems_eng[dd['eng']], dd['ms']
                    key = id(sem)
                    if seen.get(key, 0) >= val:
                        continue
                    e.wait_ge(sem, val)
                    seen[key] = val
                if o['ch'] is not None:
                    for ins in o['fn'](e):
                        ins.then_inc(sems_ch[o['ch']], 16)
                else:
                    ins = o['fn'](e)
                    if needed[i]:
                        ins.then_inc(sems_eng[engname], 1)
            if engname == 'sp':
                for ch, v in finals.items():
                    e.wait_ge(sems_ch[ch], v)

        block.tensor(lambda e: run('pe', e))
        block.scalar(lambda e: run('act', e))
        block.vector(lambda e: run('dve', e))
        block.gpsimd(lambda e: run('pool', e))
        block.sync(lambda e: run('sp', e))


def dmafn(pairs):
    def f(e):
        return [e.dma_start(out=o, in_=i, allow_slow_non_contiguous=True) for (o, i) in pairs]
    f.ndma = len(pairs)
    return f


def build(stop=None, marks=None):
    nc = bass.Bass("TRN2", target_bir_lowering=False)
    S = Sched()

    def din(name, shape, dt=F32):
        return nc.dram_tensor(name, list(shape), dt, kind="ExternalInput").ap()

    def dout(name, shape):
        return nc.dram_tensor(name, list(shape), F32, kind="ExternalOutput").ap()

    def dscr(name, shape, dt):
        return nc.dram_tensor(name, list(shape), dt).ap()

    xall = din("xall", [33 * 128, D])
    c5 = din("c5", [5, D])
    wada_t = din("wada_t", [24, 128, 16 * 512])
    bada = din("bada", [1, 6 * D])
    vecA = din("vecA", [32, 128])
    normf = din("normf", [1, D])
    watt_t = din("watt_t", [12, 128, 16 * 384])
    whg_t = din("whg_t", [8, 128, 16 * 512])
    wB1_t = din("wB1_t", [16, 128, 44 * 128])
    wo_t = din("wo_t", [4, 128, 16 * 512])
    wab_t = din("wab_t", [NJ, 128, 32 * 128])
    wd_t = din("wd_t", [16, 128, 11 * 512])
    convw = din("convw", [4 * NJ, 128])
    hglb = din("hglb", [2, 1024])
    hgnw = din("hgnw", [1, 128])
    consts = din("consts", [128, 2048])
    rope_t = din("rope_t", [3, 32, 128, 32])
    rope_m = din("rope_m", [MS, 32])
    masks = din("masks", [128, 512])
    ck = [din("ck128", [4, 128, 2, 4, 128]), din("ck512", [4, 512, 2, 4, 128]), din("ck2048", [4, 2048, 2, 4, 128])]
    sthg = din("sthg", [4, 8, 128, 128])
    stconv = din("stconv", [8, DFF])
    cvalid = din("cvalid", [128, 1])

    y_own = dout("y_own", [TOK, D])
    y_misc = dout("y_misc", [MS, D])
    kvp = [dout("kvp128", [128, 2, 4, 128]), dout("kvp512", [512, 2, 4, 128]), dout("kvp2048", [2048, 2, 4, 128])]
    hgp = dout("hgp", [8, 128, 128])
    convp = dout("convp", [2, DFF])
    kvs = [dout("kvs128", [4, 128, 2, 4, 128]), dout("kvs512", [4, 512, 2, 4, 128]), dout("kvs2048", [4, 2048, 2, 4, 128])]
    hgs = dout("hgs", [4, 8, 128, 128])
    convs = dout("convs", [8, DFF])

    oatt_d = dscr("oatt_d", [4, 128, NTOK], BF16)
    ohg_d = dscr("ohg_d", [8, 128, NTOK], BF16)
    ymix_d = dscr("ymix_d", [16, 128, NTOK], BF16)
    x1_d = dscr("x1_d", [17 * 128, D], F32)
    x2_d = dscr("x2_d", [17 * 128, D], F32)
    modr_d = dscr("modr_d", [5, 2 * D], F32)
    hkv_d = dscr("hkv_d", [168, 128, 128], BF16)
    lm_d = dscr("lm_d", [3, 2, 2176], F32)
    coef_d = dscr("coef_d", [3, 2176], F32)
    dbg_d = dscr("dbg_d", [10, 128, 128], BF16)
    dbg2_d = dscr("dbg2_d", [128, 2], F32)

    import contextlib
    es = contextlib.ExitStack()

    def sb(name, shape, dt=F32):
        return es.enter_context(nc.sbuf_tensor(name, list(shape), dt))

    def ps(name, shape, dt=F32):
        return es.enter_context(nc.psum_tensor(name, list(shape), dt))

    ph = [contextlib.ExitStack()]

    uniq = [0]

    def lsb(name, shape, dt=F32):
        uniq[0] += 1
        return ph[0].enter_context(nc.sbuf_tensor("%s_%d" % (name, uniq[0]), list(shape), dt))

    def phase_end():
        if marks is not None:
            marks.append(len(S.ops))
        S.fence()
        ph[0].close()
        ph[0] = contextlib.ExitStack()

    with es:
        BIGA = sb("BIGA", [128, 16 * NTOK], BF16)
        WB = [sb("WB0", [128, 8192], BF16), sb("WB1", [128, 8192], BF16)]
        XT = [sb("XT0", [128, D]), sb("XT1", [128, D])]
        XN = sb("XN", [128, D], BF16)
        CON = sb("CON", [128, 1280])
        identf = CON[:, 0:128]
        tri2 = CON[:, 128:256]
        blk2 = CON[:, 256:384]
        ind2 = CON[:, 384:386]
        cmask = CON[:, 512:576]
        tri_m = CON[0:32, 640:672]
        blk_m = CON[0:32, 672:704]
        ind_m = CON[0:32, 704:708]
        cmask_m = CON[0:32, 736:768]
        rowm = CON[0:32, 768:772]
        colm = CON[:, 896:1024]
        rowsel = CON[0:5, 1024:1056]
        ones_f = CON[:, 1152:1280]
        IDB = sb("IDB", [128, 128], BF16)
        ONEB = sb("ONEB", [128, 128], BF16)
        MSK = sb("MSK", [128, 512])
        ROPEM = sb("ROPEM", [MS, 32])
        MODT = sb("MODT", [128, 96 * 5])
        MR = sb("MR", [5, 512])
        SC = sb("SC", [128, 4 * 16])
        SCM = sb("SCM", [128, 4 * 16 * MS])
        NW = sb("NW", [128, 32])
        CW = sb("CW", [128, 4 * NJ])
        ST = sb("ST", [128, 64])
        STT = sb("STT", [128, 16 * 5], BF16)
        HGW = sb("HGW", [128, 128])
        CVAL = sb("CVAL", [128, 1])
        RTMP = sb("RTMP", [128, 512])
        CARRY = sb("CARRY", [128, NJ * 2])
        SCV = sb("SCV", [128, NJ * 8])
        ASV = sb("ASV", [128, NJ * 8])
        SST = sb("SST", [128, 8 * 128])
        SBF = sb("SBF", [128, 128], BF16)
        OHGPRE = sb("OHGPRE", [128, 16], BF16)
        Lc = {}

        psA = ps("psA", [128, 512])
        psB = ps("psB", [128, 512])
        psC = ps("psC", [128, 512])
        psD = ps("psD", [128, 512])
        psE = ps("psE", [128, 512])
        psF = ps("psF", [128, 512])
        psT = ps("psT", [128, 1024], BF16)
        psU = ps("psU", [128, 1024], BF16)

        hT = BIGA[:, :].rearrange("p (k t) -> p k t", k=16)
        psEb = psE[:, :].bitcast(BF16)

        def A(eng, fn, r=(), w=()):
            return S.add(eng, fn, r, w)

        def DMA(q, ch, pairs, r=(), w=()):
            return S.add(q, dmafn(pairs), r, w, ch=ch)

        def act(out, in_, func, r, w, **kw):
            A('act', lambda e: e.activation(out=out, in_=in_, func=func, **kw), r, w)

        def tt(out, a, b, op, r, w, eng='dve'):
            A(eng, lambda e: e.tensor_tensor(out, a, b, op=op), r, w)

        def tcopy(out, in_, r, w, eng='dve'):
            if eng == 'act':
                A('act', lambda e: e.copy(out=out, in_=in_), r, w)
            else:
                A(eng, lambda e: e.tensor_copy(out, in_), r, w)

        def tsc(out, a, s1, s2, op0, op1, r, w):
            A('dve', lambda e: e.tensor_scalar(out, a, s1, s2, op0=op0, op1=op1), r, w)

        def stt(out, a, s, b, op0, op1, r, w):
            A('dve', lambda e: e.scalar_tensor_tensor(out, a, s, b, op0=op0, op1=op1), r, w)

        def mms(lst, r, w):
            def f(e):
                ins = None
                for (o, l, rr, st, sp) in lst:
                    ins = e.matmul(o, l, rr, start=st, stop=sp)
                return ins
            A('pe', f, r, w)

        def trs(lst, r, w):
            def f(e):
                ins = None
                for (o, i, idn) in lst:
                    ins = e.transpose(o, i, idn)
                return ins
            A('pe', f, r, w)

        wslot = [0]

        def wload(src, ncols):
            s_ = wslot[0]
            wslot[0] ^= 1
            DMA('pool', 'w%d' % s_, [(WB[s_][:, 0:ncols], src)], w=[('WB', s_)])
            return s_

        DMA('sp', 'c0', [(CON[:, :], consts[:, 0:1280]), (MSK[:, :], masks), (ROPEM[:, :], rope_m),
                         (CVAL[:, :], cvalid), (HGW[:, :], hgnw.partition_broadcast(128))],
            w=['CON', 'MSK', 'ROPEM', 'CVAL', 'HGW'])
        S5 = lsb("S5", [5, D], BF16)
        BADA = lsb("BADA", [1, 6 * D], BF16)
        DMA('pool', 'c3', [(IDB[:, :], consts[:, 0:128]), (ONEB[:, :], consts[:, 1152:1280]), (BADA[:, :], bada)],
            w=['IDB', 'ONEB', 'BADA'])
        DMA('sp', 'c4', [(XT[0][0:5, :], c5)], w=['XT0'])
        act(S5[:, :], XT[0][0:5, :], AF.Silu, ['XT0'], ['S5'])
        trs([(psT[:, kc * 8:kc * 8 + 5], S5[0:5, kc * 128:(kc + 1) * 128], IDB[0:5, 0:5]) for kc in range(16)],
            ['S5', 'IDB'], ['psT'])
        tcopy(STT[:, :].rearrange("p (k e) -> p k e", e=5), psT[:, 0:128].rearrange("p (k e) -> p k e", e=8)[:, :, 0:5], ['psT'], ['STT'])
        DMA('sp', 'c5', [(XT[1][0:32, 0:128], vecA)], w=['XT1'])
        mms([(psB[:, 0:32], XT[1][0:32, 0:128], identf[0:32, 0:32], True, True)], ['XT1', 'CON'], ['psB'])
        tcopy(NW[:, :], psB[:, 0:32], ['psB'], ['NW'])
        DMA('sp', 'c6', [(XT[1][0:88, 128:256], convw[0:88, :]), (XT[1][0:88, 256:384], convw[88:176, :])], w=['XT1b'])
        mms([(psB[:, 64:152], XT[1][0:88, 128:256], identf[0:88, 0:88], True, True),
             (psB[:, 152:240], XT[1][0:88, 256:384], identf[0:88, 0:88], True, True)], ['XT1b', 'CON'], ['psBb'])
        tcopy(CW[:, :], psB[:, 64:240], ['psBb'], ['CW'])

        for j in range(24):
            s_ = wload(wada_t[j], 8192)
            wv = WB[s_][:, :].rearrange("p (k n) -> p k n", k=16)
            lst = []
            for cb in range(4):
                col = j * 4 + cb
                o = psA[:, col * 5:(col + 1) * 5]
                for kc in range(16):
                    lst.append((o, wv[:, kc, cb * 128:(cb + 1) * 128], STT[:, kc * 5:(kc + 1) * 5], kc == 0, False))
                lst.append((o, BADA[0:1, col * 128:(col + 1) * 128], ONEB[0:1, 0:5], False, True))
            mms(lst, [('WB', s_), 'STT', 'BADA', 'ONEB'], ['psA'])
            kind = j // 4
            if kind in (2, 5):
                lst = [(psC[0:5, :], STT[:, kc * 5:(kc + 1) * 5], wv[:, kc, :], kc == 0, False) for kc in range(16)]
                lst.append((psC[0:5, :], ONEB[0:1, 0:5], BADA[0:1, j * 512:(j + 1) * 512], False, True))
                mms(lst, [('WB', s_), 'STT', 'BADA', 'ONEB'], ['psC'])
                off = (0 if kind == 2 else D) + (j % 4) * 512
                tcopy(MR[:, :], psC[0:5, :], ['psC'], ['MR'])
                DMA('sp', 'mro', [(modr_d[:, off:off + 512], MR[:, :])], r=['MR'], w=['modr_d'])
        tcopy(MODT[:, :], psA[:, 0:480], ['psA'], ['MODT'])
        modT = MODT[:, :].rearrange("p (c r) -> p c r", r=5)
        SCv = SC[:, :].rearrange("p (a k) -> p a k", a=4)
        SCMv = SCM[:, :].rearrange("p (a k m) -> p a k m", a=4, k=16)
        for n_, (ksh, ksc, nwo) in enumerate(((0, 1, 0), (3, 4, 16))):
            tsc(ST[:, 0:16], modT[:, ksc * 16:(ksc + 1) * 16, 0], 1.0, None, ALU.add, ALU.bypass, ['MODT'], ['STa'])
            tt(SCv[:, 2 * n_, :], ST[:, 0:16], NW[:, nwo:nwo + 16], ALU.mult, ['STa', 'NW'], ['SC'])
            tcopy(SCv[:, 2 * n_ + 1, :], modT[:, ksh * 16:(ksh + 1) * 16, 0], ['MODT'], ['SC'])
            for b in range(4):
                tsc(ST[:, 16:32], modT[:, ksc * 16:(ksc + 1) * 16, 1 + b], 1.0, None, ALU.add, ALU.bypass, ['MODT'], ['STb'])
                tt(ST[:, 32:48], ST[:, 16:32], NW[:, nwo:nwo + 16], ALU.mult, ['STb', 'NW'], ['STc'])
                tcopy(SCMv[:, 2 * n_, :, 4 * b:4 * b + 4], ST[:, 32:48, None].broadcast_to([128, 16, 4]), ['STc'], ['SCM'])
                tcopy(SCMv[:, 2 * n_ + 1, :, 4 * b:4 * b + 4],
                      modT[:, ksh * 16:(ksh + 1) * 16, 1 + b:2 + b].broadcast_to([128, 16, 4]), ['MODT'], ['SCM'])
            tcopy(SCMv[:, 2 * n_, :, 16:32], SCv[:, 2 * n_, :, None].broadcast_to([128, 16, 16]), ['SC'], ['SCM'])
            tcopy(SCMv[:, 2 * n_ + 1, :, 16:32], SCv[:, 2 * n_ + 1, :, None].broadcast_to([128, 16, 16]), ['SC'], ['SCM'])

        phase_end()
        xslot = [0]

        def norm_tile(src_rows, nrows, n_, dst_cols, ncols, misc):
            s_ = xslot[0]
            xslot[0] ^= 1
            xk = 'XT%d' % s_
            X = XT[s_]
            DMA('sp', 'x%d' % s_, [(X[0:nrows, :], src_rows)], w=[xk])
            act(XN[0:nrows, :], X[0:nrows, :], AF.Square, [xk], ['XN', 'STn'], accum_out=ST[0:nrows, 48:49])
            act(ST[0:nrows, 49:50], ST[0:nrows, 48:49], AF.Sqrt, ['STn'], ['STn2'], scale=1.0 / D, bias=EPS)
            A('dve', lambda e: e.reciprocal(ST[0:nrows, 50:51], ST[0:nrows, 49:50]), ['STn2'], ['STn3'])
            act(XN[0:nrows, :], X[0:nrows, :], AF.Copy, [xk, 'STn3', 'XN'], ['XN'], scale=ST[0:nrows, 50:51])
            for half in range(2):
                P_ = psT if half == 0 else psU
                pk = 'psT' if half == 0 else 'psU'
                trs([(P_[:, q * ncols:(q + 1) * ncols], XN[0:nrows, (half * 8 + q) * 128:(half * 8 + q + 1) * 128],
                      IDB[0:nrows, 0:nrows]) for q in range(8)], ['XN', 'IDB'], [pk])
                pv = P_[:, 0:8 * ncols].rearrange("p (k t) -> p k t", k=8)
                dst = hT[:, half * 8:half * 8 + 8, dst_cols:dst_cols + ncols]
                if not misc:
                    sc_ = SCv[:, 2 * n_, half * 8:half * 8 + 8, None].broadcast_to([128, 8, ncols])
                    sh_ = SCv[:, 2 * n_ + 1, half * 8:half * 8 + 8, None].broadcast_to([128, 8, ncols])
                else:
                    sc_ = SCMv[:, 2 * n_, half * 8:half * 8 + 8, :]
                    sh_ = SCMv[:, 2 * n_ + 1, half * 8:half * 8 + 8, :]
                tmp = RTMP[:, :].bitcast(BF16)[:, 0:8 * ncols].rearrange("p (k t) -> p k t", k=8)
                tt(tmp, pv, sc_, ALU.mult, [pk, 'SC', 'SCM'], ['RTMPh'])
                tt(dst, tmp, sh_, ALU.add, ['RTMPh', 'SC', 'SCM'], [('hT', dst_cols // 128)])

        def hkeys(c0, c1):
            return [('hT', t) for t in range(c0 // 128, (c1 - 1) // 128 + 1)]

        ALLH = [('hT', t) for t in range(17)]

        ST2 = None
        QT = KT = VA = OTG = ATMP = APB = ASTAGE = MQK = MV = KVF = QKB = ROPE = HKS = LMS = CMB = None
        LB = ST3 = SR = SRB = None
        HT = HB = OHGH = S0 = S0B = YCH = GT = GBC = GBM = XQ = ACAT = UU = CROW = BIGB = NFW = None
        ropev = QTv = KTv = VAv = STG = None

        def alloc_attn():
            nonlocal ST2, QT, KT, VA, OTG, ATMP, APB, ASTAGE, MQK, MV, KVF, QKB, ROPE, HKS, LMS, CMB, YCH
            nonlocal ropev, QTv, KTv, VAv, STG
            QT = lsb("QT", [128, 16 * 128], BF16)
            KT = lsb("KT", [128, 17 * 128], BF16)
            VA = lsb("VA", [128, 17 * 128], BF16)
            OTG = [lsb("OTG%d" % g, [128, NTOK], BF16) for g in range(3)]
            ST2 = lsb("ST2", [128, 64])
            APB = lsb("APB", [128, 3 * 512], BF16)
            ASTAGE = lsb("ASTAGE", [128, 3 * 6 * 128], BF16)
            MQK = lsb("MQK", [128, 2 * MS], BF16)
            MV = lsb("MV", [MS, 128], BF16)
            KVF = lsb("KVF", [128, 2 * 256])
            QKB = lsb("QKB", [128, 2 * 256], BF16)
            ROPE = lsb("ROPE", [128, 32 * 32])
            HKS = lsb("HKS", [128, 3 * 256], BF16)
            LMS = lsb("LMS", [1, 768])
            CMB = lsb("CMB", [128, 192 + 6 * 192])
            YCH = lsb("YCHa", [128, NTOK], BF16)
            ropev = ROPE[:, :].rearrange("p (t c) -> p t c", t=32)
            QTv = QT[:, :].rearrange("p (t c) -> p t c", c=128)
            KTv = KT[:, :].rearrange("p (t c) -> p t c", c=128)
            VAv = VA[:, :].rearrange("p (t c) -> p t c", c=128)
            STG = ASTAGE[:, :].rearrange("p (t c) -> p t c", c=128)
            A('pool', lambda e, t=ASTAGE: e.memset(t[:, :], 0.0), [], [('STG', 0), ('STG', 1), ('STG', 2), ('STG5', 0), ('STG5', 1), ('STG5', 2)])
            A('pool', lambda e, t=HKS: e.memset(t[:, :], 0.0), [], [('HKS', 0), ('HKS', 1), ('HKS', 2)])
            for g in range(3):
                A('pool', lambda e, t=OTG[g]: e.memset(t[:, :], 0.0), [], [('OTG', g)])

        def alloc_hgrn():
            nonlocal HT, HB, OHGH, S0, S0B, LB, ST3, SR, SRB
            LB = lsb("LB", [128, 2 * 1024])
            DMA('sp', 'c2', [(LB[:, 0:1024], hglb[0:1, :].partition_broadcast(128)),
                             (LB[:, 1024:2048], hglb[1:2, :].partition_broadcast(128))], w=['LB'])
            tt(LB[:, 0:1024], LB[:, 0:1024], LB[:, 1024:2048], ALU.subtract, ['LB'], ['LB'])
            act(LB[:, 0:1024], LB[:, 0:1024], AF.Sigmoid, ['LB'], ['LB'])
            tsc(LB[:, 1024:2048], LB[:, 0:1024], -1.0, 1.0, ALU.mult, ALU.add, ['LB'], ['LB'])
            HT = lsb("HT", [128, 3 * 1536])
            HB = lsb("HB", [128, 3 * 1024 + 1024], BF16)
            ST3 = lsb("ST3", [128, 16])
            SR = lsb("SR", [128, 4 * 128])
            SRB = lsb("SRB", [128, 4 * 128], BF16)
            OHGH = lsb("OHGH", [128, NTOK], BF16)
            S0 = lsb("S0", [128, 4 * 128])
            S0B = lsb("S0B", [128, 4 * 128], BF16)
            A('pool', lambda e, t=OHGH: e.memset(t[:, :], 0.0), [], ['OHGH'])

        def hkv_idx(hh):
            g = hh // 4
            base = 0
            for h2 in range(hh):
                base += 2 * GROUPS[h2 // 4][1]
            return base

        def pipeline(gens, depth):
            active = []
            it = iter(gens)
            done = False
            while True:
                if not done and len(active) < depth:
                    try:
                        active.append(next(it))
                    except StopIteration:
                        done = True
                if not active and done:
                    break
                for g_ in list(active):
                    try:
                        next(g_)
                    except StopIteration:
                        active.remove(g_)

        uslot = [0]

        def proj_gen(ws, cols_ap_fn, M, g, tidx, misc, want_q, kdst, vdst, qdst, kvout, dkeys):
            ks = uslot[0] & 1
            uslot[0] += 1
            Z = psA if ks == 0 else psF
            zk = 'psA' if ks == 0 else 'psF'
            TP = psT if ks == 0 else psU
            tpk = 'psT' if ks == 0 else 'psU'
            wv = WB[ws][:, 0:16 * 384].rearrange("p (k n) -> p k n", k=16)
            mms([(Z[0:M, 0:384], cols_ap_fn(kc), wv[:, kc, :], kc == 0, kc == 15) for kc in range(16)],
                [('WB', ws)] + ALLH, [zk])
            yield
            cosv = (ROPEM[0:M, 0:16] if misc else ropev[0:M, tidx, 0:16])
            sinv = (ROPEM[0:M, 16:32] if misc else ropev[0:M, tidx, 16:32])
            zqk = Z[0:M, 0:256].rearrange("p (a c) -> p a c", a=2)
            x1 = zqk[:, :, 0:16]
            x2 = zqk[:, :, 16:32]
            cb_ = cosv[:, None, :].broadcast_to([M, 2, 16])
            sb_ = sinv[:, None, :].broadcast_to([M, 2, 16])
            T = RTMP[0:M, ks * 256:ks * 256 + 128].rearrange("p (q a c) -> p q a c", q=4, a=2)
            QF = RTMP[0:M, ks * 256 + 128:ks * 256 + 256]
            rt = lambda i: ('RT', ks, i)
            tt(T[:, 0], x1, cb_, ALU.mult, [zk, 'ROPE', 'ROPEM'], [rt(0)])
            tt(T[:, 1], x2, sb_, ALU.mult, [zk, 'ROPE', 'ROPEM'], [rt(1)])
            tt(T[:, 2], x2, cb_, ALU.mult, [zk, 'ROPE', 'ROPEM'], [rt(2)])
            tt(T[:, 3], x1, sb_, ALU.mult, [zk, 'ROPE', 'ROPEM'], [rt(3)])
            KF = KVF[0:M, ks * 256:ks * 256 + 256]
            kfk = ('KVF', ks)
            tcopy(KF[:, 128:256], Z[0:M, 256:384], [zk], [kfk], eng='act')
            tt(KF[:, 0:16], T[:, 0, 1], T[:, 1, 1], ALU.subtract, [rt(0), rt(1)], [kfk])
            tt(KF[:, 16:32], T[:, 2, 1], T[:, 3, 1], ALU.add, [rt(2), rt(3)], [kfk])
            tcopy(KF[:, 32:128], Z[0:M, 160:256], [zk], [kfk])
            qb = QKB[0:M, ks * 256:ks * 256 + 128]
            kb = QKB[0:M, ks * 256 + 128:ks * 256 + 256]
            qk_ = ('QKB', ks)
            tcopy(vdst, KF[:, 128:256], [kfk], [dkeys[1]], eng='act')
            tcopy(kb, KF[:, 0:128], [kfk], [qk_])
            if want_q:
                tt(QF[:, 0:16], T[:, 0, 0], T[:, 1, 0], ALU.subtract, [rt(0), rt(1)], [('QF', ks)])
                tt(QF[:, 16:32], T[:, 2, 0], T[:, 3, 0], ALU.add, [rt(2), rt(3)], [('QF', ks)])
                tcopy(QF[:, 32:128], Z[0:M, 32:128], [zk], [('QF', ks)])
                tcopy(qb, QF[:, 0:128], [('QF', ks)], [qk_])
            yield
            lst = [(TP[:, 128:128 + M], kb, IDB[0:M, 0:M])]
            if want_q:
                lst.append((TP[:, 0:M], qb, IDB[0:M, 0:M]))
            trs(lst, [qk_, 'IDB'], [tpk])
            if kvout is not None:
                DMA('sp', 'kvo%d' % ks, kvout(KF), r=[kfk])
            yield
            tcopy(kdst, TP[:, 128:128 + M], [tpk], [dkeys[0]])
            if want_q:
                tcopy(qdst, TP[:, 0:M], [tpk], [dkeys[2]], eng='act')

        aslot = [0]
        dbgsel = [0]

        def attn_unit_gen(g, prep, qT, kp, kc_, vp, vc, mask, outs, rk):
            s_ = aslot[0] % 3
            aslot[0] += 1
            if prep is not None:
                qT, kp, kc_, vp, vc, rk = prep(s_)
                yield
            stt_ = ST2[:, 4 * s_:4 * s_ + 2]
            pb = APB[:, s_ * 512:s_ * 512 + 256]
            pT = APB[:, s_ * 512 + 256:s_ * 512 + 512]
            P_ = (psB, psC, psD)[s_]
            pk = ('psB', 'psC', 'psD')[s_]
            TP = (psT, psU, psEb)[s_]
            tpk = ('psT', 'psU', 'psE')[s_]
            mask = mask_std if mask == 's' else mask_halo
            mms([(P_[:, 0:128], qT, kp, True, True), (P_[:, 128:256], qT, kc_, True, True)], rk, [pk])
            yield
            stt(P_[:, 0:256], P_[:, 0:256], 128.0 ** -0.5, mask, ALU.mult, ALU.add, [pk, 'MSK'], [pk])
            A('dve', lambda e: e.reduce_max(stt_[:, 0:1], P_[:, 0:256], axis=AX.X), [pk], [('mx', s_)])
            tsc(stt_[:, 1:2], stt_[:, 0:1], -1.0, None, ALU.mult, ALU.bypass, [('mx', s_)], [('nmx', s_)])
            act(pb, P_[:, 0:256], AF.Exp, [pk, ('nmx', s_)], [('pb', s_)], bias=stt_[:, 1:2])
            if dbgsel[0] == 1:
                dbgsel[0] = 2
                DMA('sp', 'dbgb', [(dbg_d[7], pb[:, 0:128]), (dbg_d[8], pb[:, 128:256]), (dbg2_d, stt_)], r=[('pb', s_), ('mx', s_), ('nmx', s_)])
            yield
            trs([(TP[:, 256:384], pb[:, 0:128], IDB[:, :]), (TP[:, 384:512], pb[:, 128:256], IDB[:, :])],
                [('pb', s_), 'IDB'], [tpk])
            yield
            tcopy(pT, TP[:, 256:512], [tpk], [('pT', s_)], eng='act')
            yield
            mms([(P_[:, 256:384], vp, pT[:, 0:128], True, False), (P_[:, 256:384], vc, pT[:, 128:256], False, True),
                 (P_[0:1, 384:512], ONEB[:, 0:1], pT[:, 0:128], True, False),
                 (P_[0:1, 384:512], ONEB[:, 0:1], pT[:, 128:256], False, True),
                 (P_[0:1, 0:128], stt_[:, 0:1], identf, True, True)],
                rk + [('pT', s_), ('mx', s_), 'ONEB', 'CON'], [pk])
            if dbgsel[0] == 2:
                dbgsel[0] = 3
                DMA('sp', 'dbg', [(dbg_d[0], vp), (dbg_d[1], vc), (dbg_d[2], pT[:, 0:128]), (dbg_d[3], pT[:, 128:256]),
                                  (dbg_d[4], kp), (dbg_d[5], kc_), (dbg_d[6], qT)], r=rk + [('pT', s_)])
            yield
            tcopy(LMS[0:1, s_ * 256:s_ * 256 + 128], P_[0:1, 384:512], [pk], [('LMS', s_)], eng='act')
            tcopy(LMS[0:1, s_ * 256 + 128:s_ * 256 + 256], P_[0:1, 0:128], [pk], [('LMS', s_)], eng='act')
            prs = []
            for (src, dst) in outs:
                tcopy(OTG[g][:, dst], P_[:, 256:384][:, src], [pk], [('OTG', g)])
                prs.append((lm_d[g, 0:1, dst], LMS[0:1, s_ * 256:s_ * 256 + 128][:, src]))
                prs.append((lm_d[g, 1:2, dst], LMS[0:1, s_ * 256 + 128:s_ * 256 + 256][:, src]))
            DMA('sp', 'lmo%d' % s_, prs, r=[('LMS', s_), ('lmtok', g)])

        mask_std = MSK[:, 0:256]
        mask_halo = MSK[:, 256:512]

        def attn_head(hh, sweep):
            g, j = hh // 4, hh % 4
            W, d = GROUPS[g]
            nT = 16 // d
            ws = wload(watt_t[hh], 16 * 384)
            hb = hkv_idx(hh)
            DMA('sp', 'c1', [(ropev, rope_t[g].rearrange("t p c -> p t c"))], w=['ROPE'])
            HKSv = HKS[:, :].rearrange("p (s c) -> p s c", s=3)
            if sweep == 'H':
                def hgen(r):
                    st_ = TOK - 128 * d + r
                    sl = r % 3
                    yield from proj_gen(ws, lambda kc, st_=st_: hT[:, kc, st_:st_ + 128 * d:d], 128, g, r * (nT + 1), False, False,
                                        HKSv[:, sl, 0:128], HKSv[:, sl, 128:256], None, None, [('HKS', sl), ('HKS', sl), None])
                    DMA('sp', 'hkvo%d' % sl, [(hkv_d[hb + r], HKSv[:, sl, 0:128]), (hkv_d[hb + d + r], HKSv[:, sl, 128:256])],
                        r=[('HKS', sl)], w=[('hkv_d', hh)])
                pipeline((hgen(r) for r in range(d)), 2)
                return
            gens = []
            for r in range(d):
                for T_ in range(nT):
                    ti = r * nT + T_
                    st_ = r + d * 128 * T_
                    o0 = st_ - (TOK - W)
                    kvout = None
                    if o0 >= 0:
                        def kvout(KF, o0=o0, d=d, g=g, j=j):
                            return [(kvp[g][o0:o0 + 127 * d + 1:d, 0, j, :], KF[:, 0:128]),
                                    (kvp[g][o0:o0 + 127 * d + 1:d, 1, j, :], KF[:, 128:256])]
                    gens.append(proj_gen(ws, lambda kc, st_=st_: hT[:, kc, st_:st_ + 128 * d:d], 128, g, r * (nT + 1) + T_ + 1, False, True,
                                         KTv[:, ti, :], VAv[:, ti, :], QTv[:, ti, :], kvout, [('KT', ti), ('VA', ti), ('QT', ti)]))

            def kvout_m(KF, g=g, j=j, W=W):
                prs = []
                for b in range(4):
                    prs.append((kvs[g][b, W - 4:W, 0, j, :], KF[4 * b:4 * b + 4, 0:128]))
                    prs.append((kvs[g][b, W - 4:W, 1, j, :], KF[4 * b:4 * b + 4, 128:256]))
                return prs
            gens.append(proj_gen(ws, lambda kc: hT[:, kc, TOK:TOK + MS], MS, g, 0, True, True,
                                 MQK[:, MS:2 * MS], MV[:, :], MQK[:, 0:MS], kvout_m, ['MK', 'MV', 'MQ']))
            pipeline(gens, 2)
            qTm = MQK[:, 0:MS]
            kTm = MQK[:, MS:2 * MS]
            STGv = ASTAGE[:, :].rearrange("p (s t c) -> p s t c", s=3, t=6)
            units = []
            for r in range(d):
                for T_ in range(nT):
                    ti = r * nT + T_
                    st_ = r + d * 128 * T_
                    outs = [(slice(0, 128), slice(st_, st_ + 128 * d, d))]
                    if T_ == 0:
                        def prep(sl, ti=ti, r=r):
                            if _os.environ.get("DBGF") == "1":
                                S.fence()
                            DMA('sp', 'hkvl%d' % sl, [(HKSv[:, sl, 0:128], hkv_d[hb + r]), (HKSv[:, sl, 128:256], hkv_d[hb + d + r])],
                                r=[('hkv_d', hh)], w=[('HKS', sl)])
                            return (QTv[:, ti, :], HKSv[:, sl, 0:128], KTv[:, ti, :], HKSv[:, sl, 128:256], VAv[:, ti, :],
                                    [('QT', ti), ('KT', ti), ('VA', ti), ('HKS', sl)])
                        units.append(attn_unit_gen(g, prep, None, None, None, None, None, 'h', outs, None))
                    else:
                        units.append(attn_unit_gen(g, None, QTv[:, ti, :], KTv[:, ti - 1, :], KTv[:, ti, :], VAv[:, ti - 1, :], VAv[:, ti, :],
                                                   's', outs, [('QT', ti), ('KT', ti), ('VA', ti), ('KT', ti - 1), ('VA', ti - 1)]))
            pre = [(0, [126, 127], [16, 17], [(126, 18), (127, 19)])] if g == 0 else \
                  [((d - 2 + t), [127], [16 + t], [(127, 18 + 2 * g + t)]) for t in range(2)]
            for (r, qrows, qslots, extras) in pre:
                def prep(sl, r=r, qrows=qrows, qslots=qslots, extras=extras):
                    sk = ('STG', sl)
                    for qr, qs in zip(qrows, qslots):
                        tcopy(STGv[:, sl, 0, qr:qr + 1], qTm[:, qs:qs + 1], ['MQ'], [sk])
                    for (sl_, ms) in extras:
                        tcopy(STGv[:, sl, 1, sl_:sl_ + 1], kTm[:, ms:ms + 1], ['MK'], [sk])
                        DMA('sp', 'stg%d' % sl, [(STGv[sl_:sl_ + 1, sl, 3, :], MV[ms:ms + 1, :])], r=['MV'], w=[sk])
                    DMA('sp', 'hkvl%d' % sl, [(HKSv[:, sl, 0:128], hkv_d[hb + r]), (HKSv[:, sl, 128:256], hkv_d[hb + d + r])],
                        r=[('hkv_d', hh)], w=[('HKS', sl)])
                    return (STGv[:, sl, 0, :], STGv[:, sl, 1, :], HKSv[:, sl, 0:128], STGv[:, sl, 3, :], HKSv[:, sl, 128:256],
                            [sk, ('HKS', sl)])
                outs = [(slice(qr, qr + 1), slice(TOK + qs, TOK + qs + 1)) for qr, qs in zip(qrows, qslots)]
                units.append(attn_unit_gen(g, prep, None, None, None, None, None, 'h', outs, None))
            for b in range(4):
                ulist = [(0, [0, 1, 2, 3])] if g == 0 else [(t, [t]) for t in range(4)]
                for (t0, ts) in ulist:
                    def prep(sl, b=b, t0=t0, ts=ts):
                        sk = ('STG', sl)
                        TPs = (psT, psU, psEb)[sl]
                        tpk = ('psT', 'psU', 'psE')[sl]
                        rows = ck[g][b, t0:t0 + 127 * d + 1:d, :, j, :] if g > 0 else ck[g][b, :, :, j, :]
                        DMA('pool', 'kc%d' % sl, [(STGv[:, sl, 5, :], rows[:, 0, :]), (STGv[:, sl, 3, :], rows[:, 1, :])], w=[('STG5', sl), sk])
                        trs([(TPs[:, 512:640], STGv[:, sl, 5, :], IDB[:, :])], [('STG5', sl), 'IDB'], [tpk])
                        tcopy(STGv[:, sl, 1, :], TPs[:, 512:640], [tpk], [sk])
                        for n_, t in enumerate(ts):
                            ms = 4 * b + t
                            tcopy(STGv[:, sl, 0, n_:n_ + 1], qTm[:, ms:ms + 1], ['MQ'], [sk])
                            tcopy(STGv[:, sl, 2, n_:n_ + 1], kTm[:, ms:ms + 1], ['MK'], [sk])
                            DMA('sp', 'stg%d' % sl, [(STGv[n_:n_ + 1, sl, 4, :], MV[ms:ms + 1, :])], r=['MV'], w=[sk])
                        return (STGv[:, sl, 0, :], STGv[:, sl, 1, :], STGv[:, sl, 2, :], STGv[:, sl, 3, :], STGv[:, sl, 4, :], [sk])
                    outs = [(slice(n_, n_ + 1), slice(TOK + 4 * b + t, TOK + 4 * b + t + 1)) for n_, t in enumerate(ts)]
                    units.append(attn_unit_gen(g, prep, None, None, None, None, None, 's', outs, None))
            pipeline(units, UDEPTH)

        def attn_combine(slot):
            lk = [('lmtok', 0), ('lmtok', 1), ('lmtok', 2)]
            K = lambda a_: ('cmb', a_)
            Cv = lambda a_: CMB[:, a_ * 17:(a_ + 1) * 17]
            DMA('sp', 'cml', [(Cv(2 * g + k_), lm_d[g, k_].rearrange("(p c) -> p c", p=128)) for g in range(3) for k_ in range(2)],
                w=lk + [K(i) for i in range(6)])
            for g in range(3):
                act(Cv(6 + g), Cv(2 * g), AF.Ln, [K(2 * g)], [K(6 + g)])
                tt(Cv(6 + g), Cv(6 + g), Cv(2 * g + 1), ALU.add, [K(6 + g), K(2 * g + 1)], [K(6 + g)])
            tt(Cv(9), Cv(6), Cv(7), ALU.max, [K(6), K(7)], [K(9)])
            tt(Cv(9), Cv(9), Cv(8), ALU.max, [K(9), K(8)], [K(9)])
            for g in range(3):
                tt(Cv(6 + g), Cv(6 + g), Cv(9), ALU.subtract, [K(6 + g), K(9)], [K(6 + g)])
                act(Cv(6 + g), Cv(6 + g), AF.Exp, [K(6 + g)], [K(6 + g)])
            tt(Cv(10), Cv(6), Cv(7), ALU.add, [K(6), K(7)], [K(10)])
            tt(Cv(10), Cv(10), Cv(8), ALU.add, [K(10), K(8)], [K(10)])
            A('dve', lambda e, Cv=Cv: e.reciprocal(Cv(10), Cv(10)), [K(10)], [K(10)])
            for g in range(3):
                A('dve', lambda e, Cv=Cv, g=g: e.reciprocal(Cv(2 * g), Cv(2 * g)), [K(2 * g)], [K(2 * g)])
                tt(Cv(6 + g), Cv(6 + g), Cv(10), ALU.mult, [K(6 + g), K(10)], [K(6 + g)])
                tt(Cv(6 + g), Cv(6 + g), Cv(2 * g), ALU.mult, [K(6 + g), K(2 * g)], [K(6 + g)])
            DMA('sp', 'cfo', [(coef_d[g].rearrange("(p c) -> p c", p=128), Cv(6 + g)) for g in range(3)],
                r=[K(6), K(7), K(8)], w=['coef_d'])
            BW = 192
            for bi, c0 in enumerate(range(0, NTOK, BW)):
                n = min(BW, NTOK - c0)
                sl = bi % 2
                Bv = lambda g, n=n, sl=sl: CMB[:, 192 + (sl * 3 + g) * BW:192 + (sl * 3 + g) * BW + n]
                bk = lambda g, sl=sl: ('cbc', sl, g)
                DMA('sp', 'cbl%d' % sl, [(Bv(g), coef_d[g:g + 1, c0:c0 + n].partition_broadcast(128)) for g in range(3)],
                    r=['coef_d'], w=[bk(0), bk(1), bk(2)])
                for g in range(3):
                    tt(Bv(g), Bv(g), OTG[g][:, c0:c0 + n], ALU.mult, [bk(g), ('OTG', g)], [bk(g)])
                tt(Bv(0), Bv(0), Bv(1), ALU.add, [bk(0), bk(1)], [bk(0)])
                tt(YCH[:, c0:c0 + n], Bv(0), Bv(2), ALU.add, [bk(0), bk(2)], ['YCH'])
            DMA('sp', 'ych', [(oatt_d[slot], YCH[:, :])], r=['YCH'], w=[('oatt_d', slot)])

        SSTv = SST[:, :].rearrange("p (h c) -> p h c", h=8)
        hslot = [0]
        ringn = [0]

        def hgrn_tile_gen(ws, h, cols_fn, need_out, outs):
            cnt = hslot[0]
            hslot[0] += 1
            s_ = cnt % 3
            p_ = cnt % 2
            n0 = ringn[0]
            ringn[0] += 2
            wv = WB[ws][:, :].rearrange("p (k n) -> p k n", k=16)
            Z, zk = (psA, 'psA') if p_ == 0 else (psF, 'psF')
            Pb, pbk = (psB, 'psB') if p_ == 0 else (psC, 'psC')
            Po, pok = (psD, 'psD') if p_ == 0 else (psE, 'psE')
            TP, tpk = (psT, 'psT') if p_ == 0 else (psU, 'psU')
            T = HT[:, s_ * 1536:(s_ + 1) * 1536]
            Bf = HB[:, s_ * 1024:(s_ + 1) * 1024]
            k_ = lambda n: ('h%s' % n, s_)
            f_, lf, kk, bs, eb, enb = (T[:, i * 128:(i + 1) * 128] for i in range(6))
            ekl, sq, sg, dif = (T[:, i * 128:(i + 1) * 128] for i in range(6, 10))
            e1, e2 = T[:, 1280:1408], T[:, 1408:1536]
            qe, ke, kl, vb, kl1 = (Bf[:, i * 128:(i + 1) * 128] for i in range(5))
            og = qe
            qkT = Bf[:, 640:896]
            attm = Bf[:, 896:1024]
            edec = ST3[:, 2 * s_:2 * s_ + 2]
            ssq = ST3[:, 8 + 2 * s_:10 + 2 * s_]
            SRv = lambda i: SR[:, (i % 4) * 128:(i % 4) * 128 + 128]
            SRBv = lambda i: SRB[:, (i % 4) * 128:(i % 4) * 128 + 128]
            rk = lambda i: ('SR', i % 4)
            rbk = lambda i: ('SRB', i % 4)
            mms([(Z[:, :], cols_fn(kc), wv[:, kc, :], kc == 0, kc == 15) for kc in range(16)], [('WB', ws)] + ALLH, [zk])
            yield
            act(f_, Z[:, 128:256], AF.Exp, [zk], [k_('f')], scale=-1.0)
            act(e1, Z[:, 0:128], AF.Exp, [zk], [k_('e1')], scale=-1.0)
            if need_out:
                act(e2, Z[:, 384:512], AF.Exp, [zk], [k_('e2')], scale=-1.0)
            tcopy(vb, Z[:, 256:384], [zk], [k_('v')], eng='act')
            tsc(f_, f_, 1.0, None, ALU.add, ALU.bypass, [k_('f')], [k_('f')])
            A('dve', lambda e: e.reciprocal(f_, f_), [k_('f')], [k_('f')])
            tt(f_, f_, LB[:, 1024 + h * 128:1024 + (h + 1) * 128], ALU.mult, [k_('f'), 'LB'], [k_('f')])
            tt(f_, f_, LB[:, h * 128:(h + 1) * 128], ALU.add, [k_('f'), 'LB'], [k_('f')])
            act(lf, f_, AF.Ln, [k_('f')], [k_('lf')])
            tsc(kk, f_, -1.0, 1.0, ALU.mult, ALU.add, [k_('f')], [k_('k')])
            tsc(e1, e1, 1.0, None, ALU.add, ALU.bypass, [k_('e1')], [k_('e1')])
            A('dve', lambda e: e.reciprocal(e1, e1), [k_('e1')], [k_('e1')])
            tt(sq, Z[:, 0:128], e1, ALU.mult, [zk, k_('e1')], [k_('sq')])
            if need_out:
                tsc(e2, e2, 1.0, None, ALU.add, ALU.bypass, [k_('e2')], [k_('e2')])
                A('dve', lambda e: e.reciprocal(e2, e2), [k_('e2')], [k_('e2')])
                tt(sg, Z[:, 384:512], e2, ALU.mult, [zk, k_('e2')], [k_('sg')])
            yield
            mms([(Pb[:, 0:128], tri2, lf, True, True), (Pb[:, 128:256], blk2, lf, True, True),
                 (Pb[:, 256:258], lf, ind2, True, True)], [k_('lf'), 'CON'], [pbk])
            yield
            tcopy(bs, Pb[:, 0:128], [pbk], [k_('bs')])
            act(edec, Pb[:, 256:258], AF.Exp, [pbk], [k_('edec')])
            tt(dif, Pb[:, 128:256], bs, ALU.subtract, [pbk, k_('bs')], [k_('dif')])
            act(eb, bs, AF.Exp, [k_('bs')], [k_('eb')])
            act(enb, bs, AF.Exp, [k_('bs')], [k_('enb')], scale=-1.0)
            act(ekl, dif, AF.Exp, [k_('dif')], [k_('ekl')])
            tt(qe, sq, eb, ALU.mult, [k_('sq'), k_('eb')], [k_('qe')])
            tt(ke, kk, enb, ALU.mult, [k_('k'), k_('enb')], [k_('ke')])
            stt(kl, kk, ind2[:, 0:1], ekl, ALU.mult, ALU.mult, [k_('k'), k_('ekl'), 'CON'], [k_('kl')])
            stt(kl1, kk, ind2[:, 1:2], ekl, ALU.mult, ALU.mult, [k_('k'), k_('ekl'), 'CON'], [k_('kl1')])
            yield
            trs([(TP[:, 512:640], qe, IDB[:, :]), (TP[:, 640:768], ke, IDB[:, :])], [k_('qe'), k_('ke'), 'IDB'], [tpk])
            yield
            tcopy(qkT, TP[:, 512:768], [tpk], [k_('qkT')])
            yield
            mms([(Po[:, 256:384], qkT[:, 128:256], qkT[:, 0:128], True, True),
                 (Po[:, 128:256], kl, vb, True, True),
                 (Po[:, 384:512], kl1, vb, True, True)], [k_('qkT'), k_('kl'), k_('kl1'), k_('v')], [pok])
            yield
            tt(attm, Po[:, 256:384], tri2, ALU.mult, [pok, 'CON'], [k_('attm')])
            stt(SRv(n0 + 1), SRv(n0), edec[:, 0:1], Po[:, 128:256], ALU.mult, ALU.add, [pok, k_('edec'), rk(n0)], [rk(n0 + 1)])
            stt(SRv(n0 + 2), SRv(n0 + 1), edec[:, 1:2], Po[:, 384:512], ALU.mult, ALU.add, [pok, k_('edec'), rk(n0 + 1)], [rk(n0 + 2)])
            if need_out:
                tcopy(SRBv(n0), SRv(n0), [rk(n0)], [rbk(n0)], eng='act')
                tcopy(SRBv(n0 + 1), SRv(n0 + 1), [rk(n0 + 1)], [rbk(n0 + 1)], eng='act')
            if not need_out:
                return
            yield
            mms([(Po[:, 0:128], attm, vb, True, False),
                 (Po[0:64, 0:128], qkT[:, 0:64], SRBv(n0), False, True),
                 (Po[64:128, 0:128], qkT[:, 64:128], SRBv(n0 + 1), False, True)],
                [k_('attm'), k_('v'), k_('qkT'), rbk(n0), rbk(n0 + 1)], [pok])
            yield
            act(dif, Po[:, 0:128], AF.Square, [pok], [k_('dif'), k_('ssq')], accum_out=ssq[:, 0:1])
            act(ssq[:, 1:2], ssq[:, 0:1], AF.Ln, [k_('ssq')], [k_('ssq2')], scale=1.0 / 128, bias=EPS)
            act(ssq[:, 1:2], ssq[:, 1:2], AF.Exp, [k_('ssq2')], [k_('rs')], scale=-0.5)
            stt(dif, Po[:, 0:128], ssq[:, 1:2], HGW[:, :], ALU.mult, ALU.mult, [pok, k_('rs'), 'HGW'], [k_('dif')])
            tt(og, dif, sg, ALU.mult, [k_('dif'), k_('sg')], [k_('qe')])
            yield
            trs([(TP[:, 768:896], og, IDB[:, :])], [k_('qe'), 'IDB'], [tpk])
            yield
            for (buf, bk, src, dst) in outs:
                tcopy(buf[:, dst], TP[:, 768:896][:, src], [tpk], [bk], eng='act')

        def hgrn_head_tiles(ws, h, nts, need_fn, outs_fn):
            ringn[0] = 0
            tcopy(SR[:, 0:128], SSTv[:, h, :], ['SST'], [('SR', 0)])
            pipeline((hgrn_tile_gen(ws, h, (lambda kc, nt=nt: hT[:, kc, nt * 128:(nt + 1) * 128]), need_fn(nt), outs_fn(nt))
                      for nt in nts), 3)
            nf = ringn[0] % 4
            tcopy(SSTv[:, h, :], SR[:, nf * 128:nf * 128 + 128], [('SR', nf)], ['SST'])

        def hgrn_misc(ws, h):
            wv = WB[ws][:, :].rearrange("p (k n) -> p k n", k=16)
            M = MS
            mms([(psA[0:M, :], hT[:, kc, TOK:TOK + MS], wv[:, kc, :], kc == 0, kc == 15) for kc in range(16)], [('WB', ws)] + ALLH, ['psA'])
            T = HT[0:M, 0:1536]
            Bf = HB[0:M, 0:1024]
            f_, lf, kk, bs, eb, enb = (T[:, i * 128:(i + 1) * 128] for i in range(6))
            ekl, sq, sg, dif = (T[:, i * 128:(i + 1) * 128] for i in range(6, 10))
            qe, ke, kl, vb, og = (Bf[:, i * 128:(i + 1) * 128] for i in range(5))
            DMA('sp', 's0', [(S0[:, :].rearrange("p (b c) -> p b c", b=4), sthg[:, h].rearrange("b d v -> d b v"))], w=['S0'])
            tcopy(S0B[:, :], S0[:, :], ['S0'], ['S0B'])
            act(f_, psA[0:M, 128:256], AF.Sigmoid, ['psA'], ['mf'])
            tt(f_, f_, LB[0:M, 1024 + h * 128:1024 + (h + 1) * 128], ALU.mult, ['mf', 'LB'], ['mf'])
            tt(f_, f_, LB[0:M, h * 128:(h + 1) * 128], ALU.add, ['mf', 'LB'], ['mf'])
            act(lf, f_, AF.Ln, ['mf'], ['mlf'])
            tsc(kk, f_, -1.0, 1.0, ALU.mult, ALU.add, ['mf'], ['mk'])
            mms([(psB[0:M, 0:128], tri_m, lf, True, True), (psB[0:M, 128:256], blk_m, lf, True, True),
                 (psB[:, 256:260], lf, ind_m, True, True)], ['mlf', 'CON'], ['psB'])
            tcopy(bs, psB[0:M, 0:128], ['psB'], ['mbs'])
            act(eb, bs, AF.Exp, ['mbs'], ['meb'])
            act(enb, bs, AF.Exp, ['mbs'], ['menb'], scale=-1.0)
            tt(dif, psB[0:M, 128:256], bs, ALU.subtract, ['psB', 'mbs'], ['mdif'])
            act(ekl, dif, AF.Exp, ['mdif'], ['mekl'])
            edec = ST[:, 40:44]
            act(edec, psB[:, 256:260], AF.Exp, ['psB'], ['medec'])
            act(sq, psA[0:M, 0:128], AF.Silu, ['psA'], ['msq'])
            act(sg, psA[0:M, 384:512], AF.Silu, ['psA'], ['msg'])
            tt(qe, sq, eb, ALU.mult, ['msq', 'meb'], ['mqe'])
            tt(ke, kk, enb, ALU.mult, ['mk', 'menb'], ['mke'])
            tt(kl, kk, ekl, ALU.mult, ['mk', 'mekl'], ['mkl'])
            tcopy(vb, psA[0:M, 256:384], ['psA'], ['mv'], eng='act')
            trs([(psT[:, 512:512 + M], qe, IDB[0:M, 0:M]), (psT[:, 640:640 + M], ke, IDB[0:M, 0:M])], ['mqe', 'mke', 'IDB'], ['psTh'])
            qeT = HB[:, 3072:3072 + M]
            keT = HB[:, 3104:3104 + M]
            qeTm = HB[:, 3200:3200 + 4 * M].rearrange("p (b t) -> p b t", b=4)
            klm = HB[0:M, 3328:3328 + 512].rearrange("p (b c) -> p b c", b=4)
            tcopy(qeT, psT[:, 512:512 + M], ['psTh'], ['mqeT'])
            tcopy(keT, psT[:, 640:640 + M], ['psTh'], ['mkeT'])
            tt(qeTm, qeT[:, None, :].broadcast_to([128, 4, M]), colm.rearrange("p (b t) -> p b t", b=4), ALU.mult, ['mqeT', 'CON'], ['mqeTm'])
            tt(klm, kl[:, None, :].broadcast_to([M, 4, 128]), rowm[:, :, None].broadcast_to([M, 4, 128]), ALU.mult, ['mkl', 'CON'], ['mklm'])
            mms([(psD[0:M, 256:256 + M], keT, qeT, True, True)], ['mqeT', 'mkeT'], ['psDh'])
            attm = HB[0:M, 896:896 + M]
            tt(attm, psD[0:M, 256:256 + M], cmask_m, ALU.mult, ['psDh', 'CON'], ['mattm'])
            lst = [(psD[0:M, 0:128], attm, vb, True, False)]
            for b in range(4):
                lst.append((psD[0:M, 0:128], qeTm[:, b, :], S0B[:, b * 128:(b + 1) * 128], False, b == 3))
            mms(lst, ['mattm', 'mv', 'mqeTm', 'S0B'], ['psDo'])
            mms([(psE[:, b * 128:(b + 1) * 128], klm[:, b, :], vb, True, True) for b in range(4)], ['mklm', 'mv'], ['psE'])
            S0v = S0[:, :].rearrange("p (b c) -> p b c", b=4)
            tt(S0v, S0v, edec[:, :, None].broadcast_to([128, 4, 128]), ALU.mult, ['S0', 'medec', 'S0B'], ['S0'])
            tt(S0[:, :], S0[:, :], psE[:, :], ALU.add, ['S0', 'psE'], ['S0'])
            DMA('sp', 's0o', [(hgs[:, h].rearrange("b d v -> d b v"), S0v)], r=['S0'])
            act(dif, psD[0:M, 0:128], AF.Square, ['psDo'], ['mdif', 'mssq'], accum_out=ST[0:M, 36:37])
            act(ST[0:M, 37:38], ST[0:M, 36:37], AF.Sqrt, ['mssq'], ['mssq2'], scale=1.0 / 128, bias=EPS)
            A('dve', lambda e: e.reciprocal(ST[0:M, 38:39], ST[0:M, 37:38]), ['mssq2'], ['mrs'])
            stt(dif, psD[0:M, 0:128], ST[0:M, 38:39], HGW[0:M, :], ALU.mult, ALU.mult, ['psDo', 'mrs', 'HGW'], ['mdif'])
            tt(og, dif, sg, ALU.mult, ['mdif', 'msg'], ['mog'])
            trs([(psT[:, 768:768 + M], og, IDB[0:M, 0:M])], ['mog', 'IDB'], ['psTg'])
            tcopy(OHGH[:, TOK:TOK + 16], psT[:, 768:784], ['psTg'], ['OHGH'], eng='act')

        A('pool', lambda e: e.memset(SST[:, :], 0.0), [], ['SST'])
        A('pool', lambda e: e.memset(SBF[:, :], 0.0), [], ['SBF'])
        ONEROW = lsb("ONEROW", [1, 2176])
        A('pool', lambda e: e.memset(ONEROW[:, :], 1.0), [], ['ONEROW'])
        DMA('sp', 'lmi', [(lm_d[g, k_:k_ + 1, :], ONEROW[:, :]) for g in range(3) for k_ in range(2)], r=['ONEROW'],
            w=[('lmtok', 0), ('lmtok', 1), ('lmtok', 2)])
        phase_end()
        for nt in range(16):
            norm_tile(xall[nt * 128:(nt + 1) * 128, :], 128, 0, nt * 128, 128, False)
        alloc_attn()
        for hh in range(12):
            attn_head(hh, 'H')
        phase_end()
        alloc_hgrn()
        OHGPv = OHGPRE[:, :].rearrange("p (h c) -> p h c", h=8)
        for h in range(8):
            ws = wload(whg_t[h], 8192)
            hgrn_head_tiles(ws, h, range(16), lambda nt: nt == 15,
                            lambda nt, h=h: [(OHGPRE, 'OHGPRE', slice(126, 128), slice(2 * h, 2 * h + 2))])
            tsc(SSTv[:, h, :], SSTv[:, h, :], CVAL[:, 0:1], None, ALU.mult, ALU.bypass, ['SST', 'CVAL'], ['SST'])
        phase_end()
        for nt in range(16):
            norm_tile(xall[(16 + nt) * 128:(17 + nt) * 128, :], 128, 0, nt * 128, 128, False)
        norm_tile(xall[32 * 128:32 * 128 + MS, :], MS, 0, TOK, MS, True)
        alloc_attn()
        for slot in range(4):
            for g in range(3):
                attn_head(4 * g + slot, 'O')
            attn_combine(slot)
        phase_end()
        alloc_hgrn()
        for h in range(8):
            ws = wload(whg_t[h], 8192)
            hgrn_head_tiles(ws, h, range(16), lambda nt: True,
                            lambda nt: [(OHGH, 'OHGH', slice(0, 128), slice(nt * 128, (nt + 1) * 128))])
            DMA('sp', 'hgp', [(hgp[h], SSTv[:, h, :])], r=['SST'])
            S.fence()
            hgrn_misc(ws, h)
            S.fence()
            tcopy(OHGH[:, TOK + 16:TOK + 18], OHGPv[:, h, :], ['OHGPRE'], ['OHGH'])
            DMA('sp', 'ohgo', [(ohg_d[h], OHGH[:, :])], r=['OHGH'], w=[('ohg_d', h)])
        phase_end()
        BIGB = lsb("BIGB", [128, 12 * NTOK], BF16)
        YCH = lsb("YCHb", [128, NTOK], BF16)
        GT = lsb("GT", [128, 2 * 512])
        OAT = BIGB[:, :].rearrange("p (k t) -> p k t", k=12)
        DMA('sp', 'oal', [(OAT[:, s_, :], oatt_d[s_]) for s_ in range(4)] + [(OAT[:, 4 + h, :], ohg_d[h]) for h in range(8)], w=['OAT'])
        for fc in range(16):
            ws = wload(wB1_t[fc], 44 * 128)
            wv = WB[ws][:, 0:44 * 128].rearrange("p (k n) -> p k n", k=44)
            for bi, c0 in enumerate(range(0, NTOK, 512)):
                n = min(512, NTOK - c0)
                lst = []
                for kc in range(16):
                    lst.append((psA[:, 0:n], wv[:, kc, :], hT[:, kc, c0:c0 + n], kc == 0, kc == 15))
                for kc in range(16):
                    lst.append((psB[:, 0:n], wv[:, 16 + kc, :], hT[:, kc, c0:c0 + n], kc == 0, kc == 15))
                for kc in range(4):
                    lst.append((psC[:, 0:n], wv[:, 32 + kc, :], OAT[:, kc, c0:c0 + n], kc == 0, kc == 3))
                for kc in range(8):
                    lst.append((psD[:, 0:n], wv[:, 36 + kc, :], OAT[:, 4 + kc, c0:c0 + n], kc == 0, kc == 7))
                mms(lst, [('WB', ws), 'OAT'] + ALLH, ['psA', 'psB', 'psC', 'psD'])
                act(GT[:, 0:n], psA[:, 0:n], AF.Sigmoid, ['psA'], ['GT0'])
                act(GT[:, 512:512 + n], psB[:, 0:n], AF.Sigmoid, ['psB'], ['GT1'])
                tt(GT[:, 0:n], GT[:, 0:n], psC[:, 0:n], ALU.mult, ['GT0', 'psC'], ['GT0'])
                tt(GT[:, 512:512 + n], GT[:, 512:512 + n], psD[:, 0:n], ALU.mult, ['GT1', 'psD'], ['GT1'])
                tt(YCH[:, c0:c0 + n], GT[:, 0:n], GT[:, 512:512 + n], ALU.add, ['GT0', 'GT1'], ['YCH'])
            DMA('sp', 'ych', [(ymix_d[fc], YCH[:, :])], r=['YCH'], w=[('ymix_d', fc)])
        phase_end()
        GT = lsb("GT2", [128, 512])
        GBC = lsb("GBC", [128, 512])
        GBM = lsb("GBM", [MS, 512])
        XQ = [lsb("XQ0", [128, 512]), lsb("XQ1", [128, 512])]
        ymT = BIGA[:, :].rearrange("p (k t) -> p k t", k=16)
        DMA('sp', 'yml', [(ymT[:, fc, :], ymix_d[fc]) for fc in range(16)], w=['ymT'])

        def gate_bc(off, cb):
            DMA('sp', 'mrl', [(MR[:, :], modr_d[:, off + cb * 512:off + (cb + 1) * 512])], r=['modr_d'], w=['MR'])
            mms([(psE[:, :], ones_f[0:1, :], MR[0:1, :], True, True),
                 (psF[0:MS, :], rowsel, MR[0:5, :], True, True)], ['MR', 'CON'], ['psE', 'psF'])
            tcopy(GBC[:, :], psE[:, :], ['psE'], ['GBC'])
            tcopy(GBM[:, :], psF[0:MS, :], ['psF'], ['GBM'], eng='act')

        for cb in range(4):
            ws = wload(wo_t[cb], 8192)
            wv = WB[ws][:, :].rearrange("p (k n) -> p k n", k=16)
            gate_bc(0, cb)
            for tl in range(17):
                M = 128 if tl < 16 else MS
                P_ = psA if tl % 2 == 0 else psB
                pk = 'psA' if tl % 2 == 0 else 'psB'
                mms([(P_[0:M, :], ymT[:, kc, tl * 128:tl * 128 + M], wv[:, kc, :], kc == 0, kc == 15) for kc in range(16)],
                    [('WB', ws), 'ymT'], [pk])
                xq = XQ[tl % 2]
                xk = 'XQ%d' % (tl % 2)
                srow = (16 + tl) * 128
                DMA('sp', 'xq%d' % (tl % 2), [(xq[0:M, :], xall[srow:srow + M, cb * 512:(cb + 1) * 512])], w=[xk])
                G_ = GBC if tl < 16 else GBM
                tt(GT[0:M, 0:512], P_[0:M, :], G_[0:M, :], ALU.mult, [pk, 'GBC', 'GBM'], ['GT0'])
                tt(xq[0:M, :], xq[0:M, :], GT[0:M, 0:512], ALU.add, [xk, 'GT0'], [xk])
                DMA('sp', 'x1o%d' % (tl % 2), [(x1_d[tl * 128:tl * 128 + M, cb * 512:(cb + 1) * 512], xq[0:M, :])], r=[xk], w=[('x1_d', tl)])
        phase_end()
        for tl in range(16):
            norm_tile(x1_d[tl * 128:(tl + 1) * 128, :], 128, 1, tl * 128, 128, False)
        norm_tile(x1_d[16 * 128:16 * 128 + MS, :], MS, 1, TOK, MS, True)
        phase_end()
        BIGB = lsb("YTB", [128, NJ * 512], BF16)
        GT = lsb("GT3", [128, 512])
        GBC = lsb("GBC3", [128, 512])
        GBM = lsb("GBM3", [MS, 512])
        XQ = [lsb("XQ03", [128, 512]), lsb("XQ13", [128, 512])]
        ACAT = lsb("ACAT", [128, 520])
        UU = lsb("UU", [128, 512])
        CROW = RTMP[0:8, :]
        YT = BIGB[:, 0:NJ * 512].rearrange("p (j t) -> p j t", j=NJ)
        CWv = CW[:, :].rearrange("p (a j) -> p a j", a=4)
        CARv = CARRY[:, :].rearrange("p (j c) -> p j c", c=2)
        SCVv = SCV[:, :].rearrange("p (j c) -> p j c", c=8)
        ASVv = ASV[:, :].rearrange("p (j c) -> p j c", c=8)
        for j0 in range(0, NJ, 4):
            DMA('sp', 'crw', [(CROW[:, :], stconv[:, j0 * 128:(j0 + 4) * 128])], w=['CROW'])
            mms([(psE[:, (j - j0) * 8:(j - j0) * 8 + 8], CROW[0:8, (j - j0) * 128:(j - j0 + 1) * 128], identf[0:8, 0:8], True, True) for j in range(j0, j0 + 4)],
                ['CROW', 'CON'], ['psE'])
            tcopy(SCV[:, j0 * 8:(j0 + 4) * 8], psE[:, 0:32], ['psE'], ['SCV'])
        blocks = [('m', TOK, MS)] + [('o', b * 512, 512) for b in range(4)]
        for (kind, c0, n) in blocks:
            for j in range(NJ):
                ws = wload(wab_t[j], 4096)
                wv = WB[ws][:, 0:4096].rearrange("p (k n) -> p k n", k=32)
                s_ = j & 1
                Pa = psA if s_ == 0 else psC
                Pb = psB if s_ == 0 else psD
                pak = 'psA' if s_ == 0 else 'psC'
                pbk = 'psB' if s_ == 0 else 'psD'
                lst = [(Pa[:, 0:n], wv[:, kc, :], hT[:, kc, c0:c0 + n], kc == 0, kc == 15) for kc in range(16)]
                lst += [(Pb[:, 0:n], wv[:, 16 + kc, :], hT[:, kc, c0:c0 + n], kc == 0, kc == 15) for kc in range(16)]
                mms(lst, [('WB', ws)] + ALLH, [pak, pbk])
                ac = ACAT[:, 0:520]
                u = UU[:, 0:512]
                ack = ('ac', 0)
                uk = ('u', 0)
                w0, w1, w2, cb_ = (CWv[:, a, j:j + 1] for a in range(4))
                if kind == 'o':
                    tcopy(ac[:, 0:2], CARv[:, j, :], ['CARRY'], [ack])
                    tcopy(ac[:, 2:2 + n], Pa[:, 0:n], [pak], [ack], eng='act')
                    tsc(u[:, 0:n], ac[:, 2:2 + n], w2, cb_, ALU.mult, ALU.add, [ack, 'CW'], [uk])
                    stt(u[:, 0:n], ac[:, 1:1 + n], w1, u[:, 0:n], ALU.mult, ALU.add, [ack, 'CW', uk], [uk])
                    stt(u[:, 0:n], ac[:, 0:n], w0, u[:, 0:n], ALU.mult, ALU.add, [ack, 'CW', uk], [uk])
                    tcopy(CARv[:, j, :], ac[:, n:n + 2], [ack], ['CARRY'])
                else:
                    acv = ac[:, 0:24].rearrange("p (b t) -> p b t", b=4)
                    tcopy(acv[:, :, 0:2], SCVv[:, j, :].rearrange("p (b t) -> p b t", b=4), ['SCV'], [ack])
                    tcopy(acv[:, :, 2:6], Pa[:, 0:16].rearrange("p (b t) -> p b t", b=4), [pak], [ack])
                    uv = u[:, 0:16].rearrange("p (b t) -> p b t", b=4)
                    tsc(uv, acv[:, :, 2:6], w2, cb_, ALU.mult, ALU.add, [ack, 'CW'], [uk])
                    stt(uv, acv[:, :, 1:5], w1, uv, ALU.mult, ALU.add, [ack, 'CW', uk], [uk])
                    stt(uv, acv[:, :, 0:4], w0, uv, ALU.mult, ALU.add, [ack, 'CW', uk], [uk])
                    A('pool', lambda e, u=u: e.memset(u[:, 16:32], 0.0), [uk], [uk])
                    tcopy(ASVv[:, j, :].rearrange("p (b t) -> p b t", b=4), acv[:, :, 4:6], [ack], ['ASV'])
                    tsc(CARv[:, j, :], Pa[:, 16:18], CVAL[:, 0:1], None, ALU.mult, ALU.bypass, [pak, 'CVAL'], ['CARRY'])
                act(u[:, 0:n], u[:, 0:n], AF.Silu, [uk], [uk])
                tt(YT[:, j, 0:n], u[:, 0:n], Pb[:, 0:n], ALU.mult, [uk, pbk], ['YT'])
            ntl = (n + 127) // 128
            for cb in range(4):
                gate_bc(D, cb)
                for jg in range(4):
                    ws = wload(wd_t[cb * 4 + jg], 11 * 512)
                    wv = WB[ws][:, 0:11 * 512].rearrange("p (k n) -> p k n", k=11)
                    Ps = [psA, psB, psC, psD]
                    for tl in range(ntl):
                        M = min(128, n - tl * 128)
                        mms([(Ps[tl][0:M, :], YT[:, jg * 11 + jj, tl * 128:tl * 128 + M], wv[:, jj, :], jg == 0 and jj == 0, jg == 3 and jj == 10)
                             for jj in range(11)], [('WB', ws), 'YT'], ['psA', 'psB', 'psC', 'psD'][tl:tl + 1])
                for tl in range(ntl):
                    M = min(128, n - tl * 128)
                    gt = (c0 // 128 + tl)
                    xq = XQ[tl % 2]
                    xk = 'XQ%d' % (tl % 2)
                    pk = ['psA', 'psB', 'psC', 'psD'][tl]
                    DMA('sp', 'xq%d' % (tl % 2), [(xq[0:M, :], x1_d[gt * 128:gt * 128 + M, cb * 512:(cb + 1) * 512])], w=[xk])
                    G_ = GBC if kind == 'o' else GBM
                    tt(GT[0:M, 0:512], [psA, psB, psC, psD][tl][0:M, :], G_[0:M, :], ALU.mult, [pk, 'GBC', 'GBM'], ['GT0'])
                    tt(xq[0:M, :], xq[0:M, :], GT[0:M, 0:512], ALU.add, [xk, 'GT0'], [xk])
                    DMA('sp', 'x1o%d' % (tl % 2), [(x2_d[gt * 128:gt * 128 + M, cb * 512:(cb + 1) * 512], xq[0:M, :])], r=[xk], w=[('x2_d', gt)])
        for (src, dstd, nr, ch_) in ((ASVv, convs, 8, 'cvo0'), (CARv, convp, 2, 'cvo1')):
            for j0 in range(0, NJ, 4):
                mms([(psE[0:nr, (j - j0) * 128:(j - j0 + 1) * 128], src[:, j, :], identf, True, True) for j in range(j0, j0 + 4)],
                    ['ASV', 'CARRY', 'CON'], ['psE'])
                tcopy(CROW[0:nr, 0:512], psE[0:nr, :], ['psE'], ['CROW'])
                DMA('sp', ch_, [(dstd[:, j0 * 128:(j0 + 4) * 128], CROW[0:nr, 0:512])], r=['CROW'], w=['CROWo'])
        phase_end()
        NFW = lsb("NFW", [128, D])
        DMA('sp', 'nfw', [(NFW[:, :], normf.partition_broadcast(128))], w=['NFW'])
        for tl in range(17):
            M = 128 if tl < 16 else MS
            s_ = tl % 2
            X = XT[s_]
            xk = 'XT%d' % s_
            DMA('sp', 'x%d' % s_, [(X[0:M, :], x2_d[tl * 128:tl * 128 + M, :])], w=[xk])
            act(XN[0:M, :], X[0:M, :], AF.Square, [xk], ['XN', 'STn'], accum_out=ST[0:M, 48:49])
            act(ST[0:M, 49:50], ST[0:M, 48:49], AF.Sqrt, ['STn'], ['STn2'], scale=1.0 / D, bias=EPS)
            A('dve', lambda e, M=M: e.reciprocal(ST[0:M, 50:51], ST[0:M, 49:50]), ['STn2'], ['STn3'])
            stt(X[0:M, :], X[0:M, :], ST[0:M, 50:51], NFW[0:M, :], ALU.mult, ALU.mult, [xk, 'STn3', 'NFW'], [xk])
            dst = y_own[tl * 128:(tl + 1) * 128, :] if tl < 16 else y_misc
            DMA('sp', 'yo%d' % s_, [(dst, X[0:M, :])], r=[xk])
        for g, (W, d) in enumerate(GROUPS):
            DMA('pool', 'kvc', [(kvs[g][b, 0:W - 4], ck[g][b, 4:W]) for b in range(4)])

        if stop is not None:
            S.ops = S.ops[:stop]
        chs = sorted({o['ch'] for o in S.ops if o['ch'] is not None})
        sems_ch = {ch: es.enter_context(nc.semaphore("c_" + ch)) for ch in chs}
        sems_eng = {e_: es.enter_context(nc.semaphore("e_" + e_)) for e_ in ('pe', 'act', 'dve', 'pool')}
        block = es.enter_context(nc.Block())
        S.emit(nc, block, sems_eng, sems_ch, chs)
        ph[0].close()
    return nc


def _tile_w(Wsub, kc):
    n = Wsub.shape[1]
    return np.ascontiguousarray(Wsub.reshape(kc, 128, n).transpose(1, 0, 2).reshape(128, kc * n))


def _consts():
    C = np.zeros((128, 2048), np.float32)
    C[:, 0:128] = np.eye(128)
    s = np.arange(128)[:, None]
    t = np.arange(128)[None, :]
    same = (s // 64) == (t // 64)
    C[:, 128:256] = (same & (s <= t))
    C[:, 256:384] = same
    C[:, 384] = (np.arange(128) < 64)
    C[:, 385] = (np.arange(128) >= 64)
    C[:, 512:576] = ((np.arange(128)[:, None] % 64) <= np.arange(64)[None, :])
    s = np.arange(32)[:, None]
    t = np.arange(32)[None, :]
    samem = ((s // 4) == (t // 4)) & (s < 16) & (t < 16)
    C[0:32, 640:672] = samem & (s <= t)
    C[0:32, 672:704] = samem
    for b in range(4):
        C[4 * b:4 * b + 4, 704 + b] = 1
        C[4 * b:4 * b + 4, 768 + b] = 1
        C[:, 896 + b * 32 + 4 * b:896 + b * 32 + 4 * b + 4] = 1
    C[0:32, 736:768] = samem & (s <= t)
    for m in range(32):
        C[(1 + m // 4) if m < 16 else 0, 1024 + m] = 1
    C[:, 1152:1280] = 1
    return C


def _rope(pos):
    half = 16
    inv = (np.float32(500000.0) ** (-np.arange(half, dtype=np.float32) * np.float32(2.0) / np.float32(32))).astype(np.float32)
    ang = pos.astype(np.float32)[:, None] * inv[None, :]
    return np.concatenate([np.cos(ang), np.sin(ang)], axis=1).astype(np.float32)


_NC = None


def kernel(x_prompt, x_sample, cache_kv_w128, cache_kv_w512, cache_kv_w2048, state_hgrn, state_conv,
           c_prompt, c_sample, w_ada, b_ada, norm1_w, w_in, hg_lb, hg_norm_w, w_pa, w_pb, w_o,
           norm2_w, w_ffn_a, w_ffn_b, conv_w, conv_b, w_ffn_down, norm_f_w):
    global _NC
    in_maps = _prep(x_prompt, x_sample, cache_kv_w128, cache_kv_w512, cache_kv_w2048, state_hgrn, state_conv,
                    c_prompt, c_sample, w_ada, b_ada, norm1_w, w_in, hg_lb, hg_norm_w, w_pa, w_pb, w_o,
                    norm2_w, w_ffn_a, w_ffn_b, conv_w, conv_b, w_ffn_down, norm_f_w)
    return _run(in_maps)


def _prep(x_prompt, x_sample, cache_kv_w128, cache_kv_w512, cache_kv_w2048, state_hgrn, state_conv,
          c_prompt, c_sample, w_ada, b_ada, norm1_w, w_in, hg_lb, hg_norm_w, w_pa, w_pb, w_o,
          norm2_w, w_ffn_a, w_ffn_b, conv_w, conv_b, w_ffn_down, norm_f_w):
    f = lambda a: np.asarray(a, dtype=np.float32)
    xp = f(x_prompt)[0]
    xs = f(x_sample)
    Win = f(w_in)[0]
    shared = {}
    shared["wada_t"] = np.stack([_tile_w(f(w_ada)[0][:, j * 512:(j + 1) * 512], 16) for j in range(24)])
    shared["bada"] = f(b_ada).reshape(1, -1)
    shared["vecA"] = np.concatenate([f(norm1_w)[0].reshape(16, 128), f(norm2_w)[0].reshape(16, 128)], 0)
    shared["normf"] = f(norm_f_w).reshape(1, -1)
    shared["watt_t"] = np.stack([_tile_w(np.concatenate([Win[:, hh * 128:(hh + 1) * 128], Win[:, 1536 + hh * 128:1536 + (hh + 1) * 128],
                                                          Win[:, 3072 + hh * 128:3072 + (hh + 1) * 128]], 1), 16) for hh in range(12)])
    shared["whg_t"] = np.stack([_tile_w(np.concatenate([Win[:, 4608 + k * 1024 + h * 128:4608 + k * 1024 + (h + 1) * 128] for k in range(4)], 1), 16)
                                for h in range(8)])
    Wpa, Wpb = f(w_pa)[0], f(w_pb)[0]
    shared["wB1_t"] = np.stack([np.concatenate([_tile_w(Win[:, 8704 + fc * 128:8704 + (fc + 1) * 128], 16),
                                                _tile_w(Win[:, 10752 + fc * 128:10752 + (fc + 1) * 128], 16),
                                                _tile_w(Wpa[:, fc * 128:(fc + 1) * 128], 4),
                                                _tile_w(Wpb[:, fc * 128:(fc + 1) * 128], 8)], 1) for fc in range(16)])
    Wo = f(w_o)[0]
    shared["wo_t"] = np.stack([_tile_w(Wo[:, cb * 512:(cb + 1) * 512], 16) for cb in range(4)])
    Wa, Wb, Wd = f(w_ffn_a)[0], f(w_ffn_b)[0], f(w_ffn_down)[0]
    shared["wab_t"] = np.stack([np.concatenate([_tile_w(Wa[:, j * 128:(j + 1) * 128], 16), _tile_w(Wb[:, j * 128:(j + 1) * 128], 16)], 1)
                                for j in range(NJ)])
    shared["wd_t"] = np.stack([_tile_w(Wd[jg * 11 * 128:(jg + 1) * 11 * 128, cb * 512:(cb + 1) * 512], 11)
                               for cb in range(4) for jg in range(4)])
    cw = f(conv_w)[0]
    shared["convw"] = np.concatenate([cw[0].reshape(NJ, 128), cw[1].reshape(NJ, 128), cw[2].reshape(NJ, 128),
                                      f(conv_b)[0].reshape(NJ, 128)], 0)
    shared["hglb"] = f(hg_lb)
    shared["hgnw"] = f(hg_norm_w).reshape(1, 128)
    shared["consts"] = _consts()
    i = np.arange(128)[:, None]
    ip = np.arange(256)[None, :]
    band = np.where((ip >= i) & (ip <= i + 128), 0.0, NEG).astype(np.float32)

    in_maps = []
    for c in range(NCORE):
        m = dict(shared)
        xall = np.zeros((33 * 128, D), np.float32)
        own0 = TOK * c

        def tokrow(p):
            return xp[p] if p >= 0 else np.zeros(D, np.float32)
        if c > 0:
            xall[0:2048] = xp[own0 - 2048:own0]
        xall[2048:4096] = xp[own0:own0 + TOK]
        mb = 32 * 128
        xall[mb:mb + 16] = xs[4 * c:4 * c + 4].reshape(16, D)
        pos_m = np.zeros(MS, np.int64)
        for b in range(4):
            for t in range(4):
                pos_m[4 * b + t] = 16384 + t
        for t in range(2):
            xall[mb + 16 + t] = tokrow(own0 - 2 + t)
            pos_m[16 + t] = own0 - 2 + t
        for g, (W, d) in enumerate(GROUPS):
            for t in range(2):
                p = own0 - 2 + t - 128 * d
                xall[mb + 18 + 2 * g + t] = tokrow(p)
                pos_m[18 + 2 * g + t] = p
        m["xall"] = xall
        m["c5"] = np.concatenate([f(c_prompt), f(c_sample)[4 * c:4 * c + 4]], 0)
        rt = np.zeros((3, 32, 128, 32), np.float32)
        for g, (W, d) in enumerate(GROUPS):
            nT = 16 // d
            for r in range(d):
                for T_ in range(-1, nT):
                    pos = own0 + r + d * (128 * T_ + np.arange(128))
                    rt[g, r * (nT + 1) + T_ + 1] = _rope(pos)
        m["rope_t"] = rt
        m["rope_m"] = _rope(pos_m)
        mh = band.copy()
        if c == 0:
            mh[:, 0:128] = NEG
        m["masks"] = np.concatenate([band, mh], 1)
        m["ck128"] = f(cache_kv_w128)[0, 4 * c:4 * c + 4]
        m["ck512"] = f(cache_kv_w512)[0, 4 * c:4 * c + 4]
        m["ck2048"] = f(cache_kv_w2048)[0, 4 * c:4 * c + 4]
        m["sthg"] = f(state_hgrn)[0, 4 * c:4 * c + 4]
        m["stconv"] = f(state_conv)[0, 4 * c:4 * c + 4].reshape(8, DFF)
        m["cvalid"] = np.full((128, 1), 0.0 if c == 0 else 1.0, np.float32)
        in_maps.append(m)
    return in_maps


def _run(in_maps):
    global _NC
    if _NC is None:
        _NC = build()
    res = run_bass_kernel_spmd(_NC, in_maps, core_ids=list(range(NCORE)))
    R = res.results
    y_prompt = np.concatenate([R[c]["y_own"] for c in range(NCORE)], 0)[None]
    y_sample = np.concatenate([R[c]["y_misc"][0:16].reshape(4, 4, D) for c in range(NCORE)], 0)
    L = R[NCORE - 1]
    outs = [y_prompt, y_sample,
            L["kvp128"][None, None], L["kvp512"][None, None], L["kvp2048"][None, None],
            L["hgp"][None, None], L["convp"].reshape(1, 1, 2, DFF)]
    for nm in ("kvs128", "kvs512", "kvs2048"):
        outs.append(np.concatenate([R[c][nm] for c in range(NCORE)], 0)[None])
    outs.append(np.concatenate([R[c]["hgs"] for c in range(NCORE)], 0)[None])
    outs.append(np.concatenate([R[c]["convs"].reshape(4, 2, DFF) for c in range(NCORE)], 0)[None])
    return tuple(np.ascontiguousarray(o, dtype=np.float32) for o in outs)
```

```python
import numpy as np
import concourse.bass as bass
import concourse.mybir as mybir
from concourse.bass_utils import run_bass_kernel_spmd

F32 = mybir.dt.float32
BF16 = mybir.dt.bfloat16
AF = mybir.ActivationFunctionType
ALU = mybir.AluOpType
AX = mybir.AxisListType

D = 2048
NCORE = 8
TOK = 2048
MS = 32
NTOK = TOK + MS
DFF = 5632
NJ = 44
EPS = 1e-6
GROUPS = ((128, 1), (512, 4), (2048, 16))
NEG = -30000.0
import os as _os
UDEPTH = int(_os.environ.get("UDEPTH", "3"))


class Sched:
    def __init__(self):
        self.ops = []
        self.lw = {}
        self.rd = {}
        self.fence_deps = set()
        self.last_eng = {}
        self.last_ch = {}

    @staticmethod
    def _nk(k):
        n = k[0] if isinstance(k, tuple) else k
        if isinstance(n, str) and len(n) >= 3 and n.startswith('ps') and n[2] in 'ABCDEFTU':
            return n[:3]
        return k

    def add(self, eng, fn, r=(), w=(), ch=None):
        r = [self._nk(k) for k in r]
        w = [self._nk(k) for k in w]
        psr = [k for k in r if isinstance(k, str) and len(k) == 3 and k.startswith('ps')]
        if psr:
            r = [k for k in r if k not in psr]
            w = list(w) + psr
        deps = set(self.fence_deps)
        for k in r:
            if k in self.lw:
                deps.add(self.lw[k])
        for k in w:
            if k in self.lw:
                deps.add(self.lw[k])
            deps.update(self.rd.get(k, ()))
        i = len(self.ops)
        import sys as _s
        fr = _s._getframe(1)
        lines = []
        while fr is not None and len(lines) < 4:
            lines.append(fr.f_lineno)
            fr = fr.f_back
        self.ops.append(dict(eng=eng, fn=fn, deps=deps, ch=ch, ln=lines))
        for k in r:
            self.rd.setdefault(k, []).append(i)
        for k in w:
            self.lw[k] = i
            self.rd[k] = []
        if ch is None:
            self.last_eng[eng] = i
        else:
            self.last_ch[ch] = i
        return i

    def fence(self):
        self.fence_deps = set(self.last_eng.values()) | set(self.last_ch.values())

    def emit(self, nc, block, sems_eng, sems_ch, final_chs):
        ops = self.ops
        needed = [False] * len(ops)
        for o in ops:
            for d in o['deps']:
                dd = ops[d]
                if dd['ch'] is None and dd['eng'] == 'pe' and o['eng'] == 'pe' and o['ch'] is None:
                    continue
                needed[d] = True
        cnt = {}
        for i, o in enumerate(ops):
            if o['ch'] is not None:
                continue
            if needed[i]:
                cnt[o['eng']] = cnt.get(o['eng'], 0) + 1
                o['ms'] = cnt[o['eng']]
        chcnt = {}
        for i, o in enumerate(ops):
            if o['ch'] is not None:
                n = o['fn'].ndma
                chcnt[o['ch']] = chcnt.get(o['ch'], 0) + 16 * n
                o['val'] = chcnt[o['ch']]
        finals = {ch: chcnt[ch] for ch in final_chs if ch in chcnt}

        def run(engname, e):
            seen = {}
            for i, o in enumerate(ops):
                if o['eng'] != engname:
                    continue
                for d in sorted(o['deps']):
                    dd = ops[d]
                    if dd['ch'] is not None:
                        sem, val = sems_ch[dd['ch']], dd['val']
                    else:
                        if dd['eng'] == 'pe' and engname == 'pe' and o['ch'] is None:
                            continue
                        sem, val = sems_eng[dd['eng']], dd['ms']
                    key = id(sem)
                    if seen.get(key, 0) >= val:
                        continue
                    e.wait_ge(sem, val)
                    seen[key] = val
                if o['ch'] is not None:
                    for ins in o['fn'](e):
                        ins.then_inc(sems_ch[o['ch']], 16)
                else:
                    ins = o['fn'](e)
                    if needed[i]:
                        ins.then_inc(sems_eng[engname], 1)
            if engname == 'sp':
                for ch, v in finals.items():
                    e.wait_ge(sems_ch[ch], v)

        block.tensor(lambda e: run('pe', e))
        block.scalar(lambda e: run('act', e))
        block.vector(lambda e: run('dve', e))
        block.gpsimd(lambda e: run('pool', e))
        block.sync(lambda e: run('sp', e))


def dmafn(pairs):
    def f(e):
        return [e.dma_start(out=o, in_=i, allow_slow_non_contiguous=True) for (o, i) in pairs]
    f.ndma = len(pairs)
    return f


def build(stop=None, marks=None):
    nc = bass.Bass("TRN2", target_bir_lowering=False)
    S = Sched()

    def din(name, shape, dt=F32):
        return nc.dram_tensor(name, list(shape), dt, kind="ExternalInput").ap()

    def dout(name, shape):
        return nc.dram_tensor(name, list(shape), F32, kind="ExternalOutput").ap()

    def dscr(name, shape, dt):
        return nc.dram_tensor(name, list(shape), dt).ap()

    xall = din("xall", [33 * 128, D])
    c5 = din("c5", [5, D])
    wada_t = din("wada_t", [24, 128, 16 * 512])
    bada = din("bada", [1, 6 * D])
    vecA = din("vecA", [32, 128])
    normf = din("normf", [1, D])
    watt_t = din("watt_t", [12, 128, 16 * 384])
    whg_t = din("whg_t", [8, 128, 16 * 512])
    wB1_t = din("wB1_t", [16, 128, 44 * 128])
    wo_t = din("wo_t", [4, 128, 16 * 512])
    wab_t = din("wab_t", [NJ, 128, 32 * 128])
    wd_t = din("wd_t", [16, 128, 11 * 512])
    convw = din("convw", [4 * NJ, 128])
    hglb = din("hglb", [2, 1024])
    hgnw = din("hgnw", [1, 128])
    consts = din("consts", [128, 2048])
    rope_t = din("rope_t", [3, 32, 128, 32])
    rope_m = din("rope_m", [MS, 32])
    masks = din("masks", [128, 512])
    ck = [din("ck128", [4, 128, 2, 4, 128]), din("ck512", [4, 512, 2, 4, 128]), din("ck2048", [4, 2048, 2, 4, 128])]
    sthg = din("sthg", [4, 8, 128, 128])
    stconv = din("stconv", [8, DFF])
    cvalid = din("cvalid", [128, 1])

    y_own = dout("y_own", [TOK, D])
    y_misc = dout("y_misc", [MS, D])
    kvp = [dout("kvp128", [128, 2, 4, 128]), dout("kvp512", [512, 2, 4, 128]), dout("kvp2048", [2048, 2, 4, 128])]
    hgp = dout("hgp", [8, 128, 128])
    convp = dout("convp", [2, DFF])
    kvs = [dout("kvs128", [4, 128, 2, 4, 128]), dout("kvs512", [4, 512, 2, 4, 128]), dout("kvs2048", [4, 2048, 2, 4, 128])]
    hgs = dout("hgs", [4, 8, 128, 128])
    convs = dout("convs", [8, DFF])

    oatt_d = dscr("oatt_d", [4, 128, NTOK], BF16)
    ohg_d = dscr("ohg_d", [8, 128, NTOK], BF16)
    ymix_d = dscr("ymix_d", [16, 128, NTOK], BF16)
    x1_d = dscr("x1_d", [17 * 128, D], F32)
    x2_d = dscr("x2_d", [17 * 128, D], F32)
    modr_d = dscr("modr_d", [5, 2 * D], F32)
    hkv_d = dscr("hkv_d", [168, 128, 128], BF16)
    lm_d = dscr("lm_d", [3, 2, 2176], F32)
    coef_d = dscr("coef_d", [3, 2176], F32)
    dbg_d = dscr("dbg_d", [10, 128, 128], BF16)
    dbg2_d = dscr("dbg2_d", [128, 2], F32)

    import contextlib
    es = contextlib.ExitStack()

    def sb(name, shape, dt=F32):
        return es.enter_context(nc.sbuf_tensor(name, list(shape), dt))

    def ps(name, shape, dt=F32):
        return es.enter_context(nc.psum_tensor(name, list(shape), dt))

    ph = [contextlib.ExitStack()]

    uniq = [0]

    def lsb(name, shape, dt=F32):
        uniq[0] += 1
        return ph[0].enter_context(nc.sbuf_tensor("%s_%d" % (name, uniq[0]), list(shape), dt))

    def phase_end():
        if marks is not None:
            marks.append(len(S.ops))
        S.fence()
        ph[0].close()
        ph[0] = contextlib.ExitStack()

    with es:
        BIGA = sb("BIGA", [128, 16 * NTOK], BF16)
        WB = [sb("WB0", [128, 8192], BF16), sb("WB1", [128, 8192], BF16)]
        XT = [sb("XT0", [128, D]), sb("XT1", [128, D])]
        XN = sb("XN", [128, D], BF16)
        CON = sb("CON", [128, 1280])
        identf = CON[:, 0:128]
        tri2 = CON[:, 128:256]
        blk2 = CON[:, 256:384]
        ind2 = CON[:, 384:386]
        cmask = CON[:, 512:576]
        tri_m = CON[0:32, 640:672]
        blk_m = CON[0:32, 672:704]
        ind_m = CON[0:32, 704:708]
        cmask_m = CON[0:32, 736:768]
        rowm = CON[0:32, 768:772]
        colm = CON[:, 896:1024]
        rowsel = CON[0:5, 1024:1056]
        ones_f = CON[:, 1152:1280]
        IDB = sb("IDB", [128, 128], BF16)
        ONEB = sb("ONEB", [128, 128], BF16)
        MSK = sb("MSK", [128, 512])
        ROPEM = sb("ROPEM", [MS, 32])
        MODT = sb("MODT", [128, 96 * 5])
        MR = sb("MR", [5, 512])
        SC = sb("SC", [128, 4 * 16])
        SCM = sb("SCM", [128, 4 * 16 * MS])
        NW = sb("NW", [128, 32])
        CW = sb("CW", [128, 4 * NJ])
        ST = sb("ST", [128, 64])
        STT = sb("STT", [128, 16 * 5], BF16)
        HGW = sb("HGW", [128, 128])
        CVAL = sb("CVAL", [128, 1])
        RTMP = sb("RTMP", [128, 512])
        CARRY = sb("CARRY", [128, NJ * 2])
        SCV = sb("SCV", [128, NJ * 8])
        ASV = sb("ASV", [128, NJ * 8])
        SST = sb("SST", [128, 8 * 128])
        SBF = sb("SBF", [128, 128], BF16)
        OHGPRE = sb("OHGPRE", [128, 16], BF16)
        Lc = {}

        psA = ps("psA", [128, 512])
        psB = ps("psB", [128, 512])
        psC = ps("psC", [128, 512])
        psD = ps("psD", [128, 512])
        psE = ps("psE", [128, 512])
        psF = ps("psF", [128, 512])
        psT = ps("psT", [128, 1024], BF16)
        psU = ps("psU", [128, 1024], BF16)

        hT = BIGA[:, :].rearrange("p (k t) -> p k t", k=16)
        psEb = psE[:, :].bitcast(BF16)

        def A(eng, fn, r=(), w=()):
            return S.add(eng, fn, r, w)

        def DMA(q, ch, pairs, r=(), w=()):
            return S.add(q, dmafn(pairs), r, w, ch=ch)

        def act(out, in_, func, r, w, **kw):
            A('act', lambda e: e.activation(out=out, in_=in_, func=func, **kw), r, w)

        def tt(out, a, b, op, r, w, eng='dve'):
            A(eng, lambda e: e.tensor_tensor(out, a, b, op=op), r, w)

        def tcopy(out, in_, r, w, eng='dve'):
            if eng == 'act':
                A('act', lambda e: e.copy(out=out, in_=in_), r, w)
            else:
                A(eng, lambda e: e.tensor_copy(out, in_), r, w)

        def tsc(out, a, s1, s2, op0, op1, r, w):
            A('dve', lambda e: e.tensor_scalar(out, a, s1, s2, op0=op0, op1=op1), r, w)

        def stt(out, a, s, b, op0, op1, r, w):
            A('dve', lambda e: e.scalar_tensor_tensor(out, a, s, b, op0=op0, op1=op1), r, w)

        def mms(lst, r, w):
            def f(e):
                ins = None
                for (o, l, rr, st, sp) in lst:
                    ins = e.matmul(o, l, rr, start=st, stop=sp)
                return ins
            A('pe', f, r, w)

        def trs(lst, r, w):
            def f(e):
                ins = None
                for (o, i, idn) in lst:
                    ins = e.transpose(o, i, idn)
                return ins
            A('pe', f, r, w)

        wslot = [0]

        def wload(src, ncols):
            s_ = wslot[0]
            wslot[0] ^= 1
            DMA('pool', 'w%d' % s_, [(WB[s_][:, 0:ncols], src)], w=[('WB', s_)])
            return s_

        DMA('sp', 'c0', [(CON[:, :], consts[:, 0:1280]), (MSK[:, :], masks), (ROPEM[:, :], rope_m),
                         (CVAL[:, :], cvalid), (HGW[:, :], hgnw.partition_broadcast(128))],
            w=['CON', 'MSK', 'ROPEM', 'CVAL', 'HGW'])
        S5 = lsb("S5", [5, D], BF16)
        BADA = lsb("BADA", [1, 6 * D], BF16)
        DMA('pool', 'c3', [(IDB[:, :], consts[:, 0:128]), (ONEB[:, :], consts[:, 1152:1280]), (BADA[:, :], bada)],
            w=['IDB', 'ONEB', 'BADA'])
        DMA('sp', 'c4', [(XT[0][0:5, :], c5)], w=['XT0'])
        act(S5[:, :], XT[0][0:5, :], AF.Silu, ['XT0'], ['S5'])
        trs([(psT[:, kc * 8:kc * 8 + 5], S5[0:5, kc * 128:(kc + 1) * 128], IDB[0:5, 0:5]) for kc in range(16)],
            ['S5', 'IDB'], ['psT'])
        tcopy(STT[:, :].rearrange("p (k e) -> p k e", e=5), psT[:, 0:128].rearrange("p (k e) -> p k e", e=8)[:, :, 0:5], ['psT'], ['STT'])
        DMA('sp', 'c5', [(XT[1][0:32, 0:128], vecA)], w=['XT1'])
        mms([(psB[:, 0:32], XT[1][0:32, 0:128], identf[0:32, 0:32], True, True)], ['XT1', 'CON'], ['psB'])
        tcopy(NW[:, :], psB[:, 0:32], ['psB'], ['NW'])
        DMA('sp', 'c6', [(XT[1][0:88, 128:256], convw[0:88, :]), (XT[1][0:88, 256:384], convw[88:176, :])], w=['XT1b'])
        mms([(psB[:, 64:152], XT[1][0:88, 128:256], identf[0:88, 0:88], True, True),
             (psB[:, 152:240], XT[1][0:88, 256:384], identf[0:88, 0:88], True, True)], ['XT1b', 'CON'], ['psBb'])
        tcopy(CW[:, :], psB[:, 64:240], ['psBb'], ['CW'])

        for j in range(24):
            s_ = wload(wada_t[j], 8192)
            wv = WB[s_][:, :].rearrange("p (k n) -> p k n", k=16)
            lst = []
            for cb in range(4):
                col = j * 4 + cb
                o = psA[:, col * 5:(col + 1) * 5]
                for kc in range(16):
                    lst.append((o, wv[:, kc, cb * 128:(cb + 1) * 128], STT[:, kc * 5:(kc + 1) * 5], kc == 0, False))
                lst.append((o, BADA[0:1, col * 128:(col + 1) * 128], ONEB[0:1, 0:5], False, True))
            mms(lst, [('WB', s_), 'STT', 'BADA', 'ONEB'], ['psA'])
            kind = j // 4
            if kind in (2, 5):
                lst = [(psC[0:5, :], STT[:, kc * 5:(kc + 1) * 5], wv[:, kc, :], kc == 0, False) for kc in range(16)]
                lst.append((psC[0:5, :], ONEB[0:1, 0:5], BADA[0:1, j * 512:(j + 1) * 512], False, True))
                mms(lst, [('WB', s_), 'STT', 'BADA', 'ONEB'], ['psC'])
                off = (0 if kind == 2 else D) + (j % 4) * 512
                tcopy(MR[:, :], psC[0:5, :], ['psC'], ['MR'])
                DMA('sp', 'mro', [(modr_d[:, off:off + 512], MR[:, :])], r=['MR'], w=['modr_d'])
        tcopy(MODT[:, :], psA[:, 0:480], ['psA'], ['MODT'])
        modT = MODT[:, :].rearrange("p (c r) -> p c r", r=5)
        SCv = SC[:, :].rearrange("p (a k) -> p a k", a=4)
        SCMv = SCM[:, :].rearrange("p (a k m) -> p a k m", a=4, k=16)
        for n_, (ksh, ksc, nwo) in enumerate(((0, 1, 0), (3, 4, 16))):
            tsc(ST[:, 0:16], modT[:, ksc * 16:(ksc + 1) * 16, 0], 1.0, None, ALU.add, ALU.bypass, ['MODT'], ['STa'])
            tt(SCv[:, 2 * n_, :], ST[:, 0:16], NW[:, nwo:nwo + 16], ALU.mult, ['STa', 'NW'], ['SC'])
            tcopy(SCv[:, 2 * n_ + 1, :], modT[:, ksh * 16:(ksh + 1) * 16, 0], ['MODT'], ['SC'])
            for b in range(4):
                tsc(ST[:, 16:32], modT[:, ksc * 16:(ksc + 1) * 16, 1 + b], 1.0, None, ALU.add, ALU.bypass, ['MODT'], ['STb'])
                tt(ST[:, 32:48], ST[:, 16:32], NW[:, nwo:nwo + 16], ALU.mult, ['STb', 'NW'], ['STc'])
                tcopy(SCMv[:, 2 * n_, :, 4 * b:4 * b + 4], ST[:, 32:48, None].broadcast_to([128, 16, 4]), ['STc'], ['SCM'])
                tcopy(SCMv[:, 2 * n_ + 1, :, 4 * b:4 * b + 4],
                      modT[:, ksh * 16:(ksh + 1) * 16, 1 + b:2 + b].broadcast_to([128, 16, 4]), ['MODT'], ['SCM'])
            tcopy(SCMv[:, 2 * n_, :, 16:32], SCv[:, 2 * n_, :, None].broadcast_to([128, 16, 16]), ['SC'], ['SCM'])
            tcopy(SCMv[:, 2 * n_ + 1, :, 16:32], SCv[:, 2 * n_ + 1, :, None].broadcast_to([128, 16, 16]), ['SC'], ['SCM'])

        phase_end()
        xslot = [0]

        XN2 = RT2 = None

        def alloc_norm():
            nonlocal XN2, RT2
            XN2 = lsb("XN2", [128, D], BF16)
            RT2 = lsb("RT2", [128, 512])

        def norm_gen(src_rows, nrows, n_, dst_cols, ncols, misc):
            s_ = xslot[0]
            xslot[0] ^= 1
            xk = 'XT%d' % s_
            X = XT[s_]
            XNs = XN if s_ == 0 else XN2
            RTs = RTMP if s_ == 0 else RT2
            nk = ('XN', s_)
            stc = ST[:, 48 + 3 * s_:51 + 3 * s_]
            sk = lambda i: ('STn', s_, i)
            DMA('sp', 'x%d' % s_, [(X[0:nrows, :], src_rows)], w=[xk])
            yield
            act(XNs[0:nrows, :], X[0:nrows, :], AF.Square, [xk], [nk, sk(0)], accum_out=stc[0:nrows, 0:1])
            act(stc[0:nrows, 1:2], stc[0:nrows, 0:1], AF.Sqrt, [sk(0)], [sk(1)], scale=1.0 / D, bias=EPS)
            A('dve', lambda e: e.reciprocal(stc[0:nrows, 2:3], stc[0:nrows, 1:2]), [sk(1)], [sk(2)])
            act(XNs[0:nrows, :], X[0:nrows, :], AF.Copy, [xk, sk(2), nk], [nk], scale=stc[0:nrows, 2:3])
            yield
            for half in range(2):
                P_ = psT if half == 0 else psU
                pk = 'psT' if half == 0 else 'psU'
                trs([(P_[:, q * ncols:(q + 1) * ncols], XNs[0:nrows, (half * 8 + q) * 128:(half * 8 + q + 1) * 128],
                      IDB[0:nrows, 0:nrows]) for q in range(8)], [nk, 'IDB'], [pk])
            yield
            for half in range(2):
                P_ = psT if half == 0 else psU
                pk = 'psT' if half == 0 else 'psU'
                pv = P_[:, 0:8 * ncols].rearrange("p (k t) -> p k t", k=8)
                dst = hT[:, half * 8:half * 8 + 8, dst_cols:dst_cols + ncols]
                if not misc:
                    sc_ = SCv[:, 2 * n_, half * 8:half * 8 + 8, None].broadcast_to([128, 8, ncols])
                    sh_ = SCv[:, 2 * n_ + 1, half * 8:half * 8 + 8, None].broadcast_to([128, 8, ncols])
                else:
                    sc_ = SCMv[:, 2 * n_, half * 8:half * 8 + 8, :]
                    sh_ = SCMv[:, 2 * n_ + 1, half * 8:half * 8 + 8, :]
                tmp = RTs[:, :].bitcast(BF16)[:, 0:8 * ncols].rearrange("p (k t) -> p k t", k=8)
                tt(tmp, pv, sc_, ALU.mult, [pk, 'SC', 'SCM'], [('RTh', s_)])
                tt(dst, tmp, sh_, ALU.add, [('RTh', s_), 'SC', 'SCM'], [('hT', dst_cols // 128)])

        def hkeys(c0, c1):
            return [('hT', t) for t in range(c0 // 128, (c1 - 1) // 128 + 1)]

        ALLH = [('hT', t) for t in range(17)]

        ST2 = None
        QT = KT = VA = OTG = ATMP = APB = ASTAGE = MQK = MV = KVF = QKB = ROPE = HKS = LMS = CMB = None
        LB = ST3 = SR = SRB = None
        HT = HB = OHGH = S0 = S0B = YCH = GT = GBC = GBM = XQ = ACAT = UU = CROW = BIGB = NFW = None
        ropev = QTv = KTv = VAv = STG = None

        def alloc_attn():
            nonlocal ST2, QT, KT, VA, OTG, ATMP, APB, ASTAGE, MQK, MV, KVF, QKB, ROPE, HKS, LMS, CMB, YCH
            nonlocal ropev, QTv, KTv, VAv, STG
            QT = lsb("QT", [128, 16 * 128], BF16)
            KT = lsb("KT", [128, 17 * 128], BF16)
            VA = lsb("VA", [128, 17 * 128], BF16)
            OTG = [lsb("OTG%d" % g, [128, NTOK], BF16) for g in range(3)]
            ST2 = lsb("ST2", [128, 64])
            APB = lsb("APB", [128, 3 * 512], BF16)
            ASTAGE = lsb("ASTAGE", [128, 3 * 6 * 128], BF16)
            MQK = lsb("MQK", [128, 2 * MS], BF16)
            MV = lsb("MV", [MS, 128], BF16)
            KVF = lsb("KVF", [128, 2 * 256])
            QKB = lsb("QKB", [128, 2 * 256], BF16)
            ROPE = lsb("ROPE", [128, 32 * 32])
            HKS = lsb("HKS", [128, 3 * 256], BF16)
            LMS = lsb("LMS", [1, 768])
            CMB = lsb("CMB", [128, 192 + 6 * 192])
            YCH = lsb("YCHa", [128, NTOK], BF16)
            ropev = ROPE[:, :].rearrange("p (t c) -> p t c", t=32)
            QTv = QT[:, :].rearrange("p (t c) -> p t c", c=128)
            KTv = KT[:, :].rearrange("p (t c) -> p t c", c=128)
            VAv = VA[:, :].rearrange("p (t c) -> p t c", c=128)
            STG = ASTAGE[:, :].rearrange("p (t c) -> p t c", c=128)
            A('pool', lambda e, t=ASTAGE: e.memset(t[:, :], 0.0), [], [('STG', 0), ('STG', 1), ('STG', 2), ('STG5', 0), ('STG5', 1), ('STG5', 2)])
            A('pool', lambda e, t=HKS: e.memset(t[:, :], 0.0), [], [('HKS', 0), ('HKS', 1), ('HKS', 2)])
            for g in range(3):
                A('pool', lambda e, t=OTG[g]: e.memset(t[:, :], 0.0), [], [('OTG', g)])

        def alloc_hgrn():
            nonlocal HT, HB, OHGH, S0, S0B, LB, ST3, SR, SRB
            LB = lsb("LB", [128, 2 * 1024])
            DMA('sp', 'c2', [(LB[:, 0:1024], hglb[0:1, :].partition_broadcast(128)),
                             (LB[:, 1024:2048], hglb[1:2, :].partition_broadcast(128))], w=['LB'])
            tt(LB[:, 0:1024], LB[:, 0:1024], LB[:, 1024:2048], ALU.subtract, ['LB'], ['LB'])
            act(LB[:, 0:1024], LB[:, 0:1024], AF.Sigmoid, ['LB'], ['LB'])
            tsc(LB[:, 1024:2048], LB[:, 0:1024], -1.0, 1.0, ALU.mult, ALU.add, ['LB'], ['LB'])
            HT = lsb("HT", [128, 3 * 1536])
            HB = lsb("HB", [128, 3 * 1024 + 1024], BF16)
            ST3 = lsb("ST3", [128, 16])
            SR = lsb("SR", [128, 4 * 128])
            SRB = lsb("SRB", [128, 4 * 128], BF16)
            OHGH = lsb("OHGH", [128, NTOK], BF16)
            S0 = lsb("S0", [128, 4 * 128])
            S0B = lsb("S0B", [128, 4 * 128], BF16)
            A('pool', lambda e, t=OHGH: e.memset(t[:, :], 0.0), [], ['OHGH'])

        def hkv_idx(hh):
            g = hh // 4
            base = 0
            for h2 in range(hh):
                base += 2 * GROUPS[h2 // 4][1]
            return base

        def pipeline(gens, depth):
            active = []
            it = iter(gens)
            done = False
            while True:
                if not done and len(active) < depth:
                    try:
                        active.append(next(it))
                    except StopIteration:
                        done = True
                if not active and done:
                    break
                for g_ in list(active):
                    try:
                        next(g_)
                    except StopIteration:
                        active.remove(g_)

        uslot = [0]

        def proj_gen(ws, cols_ap_fn, M, g, tidx, misc, want_q, kdst, vdst, qdst, kvout, dkeys):
            ks = uslot[0] & 1
            uslot[0] += 1
            Z = psA if ks == 0 else psF
            zk = 'psA' if ks == 0 else 'psF'
            TP = psT if ks == 0 else psU
            tpk = 'psT' if ks == 0 else 'psU'
            wv = WB[ws][:, 0:16 * 384].rearrange("p (k n) -> p k n", k=16)
            mms([(Z[0:M, 0:384], cols_ap_fn(kc), wv[:, kc, :], kc == 0, kc == 15) for kc in range(16)],
                [('WB', ws)] + ALLH, [zk])
            yield
            cosv = (ROPEM[0:M, 0:16] if misc else ropev[0:M, tidx, 0:16])
            sinv = (ROPEM[0:M, 16:32] if misc else ropev[0:M, tidx, 16:32])
            zqk = Z[0:M, 0:256].rearrange("p (a c) -> p a c", a=2)
            x1 = zqk[:, :, 0:16]
            x2 = zqk[:, :, 16:32]
            cb_ = cosv[:, None, :].broadcast_to([M, 2, 16])
            sb_ = sinv[:, None, :].broadcast_to([M, 2, 16])
            T = RTMP[0:M, ks * 256:ks * 256 + 128].rearrange("p (q a c) -> p q a c", q=4, a=2)
            QF = RTMP[0:M, ks * 256 + 128:ks * 256 + 256]
            rt = lambda i: ('RT', ks, i)
            tt(T[:, 0], x1, cb_, ALU.mult, [zk, 'ROPE', 'ROPEM'], [rt(0)])
            tt(T[:, 1], x2, sb_, ALU.mult, [zk, 'ROPE', 'ROPEM'], [rt(1)])
            tt(T[:, 2], x2, cb_, ALU.mult, [zk, 'ROPE', 'ROPEM'], [rt(2)])
            tt(T[:, 3], x1, sb_, ALU.mult, [zk, 'ROPE', 'ROPEM'], [rt(3)])
            KF = KVF[0:M, ks * 256:ks * 256 + 256]
            kfk = ('KVF', ks)
            tcopy(KF[:, 128:256], Z[0:M, 256:384], [zk], [kfk], eng='act')
            tt(KF[:, 0:16], T[:, 0, 1], T[:, 1, 1], ALU.subtract, [rt(0), rt(1)], [kfk])
            tt(KF[:, 16:32], T[:, 2, 1], T[:, 3, 1], ALU.add, [rt(2), rt(3)], [kfk])
            tcopy(KF[:, 32:128], Z[0:M, 160:256], [zk], [kfk])
            qb = QKB[0:M, ks * 256:ks * 256 + 128]
            kb = QKB[0:M, ks * 256 + 128:ks * 256 + 256]
            qk_ = ('QKB', ks)
            tcopy(vdst, KF[:, 128:256], [kfk], [dkeys[1]], eng='act')
            tcopy(kb, KF[:, 0:128], [kfk], [qk_])
            if want_q:
                tt(QF[:, 0:16], T[:, 0, 0], T[:, 1, 0], ALU.subtract, [rt(0), rt(1)], [('QF', ks)])
                tt(QF[:, 16:32], T[:, 2, 0], T[:, 3, 0], ALU.add, [rt(2), rt(3)], [('QF', ks)])
                tcopy(QF[:, 32:128], Z[0:M, 32:128], [zk], [('QF', ks)])
                tcopy(qb, QF[:, 0:128], [('QF', ks)], [qk_])
            yield
            lst = [(TP[:, 128:128 + M], kb, IDB[0:M, 0:M])]
            if want_q:
                lst.append((TP[:, 0:M], qb, IDB[0:M, 0:M]))
            trs(lst, [qk_, 'IDB'], [tpk])
            if kvout is not None:
                DMA('sp', 'kvo%d' % ks, kvout(KF), r=[kfk])
            yield
            tcopy(kdst, TP[:, 128:128 + M], [tpk], [dkeys[0]])
            if want_q:
                tcopy(qdst, TP[:, 0:M], [tpk], [dkeys[2]], eng='act')

        aslot = [0]
        dbgsel = [0]

        def attn_unit_gen(g, prep, qT, kp, kc_, vp, vc, mask, outs, rk):
            s_ = aslot[0] % 3
            aslot[0] += 1
            if prep is not None:
                qT, kp, kc_, vp, vc, rk = prep(s_)
                yield
            stt_ = ST2[:, 4 * s_:4 * s_ + 2]
            pb = APB[:, s_ * 512:s_ * 512 + 256]
            pT = APB[:, s_ * 512 + 256:s_ * 512 + 512]
            P_ = (psB, psC, psD)[s_]
            pk = ('psB', 'psC', 'psD')[s_]
            TP = (psT, psU, psEb)[s_]
            tpk = ('psT', 'psU', 'psE')[s_]
            mask = mask_std if mask == 's' else mask_halo
            mms([(P_[:, 0:128], qT, kp, True, True), (P_[:, 128:256], qT, kc_, True, True)], rk, [pk])
            yield
            stt(P_[:, 0:256], P_[:, 0:256], 128.0 ** -0.5, mask, ALU.mult, ALU.add, [pk, 'MSK'], [pk])
            A('dve', lambda e: e.reduce_max(stt_[:, 0:1], P_[:, 0:256], axis=AX.X), [pk], [('mx', s_)])
            tsc(stt_[:, 1:2], stt_[:, 0:1], -1.0, None, ALU.mult, ALU.bypass, [('mx', s_)], [('nmx', s_)])
            act(pb, P_[:, 0:256], AF.Exp, [pk, ('nmx', s_)], [('pb', s_)], bias=stt_[:, 1:2])
            if dbgsel[0] == 1:
                dbgsel[0] = 2
                DMA('sp', 'dbgb', [(dbg_d[7], pb[:, 0:128]), (dbg_d[8], pb[:, 128:256]), (dbg2_d, stt_)], r=[('pb', s_), ('mx', s_), ('nmx', s_)])
            yield
            trs([(TP[:, 256:384], pb[:, 0:128], IDB[:, :]), (TP[:, 384:512], pb[:, 128:256], IDB[:, :])],
                [('pb', s_), 'IDB'], [tpk])
            yield
            tcopy(pT, TP[:, 256:512], [tpk], [('pT', s_)], eng='act')
            yield
            mms([(P_[:, 256:384], vp, pT[:, 0:128], True, False), (P_[:, 256:384], vc, pT[:, 128:256], False, True),
                 (P_[0:1, 384:512], ONEB[:, 0:1], pT[:, 0:128], True, False),
                 (P_[0:1, 384:512], ONEB[:, 0:1], pT[:, 128:256], False, True),
                 (P_[0:1, 0:128], stt_[:, 0:1], identf, True, True)],
                rk + [('pT', s_), ('mx', s_), 'ONEB', 'CON'], [pk])
            if dbgsel[0] == 2:
                dbgsel[0] = 3
                DMA('sp', 'dbg', [(dbg_d[0], vp), (dbg_d[1], vc), (dbg_d[2], pT[:, 0:128]), (dbg_d[3], pT[:, 128:256]),
                                  (dbg_d[4], kp), (dbg_d[5], kc_), (dbg_d[6], qT)], r=rk + [('pT', s_)])
            yield
            tcopy(LMS[0:1, s_ * 256:s_ * 256 + 128], P_[0:1, 384:512], [pk], [('LMS', s_)], eng='act')
            tcopy(LMS[0:1, s_ * 256 + 128:s_ * 256 + 256], P_[0:1, 0:128], [pk], [('LMS', s_)], eng='act')
            prs = []
            for (src, dst) in outs:
                tcopy(OTG[g][:, dst], P_[:, 256:384][:, src], [pk], [('OTG', g)])
                prs.append((lm_d[g, 0:1, dst], LMS[0:1, s_ * 256:s_ * 256 + 128][:, src]))
                prs.append((lm_d[g, 1:2, dst], LMS[0:1, s_ * 256 + 128:s_ * 256 + 256][:, src]))
            DMA('sp', 'lmo%d' % s_, prs, r=[('LMS', s_), ('lmtok', g)])

        mask_std = MSK[:, 0:256]
        mask_halo = MSK[:, 256:512]

        def attn_head(hh, sweep):
            g, j = hh // 4, hh % 4
            W, d = GROUPS[g]
            nT = 16 // d
            ws = wload(watt_t[hh], 16 * 384)
            hb = hkv_idx(hh)
            DMA('sp', 'c1', [(ropev, rope_t[g].rearrange("t p c -> p t c"))], w=['ROPE'])
            HKSv = HKS[:, :].rearrange("p (s c) -> p s c", s=3)
            if sweep == 'H':
                def hgen(r):
                    st_ = TOK - 128 * d + r
                    sl = r % 3
                    yield from proj_gen(ws, lambda kc, st_=st_: hT[:, kc, st_:st_ + 128 * d:d], 128, g, r * (nT + 1), False, False,
                                        HKSv[:, sl, 0:128], HKSv[:, sl, 128:256], None, None, [('HKS', sl), ('HKS', sl), None])
                    DMA('sp', 'hkvo%d' % sl, [(hkv_d[hb + r], HKSv[:, sl, 0:128]), (hkv_d[hb + d + r], HKSv[:, sl, 128:256])],
                        r=[('HKS', sl)], w=[('hkv_d', hh)])
                pipeline((hgen(r) for r in range(d)), 2)
                return
            gens = []
            for r in range(d):
                for T_ in range(nT):
                    ti = r * nT + T_
                    st_ = r + d * 128 * T_
                    o0 = st_ - (TOK - W)
                    kvout = None
                    if o0 >= 0:
                        def kvout(KF, o0=o0, d=d, g=g, j=j):
                            return [(kvp[g][o0:o0 + 127 * d + 1:d, 0, j, :], KF[:, 0:128]),
                                    (kvp[g][o0:o0 + 127 * d + 1:d, 1, j, :], KF[:, 128:256])]
                    gens.append(proj_gen(ws, lambda kc, st_=st_: hT[:, kc, st_:st_ + 128 * d:d], 128, g, r * (nT + 1) + T_ + 1, False, True,
                                         KTv[:, ti, :], VAv[:, ti, :], QTv[:, ti, :], kvout, [('KT', ti), ('VA', ti), ('QT', ti)]))

            def kvout_m(KF, g=g, j=j, W=W):
                prs = []
                for b in range(4):
                    prs.append((kvs[g][b, W - 4:W, 0, j, :], KF[4 * b:4 * b + 4, 0:128]))
                    prs.append((kvs[g][b, W - 4:W, 1, j, :], KF[4 * b:4 * b + 4, 128:256]))
                return prs
            gens.append(proj_gen(ws, lambda kc: hT[:, kc, TOK:TOK + MS], MS, g, 0, True, True,
                                 MQK[:, MS:2 * MS], MV[:, :], MQK[:, 0:MS], kvout_m, ['MK', 'MV', 'MQ']))
            pipeline(gens, 2)
            qTm = MQK[:, 0:MS]
            kTm = MQK[:, MS:2 * MS]
            STGv = ASTAGE[:, :].rearrange("p (s t c) -> p s t c", s=3, t=6)
            units = []
            for r in range(d):
                for T_ in range(nT):
                    ti = r * nT + T_
                    st_ = r + d * 128 * T_
                    outs = [(slice(0, 128), slice(st_, st_ + 128 * d, d))]
                    if T_ == 0:
                        def prep(sl, ti=ti, r=r):
                            if _os.environ.get("DBGF") == "1":
                                S.fence()
                            DMA('sp', 'hkvl%d' % sl, [(HKSv[:, sl, 0:128], hkv_d[hb + r]), (HKSv[:, sl, 128:256], hkv_d[hb + d + r])],
                                r=[('hkv_d', hh)], w=[('HKS', sl)])
                            return (QTv[:, ti, :], HKSv[:, sl, 0:128], KTv[:, ti, :], HKSv[:, sl, 128:256], VAv[:, ti, :],
                                    [('QT', ti), ('KT', ti), ('VA', ti), ('HKS', sl)])
                        units.append(attn_unit_gen(g, prep, None, None, None, None, None, 'h', outs, None))
                    else:
                        units.append(attn_unit_gen(g, None, QTv[:, ti, :], KTv[:, ti - 1, :], KTv[:, ti, :], VAv[:, ti - 1, :], VAv[:, ti, :],
                                                   's', outs, [('QT', ti), ('KT', ti), ('VA', ti), ('KT', ti - 1), ('VA', ti - 1)]))
            pre = [(0, [126, 127], [16, 17], [(126, 18), (127, 19)])] if g == 0 else \
                  [((d - 2 + t), [127], [16 + t], [(127, 18 + 2 * g + t)]) for t in range(2)]
            for (r, qrows, qslots, extras) in pre:
                def prep(sl, r=r, qrows=qrows, qslots=qslots, extras=extras):
                    sk = ('STG', sl)
                    for qr, qs in zip(qrows, qslots):
                        tcopy(STGv[:, sl, 0, qr:qr + 1], qTm[:, qs:qs + 1], ['MQ'], [sk])
                    for (sl_, ms) in extras:
                        tcopy(STGv[:, sl, 1, sl_:sl_ + 1], kTm[:, ms:ms + 1], ['MK'], [sk])
                        DMA('sp', 'stg%d' % sl, [(STGv[sl_:sl_ + 1, sl, 3, :], MV[ms:ms + 1, :])], r=['MV'], w=[sk])
                    DMA('sp', 'hkvl%d' % sl, [(HKSv[:, sl, 0:128], hkv_d[hb + r]), (HKSv[:, sl, 128:256], hkv_d[hb + d + r])],
                        r=[('hkv_d', hh)], w=[('HKS', sl)])
                    return (STGv[:, sl, 0, :], STGv[:, sl, 1, :], HKSv[:, sl, 0:128], STGv[:, sl, 3, :], HKSv[:, sl, 128:256],
                            [sk, ('HKS', sl)])
                outs = [(slice(qr, qr + 1), slice(TOK + qs, TOK + qs + 1)) for qr, qs in zip(qrows, qslots)]
                units.append(attn_unit_gen(g, prep, None, None, None, None, None, 'h', outs, None))
            for b in range(4):
                ulist = [(0, [0, 1, 2, 3])] if g == 0 else [(t, [t]) for t in range(4)]
                for (t0, ts) in ulist:
                    def prep(sl, b=b, t0=t0, ts=ts):
                        sk = ('STG', sl)
                        TPs = (psT, psU, psEb)[sl]
                        tpk = ('psT', 'psU', 'psE')[sl]
                        rows = ck[g][b, t0:t0 + 127 * d + 1:d, :, j, :] if g > 0 else ck[g][b, :, :, j, :]
                        DMA('pool', 'kc%d' % sl, [(STGv[:, sl, 5, :], rows[:, 0, :]), (STGv[:, sl, 3, :], rows[:, 1, :])], w=[('STG5', sl), sk])
                        trs([(TPs[:, 512:640], STGv[:, sl, 5, :], IDB[:, :])], [('STG5', sl), 'IDB'], [tpk])
                        tcopy(STGv[:, sl, 1, :], TPs[:, 512:640], [tpk], [sk])
                        for n_, t in enumerate(ts):
                            ms = 4 * b + t
                            tcopy(STGv[:, sl, 0, n_:n_ + 1], qTm[:, ms:ms + 1], ['MQ'], [sk])
                            tcopy(STGv[:, sl, 2, n_:n_ + 1], kTm[:, ms:ms + 1], ['MK'], [sk])
                            DMA('sp', 'stg%d' % sl, [(STGv[n_:n_ + 1, sl, 4, :], MV[ms:ms + 1, :])], r=['MV'], w=[sk])
                        return (STGv[:, sl, 0, :], STGv[:, sl, 1, :], STGv[:, sl, 2, :], STGv[:, sl, 3, :], STGv[:, sl, 4, :], [sk])
                    outs = [(slice(n_, n_ + 1), slice(TOK + 4 * b + t, TOK + 4 * b + t + 1)) for n_, t in enumerate(ts)]
                    units.append(attn_unit_gen(g, prep, None, None, None, None, None, 's', outs, None))
            pipeline(units, UDEPTH)

        def attn_combine(slot):
            lk = [('lmtok', 0), ('lmtok', 1), ('lmtok', 2)]
            K = lambda a_: ('cmb', a_)
            Cv = lambda a_: CMB[:, a_ * 17:(a_ + 1) * 17]
            DMA('sp', 'cml', [(Cv(2 * g + k_), lm_d[g, k_].rearrange("(p c) -> p c", p=128)) for g in range(3) for k_ in range(2)],
                w=lk + [K(i) for i in range(6)])
            for g in range(3):
                act(Cv(6 + g), Cv(2 * g), AF.Ln, [K(2 * g)], [K(6 + g)])
                tt(Cv(6 + g), Cv(6 + g), Cv(2 * g + 1), ALU.add, [K(6 + g), K(2 * g + 1)], [K(6 + g)])
            tt(Cv(9), Cv(6), Cv(7), ALU.max, [K(6), K(7)], [K(9)])
            tt(Cv(9), Cv(9), Cv(8), ALU.max, [K(9), K(8)], [K(9)])
            for g in range(3):
                tt(Cv(6 + g), Cv(6 + g), Cv(9), ALU.subtract, [K(6 + g), K(9)], [K(6 + g)])
                act(Cv(6 + g), Cv(6 + g), AF.Exp, [K(6 + g)], [K(6 + g)])
            tt(Cv(10), Cv(6), Cv(7), ALU.add, [K(6), K(7)], [K(10)])
            tt(Cv(10), Cv(10), Cv(8), ALU.add, [K(10), K(8)], [K(10)])
            A('dve', lambda e, Cv=Cv: e.reciprocal(Cv(10), Cv(10)), [K(10)], [K(10)])
            for g in range(3):
                A('dve', lambda e, Cv=Cv, g=g: e.reciprocal(Cv(2 * g), Cv(2 * g)), [K(2 * g)], [K(2 * g)])
                tt(Cv(6 + g), Cv(6 + g), Cv(10), ALU.mult, [K(6 + g), K(10)], [K(6 + g)])
                tt(Cv(6 + g), Cv(6 + g), Cv(2 * g), ALU.mult, [K(6 + g), K(2 * g)], [K(6 + g)])
            DMA('sp', 'cfo', [(coef_d[g].rearrange("(p c) -> p c", p=128), Cv(6 + g)) for g in range(3)],
                r=[K(6), K(7), K(8)], w=['coef_d'])
            BW = 192
            for bi, c0 in enumerate(range(0, NTOK, BW)):
                n = min(BW, NTOK - c0)
                sl = bi % 2
                Bv = lambda g, n=n, sl=sl: CMB[:, 192 + (sl * 3 + g) * BW:192 + (sl * 3 + g) * BW + n]
                bk = lambda g, sl=sl: ('cbc', sl, g)
                DMA('sp', 'cbl%d' % sl, [(Bv(g), coef_d[g:g + 1, c0:c0 + n].partition_broadcast(128)) for g in range(3)],
                    r=['coef_d'], w=[bk(0), bk(1), bk(2)])
                for g in range(3):
                    tt(Bv(g), Bv(g), OTG[g][:, c0:c0 + n], ALU.mult, [bk(g), ('OTG', g)], [bk(g)])
                tt(Bv(0), Bv(0), Bv(1), ALU.add, [bk(0), bk(1)], [bk(0)])
                tt(YCH[:, c0:c0 + n], Bv(0), Bv(2), ALU.add, [bk(0), bk(2)], ['YCH'])
            DMA('sp', 'ych', [(oatt_d[slot], YCH[:, :])], r=['YCH'], w=[('oatt_d', slot)])

        SSTv = SST[:, :].rearrange("p (h c) -> p h c", h=8)
        hslot = [0]
        ringn = [0]

        def hgrn_tile_gen(ws, h, cols_fn, need_out, outs):
            cnt = hslot[0]
            hslot[0] += 1
            s_ = cnt % 3
            p_ = cnt % 2
            n0 = ringn[0]
            ringn[0] += 2
            wv = WB[ws][:, :].rearrange("p (k n) -> p k n", k=16)
            Z, zk = (psA, 'psA') if p_ == 0 else (psF, 'psF')
            Pb, pbk = (psB, 'psB') if p_ == 0 else (psC, 'psC')
            Po, pok = (psD, 'psD') if p_ == 0 else (psE, 'psE')
            TP, tpk = (psT, 'psT') if p_ == 0 else (psU, 'psU')
            T = HT[:, s_ * 1536:(s_ + 1) * 1536]
            Bf = HB[:, s_ * 1024:(s_ + 1) * 1024]
            k_ = lambda n: ('h%s' % n, s_)
            f_, lf, kk, bs, eb, enb = (T[:, i * 128:(i + 1) * 128] for i in range(6))
            ekl, sq, sg, dif = (T[:, i * 128:(i + 1) * 128] for i in range(6, 10))
            e1, e2 = T[:, 1280:1408], T[:, 1408:1536]
            qe, ke, kl, vb, kl1 = (Bf[:, i * 128:(i + 1) * 128] for i in range(5))
            og = qe
            qkT = Bf[:, 640:896]
            attm = Bf[:, 896:1024]
            edec = ST3[:, 2 * s_:2 * s_ + 2]
            ssq = ST3[:, 8 + 2 * s_:10 + 2 * s_]
            SRv = lambda i: SR[:, (i % 4) * 128:(i % 4) * 128 + 128]
            SRBv = lambda i: SRB[:, (i % 4) * 128:(i % 4) * 128 + 128]
            rk = lambda i: ('SR', i % 4)
            rbk = lambda i: ('SRB', i % 4)
            mms([(Z[:, :], cols_fn(kc), wv[:, kc, :], kc == 0, kc == 15) for kc in range(16)], [('WB', ws)] + ALLH, [zk])
            yield
            act(f_, Z[:, 128:256], AF.Exp, [zk], [k_('f')], scale=-1.0)
            act(e1, Z[:, 0:128], AF.Exp, [zk], [k_('e1')], scale=-1.0)
            if need_out:
                act(e2, Z[:, 384:512], AF.Exp, [zk], [k_('e2')], scale=-1.0)
            tcopy(vb, Z[:, 256:384], [zk], [k_('v')], eng='act')
            tsc(f_, f_, 1.0, None, ALU.add, ALU.bypass, [k_('f')], [k_('f')])
            A('dve', lambda e: e.reciprocal(f_, f_), [k_('f')], [k_('f')])
            tt(f_, f_, LB[:, 1024 + h * 128:1024 + (h + 1) * 128], ALU.mult, [k_('f'), 'LB'], [k_('f')])
            tt(f_, f_, LB[:, h * 128:(h + 1) * 128], ALU.add, [k_('f'), 'LB'], [k_('f')])
            act(lf, f_, AF.Ln, [k_('f')], [k_('lf')])
            tsc(kk, f_, -1.0, 1.0, ALU.mult, ALU.add, [k_('f')], [k_('k')])
            tsc(e1, e1, 1.0, None, ALU.add, ALU.bypass, [k_('e1')], [k_('e1')])
            A('dve', lambda e: e.reciprocal(e1, e1), [k_('e1')], [k_('e1')])
            tt(sq, Z[:, 0:128], e1, ALU.mult, [zk, k_('e1')], [k_('sq')])
            if need_out:
                tsc(e2, e2, 1.0, None, ALU.add, ALU.bypass, [k_('e2')], [k_('e2')])
                A('dve', lambda e: e.reciprocal(e2, e2), [k_('e2')], [k_('e2')])
                tt(sg, Z[:, 384:512], e2, ALU.mult, [zk, k_('e2')], [k_('sg')])
            yield
            mms([(Pb[:, 0:128], tri2, lf, True, True), (Pb[:, 128:256], blk2, lf, True, True),
                 (Pb[:, 256:258], lf, ind2, True, True)], [k_('lf'), 'CON'], [pbk])
            yield
            tcopy(bs, Pb[:, 0:128], [pbk], [k_('bs')])
            act(edec, Pb[:, 256:258], AF.Exp, [pbk], [k_('edec')])
            tt(dif, Pb[:, 128:256], bs, ALU.subtract, [pbk, k_('bs')], [k_('dif')])
            act(eb, bs, AF.Exp, [k_('bs')], [k_('eb')])
            act(enb, bs, AF.Exp, [k_('bs')], [k_('enb')], scale=-1.0)
            act(ekl, dif, AF.Exp, [k_('dif')], [k_('ekl')])
            tt(qe, sq, eb, ALU.mult, [k_('sq'), k_('eb')], [k_('qe')])
            tt(ke, kk, enb, ALU.mult, [k_('k'), k_('enb')], [k_('ke')])
            stt(kl, kk, ind2[:, 0:1], ekl, ALU.mult, ALU.mult, [k_('k'), k_('ekl'), 'CON'], [k_('kl')])
            stt(kl1, kk, ind2[:, 1:2], ekl, ALU.mult, ALU.mult, [k_('k'), k_('ekl'), 'CON'], [k_('kl1')])
            yield
            trs([(TP[:, 512:640], qe, IDB[:, :]), (TP[:, 640:768], ke, IDB[:, :])], [k_('qe'), k_('ke'), 'IDB'], [tpk])
            yield
            tcopy(qkT, TP[:, 512:768], [tpk], [k_('qkT')])
            yield
            mms([(Po[:, 256:384], qkT[:, 128:256], qkT[:, 0:128], True, True),
                 (Po[:, 128:256], kl, vb, True, True),
                 (Po[:, 384:512], kl1, vb, True, True)], [k_('qkT'), k_('kl'), k_('kl1'), k_('v')], [pok])
            yield
            tt(attm, Po[:, 256:384], tri2, ALU.mult, [pok, 'CON'], [k_('attm')])
            stt(SRv(n0 + 1), SRv(n0), edec[:, 0:1], Po[:, 128:256], ALU.mult, ALU.add, [pok, k_('edec'), rk(n0)], [rk(n0 + 1)])
            stt(SRv(n0 + 2), SRv(n0 + 1), edec[:, 1:2], Po[:, 384:512], ALU.mult, ALU.add, [pok, k_('edec'), rk(n0 + 1)], [rk(n0 + 2)])
            if need_out:
                tcopy(SRBv(n0), SRv(n0), [rk(n0)], [rbk(n0)], eng='act')
                tcopy(SRBv(n0 + 1), SRv(n0 + 1), [rk(n0 + 1)], [rbk(n0 + 1)], eng='act')
            if not need_out:
                return
            yield
            mms([(Po[:, 0:128], attm, vb, True, False),
                 (Po[0:64, 0:128], qkT[:, 0:64], SRBv(n0), False, True),
                 (Po[64:128, 0:128], qkT[:, 64:128], SRBv(n0 + 1), False, True)],
                [k_('attm'), k_('v'), k_('qkT'), rbk(n0), rbk(n0 + 1)], [pok])
            yield
            act(dif, Po[:, 0:128], AF.Square, [pok], [k_('dif'), k_('ssq')], accum_out=ssq[:, 0:1])
            act(ssq[:, 1:2], ssq[:, 0:1], AF.Ln, [k_('ssq')], [k_('ssq2')], scale=1.0 / 128, bias=EPS)
            act(ssq[:, 1:2], ssq[:, 1:2], AF.Exp, [k_('ssq2')], [k_('rs')], scale=-0.5)
            stt(dif, Po[:, 0:128], ssq[:, 1:2], HGW[:, :], ALU.mult, ALU.mult, [pok, k_('rs'), 'HGW'], [k_('dif')])
            tt(og, dif, sg, ALU.mult, [k_('dif'), k_('sg')], [k_('qe')])
            yield
            trs([(TP[:, 768:896], og, IDB[:, :])], [k_('qe'), 'IDB'], [tpk])
            yield
            for (buf, bk, src, dst) in outs:
                tcopy(buf[:, dst], TP[:, 768:896][:, src], [tpk], [bk], eng='act')

        def hgrn_head_tiles(ws, h, nts, need_fn, outs_fn):
            ringn[0] = 0
            tcopy(SR[:, 0:128], SSTv[:, h, :], ['SST'], [('SR', 0)])
            pipeline((hgrn_tile_gen(ws, h, (lambda kc, nt=nt: hT[:, kc, nt * 128:(nt + 1) * 128]), need_fn(nt), outs_fn(nt))
                      for nt in nts), 3)
            nf = ringn[0] % 4
            tcopy(SSTv[:, h, :], SR[:, nf * 128:nf * 128 + 128], [('SR', nf)], ['SST'])

        def hgrn_misc(ws, h):
            wv = WB[ws][:, :].rearrange("p (k n) -> p k n", k=16)
            M = MS
            mms([(psA[0:M, :], hT[:, kc, TOK:TOK + MS], wv[:, kc, :], kc == 0, kc == 15) for kc in range(16)], [('WB', ws)] + ALLH, ['psA'])
            T = HT[0:M, 0:1536]
            Bf = HB[0:M, 0:1024]
            f_, lf, kk, bs, eb, enb = (T[:, i * 128:(i + 1) * 128] for i in range(6))
            ekl, sq, sg, dif = (T[:, i * 128:(i + 1) * 128] for i in range(6, 10))
            qe, ke, kl, vb, og = (Bf[:, i * 128:(i + 1) * 128] for i in range(5))
            DMA('sp', 's0', [(S0[:, :].rearrange("p (b c) -> p b c", b=4), sthg[:, h].rearrange("b d v -> d b v"))], w=['S0'])
            tcopy(S0B[:, :], S0[:, :], ['S0'], ['S0B'])
            act(f_, psA[0:M, 128:256], AF.Sigmoid, ['psA'], ['mf'])
            tt(f_, f_, LB[0:M, 1024 + h * 128:1024 + (h + 1) * 128], ALU.mult, ['mf', 'LB'], ['mf'])
            tt(f_, f_, LB[0:M, h * 128:(h + 1) * 128], ALU.add, ['mf', 'LB'], ['mf'])
            act(lf, f_, AF.Ln, ['mf'], ['mlf'])
            tsc(kk, f_, -1.0, 1.0, ALU.mult, ALU.add, ['mf'], ['mk'])
            mms([(psB[0:M, 0:128], tri_m, lf, True, True), (psB[0:M, 128:256], blk_m, lf, True, True),
                 (psB[:, 256:260], lf, ind_m, True, True)], ['mlf', 'CON'], ['psB'])
            tcopy(bs, psB[0:M, 0:128], ['psB'], ['mbs'])
            act(eb, bs, AF.Exp, ['mbs'], ['meb'])
            act(enb, bs, AF.Exp, ['mbs'], ['menb'], scale=-1.0)
            tt(dif, psB[0:M, 128:256], bs, ALU.subtract, ['psB', 'mbs'], ['mdif'])
            act(ekl, dif, AF.Exp, ['mdif'], ['mekl'])
            edec = ST[:, 40:44]
            act(edec, psB[:, 256:260], AF.Exp, ['psB'], ['medec'])
            act(sq, psA[0:M, 0:128], AF.Silu, ['psA'], ['msq'])
            act(sg, psA[0:M, 384:512], AF.Silu, ['psA'], ['msg'])
            tt(qe, sq, eb, ALU.mult, ['msq', 'meb'], ['mqe'])
            tt(ke, kk, enb, ALU.mult, ['mk', 'menb'], ['mke'])
            tt(kl, kk, ekl, ALU.mult, ['mk', 'mekl'], ['mkl'])
            tcopy(vb, psA[0:M, 256:384], ['psA'], ['mv'], eng='act')
            trs([(psT[:, 512:512 + M], qe, IDB[0:M, 0:M]), (psT[:, 640:640 + M], ke, IDB[0:M, 0:M])], ['mqe', 'mke', 'IDB'], ['psTh'])
            qeT = HB[:, 3072:3072 + M]
            keT = HB[:, 3104:3104 + M]
            qeTm = HB[:, 3200:3200 + 4 * M].rearrange("p (b t) -> p b t", b=4)
            klm = HB[0:M, 3328:3328 + 512].rearrange("p (b c) -> p b c", b=4)
            tcopy(qeT, psT[:, 512:512 + M], ['psTh'], ['mqeT'])
            tcopy(keT, psT[:, 640:640 + M], ['psTh'], ['mkeT'])
            tt(qeTm, qeT[:, None, :].broadcast_to([128, 4, M]), colm.rearrange("p (b t) -> p b t", b=4), ALU.mult, ['mqeT', 'CON'], ['mqeTm'])
            tt(klm, kl[:, None, :].broadcast_to([M, 4, 128]), rowm[:, :, None].broadcast_to([M, 4, 128]), ALU.mult, ['mkl', 'CON'], ['mklm'])
            mms([(psD[0:M, 256:256 + M], keT, qeT, True, True)], ['mqeT', 'mkeT'], ['psDh'])
            attm = HB[0:M, 896:896 + M]
            tt(attm, psD[0:M, 256:256 + M], cmask_m, ALU.mult, ['psDh', 'CON'], ['mattm'])
            lst = [(psD[0:M, 0:128], attm, vb, True, False)]
            for b in range(4):
                lst.append((psD[0:M, 0:128], qeTm[:, b, :], S0B[:, b * 128:(b + 1) * 128], False, b == 3))
            mms(lst, ['mattm', 'mv', 'mqeTm', 'S0B'], ['psDo'])
            mms([(psE[:, b * 128:(b + 1) * 128], klm[:, b, :], vb, True, True) for b in range(4)], ['mklm', 'mv'], ['psE'])
            S0v = S0[:, :].rearrange("p (b c) -> p b c", b=4)
            tt(S0v, S0v, edec[:, :, None].broadcast_to([128, 4, 128]), ALU.mult, ['S0', 'medec', 'S0B'], ['S0'])
            tt(S0[:, :], S0[:, :], psE[:, :], ALU.add, ['S0', 'psE'], ['S0'])
            DMA('sp', 's0o', [(hgs[:, h].rearrange("b d v -> d b v"), S0v)], r=['S0'])
            act(dif, psD[0:M, 0:128], AF.Square, ['psDo'], ['mdif', 'mssq'], accum_out=ST[0:M, 36:37])
            act(ST[0:M, 37:38], ST[0:M, 36:37], AF.Sqrt, ['mssq'], ['mssq2'], scale=1.0 / 128, bias=EPS)
            A('dve', lambda e: e.reciprocal(ST[0:M, 38:39], ST[0:M, 37:38]), ['mssq2'], ['mrs'])
            stt(dif, psD[0:M, 0:128], ST[0:M, 38:39], HGW[0:M, :], ALU.mult, ALU.mult, ['psDo', 'mrs', 'HGW'], ['mdif'])
            tt(og, dif, sg, ALU.mult, ['mdif', 'msg'], ['mog'])
            trs([(psT[:, 768:768 + M], og, IDB[0:M, 0:M])], ['mog', 'IDB'], ['psTg'])
            tcopy(OHGH[:, TOK:TOK + 16], psT[:, 768:784], ['psTg'], ['OHGH'], eng='act')

        A('pool', lambda e: e.memset(SST[:, :], 0.0), [], ['SST'])
        A('pool', lambda e: e.memset(SBF[:, :], 0.0), [], ['SBF'])
        ONEROW = lsb("ONEROW", [1, 2176])
        A('pool', lambda e: e.memset(ONEROW[:, :], 1.0), [], ['ONEROW'])
        DMA('sp', 'lmi', [(lm_d[g, k_:k_ + 1, :], ONEROW[:, :]) for g in range(3) for k_ in range(2)], r=['ONEROW'],
            w=[('lmtok', 0), ('lmtok', 1), ('lmtok', 2)])
        phase_end()
        alloc_norm()
        pipeline((norm_gen(xall[nt * 128:(nt + 1) * 128, :], 128, 0, nt * 128, 128, False) for nt in range(16)), 3)
        phase_end()
        alloc_attn()
        for hh in range(12):
            attn_head(hh, 'H')
        phase_end()
        alloc_hgrn()
        OHGPv = OHGPRE[:, :].rearrange("p (h c) -> p h c", h=8)
        for h in range(8):
            ws = wload(whg_t[h], 8192)
            hgrn_head_tiles(ws, h, range(16), lambda nt: nt == 15,
                            lambda nt, h=h: [(OHGPRE, 'OHGPRE', slice(126, 128), slice(2 * h, 2 * h + 2))])
            tsc(SSTv[:, h, :], SSTv[:, h, :], CVAL[:, 0:1], None, ALU.mult, ALU.bypass, ['SST', 'CVAL'], ['SST'])
        phase_end()
        alloc_norm()
        pipeline([norm_gen(xall[(16 + nt) * 128:(17 + nt) * 128, :], 128, 0, nt * 128, 128, False) for nt in range(16)]
                 + [norm_gen(xall[32 * 128:32 * 128 + MS, :], MS, 0, TOK, MS, True)], 3)
        phase_end()
        alloc_attn()
        for slot in range(4):
            for g in range(3):
                attn_head(4 * g + slot, 'O')
            attn_combine(slot)
        phase_end()
        alloc_hgrn()
        for h in range(8):
            ws = wload(whg_t[h], 8192)
            hgrn_head_tiles(ws, h, range(16), lambda nt: True,
                            lambda nt: [(OHGH, 'OHGH', slice(0, 128), slice(nt * 128, (nt + 1) * 128))])
            DMA('sp', 'hgp', [(hgp[h], SSTv[:, h, :])], r=['SST'])
            S.fence()
            hgrn_misc(ws, h)
            S.fence()
            tcopy(OHGH[:, TOK + 16:TOK + 18], OHGPv[:, h, :], ['OHGPRE'], ['OHGH'])
            DMA('sp', 'ohgo', [(ohg_d[h], OHGH[:, :])], r=['OHGH'], w=[('ohg_d', h)])
        phase_end()
        BIGB = lsb("BIGB", [128, 12 * NTOK], BF16)
        YCH = lsb("YCHb", [128, NTOK], BF16)
        GT = lsb("GT", [128, 2 * 512])
        OAT = BIGB[:, :].rearrange("p (k t) -> p k t", k=12)
        DMA('sp', 'oal', [(OAT[:, s_, :], oatt_d[s_]) for s_ in range(4)] + [(OAT[:, 4 + h, :], ohg_d[h]) for h in range(8)], w=['OAT'])
        for fc in range(16):
            ws = wload(wB1_t[fc], 44 * 128)
            wv = WB[ws][:, 0:44 * 128].rearrange("p (k n) -> p k n", k=44)
            for bi, c0 in enumerate(range(0, NTOK, 512)):
                n = min(512, NTOK - c0)
                lst = []
                for kc in range(16):
                    lst.append((psA[:, 0:n], wv[:, kc, :], hT[:, kc, c0:c0 + n], kc == 0, kc == 15))
                for kc in range(16):
                    lst.append((psB[:, 0:n], wv[:, 16 + kc, :], hT[:, kc, c0:c0 + n], kc == 0, kc == 15))
                for kc in range(4):
                    lst.append((psC[:, 0:n], wv[:, 32 + kc, :], OAT[:, kc, c0:c0 + n], kc == 0, kc == 3))
                for kc in range(8):
                    lst.append((psD[:, 0:n], wv[:, 36 + kc, :], OAT[:, 4 + kc, c0:c0 + n], kc == 0, kc == 7))
                mms(lst, [('WB', ws), 'OAT'] + ALLH, ['psA', 'psB', 'psC', 'psD'])
                act(GT[:, 0:n], psA[:, 0:n], AF.Sigmoid, ['psA'], ['GT0'])
                act(GT[:, 512:512 + n], psB[:, 0:n], AF.Sigmoid, ['psB'], ['GT1'])
                tt(GT[:, 0:n], GT[:, 0:n], psC[:, 0:n], ALU.mult, ['GT0', 'psC'], ['GT0'])
                tt(GT[:, 512:512 + n], GT[:, 512:512 + n], psD[:, 0:n], ALU.mult, ['GT1', 'psD'], ['GT1'])
                tt(YCH[:, c0:c0 + n], GT[:, 0:n], GT[:, 512:512 + n], ALU.add, ['GT0', 'GT1'], ['YCH'])
            DMA('sp', 'ych', [(ymix_d[fc], YCH[:, :])], r=['YCH'], w=[('ymix_d', fc)])
        phase_end()
        GT = lsb("GT2", [128, 512])
        GBC = lsb("GBC", [128, 512])
        GBM = lsb("GBM", [MS, 512])
        XQ = [lsb("XQ0", [128, 512]), lsb("XQ1", [128, 512])]
        ymT = BIGA[:, :].rearrange("p (k t) -> p k t", k=16)
        DMA('sp', 'yml', [(ymT[:, fc, :], ymix_d[fc]) for fc in range(16)], w=['ymT'])

        def gate_bc(off, cb):
            DMA('sp', 'mrl', [(MR[:, :], modr_d[:, off + cb * 512:off + (cb + 1) * 512])], r=['modr_d'], w=['MR'])
            mms([(psE[:, :], ones_f[0:1, :], MR[0:1, :], True, True),
                 (psF[0:MS, :], rowsel, MR[0:5, :], True, True)], ['MR', 'CON'], ['psE', 'psF'])
            tcopy(GBC[:, :], psE[:, :], ['psE'], ['GBC'])
            tcopy(GBM[:, :], psF[0:MS, :], ['psF'], ['GBM'], eng='act')

        for cb in range(4):
            ws = wload(wo_t[cb], 8192)
            wv = WB[ws][:, :].rearrange("p (k n) -> p k n", k=16)
            gate_bc(0, cb)
            for tl in range(17):
                M = 128 if tl < 16 else MS
                P_ = psA if tl % 2 == 0 else psB
                pk = 'psA' if tl % 2 == 0 else 'psB'
                mms([(P_[0:M, :], ymT[:, kc, tl * 128:tl * 128 + M], wv[:, kc, :], kc == 0, kc == 15) for kc in range(16)],
                    [('WB', ws), 'ymT'], [pk])
                xq = XQ[tl % 2]
                xk = 'XQ%d' % (tl % 2)
                srow = (16 + tl) * 128
                DMA('sp', 'xq%d' % (tl % 2), [(xq[0:M, :], xall[srow:srow + M, cb * 512:(cb + 1) * 512])], w=[xk])
                G_ = GBC if tl < 16 else GBM
                tt(GT[0:M, 0:512], P_[0:M, :], G_[0:M, :], ALU.mult, [pk, 'GBC', 'GBM'], ['GT0'])
                tt(xq[0:M, :], xq[0:M, :], GT[0:M, 0:512], ALU.add, [xk, 'GT0'], [xk])
                DMA('sp', 'x1o%d' % (tl % 2), [(x1_d[tl * 128:tl * 128 + M, cb * 512:(cb + 1) * 512], xq[0:M, :])], r=[xk], w=[('x1_d', tl)])
        phase_end()
        alloc_norm()
        pipeline([norm_gen(x1_d[tl * 128:(tl + 1) * 128, :], 128, 1, tl * 128, 128, False) for tl in range(16)]
                 + [norm_gen(x1_d[16 * 128:16 * 128 + MS, :], MS, 1, TOK, MS, True)], 3)
        phase_end()
        BIGB = lsb("YTB", [128, NJ * 512], BF16)
        GT = lsb("GT3", [128, 512])
        GBC = lsb("GBC3", [128, 512])
        GBM = lsb("GBM3", [MS, 512])
        XQ = [lsb("XQ03", [128, 512]), lsb("XQ13", [128, 512])]
        ACAT = lsb("ACAT", [128, 520])
        UU = lsb("UU", [128, 512])
        CROW = RTMP[0:8, :]
        YT = BIGB[:, 0:NJ * 512].rearrange("p (j t) -> p j t", j=NJ)
        CWv = CW[:, :].rearrange("p (a j) -> p a j", a=4)
        CARv = CARRY[:, :].rearrange("p (j c) -> p j c", c=2)
        SCVv = SCV[:, :].rearrange("p (j c) -> p j c", c=8)
        ASVv = ASV[:, :].rearrange("p (j c) -> p j c", c=8)
        for j0 in range(0, NJ, 4):
            DMA('sp', 'crw', [(CROW[:, :], stconv[:, j0 * 128:(j0 + 4) * 128])], w=['CROW'])
            mms([(psE[:, (j - j0) * 8:(j - j0) * 8 + 8], CROW[0:8, (j - j0) * 128:(j - j0 + 1) * 128], identf[0:8, 0:8], True, True) for j in range(j0, j0 + 4)],
                ['CROW', 'CON'], ['psE'])
            tcopy(SCV[:, j0 * 8:(j0 + 4) * 8], psE[:, 0:32], ['psE'], ['SCV'])
        blocks = [('m', TOK, MS)] + [('o', b * 512, 512) for b in range(4)]
        for (kind, c0, n) in blocks:
            for j in range(NJ):
                ws = wload(wab_t[j], 4096)
                wv = WB[ws][:, 0:4096].rearrange("p (k n) -> p k n", k=32)
                s_ = j & 1
                Pa = psA if s_ == 0 else psC
                Pb = psB if s_ == 0 else psD
                pak = 'psA' if s_ == 0 else 'psC'
                pbk = 'psB' if s_ == 0 else 'psD'
                lst = [(Pa[:, 0:n], wv[:, kc, :], hT[:, kc, c0:c0 + n], kc == 0, kc == 15) for kc in range(16)]
                lst += [(Pb[:, 0:n], wv[:, 16 + kc, :], hT[:, kc, c0:c0 + n], kc == 0, kc == 15) for kc in range(16)]
                mms(lst, [('WB', ws)] + ALLH, [pak, pbk])
                ac = ACAT[:, 0:520]
                u = UU[:, 0:512]
                ack = ('ac', 0)
                uk = ('u', 0)
                w0, w1, w2, cb_ = (CWv[:, a, j:j + 1] for a in range(4))
                if kind == 'o':
                    tcopy(ac[:, 0:2], CARv[:, j, :], ['CARRY'], [ack])
                    tcopy(ac[:, 2:2 + n], Pa[:, 0:n], [pak], [ack], eng='act')
                    tsc(u[:, 0:n], ac[:, 2:2 + n], w2, cb_, ALU.mult, ALU.add, [ack, 'CW'], [uk])
                    stt(u[:, 0:n], ac[:, 1:1 + n], w1, u[:, 0:n], ALU.mult, ALU.add, [ack, 'CW', uk], [uk])
                    stt(u[:, 0:n], ac[:, 0:n], w0, u[:, 0:n], ALU.mult, ALU.add, [ack, 'CW', uk], [uk])
                    tcopy(CARv[:, j, :], ac[:, n:n + 2], [ack], ['CARRY'])
                else:
                    acv = ac[:, 0:24].rearrange("p (b t) -> p b t", b=4)
                    tcopy(acv[:, :, 0:2], SCVv[:, j, :].rearrange("p (b t) -> p b t", b=4), ['SCV'], [ack])
                    tcopy(acv[:, :, 2:6], Pa[:, 0:16].rearrange("p (b t) -> p b t", b=4), [pak], [ack])
                    uv = u[:, 0:16].rearrange("p (b t) -> p b t", b=4)
                    tsc(uv, acv[:, :, 2:6], w2, cb_, ALU.mult, ALU.add, [ack, 'CW'], [uk])
                    stt(uv, acv[:, :, 1:5], w1, uv, ALU.mult, ALU.add, [ack, 'CW', uk], [uk])
                    stt(uv, acv[:, :, 0:4], w0, uv, ALU.mult, ALU.add, [ack, 'CW', uk], [uk])
                    A('pool', lambda e, u=u: e.memset(u[:, 16:32], 0.0), [uk], [uk])
                    tcopy(ASVv[:, j, :].rearrange("p (b t) -> p b t", b=4), acv[:, :, 4:6], [ack], ['ASV'])
                    tsc(CARv[:, j, :], Pa[:, 16:18], CVAL[:, 0:1], None, ALU.mult, ALU.bypass, [pak, 'CVAL'], ['CARRY'])
                act(u[:, 0:n], u[:, 0:n], AF.Silu, [uk], [uk])
                tt(YT[:, j, 0:n], u[:, 0:n], Pb[:, 0:n], ALU.mult, [uk, pbk], ['YT'])
            ntl = (n + 127) // 128
            for cb in range(4):
                gate_bc(D, cb)
                for jg in range(4):
                    ws = wload(wd_t[cb * 4 + jg], 11 * 512)
                    wv = WB[ws][:, 0:11 * 512].rearrange("p (k n) -> p k n", k=11)
                    Ps = [psA, psB, psC, psD]
                    for tl in range(ntl):
                        M = min(128, n - tl * 128)
                        mms([(Ps[tl][0:M, :], YT[:, jg * 11 + jj, tl * 128:tl * 128 + M], wv[:, jj, :], jg == 0 and jj == 0, jg == 3 and jj == 10)
                             for jj in range(11)], [('WB', ws), 'YT'], ['psA', 'psB', 'psC', 'psD'][tl:tl + 1])
                for tl in range(ntl):
                    M = min(128, n - tl * 128)
                    gt = (c0 // 128 + tl)
                    xq = XQ[tl % 2]
                    xk = 'XQ%d' % (tl % 2)
                    pk = ['psA', 'psB', 'psC', 'psD'][tl]
                    DMA('sp', 'xq%d' % (tl % 2), [(xq[0:M, :], x1_d[gt * 128:gt * 128 + M, cb * 512:(cb + 1) * 512])], w=[xk])
                    G_ = GBC if kind == 'o' else GBM
                    tt(GT[0:M, 0:512], [psA, psB, psC, psD][tl][0:M, :], G_[0:M, :], ALU.mult, [pk, 'GBC', 'GBM'], ['GT0'])
                    tt(xq[0:M, :], xq[0:M, :], GT[0:M, 0:512], ALU.add, [xk, 'GT0'], [xk])
                    DMA('sp', 'x1o%d' % (tl % 2), [(x2_d[gt * 128:gt * 128 + M, cb * 512:(cb + 1) * 512], xq[0:M, :])], r=[xk], w=[('x2_d', gt)])
        for (src, dstd, nr, ch_) in ((ASVv, convs, 8, 'cvo0'), (CARv, convp, 2, 'cvo1')):
            for j0 in range(0, NJ, 4):
                mms([(psE[0:nr, (j - j0) * 128:(j - j0 + 1) * 128], src[:, j, :], identf, True, True) for j in range(j0, j0 + 4)],
                    ['ASV', 'CARRY', 'CON'], ['psE'])
                tcopy(CROW[0:nr, 0:512], psE[0:nr, :], ['psE'], ['CROW'])
                DMA('sp', ch_, [(dstd[:, j0 * 128:(j0 + 4) * 128], CROW[0:nr, 0:512])], r=['CROW'], w=['CROWo'])
        phase_end()
        NFW = lsb("NFW", [128, D])
        DMA('sp', 'nfw', [(NFW[:, :], normf.partition_broadcast(128))], w=['NFW'])
        for tl in range(17):
            M = 128 if tl < 16 else MS
            s_ = tl % 2
            X = XT[s_]
            xk = 'XT%d' % s_
            DMA('sp', 'x%d' % s_, [(X[0:M, :], x2_d[tl * 128:tl * 128 + M, :])], w=[xk])
            act(XN[0:M, :], X[0:M, :], AF.Square, [xk], ['XN', 'STn'], accum_out=ST[0:M, 48:49])
            act(ST[0:M, 49:50], ST[0:M, 48:49], AF.Sqrt, ['STn'], ['STn2'], scale=1.0 / D, bias=EPS)
            A('dve', lambda e, M=M: e.reciprocal(ST[0:M, 50:51], ST[0:M, 49:50]), ['STn2'], ['STn3'])
            stt(X[0:M, :], X[0:M, :], ST[0:M, 50:51], NFW[0:M, :], ALU.mult, ALU.mult, [xk, 'STn3', 'NFW'], [xk])
            dst = y_own[tl * 128:(tl + 1) * 128, :] if tl < 16 else y_misc
            DMA('sp', 'yo%d' % s_, [(dst, X[0:M, :])], r=[xk])
        for g, (W, d) in enumerate(GROUPS):
            DMA('pool', 'kvc', [(kvs[g][b, 0:W - 4], ck[g][b, 4:W]) for b in range(4)])

        if stop is not None:
            S.ops = S.ops[:stop]
        chs = sorted({o['ch'] for o in S.ops if o['ch'] is not None})
        sems_ch = {ch: es.enter_context(nc.semaphore("c_" + ch)) for ch in chs}
        sems_eng = {e_: es.enter_context(nc.semaphore("e_" + e_)) for e_ in ('pe', 'act', 'dve', 'pool')}
        block = es.enter_context(nc.Block())
        S.emit(nc, block, sems_eng, sems_ch, chs)
        ph[0].close()
    return nc


def _tile_w(Wsub, kc):
    n = Wsub.shape[1]
    return np.ascontiguousarray(Wsub.reshape(kc, 128, n).transpose(1, 0, 2).reshape(128, kc * n))


def _consts():
    C = np.zeros((128, 2048), np.float32)
    C[:, 0:128] = np.eye(128)
    s = np.arange(128)[:, None]
    t = np.arange(128)[None, :]
    same = (s // 64) == (t // 64)
    C[:, 128:256] = (same & (s <= t))
    C[:, 256:384] = same
    C[:, 384] = (np.arange(128) < 64)
    C[:, 385] = (np.arange(128) >= 64)
    C[:, 512:576] = ((np.arange(128)[:, None] % 64) <= np.arange(64)[None, :])
    s = np.arange(32)[:, None]
    t = np.arange(32)[None, :]
    samem = ((s // 4) == (t // 4)) & (s < 16) & (t < 16)
    C[0:32, 640:672] = samem & (s <= t)
    C[0:32, 672:704] = samem
    for b in range(4):
        C[4 * b:4 * b + 4, 704 + b] = 1
        C[4 * b:4 * b + 4, 768 + b] = 1
        C[:, 896 + b * 32 + 4 * b:896 + b * 32 + 4 * b + 4] = 1
    C[0:32, 736:768] = samem & (s <= t)
    for m in range(32):
        C[(1 + m // 4) if m < 16 else 0, 1024 + m] = 1
    C[:, 1152:1280] = 1
    return C


def _rope(pos):
    half = 16
    inv = (np.float32(500000.0) ** (-np.arange(half, dtype=np.float32) * np.float32(2.0) / np.float32(32))).astype(np.float32)
    ang = pos.astype(np.float32)[:, None] * inv[None, :]
    return np.concatenate([np.cos(ang), np.sin(ang)], axis=1).astype(np.float32)


_NC = None


def kernel(x_prompt, x_sample, cache_kv_w128, cache_kv_w512, cache_kv_w2048, state_hgrn, state_conv,
           c_prompt, c_sample, w_ada, b_ada, norm1_w, w_in, hg_lb, hg_norm_w, w_pa, w_pb, w_o,
           norm2_w, w_ffn_a, w_ffn_b, conv_w, conv_b, w_ffn_down, norm_f_w):
    global _NC
    in_maps = _prep(x_prompt, x_sample, cache_kv_w128, cache_kv_w512, cache_kv_w2048, state_hgrn, state_conv,
                    c_prompt, c_sample, w_ada, b_ada, norm1_w, w_in, hg_lb, hg_norm_w, w_pa, w_pb, w_o,
                    norm2_w, w_ffn_a, w_ffn_b, conv_w, conv_b, w_ffn_down, norm_f_w)
    return _run(in_maps)


def _prep(x_prompt, x_sample, cache_kv_w128, cache_kv_w512, cache_kv_w2048, state_hgrn, state_conv,
          c_prompt, c_sample, w_ada, b_ada, norm1_w, w_in, hg_lb, hg_norm_w, w_pa, w_pb, w_o,
          norm2_w, w_ffn_a, w_ffn_b, conv_w, conv_b, w_ffn_down, norm_f_w):
    f = lambda a: np.asarray(a, dtype=np.float32)
    xp = f(x_prompt)[0]
    xs = f(x_sample)
    Win = f(w_in)[0]
    shared = {}
    shared["wada_t"] = np.stack([_tile_w(f(w_ada)[0][:, j * 512:(j + 1) * 512], 16) for j in range(24)])
    shared["bada"] = f(b_ada).reshape(1, -1)
    shared["vecA"] = np.concatenate([f(norm1_w)[0].reshape(16, 128), f(norm2_w)[0].reshape(16, 128)], 0)
    shared["normf"] = f(norm_f_w).reshape(1, -1)
    shared["watt_t"] = np.stack([_tile_w(np.concatenate([Win[:, hh * 128:(hh + 1) * 128], Win[:, 1536 + hh * 128:1536 + (hh + 1) * 128],
                                                          Win[:, 3072 + hh * 128:3072 + (hh + 1) * 128]], 1), 16) for hh in range(12)])
    shared["whg_t"] = np.stack([_tile_w(np.concatenate([Win[:, 4608 + k * 1024 + h * 128:4608 + k * 1024 + (h + 1) * 128] for k in range(4)], 1), 16)
                                for h in range(8)])
    Wpa, Wpb = f(w_pa)[0], f(w_pb)[0]
    shared["wB1_t"] = np.stack([np.concatenate([_tile_w(Win[:, 8704 + fc * 128:8704 + (fc + 1) * 128], 16),
                                                _tile_w(Win[:, 10752 + fc * 128:10752 + (fc + 1) * 128], 16),
                                                _tile_w(Wpa[:, fc * 128:(fc + 1) * 128], 4),
                                                _tile_w(Wpb[:, fc * 128:(fc + 1) * 128], 8)], 1) for fc in range(16)])
    Wo = f(w_o)[0]
    shared["wo_t"] = np.stack([_tile_w(Wo[:, cb * 512:(cb + 1) * 512], 16) for cb in range(4)])
    Wa, Wb, Wd = f(w_ffn_a)[0], f(w_ffn_b)[0], f(w_ffn_down)[0]
    shared["wab_t"] = np.stack([np.concatenate([_tile_w(Wa[:, j * 128:(j + 1) * 128], 16), _tile_w(Wb[:, j * 128:(j + 1) * 128], 16)], 1)
                                for j in range(NJ)])
    shared["wd_t"] = np.stack([_tile_w(Wd[jg * 11 * 128:(jg + 1) * 11 * 128, cb * 512:(cb + 1) * 512], 11)
                               for cb in range(4) for jg in range(4)])
    cw = f(conv_w)[0]
    shared["convw"] = np.concatenate([cw[0].reshape(NJ, 128), cw[1].reshape(NJ, 128), cw[2].reshape(NJ, 128),
                                      f(conv_b)[0].reshape(NJ, 128)], 0)
    shared["hglb"] = f(hg_lb)
    shared["hgnw"] = f(hg_norm_w).reshape(1, 128)
    shared["consts"] = _consts()
    i = np.arange(128)[:, None]
    ip = np.arange(256)[None, :]
    band = np.where((ip >= i) & (ip <= i + 128), 0.0, NEG).astype(np.float32)

    in_maps = []
    for c in range(NCORE):
        m = dict(shared)
        xall = np.zeros((33 * 128, D), np.float32)
        own0 = TOK * c

        def tokrow(p):
            return xp[p] if p >= 0 else np.zeros(D, np.float32)
        if c > 0:
            xall[0:2048] = xp[own0 - 2048:own0]
        xall[2048:4096] = xp[own0:own0 + TOK]
        mb = 32 * 128
        xall[mb:mb + 16] = xs[4 * c:4 * c + 4].reshape(16, D)
        pos_m = np.zeros(MS, np.int64)
        for b in range(4):
            for t in range(4):
                pos_m[4 * b + t] = 16384 + t
        for t in range(2):
            xall[mb + 16 + t] = tokrow(own0 - 2 + t)
            pos_m[16 + t] = own0 - 2 + t
        for g, (W, d) in enumerate(GROUPS):
            for t in range(2):
                p = own0 - 2 + t - 128 * d
                xall[mb + 18 + 2 * g + t] = tokrow(p)
                pos_m[18 + 2 * g + t] = p
        m["xall"] = xall
        m["c5"] = np.concatenate([f(c_prompt), f(c_sample)[4 * c:4 * c + 4]], 0)
        rt = np.zeros((3, 32, 128, 32), np.float32)
        for g, (W, d) in enumerate(GROUPS):
            nT = 16 // d
            for r in range(d):
                for T_ in range(-1, nT):
                    pos = own0 + r + d * (128 * T_ + np.arange(128))
                    rt[g, r * (nT + 1) + T_ + 1] = _rope(pos)
        m["rope_t"] = rt
        m["rope_m"] = _rope(pos_m)
        mh = band.copy()
        if c == 0:
            mh[:, 0:128] = NEG
        m["masks"] = np.concatenate([band, mh], 1)
        m["ck128"] = f(cache_kv_w128)[0, 4 * c:4 * c + 4]
        m["ck512"] = f(cache_kv_w512)[0, 4 * c:4 * c + 4]
        m["ck2048"] = f(cache_kv_w2048)[0, 4 * c:4 * c + 4]
        m["sthg"] = f(state_hgrn)[0, 4 * c:4 * c + 4]
        m["stconv"] = f(state_conv)[0, 4 * c:4 * c + 4].reshape(8, DFF)
        m["cvalid"] = np.full((128, 1), 0.0 if c == 0 else 1.0, np.float32)
        in_maps.append(m)
    return in_maps


def _run(in_maps):
    global _NC
    if _NC is None:
        _NC = build()
    res = run_bass_kernel_spmd(_NC, in_maps, core_ids=list(range(NCORE)))
    R = res.results
    y_prompt = np.concatenate([R[c]["y_own"] for c in range(NCORE)], 0)[None]
    y_sample = np.concatenate([R[c]["y_misc"][0:16].reshape(4, 4, D) for c in range(NCORE)], 0)
    L = R[NCORE - 1]
    outs = [y_prompt, y_sample,
            L["kvp128"][None, None], L["kvp512"][None, None], L["kvp2048"][None, None],
            L["hgp"][None, None], L["convp"].reshape(1, 1, 2, DFF)]
    for nm in ("kvs128", "kvs512", "kvs2048"):
        outs.append(np.concatenate([R[c][nm] for c in range(NCORE)], 0)[None])
    outs.append(np.concatenate([R[c]["hgs"] for c in range(NCORE)], 0)[None])
    outs.append(np.concatenate([R[c]["convs"].reshape(4, 2, DFF) for c in range(NCORE)], 0)[None])
    return tuple(np.ascontiguousarray(o, dtype=np.float32) for o in outs)
```

```python
import numpy as np
import concourse.bass as bass
import concourse.mybir as mybir
from concourse.bass_utils import run_bass_kernel_spmd

F32 = mybir.dt.float32
BF16 = mybir.dt.bfloat16
AF = mybir.ActivationFunctionType
ALU = mybir.AluOpType
AX = mybir.AxisListType

D = 2048
NCORE = 8
TOK = 2048
MS = 32
NTOK = TOK + MS
DFF = 5632
NJ = 44
EPS = 1e-6
GROUPS = ((128, 1), (512, 4), (2048, 16))
NEG = -30000.0
import os as _os
UDEPTH = int(_os.environ.get("UDEPTH", "3"))


class Sched:
    def __init__(self):
        self.ops = []
        self.lw = {}
        self.rd = {}
        self.fence_deps = set()
        self.last_eng = {}
        self.last_ch = {}

    @staticmethod
    def _nk(k):
        n = k[0] if isinstance(k, tuple) else k
        if isinstance(n, str) and len(n) >= 3 and n.startswith('ps') and n[2] in 'ABCDEFTU':
            return n[:3]
        return k

    def add(self, eng, fn, r=(), w=(), ch=None):
        r = [self._nk(k) for k in r]
        w = [self._nk(k) for k in w]
        psr = [k for k in r if isinstance(k, str) and len(k) == 3 and k.startswith('ps')]
        if psr:
            r = [k for k in r if k not in psr]
            w = list(w) + psr
        deps = set(self.fence_deps)
        for k in r:
            if k in self.lw:
                deps.add(self.lw[k])
        for k in w:
            if k in self.lw:
                deps.add(self.lw[k])
            deps.update(self.rd.get(k, ()))
        i = len(self.ops)
        import sys as _s
        fr = _s._getframe(1)
        lines = []
        while fr is not None and len(lines) < 4:
            lines.append(fr.f_lineno)
            fr = fr.f_back
        self.ops.append(dict(eng=eng, fn=fn, deps=deps, ch=ch, ln=lines))
        for k in r:
            self.rd.setdefault(k, []).append(i)
        for k in w:
            self.lw[k] = i
            self.rd[k] = []
        if ch is None:
            self.last_eng[eng] = i
        else:
            self.last_ch[ch] = i
        return i

    def fence(self):
        self.fence_deps = set(self.last_eng.values()) | set(self.last_ch.values())

    def emit(self, nc, block, sems_eng, sems_ch, final_chs):
        ops = self.ops
        needed = [False] * len(ops)
        for o in ops:
            for d in o['deps']:
                dd = ops[d]
                if dd['ch'] is None and dd['eng'] == 'pe' and o['eng'] == 'pe' and o['ch'] is None:
                    continue
                needed[d] = True
        cnt = {}
        for i, o in enumerate(ops):
            if o['ch'] is not None:
                continue
            if needed[i]:
                cnt[o['eng']] = cnt.get(o['eng'], 0) + 1
                o['ms'] = cnt[o['eng']]
        chcnt = {}
        for i, o in enumerate(ops):
            if o['ch'] is not None:
                n = o['fn'].ndma
                chcnt[o['ch']] = chcnt.get(o['ch'], 0) + 16 * n
                o['val'] = chcnt[o['ch']]
        finals = {ch: chcnt[ch] for ch in final_chs if ch in chcnt}

        def run(engname, e):
            seen = {}
            for i, o in enumerate(ops):
                if o['eng'] != engname:
                    continue
                for d in sorted(o['deps']):
                    dd = ops[d]
                    if dd['ch'] is not None:
                        sem, val = sems_ch[dd['ch']], dd['val']
                    else:
                        if dd['eng'] == 'pe' and engname == 'pe' and o['ch'] is None:
                            continue
                        sem, val = sems_eng[dd['eng']], dd['ms']
                    key = id(sem)
                    if seen.get(key, 0) >= val:
                        continue
                    e.wait_ge(sem, val)
                    seen[key] = val
                if o['ch'] is not None:
                    for ins in o['fn'](e):
                        ins.then_inc(sems_ch[o['ch']], 16)
                else:
                    ins = o['fn'](e)
                    if needed[i]:
                        ins.then_inc(sems_eng[engname], 1)
            if engname == 'sp':
                for ch, v in finals.items():
                    e.wait_ge(sems_ch[ch], v)

        block.tensor(lambda e: run('pe', e))
        block.scalar(lambda e: run('act', e))
        block.vector(lambda e: run('dve', e))
        block.gpsimd(lambda e: run('pool', e))
        block.sync(lambda e: run('sp', e))


def dmafn(pairs):
    def f(e):
        return [e.dma_start(out=o, in_=i, allow_slow_non_contiguous=True) for (o, i) in pairs]
    f.ndma = len(pairs)
    return f


def build(stop=None, marks=None):
    nc = bass.Bass("TRN2", target_bir_lowering=False)
    S = Sched()

    def din(name, shape, dt=F32):
        return nc.dram_tensor(name, list(shape), dt, kind="ExternalInput").ap()

    def dout(name, shape):
        return nc.dram_tensor(name, list(shape), F32, kind="ExternalOutput").ap()

    def dscr(name, shape, dt):
        return nc.dram_tensor(name, list(shape), dt).ap()

    xall = din("xall", [33 * 128, D])
    c5 = din("c5", [5, D])
    wada_t = din("wada_t", [24, 128, 16 * 512])
    bada = din("bada", [1, 6 * D])
    vecA = din("vecA", [32, 128])
    normf = din("normf", [1, D])
    watt_t = din("watt_t", [12, 128, 16 * 384])
    whg_t = din("whg_t", [8, 128, 16 * 512])
    wB1_t = din("wB1_t", [16, 128, 44 * 128])
    wo_t = din("wo_t", [4, 128, 16 * 512])
    wab_t = din("wab_t", [NJ, 128, 32 * 128])
    wd_t = din("wd_t", [16, 128, 11 * 512])
    convw = din("convw", [4 * NJ, 128])
    hglb = din("hglb", [2, 1024])
    hgnw = din("hgnw", [1, 128])
    consts = din("consts", [128, 2048])
    rope_t = din("rope_t", [3, 32, 128, 32])
    rope_m = din("rope_m", [MS, 32])
    masks = din("masks", [128, 512])
    ck = [din("ck128", [4, 128, 2, 4, 128]), din("ck512", [4, 512, 2, 4, 128]), din("ck2048", [4, 2048, 2, 4, 128])]
    sthg = din("sthg", [4, 8, 128, 128])
    stconv = din("stconv", [8, DFF])
    cvalid = din("cvalid", [128, 1])

    y_own = dout("y_own", [TOK, D])
    y_misc = dout("y_misc", [MS, D])
    kvp = [dout("kvp128", [128, 2, 4, 128]), dout("kvp512", [512, 2, 4, 128]), dout("kvp2048", [2048, 2, 4, 128])]
    hgp = dout("hgp", [8, 128, 128])
    convp = dout("convp", [2, DFF])
    kvs = [dout("kvs128", [4, 128, 2, 4, 128]), dout("kvs512", [4, 512, 2, 4, 128]), dout("kvs2048", [4, 2048, 2, 4, 128])]
    hgs = dout("hgs", [4, 8, 128, 128])
    convs = dout("convs", [8, DFF])

    oatt_d = dscr("oatt_d", [4, 128, NTOK], BF16)
    ohg_d = dscr("ohg_d", [8, 128, NTOK], BF16)
    ymix_d = dscr("ymix_d", [16, 128, NTOK], BF16)
    x1_d = dscr("x1_d", [17 * 128, D], F32)
    x2_d = dscr("x2_d", [17 * 128, D], F32)
    modr_d = dscr("modr_d", [5, 2 * D], F32)
    hkv_d = dscr("hkv_d", [168, 128, 128], BF16)
    lm_d = dscr("lm_d", [3, 2, 2176], F32)
    coef_d = dscr("coef_d", [3, 2176], F32)
    dbg_d = dscr("dbg_d", [10, 128, 128], BF16)
    dbg2_d = dscr("dbg2_d", [128, 2], F32)

    import contextlib
    es = contextlib.ExitStack()

    def sb(name, shape, dt=F32):
        return es.enter_context(nc.sbuf_tensor(name, list(shape), dt))

    def ps(name, shape, dt=F32):
        return es.enter_context(nc.psum_tensor(name, list(shape), dt))

    ph = [contextlib.ExitStack()]

    uniq = [0]

    def lsb(name, shape, dt=F32):
        uniq[0] += 1
        return ph[0].enter_context(nc.sbuf_tensor("%s_%d" % (name, uniq[0]), list(shape), dt))

    def phase_end():
        if marks is not None:
            marks.append(len(S.ops))
        S.fence()
        ph[0].close()
        ph[0] = contextlib.ExitStack()

    with es:
        BIGA = sb("BIGA", [128, 16 * NTOK], BF16)
        WB = [sb("WB0", [128, 8192], BF16), sb("WB1", [128, 8192], BF16)]
        XT = [sb("XT0", [128, D]), sb("XT1", [128, D])]
        XN = sb("XN", [128, D], BF16)
        CON = sb("CON", [128, 1280])
        identf = CON[:, 0:128]
        tri2 = CON[:, 128:256]
        blk2 = CON[:, 256:384]
        ind2 = CON[:, 384:386]
        cmask = CON[:, 512:576]
        tri_m = CON[0:32, 640:672]
        blk_m = CON[0:32, 672:704]
        ind_m = CON[0:32, 704:708]
        cmask_m = CON[0:32, 736:768]
        rowm = CON[0:32, 768:772]
        colm = CON[:, 896:1024]
        rowsel = CON[0:5, 1024:1056]
        ones_f = CON[:, 1152:1280]
        IDB = sb("IDB", [128, 128], BF16)
        ONEB = sb("ONEB", [128, 128], BF16)
        MSK = sb("MSK", [128, 512])
        ROPEM = sb("ROPEM", [MS, 32])
        MODT = sb("MODT", [128, 96 * 5])
        MR = sb("MR", [5, 512])
        SC = sb("SC", [128, 4 * 16])
        SCM = sb("SCM", [128, 4 * 16 * MS])
        NW = sb("NW", [128, 32])
        CW = sb("CW", [128, 4 * NJ])
        ST = sb("ST", [128, 64])
        STT = sb("STT", [128, 16 * 5], BF16)
        HGW = sb("HGW", [128, 128])
        CVAL = sb("CVAL", [128, 1])
        RTMP = sb("RTMP", [128, 512])
        CARRY = sb("CARRY", [128, NJ * 2])
        SCV = sb("SCV", [128, NJ * 8])
        ASV = sb("ASV", [128, NJ * 8])
        SST = sb("SST", [128, 8 * 128])
        SBF = sb("SBF", [128, 128], BF16)
        OHGPRE = sb("OHGPRE", [128, 16], BF16)
        Lc = {}

        psA = ps("psA", [128, 512])
        psB = ps("psB", [128, 512])
        psC = ps("psC", [128, 512])
        psD = ps("psD", [128, 512])
        psE = ps("psE", [128, 512])
        psF = ps("psF", [128, 512])
        psT = ps("psT", [128, 1024], BF16)
        psU = ps("psU", [128, 1024], BF16)

        hT = BIGA[:, :].rearrange("p (k t) -> p k t", k=16)
        psEb = psE[:, :].bitcast(BF16)

        def A(eng, fn, r=(), w=()):
            return S.add(eng, fn, r, w)

        def DMA(q, ch, pairs, r=(), w=()):
            return S.add(q, dmafn(pairs), r, w, ch=ch)

        def act(out, in_, func, r, w, **kw):
            A('act', lambda e: e.activation(out=out, in_=in_, func=func, **kw), r, w)

        def tt(out, a, b, op, r, w, eng='dve'):
            A(eng, lambda e: e.tensor_tensor(out, a, b, op=op), r, w)

        def tcopy(out, in_, r, w, eng='dve'):
            if eng == 'act':
                A('act', lambda e: e.copy(out=out, in_=in_), r, w)
            else:
                A(eng, lambda e: e.tensor_copy(out, in_), r, w)

        def tsc(out, a, s1, s2, op0, op1, r, w):
            A('dve', lambda e: e.tensor_scalar(out, a, s1, s2, op0=op0, op1=op1), r, w)

        def stt(out, a, s, b, op0, op1, r, w):
            A('dve', lambda e: e.scalar_tensor_tensor(out, a, s, b, op0=op0, op1=op1), r, w)

        def mms(lst, r, w):
            def f(e):
                ins = None
                for (o, l, rr, st, sp) in lst:
                    ins = e.matmul(o, l, rr, start=st, stop=sp)
                return ins
            A('pe', f, r, w)

        def trs(lst, r, w):
            def f(e):
                ins = None
                for (o, i, idn) in lst:
                    ins = e.transpose(o, i, idn)
                return ins
            A('pe', f, r, w)

        wslot = [0]

        def wload(src, ncols):
            s_ = wslot[0]
            wslot[0] ^= 1
            DMA('pool', 'w%d' % s_, [(WB[s_][:, 0:ncols], src)], w=[('WB', s_)])
            return s_

        DMA('sp', 'c0', [(CON[:, :], consts[:, 0:1280]), (MSK[:, :], masks), (ROPEM[:, :], rope_m),
                         (CVAL[:, :], cvalid), (HGW[:, :], hgnw.partition_broadcast(128))],
            w=['CON', 'MSK', 'ROPEM', 'CVAL', 'HGW'])
        S5 = lsb("S5", [5, D], BF16)
        BADA = lsb("BADA", [1, 6 * D], BF16)
        DMA('pool', 'c3', [(IDB[:, :], consts[:, 0:128]), (ONEB[:, :], consts[:, 1152:1280]), (BADA[:, :], bada)],
            w=['IDB', 'ONEB', 'BADA'])
        DMA('sp', 'c4', [(XT[0][0:5, :], c5)], w=['XT0'])
        act(S5[:, :], XT[0][0:5, :], AF.Silu, ['XT0'], ['S5'])
        trs([(psT[:, kc * 8:kc * 8 + 5], S5[0:5, kc * 128:(kc + 1) * 128], IDB[0:5, 0:5]) for kc in range(16)],
            ['S5', 'IDB'], ['psT'])
        tcopy(STT[:, :].rearrange("p (k e) -> p k e", e=5), psT[:, 0:128].rearrange("p (k e) -> p k e", e=8)[:, :, 0:5], ['psT'], ['STT'])
        DMA('sp', 'c5', [(XT[1][0:32, 0:128], vecA)], w=['XT1'])
        mms([(psB[:, 0:32], XT[1][0:32, 0:128], identf[0:32, 0:32], True, True)], ['XT1', 'CON'], ['psB'])
        tcopy(NW[:, :], psB[:, 0:32], ['psB'], ['NW'])
        DMA('sp', 'c6', [(XT[1][0:88, 128:256], convw[0:88, :]), (XT[1][0:88, 256:384], convw[88:176, :])], w=['XT1b'])
        mms([(psB[:, 64:152], XT[1][0:88, 128:256], identf[0:88, 0:88], True, True),
             (psB[:, 152:240], XT[1][0:88, 256:384], identf[0:88, 0:88], True, True)], ['XT1b', 'CON'], ['psBb'])
        tcopy(CW[:, :], psB[:, 64:240], ['psBb'], ['CW'])

        for j in range(24):
            s_ = wload(wada_t[j], 8192)
            wv = WB[s_][:, :].rearrange("p (k n) -> p k n", k=16)
            lst = []
            for cb in range(4):
                col = j * 4 + cb
                o = psA[:, col * 5:(col + 1) * 5]
                for kc in range(16):
                    lst.append((o, wv[:, kc, cb * 128:(cb + 1) * 128], STT[:, kc * 5:(kc + 1) * 5], kc == 0, False))
                lst.append((o, BADA[0:1, col * 128:(col + 1) * 128], ONEB[0:1, 0:5], False, True))
            mms(lst, [('WB', s_), 'STT', 'BADA', 'ONEB'], ['psA'])
            kind = j // 4
            if kind in (2, 5):
                lst = [(psC[0:5, :], STT[:, kc * 5:(kc + 1) * 5], wv[:, kc, :], kc == 0, False) for kc in range(16)]
                lst.append((psC[0:5, :], ONEB[0:1, 0:5], BADA[0:1, j * 512:(j + 1) * 512], False, True))
                mms(lst, [('WB', s_), 'STT', 'BADA', 'ONEB'], ['psC'])
                off = (0 if kind == 2 else D) + (j % 4) * 512
                tcopy(MR[:, :], psC[0:5, :], ['psC'], ['MR'])
                DMA('sp', 'mro', [(modr_d[:, off:off + 512], MR[:, :])], r=['MR'], w=['modr_d'])
        tcopy(MODT[:, :], psA[:, 0:480], ['psA'], ['MODT'])
        modT = MODT[:, :].rearrange("p (c r) -> p c r", r=5)
        SCv = SC[:, :].rearrange("p (a k) -> p a k", a=4)
        SCMv = SCM[:, :].rearrange("p (a k m) -> p a k m", a=4, k=16)
        for n_, (ksh, ksc, nwo) in enumerate(((0, 1, 0), (3, 4, 16))):
            tsc(ST[:, 0:16], modT[:, ksc * 16:(ksc + 1) * 16, 0], 1.0, None, ALU.add, ALU.bypass, ['MODT'], ['STa'])
            tt(SCv[:, 2 * n_, :], ST[:, 0:16], NW[:, nwo:nwo + 16], ALU.mult, ['STa', 'NW'], ['SC'])
            tcopy(SCv[:, 2 * n_ + 1, :], modT[:, ksh * 16:(ksh + 1) * 16, 0], ['MODT'], ['SC'])
            for b in range(4):
                tsc(ST[:, 16:32], modT[:, ksc * 16:(ksc + 1) * 16, 1 + b], 1.0, None, ALU.add, ALU.bypass, ['MODT'], ['STb'])
                tt(ST[:, 32:48], ST[:, 16:32], NW[:, nwo:nwo + 16], ALU.mult, ['STb', 'NW'], ['STc'])
                tcopy(SCMv[:, 2 * n_, :, 4 * b:4 * b + 4], ST[:, 32:48, None].broadcast_to([128, 16, 4]), ['STc'], ['SCM'])
                tcopy(SCMv[:, 2 * n_ + 1, :, 4 * b:4 * b + 4],
                      modT[:, ksh * 16:(ksh + 1) * 16, 1 + b:2 + b].broadcast_to([128, 16, 4]), ['MODT'], ['SCM'])
            tcopy(SCMv[:, 2 * n_, :, 16:32], SCv[:, 2 * n_, :, None].broadcast_to([128, 16, 16]), ['SC'], ['SCM'])
            tcopy(SCMv[:, 2 * n_ + 1, :, 16:32], SCv[:, 2 * n_ + 1, :, None].broadcast_to([128, 16, 16]), ['SC'], ['SCM'])

        phase_end()
        xslot = [0]

        XN2 = RT2 = None

        def alloc_norm():
            nonlocal XN2, RT2
            XN2 = lsb("XN2", [128, D], BF16)
            RT2 = lsb("RT2", [128, 512])

        def norm_gen(src_rows, nrows, n_, dst_cols, ncols, misc):
            s_ = xslot[0]
            xslot[0] ^= 1
            xk = 'XT%d' % s_
            X = XT[s_]
            XNs = XN if s_ == 0 else XN2
            RTs = RTMP if s_ == 0 else RT2
            nk = ('XN', s_)
            stc = ST[:, 48 + 3 * s_:51 + 3 * s_]
            sk = lambda i: ('STn', s_, i)
            DMA('sp', 'x%d' % s_, [(X[0:nrows, :], src_rows)], w=[xk])
            yield
            act(XNs[0:nrows, :], X[0:nrows, :], AF.Square, [xk], [nk, sk(0)], accum_out=stc[0:nrows, 0:1])
            act(stc[0:nrows, 1:2], stc[0:nrows, 0:1], AF.Sqrt, [sk(0)], [sk(1)], scale=1.0 / D, bias=EPS)
            A('dve', lambda e: e.reciprocal(stc[0:nrows, 2:3], stc[0:nrows, 1:2]), [sk(1)], [sk(2)])
            act(XNs[0:nrows, :], X[0:nrows, :], AF.Copy, [xk, sk(2), nk], [nk], scale=stc[0:nrows, 2:3])
            yield
            for half in range(2):
                P_ = psT if half == 0 else psU
                pk = 'psT' if half == 0 else 'psU'
                trs([(P_[:, q * ncols:(q + 1) * ncols], XNs[0:nrows, (half * 8 + q) * 128:(half * 8 + q + 1) * 128],
                      IDB[0:nrows, 0:nrows]) for q in range(8)], [nk, 'IDB'], [pk])
            yield
            for half in range(2):
                P_ = psT if half == 0 else psU
                pk = 'psT' if half == 0 else 'psU'
                pv = P_[:, 0:8 * ncols].rearrange("p (k t) -> p k t", k=8)
                dst = hT[:, half * 8:half * 8 + 8, dst_cols:dst_cols + ncols]
                if not misc:
                    sc_ = SCv[:, 2 * n_, half * 8:half * 8 + 8, None].broadcast_to([128, 8, ncols])
                    sh_ = SCv[:, 2 * n_ + 1, half * 8:half * 8 + 8, None].broadcast_to([128, 8, ncols])
                else:
                    sc_ = SCMv[:, 2 * n_, half * 8:half * 8 + 8, :]
                    sh_ = SCMv[:, 2 * n_ + 1, half * 8:half * 8 + 8, :]
                tmp = RTs[:, :].bitcast(BF16)[:, 0:8 * ncols].rearrange("p (k t) -> p k t", k=8)
                tt(tmp, pv, sc_, ALU.mult, [pk, 'SC', 'SCM'], [('RTh', s_)])
                tt(dst, tmp, sh_, ALU.add, [('RTh', s_), 'SC', 'SCM'], [('hT', dst_cols // 128)])

        def hkeys(c0, c1):
            return [('hT', t) for t in range(c0 // 128, (c1 - 1) // 128 + 1)]

        ALLH = [('hT', t) for t in range(17)]

        ST2 = None
        QT = KT = VA = OTG = ATMP = APB = ASTAGE = MQK = MV = KVF = QKB = ROPE = HKS = LMS = CMB = None
        LB = ST3 = SR = SRB = None
        HT = HB = OHGH = S0 = S0B = YCH = GT = GBC = GBM = XQ = ACAT = UU = CROW = BIGB = NFW = None
        ropev = QTv = KTv = VAv = STG = None

        def alloc_attn():
            nonlocal ST2, QT, KT, VA, OTG, ATMP, APB, ASTAGE, MQK, MV, KVF, QKB, ROPE, HKS, LMS, CMB, YCH
            nonlocal ropev, QTv, KTv, VAv, STG
            QT = lsb("QT", [128, 16 * 128], BF16)
            KT = lsb("KT", [128, 17 * 128], BF16)
            VA = lsb("VA", [128, 17 * 128], BF16)
            OTG = [lsb("OTG%d" % g, [128, NTOK], BF16) for g in range(3)]
            ST2 = lsb("ST2", [128, 64])
            APB = lsb("APB", [128, 3 * 512], BF16)
            ASTAGE = lsb("ASTAGE", [128, 3 * 6 * 128], BF16)
            MQK = lsb("MQK", [128, 2 * MS], BF16)
            MV = lsb("MV", [MS, 128], BF16)
            KVF = lsb("KVF", [128, 2 * 256])
            QKB = lsb("QKB", [128, 2 * 256], BF16)
            ROPE = lsb("ROPE", [128, 32 * 32])
            HKS = lsb("HKS", [128, 3 * 256], BF16)
            LMS = lsb("LMS", [1, 768])
            CMB = lsb("CMB", [128, 192 + 6 * 192])
            YCH = lsb("YCHa", [128, NTOK], BF16)
            ropev = ROPE[:, :].rearrange("p (t c) -> p t c", t=32)
            QTv = QT[:, :].rearrange("p (t c) -> p t c", c=128)
            KTv = KT[:, :].rearrange("p (t c) -> p t c", c=128)
            VAv = VA[:, :].rearrange("p (t c) -> p t c", c=128)
            STG = ASTAGE[:, :].rearrange("p (t c) -> p t c", c=128)
            A('pool', lambda e, t=ASTAGE: e.memset(t[:, :], 0.0), [], [('STG', 0), ('STG', 1), ('STG', 2), ('STG5', 0), ('STG5', 1), ('STG5', 2)])
            A('pool', lambda e, t=HKS: e.memset(t[:, :], 0.0), [], [('HKS', 0), ('HKS', 1), ('HKS', 2)])
            for g in range(3):
                A('pool', lambda e, t=OTG[g]: e.memset(t[:, :], 0.0), [], [('OTG', g)])

        def alloc_hgrn():
            nonlocal HT, HB, OHGH, S0, S0B, LB, ST3, SR, SRB
            LB = lsb("LB", [128, 2 * 1024])
            DMA('sp', 'c2', [(LB[:, 0:1024], hglb[0:1, :].partition_broadcast(128)),
                             (LB[:, 1024:2048], hglb[1:2, :].partition_broadcast(128))], w=['LB'])
            tt(LB[:, 0:1024], LB[:, 0:1024], LB[:, 1024:2048], ALU.subtract, ['LB'], ['LB'])
            act(LB[:, 0:1024], LB[:, 0:1024], AF.Sigmoid, ['LB'], ['LB'])
            tsc(LB[:, 1024:2048], LB[:, 0:1024], -1.0, 1.0, ALU.mult, ALU.add, ['LB'], ['LB'])
            HT = lsb("HT", [128, 3 * 1536])
            HB = lsb("HB", [128, 3 * 1024 + 1024], BF16)
            ST3 = lsb("ST3", [128, 16])
            SR = lsb("SR", [128, 4 * 128])
            SRB = lsb("SRB", [128, 4 * 128], BF16)
            OHGH = lsb("OHGH", [128, NTOK], BF16)
            S0 = lsb("S0", [128, 4 * 128])
            S0B = lsb("S0B", [128, 4 * 128], BF16)
            A('pool', lambda e, t=OHGH: e.memset(t[:, :], 0.0), [], ['OHGH'])

        def hkv_idx(hh):
            g = hh // 4
            base = 0
            for h2 in range(hh):
                base += 2 * GROUPS[h2 // 4][1]
            return base

        def pipeline(gens, depth):
            active = []
            it = iter(gens)
            done = False
            while True:
                if not done and len(active) < depth:
                    try:
                        active.append(next(it))
                    except StopIteration:
                        done = True
                if not active and done:
                    break
                for g_ in list(active):
                    try:
                        next(g_)
                    except StopIteration:
                        active.remove(g_)

        uslot = [0]

        def proj_gen(ws, cols_ap_fn, M, g, tidx, misc, want_q, kdst, vdst, qdst, kvout, dkeys):
            ks = uslot[0] & 1
            uslot[0] += 1
            Z = psA if ks == 0 else psF
            zk = 'psA' if ks == 0 else 'psF'
            TP = psT if ks == 0 else psU
            tpk = 'psT' if ks == 0 else 'psU'
            wv = WB[ws][:, 0:16 * 384].rearrange("p (k n) -> p k n", k=16)
            mms([(Z[0:M, 0:384], cols_ap_fn(kc), wv[:, kc, :], kc == 0, kc == 15) for kc in range(16)],
                [('WB', ws)] + ALLH, [zk])
            yield
            cosv = (ROPEM[0:M, 0:16] if misc else ropev[0:M, tidx, 0:16])
            sinv = (ROPEM[0:M, 16:32] if misc else ropev[0:M, tidx, 16:32])
            zqk = Z[0:M, 0:256].rearrange("p (a c) -> p a c", a=2)
            x1 = zqk[:, :, 0:16]
            x2 = zqk[:, :, 16:32]
            cb_ = cosv[:, None, :].broadcast_to([M, 2, 16])
            sb_ = sinv[:, None, :].broadcast_to([M, 2, 16])
            T = RTMP[0:M, ks * 256:ks * 256 + 128].rearrange("p (q a c) -> p q a c", q=4, a=2)
            QF = RTMP[0:M, ks * 256 + 128:ks * 256 + 256]
            rt = lambda i: ('RT', ks, i)
            tt(T[:, 0], x1, cb_, ALU.mult, [zk, 'ROPE', 'ROPEM'], [rt(0)])
            tt(T[:, 1], x2, sb_, ALU.mult, [zk, 'ROPE', 'ROPEM'], [rt(1)])
            tt(T[:, 2], x2, cb_, ALU.mult, [zk, 'ROPE', 'ROPEM'], [rt(2)])
            tt(T[:, 3], x1, sb_, ALU.mult, [zk, 'ROPE', 'ROPEM'], [rt(3)])
            KF = KVF[0:M, ks * 256:ks * 256 + 256]
            kfk = ('KVF', ks)
            tcopy(KF[:, 128:256], Z[0:M, 256:384], [zk], [kfk], eng='act')
            tt(KF[:, 0:16], T[:, 0, 1], T[:, 1, 1], ALU.subtract, [rt(0), rt(1)], [kfk])
            tt(KF[:, 16:32], T[:, 2, 1], T[:, 3, 1], ALU.add, [rt(2), rt(3)], [kfk])
            tcopy(KF[:, 32:128], Z[0:M, 160:256], [zk], [kfk])
            qb = QKB[0:M, ks * 256:ks * 256 + 128]
            kb = QKB[0:M, ks * 256 + 128:ks * 256 + 256]
            qk_ = ('QKB', ks)
            tcopy(vdst, KF[:, 128:256], [kfk], [dkeys[1]], eng='act')
            tcopy(kb, KF[:, 0:128], [kfk], [qk_])
            if want_q:
                tt(QF[:, 0:16], T[:, 0, 0], T[:, 1, 0], ALU.subtract, [rt(0), rt(1)], [('QF', ks)])
                tt(QF[:, 16:32], T[:, 2, 0], T[:, 3, 0], ALU.add, [rt(2), rt(3)], [('QF', ks)])
                tcopy(QF[:, 32:128], Z[0:M, 32:128], [zk], [('QF', ks)])
                tcopy(qb, QF[:, 0:128], [('QF', ks)], [qk_])
            yield
            lst = [(TP[:, 128:128 + M], kb, IDB[0:M, 0:M])]
            if want_q:
                lst.append((TP[:, 0:M], qb, IDB[0:M, 0:M]))
            trs(lst, [qk_, 'IDB'], [tpk])
            if kvout is not None:
                DMA('sp', 'kvo%d' % ks, kvout(KF), r=[kfk])
            yield
            tcopy(kdst, TP[:, 128:128 + M], [tpk], [dkeys[0]])
            if want_q:
                tcopy(qdst, TP[:, 0:M], [tpk], [dkeys[2]], eng='act')

        aslot = [0]
        dbgsel = [0]

        def attn_unit_gen(g, prep, qT, kp, kc_, vp, vc, mask, outs, rk):
            s_ = aslot[0] % 3
            aslot[0] += 1
            if prep is not None:
                qT, kp, kc_, vp, vc, rk = prep(s_)
                yield
            stt_ = ST2[:, 4 * s_:4 * s_ + 2]
            pb = APB[:, s_ * 512:s_ * 512 + 256]
            pT = APB[:, s_ * 512 + 256:s_ * 512 + 512]
            P_ = (psB, psC, psD)[s_]
            pk = ('psB', 'psC', 'psD')[s_]
            TP = (psT, psU, psEb)[s_]
            tpk = ('psT', 'psU', 'psE')[s_]
            mask = mask_std if mask == 's' else mask_halo
            mms([(P_[:, 0:128], qT, kp, True, True), (P_[:, 128:256], qT, kc_, True, True)], rk, [pk])
            yield
            stt(P_[:, 0:256], P_[:, 0:256], 128.0 ** -0.5, mask, ALU.mult, ALU.add, [pk, 'MSK'], [pk])
            A('dve', lambda e: e.reduce_max(stt_[:, 0:1], P_[:, 0:256], axis=AX.X), [pk], [('mx', s_)])
            tsc(stt_[:, 1:2], stt_[:, 0:1], -1.0, None, ALU.mult, ALU.bypass, [('mx', s_)], [('nmx', s_)])
            act(pb, P_[:, 0:256], AF.Exp, [pk, ('nmx', s_)], [('pb', s_)], bias=stt_[:, 1:2])
            if dbgsel[0] == 1:
                dbgsel[0] = 2
                DMA('sp', 'dbgb', [(dbg_d[7], pb[:, 0:128]), (dbg_d[8], pb[:, 128:256]), (dbg2_d, stt_)], r=[('pb', s_), ('mx', s_), ('nmx', s_)])
            yield
            trs([(TP[:, 256:384], pb[:, 0:128], IDB[:, :]), (TP[:, 384:512], pb[:, 128:256], IDB[:, :])],
                [('pb', s_), 'IDB'], [tpk])
            yield
            tcopy(pT, TP[:, 256:512], [tpk], [('pT', s_)], eng='act')
            yield
            mms([(P_[:, 256:384], vp, pT[:, 0:128], True, False), (P_[:, 256:384], vc, pT[:, 128:256], False, True),
                 (P_[0:1, 384:512], ONEB[:, 0:1], pT[:, 0:128], True, False),
                 (P_[0:1, 384:512], ONEB[:, 0:1], pT[:, 128:256], False, True),
                 (P_[0:1, 0:128], stt_[:, 0:1], identf, True, True)],
                rk + [('pT', s_), ('mx', s_), 'ONEB', 'CON'], [pk])
            if dbgsel[0] == 2:
                dbgsel[0] = 3
                DMA('sp', 'dbg', [(dbg_d[0], vp), (dbg_d[1], vc), (dbg_d[2], pT[:, 0:128]), (dbg_d[3], pT[:, 128:256]),
                                  (dbg_d[4], kp), (dbg_d[5], kc_), (dbg_d[6], qT)], r=rk + [('pT', s_)])
            yield
            tcopy(LMS[0:1, s_ * 256:s_ * 256 + 128], P_[0:1, 384:512], [pk], [('LMS', s_)], eng='act')
            tcopy(LMS[0:1, s_ * 256 + 128:s_ * 256 + 256], P_[0:1, 0:128], [pk], [('LMS', s_)], eng='act')
            prs = []
            for (src, dst) in outs:
                tcopy(OTG[g][:, dst], P_[:, 256:384][:, src], [pk], [('OTG', g)])
                prs.append((lm_d[g, 0:1, dst], LMS[0:1, s_ * 256:s_ * 256 + 128][:, src]))
                prs.append((lm_d[g, 1:2, dst], LMS[0:1, s_ * 256 + 128:s_ * 256 + 256][:, src]))
            DMA('sp', 'lmo%d' % s_, prs, r=[('LMS', s_), ('lmtok', g)])

        mask_std = MSK[:, 0:256]
        mask_halo = MSK[:, 256:512]

        def attn_head(hh, sweep):
            g, j = hh // 4, hh % 4
            W, d = GROUPS[g]
            nT = 16 // d
            ws = wload(watt_t[hh], 16 * 384)
            hb = hkv_idx(hh)
            DMA('sp', 'c1', [(ropev, rope_t[g].rearrange("t p c -> p t c"))], w=['ROPE'])
            HKSv = HKS[:, :].rearrange("p (s c) -> p s c", s=3)
            if sweep == 'H':
                def hgen(r):
                    st_ = TOK - 128 * d + r
                    sl = r % 3
                    yield from proj_gen(ws, lambda kc, st_=st_: hT[:, kc, st_:st_ + 128 * d:d], 128, g, r * (nT + 1), False, False,
                                        HKSv[:, sl, 0:128], HKSv[:, sl, 128:256], None, None, [('HKS', sl), ('HKS', sl), None])
                    DMA('sp', 'hkvo%d' % sl, [(hkv_d[hb + r], HKSv[:, sl, 0:128]), (hkv_d[hb + d + r], HKSv[:, sl, 128:256])],
                        r=[('HKS', sl)], w=[('hkv_d', hh)])
                pipeline((hgen(r) for r in range(d)), 2)
                return
            gens = []
            for r in range(d):
                for T_ in range(nT):
                    ti = r * nT + T_
                    st_ = r + d * 128 * T_
                    o0 = st_ - (TOK - W)
                    kvout = None
                    if o0 >= 0:
                        def kvout(KF, o0=o0, d=d, g=g, j=j):
                            return [(kvp[g][o0:o0 + 127 * d + 1:d, 0, j, :], KF[:, 0:128]),
                                    (kvp[g][o0:o0 + 127 * d + 1:d, 1, j, :], KF[:, 128:256])]
                    gens.append(proj_gen(ws, lambda kc, st_=st_: hT[:, kc, st_:st_ + 128 * d:d], 128, g, r * (nT + 1) + T_ + 1, False, True,
                                         KTv[:, ti, :], VAv[:, ti, :], QTv[:, ti, :], kvout, [('KT', ti), ('VA', ti), ('QT', ti)]))

            def kvout_m(KF, g=g, j=j, W=W):
                prs = []
                for b in range(4):
                    prs.append((kvs[g][b, W - 4:W, 0, j, :], KF[4 * b:4 * b + 4, 0:128]))
                    prs.append((kvs[g][b, W - 4:W, 1, j, :], KF[4 * b:4 * b + 4, 128:256]))
                return prs
            gens.append(proj_gen(ws, lambda kc: hT[:, kc, TOK:TOK + MS], MS, g, 0, True, True,
                                 MQK[:, MS:2 * MS], MV[:, :], MQK[:, 0:MS], kvout_m, ['MK', 'MV', 'MQ']))
            pipeline(gens, 2)
            qTm = MQK[:, 0:MS]
            kTm = MQK[:, MS:2 * MS]
            STGv = ASTAGE[:, :].rearrange("p (s t c) -> p s t c", s=3, t=6)
            units = []
            for r in range(d):
                for T_ in range(nT):
                    ti = r * nT + T_
                    st_ = r + d * 128 * T_
                    outs = [(slice(0, 128), slice(st_, st_ + 128 * d, d))]
                    if T_ == 0:
                        def prep(sl, ti=ti, r=r):
                            if _os.environ.get("DBGF") == "1":
                                S.fence()
                            DMA('sp', 'hkvl%d' % sl, [(HKSv[:, sl, 0:128], hkv_d[hb + r]), (HKSv[:, sl, 128:256], hkv_d[hb + d + r])],
                                r=[('hkv_d', hh)], w=[('HKS', sl)])
                            return (QTv[:, ti, :], HKSv[:, sl, 0:128], KTv[:, ti, :], HKSv[:, sl, 128:256], VAv[:, ti, :],
                                    [('QT', ti), ('KT', ti), ('VA', ti), ('HKS', sl)])
                        units.append(attn_unit_gen(g, prep, None, None, None, None, None, 'h', outs, None))
                    else:
                        units.append(attn_unit_gen(g, None, QTv[:, ti, :], KTv[:, ti - 1, :], KTv[:, ti, :], VAv[:, ti - 1, :], VAv[:, ti, :],
                                                   's', outs, [('QT', ti), ('KT', ti), ('VA', ti), ('KT', ti - 1), ('VA', ti - 1)]))
            pre = [(0, [126, 127], [16, 17], [(126, 18), (127, 19)])] if g == 0 else \
                  [((d - 2 + t), [127], [16 + t], [(127, 18 + 2 * g + t)]) for t in range(2)]
            for (r, qrows, qslots, extras) in pre:
                def prep(sl, r=r, qrows=qrows, qslots=qslots, extras=extras):
                    sk = ('STG', sl)
                    for qr, qs in zip(qrows, qslots):
                        tcopy(STGv[:, sl, 0, qr:qr + 1], qTm[:, qs:qs + 1], ['MQ'], [sk])
                    for (sl_, ms) in extras:
                        tcopy(STGv[:, sl, 1, sl_:sl_ + 1], kTm[:, ms:ms + 1], ['MK'], [sk])
                        DMA('sp', 'stg%d' % sl, [(STGv[sl_:sl_ + 1, sl, 3, :], MV[ms:ms + 1, :])], r=['MV'], w=[sk])
                    DMA('sp', 'hkvl%d' % sl, [(HKSv[:, sl, 0:128], hkv_d[hb + r]), (HKSv[:, sl, 128:256], hkv_d[hb + d + r])],
                        r=[('hkv_d', hh)], w=[('HKS', sl)])
                    return (STGv[:, sl, 0, :], STGv[:, sl, 1, :], HKSv[:, sl, 0:128], STGv[:, sl, 3, :], HKSv[:, sl, 128:256],
                            [sk, ('HKS', sl)])
                outs = [(slice(qr, qr + 1), slice(TOK + qs, TOK + qs + 1)) for qr, qs in zip(qrows, qslots)]
                units.append(attn_unit_gen(g, prep, None, None, None, None, None, 'h', outs, None))
            for b in range(4):
                ulist = [(0, [0, 1, 2, 3])] if g == 0 else [(t, [t]) for t in range(4)]
                for (t0, ts) in ulist:
                    def prep(sl, b=b, t0=t0, ts=ts):
                        sk = ('STG', sl)
                        TPs = (psT, psU, psEb)[sl]
                        tpk = ('psT', 'psU', 'psE')[sl]
                        rows = ck[g][b, t0:t0 + 127 * d + 1:d, :, j, :] if g > 0 else ck[g][b, :, :, j, :]
                        DMA('pool', 'kc%d' % sl, [(STGv[:, sl, 5, :], rows[:, 0, :]), (STGv[:, sl, 3, :], rows[:, 1, :])], w=[('STG5', sl), sk])
                        trs([(TPs[:, 512:640], STGv[:, sl, 5, :], IDB[:, :])], [('STG5', sl), 'IDB'], [tpk])
                        tcopy(STGv[:, sl, 1, :], TPs[:, 512:640], [tpk], [sk])
                        for n_, t in enumerate(ts):
                            ms = 4 * b + t
                            tcopy(STGv[:, sl, 0, n_:n_ + 1], qTm[:, ms:ms + 1], ['MQ'], [sk])
                            tcopy(STGv[:, sl, 2, n_:n_ + 1], kTm[:, ms:ms + 1], ['MK'], [sk])
                            DMA('sp', 'stg%d' % sl, [(STGv[n_:n_ + 1, sl, 4, :], MV[ms:ms + 1, :])], r=['MV'], w=[sk])
                        return (STGv[:, sl, 0, :], STGv[:, sl, 1, :], STGv[:, sl, 2, :], STGv[:, sl, 3, :], STGv[:, sl, 4, :], [sk])
                    outs = [(slice(n_, n_ + 1), slice(TOK + 4 * b + t, TOK + 4 * b + t + 1)) for n_, t in enumerate(ts)]
                    units.append(attn_unit_gen(g, prep, None, None, None, None, None, 's', outs, None))
            pipeline(units, UDEPTH)

        def attn_combine(slot):
            lk = [('lmtok', 0), ('lmtok', 1), ('lmtok', 2)]
            K = lambda a_: ('cmb', a_)
            Cv = lambda a_: CMB[:, a_ * 17:(a_ + 1) * 17]
            DMA('sp', 'cml', [(Cv(2 * g + k_), lm_d[g, k_].rearrange("(p c) -> p c", p=128)) for g in range(3) for k_ in range(2)],
                w=lk + [K(i) for i in range(6)])
            for g in range(3):
                act(Cv(6 + g), Cv(2 * g), AF.Ln, [K(2 * g)], [K(6 + g)])
                tt(Cv(6 + g), Cv(6 + g), Cv(2 * g + 1), ALU.add, [K(6 + g), K(2 * g + 1)], [K(6 + g)])
            tt(Cv(9), Cv(6), Cv(7), ALU.max, [K(6), K(7)], [K(9)])
            tt(Cv(9), Cv(9), Cv(8), ALU.max, [K(9), K(8)], [K(9)])
            for g in range(3):
                tt(Cv(6 + g), Cv(6 + g), Cv(9), ALU.subtract, [K(6 + g), K(9)], [K(6 + g)])
                act(Cv(6 + g), Cv(6 + g), AF.Exp, [K(6 + g)], [K(6 + g)])
            tt(Cv(10), Cv(6), Cv(7), ALU.add, [K(6), K(7)], [K(10)])
            tt(Cv(10), Cv(10), Cv(8), ALU.add, [K(10), K(8)], [K(10)])
            A('dve', lambda e, Cv=Cv: e.reciprocal(Cv(10), Cv(10)), [K(10)], [K(10)])
            for g in range(3):
                A('dve', lambda e, Cv=Cv, g=g: e.reciprocal(Cv(2 * g), Cv(2 * g)), [K(2 * g)], [K(2 * g)])
                tt(Cv(6 + g), Cv(6 + g), Cv(10), ALU.mult, [K(6 + g), K(10)], [K(6 + g)])
                tt(Cv(6 + g), Cv(6 + g), Cv(2 * g), ALU.mult, [K(6 + g), K(2 * g)], [K(6 + g)])
            DMA('sp', 'cfo', [(coef_d[g].rearrange("(p c) -> p c", p=128), Cv(6 + g)) for g in range(3)],
                r=[K(6), K(7), K(8)], w=['coef_d'])
            BW = 192
            for bi, c0 in enumerate(range(0, NTOK, BW)):
                n = min(BW, NTOK - c0)
                sl = bi % 2
                Bv = lambda g, n=n, sl=sl: CMB[:, 192 + (sl * 3 + g) * BW:192 + (sl * 3 + g) * BW + n]
                bk = lambda g, sl=sl: ('cbc', sl, g)
                DMA('sp', 'cbl%d' % sl, [(Bv(g), coef_d[g:g + 1, c0:c0 + n].partition_broadcast(128)) for g in range(3)],
                    r=['coef_d'], w=[bk(0), bk(1), bk(2)])
                for g in range(3):
                    tt(Bv(g), Bv(g), OTG[g][:, c0:c0 + n], ALU.mult, [bk(g), ('OTG', g)], [bk(g)])
                tt(Bv(0), Bv(0), Bv(1), ALU.add, [bk(0), bk(1)], [bk(0)])
                tt(YCH[:, c0:c0 + n], Bv(0), Bv(2), ALU.add, [bk(0), bk(2)], ['YCH'])
            DMA('sp', 'ych', [(oatt_d[slot], YCH[:, :])], r=['YCH'], w=[('oatt_d', slot)])

        SSTv = SST[:, :].rearrange("p (h c) -> p h c", h=8)
        hslot = [0]
        ringn = [0]

        def hgrn_tile_gen(ws, h, cols_fn, need_out, outs):
            cnt = hslot[0]
            hslot[0] += 1
            s_ = cnt % 3
            p_ = cnt % 2
            n0 = ringn[0]
            ringn[0] += 2
            wv = WB[ws][:, :].rearrange("p (k n) -> p k n", k=16)
            Z, zk = (psA, 'psA') if p_ == 0 else (psF, 'psF')
            Pb, pbk = (psB, 'psB') if p_ == 0 else (psC, 'psC')
            Po, pok = (psD, 'psD') if p_ == 0 else (psE, 'psE')
            TP, tpk = (psT, 'psT') if p_ == 0 else (psU, 'psU')
            T = HT[:, s_ * 1536:(s_ + 1) * 1536]
            Bf = HB[:, s_ * 1024:(s_ + 1) * 1024]
            k_ = lambda n: ('h%s' % n, s_)
            f_, lf, kk, bs, eb, enb = (T[:, i * 128:(i + 1) * 128] for i in range(6))
            ekl, sq, sg, dif = (T[:, i * 128:(i + 1) * 128] for i in range(6, 10))
            e1, e2 = T[:, 1280:1408], T[:, 1408:1536]
            qe, ke, kl, vb, kl1 = (Bf[:, i * 128:(i + 1) * 128] for i in range(5))
            og = qe
            qkT = Bf[:, 640:896]
            attm = Bf[:, 896:1024]
            edec = ST3[:, 2 * s_:2 * s_ + 2]
            ssq = ST3[:, 8 + 2 * s_:10 + 2 * s_]
            SRv = lambda i: SR[:, (i % 4) * 128:(i % 4) * 128 + 128]
            SRBv = lambda i: SRB[:, (i % 4) * 128:(i % 4) * 128 + 128]
            rk = lambda i: ('SR', i % 4)
            rbk = lambda i: ('SRB', i % 4)
            mms([(Z[:, :], cols_fn(kc), wv[:, kc, :], kc == 0, kc == 15) for kc in range(16)], [('WB', ws)] + ALLH, [zk])
            yield
            act(f_, Z[:, 128:256], AF.Exp, [zk], [k_('f')], scale=-1.0)
            act(e1, Z[:, 0:128], AF.Exp, [zk], [k_('e1')], scale=-1.0)
            if need_out:
                act(e2, Z[:, 384:512], AF.Exp, [zk], [k_('e2')], scale=-1.0)
            tcopy(vb, Z[:, 256:384], [zk], [k_('v')], eng='act')
            tsc(f_, f_, 1.0, None, ALU.add, ALU.bypass, [k_('f')], [k_('f')])
            A('dve', lambda e: e.reciprocal(f_, f_), [k_('f')], [k_('f')])
            tt(f_, f_, LB[:, 1024 + h * 128:1024 + (h + 1) * 128], ALU.mult, [k_('f'), 'LB'], [k_('f')], eng='pool')
            tt(f_, f_, LB[:, h * 128:(h + 1) * 128], ALU.add, [k_('f'), 'LB'], [k_('f')], eng='pool')
            act(lf, f_, AF.Ln, [k_('f')], [k_('lf')])
            A('pool', lambda e: e.tensor_scalar(kk, f_, -1.0, 1.0, op0=ALU.mult, op1=ALU.add), [k_('f')], [k_('k')])
            tsc(e1, e1, 1.0, None, ALU.add, ALU.bypass, [k_('e1')], [k_('e1')])
            A('dve', lambda e: e.reciprocal(e1, e1), [k_('e1')], [k_('e1')])
            tt(sq, Z[:, 0:128], e1, ALU.mult, [zk, k_('e1')], [k_('sq')])
            if need_out:
                tsc(e2, e2, 1.0, None, ALU.add, ALU.bypass, [k_('e2')], [k_('e2')])
                A('dve', lambda e: e.reciprocal(e2, e2), [k_('e2')], [k_('e2')])
                tt(sg, Z[:, 384:512], e2, ALU.mult, [zk, k_('e2')], [k_('sg')])
            yield
            mms([(Pb[:, 0:128], tri2, lf, True, True), (Pb[:, 128:256], blk2, lf, True, True),
                 (Pb[:, 256:258], lf, ind2, True, True)], [k_('lf'), 'CON'], [pbk])
            yield
            tcopy(bs, Pb[:, 0:128], [pbk], [k_('bs')])
            act(edec, Pb[:, 256:258], AF.Exp, [pbk], [k_('edec')])
            tt(dif, Pb[:, 128:256], bs, ALU.subtract, [pbk, k_('bs')], [k_('dif')])
            act(eb, bs, AF.Exp, [k_('bs')], [k_('eb')])
            act(enb, bs, AF.Exp, [k_('bs')], [k_('enb')], scale=-1.0)
            act(ekl, dif, AF.Exp, [k_('dif')], [k_('ekl')])
            tt(qe, sq, eb, ALU.mult, [k_('sq'), k_('eb')], [k_('qe')])
            tt(ke, kk, enb, ALU.mult, [k_('k'), k_('enb')], [k_('ke')])
            stt(kl, kk, ind2[:, 0:1], ekl, ALU.mult, ALU.mult, [k_('k'), k_('ekl'), 'CON'], [k_('kl')])
            stt(kl1, kk, ind2[:, 1:2], ekl, ALU.mult, ALU.mult, [k_('k'), k_('ekl'), 'CON'], [k_('kl1')])
            yield
            trs([(TP[:, 512:640], qe, IDB[:, :]), (TP[:, 640:768], ke, IDB[:, :])], [k_('qe'), k_('ke'), 'IDB'], [tpk])
            yield
            tcopy(qkT, TP[:, 512:768], [tpk], [k_('qkT')])
            yield
            mms([(Po[:, 256:384], qkT[:, 128:256], qkT[:, 0:128], True, True),
                 (Po[:, 128:256], kl, vb, True, True),
                 (Po[:, 384:512], kl1, vb, True, True)], [k_('qkT'), k_('kl'), k_('kl1'), k_('v')], [pok])
            yield
            tt(attm, Po[:, 256:384], tri2, ALU.mult, [pok, 'CON'], [k_('attm')])
            stt(SRv(n0 + 1), SRv(n0), edec[:, 0:1], Po[:, 128:256], ALU.mult, ALU.add, [pok, k_('edec'), rk(n0)], [rk(n0 + 1)])
            stt(SRv(n0 + 2), SRv(n0 + 1), edec[:, 1:2], Po[:, 384:512], ALU.mult, ALU.add, [pok, k_('edec'), rk(n0 + 1)], [rk(n0 + 2)])
            if need_out:
                tcopy(SRBv(n0), SRv(n0), [rk(n0)], [rbk(n0)], eng='act')
                tcopy(SRBv(n0 + 1), SRv(n0 + 1), [rk(n0 + 1)], [rbk(n0 + 1)], eng='act')
            if not need_out:
                return
            yield
            mms([(Po[:, 0:128], attm, vb, True, False),
                 (Po[0:64, 0:128], qkT[:, 0:64], SRBv(n0), False, True),
                 (Po[64:128, 0:128], qkT[:, 64:128], SRBv(n0 + 1), False, True)],
                [k_('attm'), k_('v'), k_('qkT'), rbk(n0), rbk(n0 + 1)], [pok])
            yield
            act(dif, Po[:, 0:128], AF.Square, [pok], [k_('dif'), k_('ssq')], accum_out=ssq[:, 0:1])
            act(ssq[:, 1:2], ssq[:, 0:1], AF.Ln, [k_('ssq')], [k_('ssq2')], scale=1.0 / 128, bias=EPS)
            act(ssq[:, 1:2], ssq[:, 1:2], AF.Exp, [k_('ssq2')], [k_('rs')], scale=-0.5)
            stt(dif, Po[:, 0:128], ssq[:, 1:2], HGW[:, :], ALU.mult, ALU.mult, [pok, k_('rs'), 'HGW'], [k_('dif')])
            tt(og, dif, sg, ALU.mult, [k_('dif'), k_('sg')], [k_('qe')])
            yield
            trs([(TP[:, 768:896], og, IDB[:, :])], [k_('qe'), 'IDB'], [tpk])
            yield
            for (buf, bk, src, dst) in outs:
                tcopy(buf[:, dst], TP[:, 768:896][:, src], [tpk], [bk], eng='act')

        def hgrn_head_tiles(ws, h, nts, need_fn, outs_fn):
            ringn[0] = 0
            tcopy(SR[:, 0:128], SSTv[:, h, :], ['SST'], [('SR', 0)])
            pipeline((hgrn_tile_gen(ws, h, (lambda kc, nt=nt: hT[:, kc, nt * 128:(nt + 1) * 128]), need_fn(nt), outs_fn(nt))
                      for nt in nts), 3)
            nf = ringn[0] % 4
            tcopy(SSTv[:, h, :], SR[:, nf * 128:nf * 128 + 128], [('SR', nf)], ['SST'])

        def hgrn_misc(ws, h):
            wv = WB[ws][:, :].rearrange("p (k n) -> p k n", k=16)
            M = MS
            mms([(psA[0:M, :], hT[:, kc, TOK:TOK + MS], wv[:, kc, :], kc == 0, kc == 15) for kc in range(16)], [('WB', ws)] + ALLH, ['psA'])
            T = HT[0:M, 0:1536]
            Bf = HB[0:M, 0:1024]
            f_, lf, kk, bs, eb, enb = (T[:, i * 128:(i + 1) * 128] for i in range(6))
            ekl, sq, sg, dif = (T[:, i * 128:(i + 1) * 128] for i in range(6, 10))
            qe, ke, kl, vb, og = (Bf[:, i * 128:(i + 1) * 128] for i in range(5))
            DMA('sp', 's0', [(S0[:, :].rearrange("p (b c) -> p b c", b=4), sthg[:, h].rearrange("b d v -> d b v"))], w=['S0'])
            tcopy(S0B[:, :], S0[:, :], ['S0'], ['S0B'])
            act(f_, psA[0:M, 128:256], AF.Sigmoid, ['psA'], ['mf'])
            tt(f_, f_, LB[0:M, 1024 + h * 128:1024 + (h + 1) * 128], ALU.mult, ['mf', 'LB'], ['mf'])
            tt(f_, f_, LB[0:M, h * 128:(h + 1) * 128], ALU.add, ['mf', 'LB'], ['mf'])
            act(lf, f_, AF.Ln, ['mf'], ['mlf'])
            tsc(kk, f_, -1.0, 1.0, ALU.mult, ALU.add, ['mf'], ['mk'])
            mms([(psB[0:M, 0:128], tri_m, lf, True, True), (psB[0:M, 128:256], blk_m, lf, True, True),
                 (psB[:, 256:260], lf, ind_m, True, True)], ['mlf', 'CON'], ['psB'])
            tcopy(bs, psB[0:M, 0:128], ['psB'], ['mbs'])
            act(eb, bs, AF.Exp, ['mbs'], ['meb'])
            act(enb, bs, AF.Exp, ['mbs'], ['menb'], scale=-1.0)
            tt(dif, psB[0:M, 128:256], bs, ALU.subtract, ['psB', 'mbs'], ['mdif'])
            act(ekl, dif, AF.Exp, ['mdif'], ['mekl'])
            edec = ST[:, 40:44]
            act(edec, psB[:, 256:260], AF.Exp, ['psB'], ['medec'])
            act(sq, psA[0:M, 0:128], AF.Silu, ['psA'], ['msq'])
            act(sg, psA[0:M, 384:512], AF.Silu, ['psA'], ['msg'])
            tt(qe, sq, eb, ALU.mult, ['msq', 'meb'], ['mqe'])
            tt(ke, kk, enb, ALU.mult, ['mk', 'menb'], ['mke'])
            tt(kl, kk, ekl, ALU.mult, ['mk', 'mekl'], ['mkl'])
            tcopy(vb, psA[0:M, 256:384], ['psA'], ['mv'], eng='act')
            trs([(psT[:, 512:512 + M], qe, IDB[0:M, 0:M]), (psT[:, 640:640 + M], ke, IDB[0:M, 0:M])], ['mqe', 'mke', 'IDB'], ['psTh'])
            qeT = HB[:, 3072:3072 + M]
            keT = HB[:, 3104:3104 + M]
            qeTm = HB[:, 3200:3200 + 4 * M].rearrange("p (b t) -> p b t", b=4)
            klm = HB[0:M, 3328:3328 + 512].rearrange("p (b c) -> p b c", b=4)
            tcopy(qeT, psT[:, 512:512 + M], ['psTh'], ['mqeT'])
            tcopy(keT, psT[:, 640:640 + M], ['psTh'], ['mkeT'])
            tt(qeTm, qeT[:, None, :].broadcast_to([128, 4, M]), colm.rearrange("p (b t) -> p b t", b=4), ALU.mult, ['mqeT', 'CON'], ['mqeTm'])
            tt(klm, kl[:, None, :].broadcast_to([M, 4, 128]), rowm[:, :, None].broadcast_to([M, 4, 128]), ALU.mult, ['mkl', 'CON'], ['mklm'])
            mms([(psD[0:M, 256:256 + M], keT, qeT, True, True)], ['mqeT', 'mkeT'], ['psDh'])
            attm = HB[0:M, 896:896 + M]
            tt(attm, psD[0:M, 256:256 + M], cmask_m, ALU.mult, ['psDh', 'CON'], ['mattm'])
            lst = [(psD[0:M, 0:128], attm, vb, True, False)]
            for b in range(4):
                lst.append((psD[0:M, 0:128], qeTm[:, b, :], S0B[:, b * 128:(b + 1) * 128], False, b == 3))
            mms(lst, ['mattm', 'mv', 'mqeTm', 'S0B'], ['psDo'])
            mms([(psE[:, b * 128:(b + 1) * 128], klm[:, b, :], vb, True, True) for b in range(4)], ['mklm', 'mv'], ['psE'])
            S0v = S0[:, :].rearrange("p (b c) -> p b c", b=4)
            tt(S0v, S0v, edec[:, :, None].broadcast_to([128, 4, 128]), ALU.mult, ['S0', 'medec', 'S0B'], ['S0'])
            tt(S0[:, :], S0[:, :], psE[:, :], ALU.add, ['S0', 'psE'], ['S0'])
            DMA('sp', 's0o', [(hgs[:, h].rearrange("b d v -> d b v"), S0v)], r=['S0'])
            act(dif, psD[0:M, 0:128], AF.Square, ['psDo'], ['mdif', 'mssq'], accum_out=ST[0:M, 36:37])
            act(ST[0:M, 37:38], ST[0:M, 36:37], AF.Sqrt, ['mssq'], ['mssq2'], scale=1.0 / 128, bias=EPS)
            A('dve', lambda e: e.reciprocal(ST[0:M, 38:39], ST[0:M, 37:38]), ['mssq2'], ['mrs'])
            stt(dif, psD[0:M, 0:128], ST[0:M, 38:39], HGW[0:M, :], ALU.mult, ALU.mult, ['psDo', 'mrs', 'HGW'], ['mdif'])
            tt(og, dif, sg, ALU.mult, ['mdif', 'msg'], ['mog'])
            trs([(psT[:, 768:768 + M], og, IDB[0:M, 0:M])], ['mog', 'IDB'], ['psTg'])
            tcopy(OHGH[:, TOK:TOK + 16], psT[:, 768:784], ['psTg'], ['OHGH'], eng='act')

        A('pool', lambda e: e.memset(SST[:, :], 0.0), [], ['SST'])
        A('pool', lambda e: e.memset(SBF[:, :], 0.0), [], ['SBF'])
        ONEROW = lsb("ONEROW", [1, 2176])
        A('pool', lambda e: e.memset(ONEROW[:, :], 1.0), [], ['ONEROW'])
        DMA('sp', 'lmi', [(lm_d[g, k_:k_ + 1, :], ONEROW[:, :]) for g in range(3) for k_ in range(2)], r=['ONEROW'],
            w=[('lmtok', 0), ('lmtok', 1), ('lmtok', 2)])
        phase_end()
        alloc_norm()
        pipeline((norm_gen(xall[nt * 128:(nt + 1) * 128, :], 128, 0, nt * 128, 128, False) for nt in range(16)), 3)
        phase_end()
        alloc_attn()
        for hh in range(12):
            attn_head(hh, 'H')
        phase_end()
        alloc_hgrn()
        OHGPv = OHGPRE[:, :].rearrange("p (h c) -> p h c", h=8)
        for h in range(8):
            ws = wload(whg_t[h], 8192)
            hgrn_head_tiles(ws, h, range(16), lambda nt: nt == 15,
                            lambda nt, h=h: [(OHGPRE, 'OHGPRE', slice(126, 128), slice(2 * h, 2 * h + 2))])
            tsc(SSTv[:, h, :], SSTv[:, h, :], CVAL[:, 0:1], None, ALU.mult, ALU.bypass, ['SST', 'CVAL'], ['SST'])
        phase_end()
        alloc_norm()
        pipeline([norm_gen(xall[(16 + nt) * 128:(17 + nt) * 128, :], 128, 0, nt * 128, 128, False) for nt in range(16)]
                 + [norm_gen(xall[32 * 128:32 * 128 + MS, :], MS, 0, TOK, MS, True)], 3)
        phase_end()
        alloc_attn()
        for slot in range(4):
            for g in range(3):
                attn_head(4 * g + slot, 'O')
            attn_combine(slot)
        phase_end()
        alloc_hgrn()
        for h in range(8):
            ws = wload(whg_t[h], 8192)
            hgrn_head_tiles(ws, h, range(16), lambda nt: True,
                            lambda nt: [(OHGH, 'OHGH', slice(0, 128), slice(nt * 128, (nt + 1) * 128))])
            DMA('sp', 'hgp', [(hgp[h], SSTv[:, h, :])], r=['SST'])
            S.fence()
            hgrn_misc(ws, h)
            S.fence()
            tcopy(OHGH[:, TOK + 16:TOK + 18], OHGPv[:, h, :], ['OHGPRE'], ['OHGH'])
            DMA('sp', 'ohgo', [(ohg_d[h], OHGH[:, :])], r=['OHGH'], w=[('ohg_d', h)])
        phase_end()
        BIGB = lsb("BIGB", [128, 12 * NTOK], BF16)
        YCH = lsb("YCHb", [128, NTOK], BF16)
        GT = lsb("GT", [128, 2 * 512])
        OAT = BIGB[:, :].rearrange("p (k t) -> p k t", k=12)
        DMA('sp', 'oal', [(OAT[:, s_, :], oatt_d[s_]) for s_ in range(4)] + [(OAT[:, 4 + h, :], ohg_d[h]) for h in range(8)], w=['OAT'])
        for fc in range(16):
            ws = wload(wB1_t[fc], 44 * 128)
            wv = WB[ws][:, 0:44 * 128].rearrange("p (k n) -> p k n", k=44)
            for bi, c0 in enumerate(range(0, NTOK, 512)):
                n = min(512, NTOK - c0)
                lst = []
                for kc in range(16):
                    lst.append((psA[:, 0:n], wv[:, kc, :], hT[:, kc, c0:c0 + n], kc == 0, kc == 15))
                for kc in range(16):
                    lst.append((psB[:, 0:n], wv[:, 16 + kc, :], hT[:, kc, c0:c0 + n], kc == 0, kc == 15))
                for kc in range(4):
                    lst.append((psC[:, 0:n], wv[:, 32 + kc, :], OAT[:, kc, c0:c0 + n], kc == 0, kc == 3))
                for kc in range(8):
                    lst.append((psD[:, 0:n], wv[:, 36 + kc, :], OAT[:, 4 + kc, c0:c0 + n], kc == 0, kc == 7))
                mms(lst, [('WB', ws), 'OAT'] + ALLH, ['psA', 'psB', 'psC', 'psD'])
                act(GT[:, 0:n], psA[:, 0:n], AF.Sigmoid, ['psA'], ['GT0'])
                act(GT[:, 512:512 + n], psB[:, 0:n], AF.Sigmoid, ['psB'], ['GT1'])
                tt(GT[:, 0:n], GT[:, 0:n], psC[:, 0:n], ALU.mult, ['GT0', 'psC'], ['GT0'])
                tt(GT[:, 512:512 + n], GT[:, 512:512 + n], psD[:, 0:n], ALU.mult, ['GT1', 'psD'], ['GT1'])
                tt(YCH[:, c0:c0 + n], GT[:, 0:n], GT[:, 512:512 + n], ALU.add, ['GT0', 'GT1'], ['YCH'])
            DMA('sp', 'ych', [(ymix_d[fc], YCH[:, :])], r=['YCH'], w=[('ymix_d', fc)])
        phase_end()
        GT = lsb("GT2", [128, 512])
        GBC = lsb("GBC", [128, 512])
        GBM = lsb("GBM", [MS, 512])
        XQ = [lsb("XQ0", [128, 512]), lsb("XQ1", [128, 512])]
        ymT = BIGA[:, :].rearrange("p (k t) -> p k t", k=16)
        DMA('sp', 'yml', [(ymT[:, fc, :], ymix_d[fc]) for fc in range(16)], w=['ymT'])

        def gate_bc(off, cb):
            DMA('sp', 'mrl', [(MR[:, :], modr_d[:, off + cb * 512:off + (cb + 1) * 512])], r=['modr_d'], w=['MR'])
            mms([(psE[:, :], ones_f[0:1, :], MR[0:1, :], True, True),
                 (psF[0:MS, :], rowsel, MR[0:5, :], True, True)], ['MR', 'CON'], ['psE', 'psF'])
            tcopy(GBC[:, :], psE[:, :], ['psE'], ['GBC'])
            tcopy(GBM[:, :], psF[0:MS, :], ['psF'], ['GBM'], eng='act')

        for cb in range(4):
            ws = wload(wo_t[cb], 8192)
            wv = WB[ws][:, :].rearrange("p (k n) -> p k n", k=16)
            gate_bc(0, cb)
            for tl in range(17):
                M = 128 if tl < 16 else MS
                P_ = psA if tl % 2 == 0 else psB
                pk = 'psA' if tl % 2 == 0 else 'psB'
                mms([(P_[0:M, :], ymT[:, kc, tl * 128:tl * 128 + M], wv[:, kc, :], kc == 0, kc == 15) for kc in range(16)],
                    [('WB', ws), 'ymT'], [pk])
                xq = XQ[tl % 2]
                xk = 'XQ%d' % (tl % 2)
                srow = (16 + tl) * 128
                DMA('sp', 'xq%d' % (tl % 2), [(xq[0:M, :], xall[srow:srow + M, cb * 512:(cb + 1) * 512])], w=[xk])
                G_ = GBC if tl < 16 else GBM
                tt(GT[0:M, 0:512], P_[0:M, :], G_[0:M, :], ALU.mult, [pk, 'GBC', 'GBM'], ['GT0'])
                tt(xq[0:M, :], xq[0:M, :], GT[0:M, 0:512], ALU.add, [xk, 'GT0'], [xk])
                DMA('sp', 'x1o%d' % (tl % 2), [(x1_d[tl * 128:tl * 128 + M, cb * 512:(cb + 1) * 512], xq[0:M, :])], r=[xk], w=[('x1_d', tl)])
        phase_end()
        alloc_norm()
        pipeline([norm_gen(x1_d[tl * 128:(tl + 1) * 128, :], 128, 1, tl * 128, 128, False) for tl in range(16)]
                 + [norm_gen(x1_d[16 * 128:16 * 128 + MS, :], MS, 1, TOK, MS, True)], 3)
        phase_end()
        BIGB = lsb("YTB", [128, NJ * 512], BF16)
        GT = lsb("GT3", [128, 512])
        GBC = lsb("GBC3", [128, 512])
        GBM = lsb("GBM3", [MS, 512])
        XQ = [lsb("XQ03", [128, 512]), lsb("XQ13", [128, 512])]
        ACAT = lsb("ACAT", [128, 520])
        UU = lsb("UU", [128, 512])
        CROW = RTMP[0:8, :]
        YT = BIGB[:, 0:NJ * 512].rearrange("p (j t) -> p j t", j=NJ)
        CWv = CW[:, :].rearrange("p (a j) -> p a j", a=4)
        CARv = CARRY[:, :].rearrange("p (j c) -> p j c", c=2)
        SCVv = SCV[:, :].rearrange("p (j c) -> p j c", c=8)
        ASVv = ASV[:, :].rearrange("p (j c) -> p j c", c=8)
        for j0 in range(0, NJ, 4):
            DMA('sp', 'crw', [(CROW[:, :], stconv[:, j0 * 128:(j0 + 4) * 128])], w=['CROW'])
            mms([(psE[:, (j - j0) * 8:(j - j0) * 8 + 8], CROW[0:8, (j - j0) * 128:(j - j0 + 1) * 128], identf[0:8, 0:8], True, True) for j in range(j0, j0 + 4)],
                ['CROW', 'CON'], ['psE'])
            tcopy(SCV[:, j0 * 8:(j0 + 4) * 8], psE[:, 0:32], ['psE'], ['SCV'])
        blocks = [('m', TOK, MS)] + [('o', b * 512, 512) for b in range(4)]
        for (kind, c0, n) in blocks:
            for j in range(NJ):
                ws = wload(wab_t[j], 4096)
                wv = WB[ws][:, 0:4096].rearrange("p (k n) -> p k n", k=32)
                s_ = j & 1
                Pa = psA if s_ == 0 else psC
                Pb = psB if s_ == 0 else psD
                pak = 'psA' if s_ == 0 else 'psC'
                pbk = 'psB' if s_ == 0 else 'psD'
                lst = [(Pa[:, 0:n], wv[:, kc, :], hT[:, kc, c0:c0 + n], kc == 0, kc == 15) for kc in range(16)]
                lst += [(Pb[:, 0:n], wv[:, 16 + kc, :], hT[:, kc, c0:c0 + n], kc == 0, kc == 15) for kc in range(16)]
                mms(lst, [('WB', ws)] + ALLH, [pak, pbk])
                ac = ACAT[:, 0:520]
                u = UU[:, 0:512]
                ack = ('ac', 0)
                uk = ('u', 0)
                w0, w1, w2, cb_ = (CWv[:, a, j:j + 1] for a in range(4))
                if kind == 'o':
                    tcopy(ac[:, 0:2], CARv[:, j, :], ['CARRY'], [ack])
                    tcopy(ac[:, 2:2 + n], Pa[:, 0:n], [pak], [ack], eng='act')
                    tsc(u[:, 0:n], ac[:, 2:2 + n], w2, cb_, ALU.mult, ALU.add, [ack, 'CW'], [uk])
                    stt(u[:, 0:n], ac[:, 1:1 + n], w1, u[:, 0:n], ALU.mult, ALU.add, [ack, 'CW', uk], [uk])
                    stt(u[:, 0:n], ac[:, 0:n], w0, u[:, 0:n], ALU.mult, ALU.add, [ack, 'CW', uk], [uk])
                    tcopy(CARv[:, j, :], ac[:, n:n + 2], [ack], ['CARRY'])
                else:
                    acv = ac[:, 0:24].rearrange("p (b t) -> p b t", b=4)
                    tcopy(acv[:, :, 0:2], SCVv[:, j, :].rearrange("p (b t) -> p b t", b=4), ['SCV'], [ack])
                    tcopy(acv[:, :, 2:6], Pa[:, 0:16].rearrange("p (b t) -> p b t", b=4), [pak], [ack])
                    uv = u[:, 0:16].rearrange("p (b t) -> p b t", b=4)
                    tsc(uv, acv[:, :, 2:6], w2, cb_, ALU.mult, ALU.add, [ack, 'CW'], [uk])
                    stt(uv, acv[:, :, 1:5], w1, uv, ALU.mult, ALU.add, [ack, 'CW', uk], [uk])
                    stt(uv, acv[:, :, 0:4], w0, uv, ALU.mult, ALU.add, [ack, 'CW', uk], [uk])
                    A('pool', lambda e, u=u: e.memset(u[:, 16:32], 0.0), [uk], [uk])
                    tcopy(ASVv[:, j, :].rearrange("p (b t) -> p b t", b=4), acv[:, :, 4:6], [ack], ['ASV'])
                    tsc(CARv[:, j, :], Pa[:, 16:18], CVAL[:, 0:1], None, ALU.mult, ALU.bypass, [pak, 'CVAL'], ['CARRY'])
                act(u[:, 0:n], u[:, 0:n], AF.Silu, [uk], [uk])
                tt(YT[:, j, 0:n], u[:, 0:n], Pb[:, 0:n], ALU.mult, [uk, pbk], ['YT'])
            ntl = (n + 127) // 128
            for cb in range(4):
                gate_bc(D, cb)
                for jg in range(4):
                    ws = wload(wd_t[cb * 4 + jg], 11 * 512)
                    wv = WB[ws][:, 0:11 * 512].rearrange("p (k n) -> p k n", k=11)
                    Ps = [psA, psB, psC, psD]
                    for tl in range(ntl):
                        M = min(128, n - tl * 128)
                        mms([(Ps[tl][0:M, :], YT[:, jg * 11 + jj, tl * 128:tl * 128 + M], wv[:, jj, :], jg == 0 and jj == 0, jg == 3 and jj == 10)
                             for jj in range(11)], [('WB', ws), 'YT'], ['psA', 'psB', 'psC', 'psD'][tl:tl + 1])
                for tl in range(ntl):
                    M = min(128, n - tl * 128)
                    gt = (c0 // 128 + tl)
                    xq = XQ[tl % 2]
                    xk = 'XQ%d' % (tl % 2)
                    pk = ['psA', 'psB', 'psC', 'psD'][tl]
                    DMA('sp', 'xq%d' % (tl % 2), [(xq[0:M, :], x1_d[gt * 128:gt * 128 + M, cb * 512:(cb + 1) * 512])], w=[xk])
                    G_ = GBC if kind == 'o' else GBM
                    tt(GT[0:M, 0:512], [psA, psB, psC, psD][tl][0:M, :], G_[0:M, :], ALU.mult, [pk, 'GBC', 'GBM'], ['GT0'])
                    tt(xq[0:M, :], xq[0:M, :], GT[0:M, 0:512], ALU.add, [xk, 'GT0'], [xk])
                    DMA('sp', 'x1o%d' % (tl % 2), [(x2_d[gt * 128:gt * 128 + M, cb * 512:(cb + 1) * 512], xq[0:M, :])], r=[xk], w=[('x2_d', gt)])
        for (src, dstd, nr, ch_) in ((ASVv, convs, 8, 'cvo0'), (CARv, convp, 2, 'cvo1')):
            for j0 in range(0, NJ, 4):
                mms([(psE[0:nr, (j - j0) * 128:(j - j0 + 1) * 128], src[:, j, :], identf, True, True) for j in range(j0, j0 + 4)],
                    ['ASV', 'CARRY', 'CON'], ['psE'])
                tcopy(CROW[0:nr, 0:512], psE[0:nr, :], ['psE'], ['CROW'])
                DMA('sp', ch_, [(dstd[:, j0 * 128:(j0 + 4) * 128], CROW[0:nr, 0:512])], r=['CROW'], w=['CROWo'])
        phase_end()
        NFW = lsb("NFW", [128, D])
        XNF2 = lsb("XNF2", [128, D], BF16)
        DMA('sp', 'nfw', [(NFW[:, :], normf.partition_broadcast(128))], w=['NFW'])
        for tl in range(17):
            M = 128 if tl < 16 else MS
            s_ = tl % 2
            X = XT[s_]
            xk = 'XT%d' % s_
            DMA('sp', 'x%d' % s_, [(X[0:M, :], x2_d[tl * 128:tl * 128 + M, :])], w=[xk])
            JK = XN if s_ == 0 else XNF2
            stc = ST[:, 48 + 3 * s_:51 + 3 * s_]
            act(JK[0:M, :], X[0:M, :], AF.Square, [xk], [('XNf', s_), ('STf', s_, 0)], accum_out=stc[0:M, 0:1])
            act(stc[0:M, 1:2], stc[0:M, 0:1], AF.Sqrt, [('STf', s_, 0)], [('STf', s_, 1)], scale=1.0 / D, bias=EPS)
            A('dve', lambda e, M=M, stc=stc: e.reciprocal(stc[0:M, 2:3], stc[0:M, 1:2]), [('STf', s_, 1)], [('STf', s_, 2)])
            stt(X[0:M, :], X[0:M, :], stc[0:M, 2:3], NFW[0:M, :], ALU.mult, ALU.mult, [xk, ('STf', s_, 2), 'NFW'], [xk])
            dst = y_own[tl * 128:(tl + 1) * 128, :] if tl < 16 else y_misc
            DMA('sp', 'yo%d' % s_, [(dst, X[0:M, :])], r=[xk])
        for g, (W, d) in enumerate(GROUPS):
            DMA('pool', 'kvc', [(kvs[g][b, 0:W - 4], ck[g][b, 4:W]) for b in range(4)])

        if stop is not None:
            S.ops = S.ops[:stop]
        chs = sorted({o['ch'] for o in S.ops if o['ch'] is not None})
        sems_ch = {ch: es.enter_context(nc.semaphore("c_" + ch)) for ch in chs}
        sems_eng = {e_: es.enter_context(nc.semaphore("e_" + e_)) for e_ in ('pe', 'act', 'dve', 'pool')}
        block = es.enter_context(nc.Block())
        S.emit(nc, block, sems_eng, sems_ch, chs)
        ph[0].close()
    return nc


def _tile_w(Wsub, kc):
    n = Wsub.shape[1]
    return np.ascontiguousarray(Wsub.reshape(kc, 128, n).transpose(1, 0, 2).reshape(128, kc * n))


def _consts():
    C = np.zeros((128, 2048), np.float32)
    C[:, 0:128] = np.eye(128)
    s = np.arange(128)[:, None]
    t = np.arange(128)[None, :]
    same = (s // 64) == (t // 64)
    C[:, 128:256] = (same & (s <= t))
    C[:, 256:384] = same
    C[:, 384] = (np.arange(128) < 64)
    C[:, 385] = (np.arange(128) >= 64)
    C[:, 512:576] = ((np.arange(128)[:, None] % 64) <= np.arange(64)[None, :])
    s = np.arange(32)[:, None]
    t = np.arange(32)[None, :]
    samem = ((s // 4) == (t // 4)) & (s < 16) & (t < 16)
    C[0:32, 640:672] = samem & (s <= t)
    C[0:32, 672:704] = samem
    for b in range(4):
        C[4 * b:4 * b + 4, 704 + b] = 1
        C[4 * b:4 * b + 4, 768 + b] = 1
        C[:, 896 + b * 32 + 4 * b:896 + b * 32 + 4 * b + 4] = 1
    C[0:32, 736:768] = samem & (s <= t)
    for m in range(32):
        C[(1 + m // 4) if m < 16 else 0, 1024 + m] = 1
    C[:, 1152:1280] = 1
    return C


def _rope(pos):
    half = 16
    inv = (np.float32(500000.0) ** (-np.arange(half, dtype=np.float32) * np.float32(2.0) / np.float32(32))).astype(np.float32)
    ang = pos.astype(np.float32)[:, None] * inv[None, :]
    return np.concatenate([np.cos(ang), np.sin(ang)], axis=1).astype(np.float32)


_NC = None


def kernel(x_prompt, x_sample, cache_kv_w128, cache_kv_w512, cache_kv_w2048, state_hgrn, state_conv,
           c_prompt, c_sample, w_ada, b_ada, norm1_w, w_in, hg_lb, hg_norm_w, w_pa, w_pb, w_o,
           norm2_w, w_ffn_a, w_ffn_b, conv_w, conv_b, w_ffn_down, norm_f_w):
    global _NC
    in_maps = _prep(x_prompt, x_sample, cache_kv_w128, cache_kv_w512, cache_kv_w2048, state_hgrn, state_conv,
                    c_prompt, c_sample, w_ada, b_ada, norm1_w, w_in, hg_lb, hg_norm_w, w_pa, w_pb, w_o,
                    norm2_w, w_ffn_a, w_ffn_b, conv_w, conv_b, w_ffn_down, norm_f_w)
    return _run(in_maps)


def _prep(x_prompt, x_sample, cache_kv_w128, cache_kv_w512, cache_kv_w2048, state_hgrn, state_conv,
          c_prompt, c_sample, w_ada, b_ada, norm1_w, w_in, hg_lb, hg_norm_w, w_pa, w_pb, w_o,
          norm2_w, w_ffn_a, w_ffn_b, conv_w, conv_b, w_ffn_down, norm_f_w):
    f = lambda a: np.asarray(a, dtype=np.float32)
    xp = f(x_prompt)[0]
    xs = f(x_sample)
    Win = f(w_in)[0]
    shared = {}
    shared["wada_t"] = np.stack([_tile_w(f(w_ada)[0][:, j * 512:(j + 1) * 512], 16) for j in range(24)])
    shared["bada"] = f(b_ada).reshape(1, -1)
    shared["vecA"] = np.concatenate([f(norm1_w)[0].reshape(16, 128), f(norm2_w)[0].reshape(16, 128)], 0)
    shared["normf"] = f(norm_f_w).reshape(1, -1)
    shared["watt_t"] = np.stack([_tile_w(np.concatenate([Win[:, hh * 128:(hh + 1) * 128], Win[:, 1536 + hh * 128:1536 + (hh + 1) * 128],
                                                          Win[:, 3072 + hh * 128:3072 + (hh + 1) * 128]], 1), 16) for hh in range(12)])
    shared["whg_t"] = np.stack([_tile_w(np.concatenate([Win[:, 4608 + k * 1024 + h * 128:4608 + k * 1024 + (h + 1) * 128] for k in range(4)], 1), 16)
                                for h in range(8)])
    Wpa, Wpb = f(w_pa)[0], f(w_pb)[0]
    shared["wB1_t"] = np.stack([np.concatenate([_tile_w(Win[:, 8704 + fc * 128:8704 + (fc + 1) * 128], 16),
                                                _tile_w(Win[:, 10752 + fc * 128:10752 + (fc + 1) * 128], 16),
                                                _tile_w(Wpa[:, fc * 128:(fc + 1) * 128], 4),
                                                _tile_w(Wpb[:, fc * 128:(fc + 1) * 128], 8)], 1) for fc in range(16)])
    Wo = f(w_o)[0]
    shared["wo_t"] = np.stack([_tile_w(Wo[:, cb * 512:(cb + 1) * 512], 16) for cb in range(4)])
    Wa, Wb, Wd = f(w_ffn_a)[0], f(w_ffn_b)[0], f(w_ffn_down)[0]
    shared["wab_t"] = np.stack([np.concatenate([_tile_w(Wa[:, j * 128:(j + 1) * 128], 16), _tile_w(Wb[:, j * 128:(j + 1) * 128], 16)], 1)
                                for j in range(NJ)])
    shared["wd_t"] = np.stack([_tile_w(Wd[jg * 11 * 128:(jg + 1) * 11 * 128, cb * 512:(cb + 1) * 512], 11)
                               for cb in range(4) for jg in range(4)])
    cw = f(conv_w)[0]
    shared["convw"] = np.concatenate([cw[0].reshape(NJ, 128), cw[1].reshape(NJ, 128), cw[2].reshape(NJ, 128),
                                      f(conv_b)[0].reshape(NJ, 128)], 0)
    shared["hglb"] = f(hg_lb)
    shared["hgnw"] = f(hg_norm_w).reshape(1, 128)
    shared["consts"] = _consts()
    i = np.arange(128)[:, None]
    ip = np.arange(256)[None, :]
    band = np.where((ip >= i) & (ip <= i + 128), 0.0, NEG).astype(np.float32)

    in_maps = []
    for c in range(NCORE):
        m = dict(shared)
        xall = np.zeros((33 * 128, D), np.float32)
        own0 = TOK * c

        def tokrow(p):
            return xp[p] if p >= 0 else np.zeros(D, np.float32)
        if c > 0:
            xall[0:2048] = xp[own0 - 2048:own0]
        xall[2048:4096] = xp[own0:own0 + TOK]
        mb = 32 * 128
        xall[mb:mb + 16] = xs[4 * c:4 * c + 4].reshape(16, D)
        pos_m = np.zeros(MS, np.int64)
        for b in range(4):
            for t in range(4):
                pos_m[4 * b + t] = 16384 + t
        for t in range(2):
            xall[mb + 16 + t] = tokrow(own0 - 2 + t)
            pos_m[16 + t] = own0 - 2 + t
        for g, (W, d) in enumerate(GROUPS):
            for t in range(2):
                p = own0 - 2 + t - 128 * d
                xall[mb + 18 + 2 * g + t] = tokrow(p)
                pos_m[18 + 2 * g + t] = p
        m["xall"] = xall
        m["c5"] = np.concatenate([f(c_prompt), f(c_sample)[4 * c:4 * c + 4]], 0)
        rt = np.zeros((3, 32, 128, 32), np.float32)
        for g, (W, d) in enumerate(GROUPS):
            nT = 16 // d
            for r in range(d):
                for T_ in range(-1, nT):
                    pos = own0 + r + d * (128 * T_ + np.arange(128))
                    rt[g, r * (nT + 1) + T_ + 1] = _rope(pos)
        m["rope_t"] = rt
        m["rope_m"] = _rope(pos_m)
        mh = band.copy()
        if c == 0:
            mh[:, 0:128] = NEG
        m["masks"] = np.concatenate([band, mh], 1)
        m["ck128"] = f(cache_kv_w128)[0, 4 * c:4 * c + 4]
        m["ck512"] = f(cache_kv_w512)[0, 4 * c:4 * c + 4]
        m["ck2048"] = f(cache_kv_w2048)[0, 4 * c:4 * c + 4]
        m["sthg"] = f(state_hgrn)[0, 4 * c:4 * c + 4]
        m["stconv"] = f(state_conv)[0, 4 * c:4 * c + 4].reshape(8, DFF)
        m["cvalid"] = np.full((128, 1), 0.0 if c == 0 else 1.0, np.float32)
        in_maps.append(m)
    return in_maps


def _run(in_maps):
    global _NC
    if _NC is None:
        _NC = build()
    res = run_bass_kernel_spmd(_NC, in_maps, core_ids=list(range(NCORE)))
    R = res.results
    y_prompt = np.concatenate([R[c]["y_own"] for c in range(NCORE)], 0)[None]
    y_sample = np.concatenate([R[c]["y_misc"][0:16].reshape(4, 4, D) for c in range(NCORE)], 0)
    L = R[NCORE - 1]
    outs = [y_prompt, y_sample,
            L["kvp128"][None, None], L["kvp512"][None, None], L["kvp2048"][None, None],
            L["hgp"][None, None], L["convp"].reshape(1, 1, 2, DFF)]
    for nm in ("kvs128", "kvs512", "kvs2048"):
        outs.append(np.concatenate([R[c][nm] for c in range(NCORE)], 0)[None])
    outs.append(np.concatenate([R[c]["hgs"] for c in range(NCORE)], 0)[None])
    outs.append(np.concatenate([R[c]["convs"].reshape(4, 2, DFF) for c in range(NCORE)], 0)[None])
    return tuple(np.ascontiguousarray(o, dtype=np.float32) for o in outs)
```
